# Optimizing a Trainium2 kernel written in Bass

```python
import jax, jax.numpy as jnp
from jax import lax
import numpy as np

D_MODEL = 1024
BATCH = 16
SEQ = 2048
DEPTH = 1

GRID_W = 64
CTX_LEN = 256

MLA_HEADS = 8
MLA_Q_RANK = 256
MLA_KV_RANK = 128
MLA_NOPE = 64
MLA_ROPE = 32
MLA_V = 64
MLA_WIDTH = MLA_HEADS * MLA_V

HG_HEADS = 4
HG_DK = 128
HG_DV = 128
HG_KW = HG_HEADS * HG_DK
HG_WIDTH = HG_HEADS * HG_DV

MIX_WIDTH = MLA_WIDTH + HG_WIDTH
D_FF = 4 * D_MODEL
ROPE_BASE = 10000.0
Q_BLOCK = 128
CHUNK = 64
EPS = 1e-6

IN_SPLITS = (MLA_Q_RANK, MLA_KV_RANK, MLA_ROPE,
             HG_KW, HG_KW, HG_KW, HG_WIDTH, HG_WIDTH)
IN_WIDTH = sum(IN_SPLITS)

kernel_name = "hymba_mla_hgrn2_dit_block"


def rmsnorm(x, g):
    xf = x.astype(jnp.float32)
    xf = xf * lax.rsqrt(jnp.mean(xf * xf, axis=-1, keepdims=True) + EPS)
    return xf.astype(x.dtype) * g


def modulate(h, shift, scale):
    return h * (1 + scale) + shift


def split_cols(p):
    out, start = [], 0
    for n in IN_SPLITS:
        out.append(p[..., start:start + n])
        start += n
    return out


def axial_angles(T):
    rows = T // GRID_W
    row = jnp.repeat(jnp.arange(rows), GRID_W).astype(jnp.float32)
    col = jnp.tile(jnp.arange(GRID_W), rows).astype(jnp.float32)
    n = MLA_ROPE // 4
    inv = ROPE_BASE ** (-jnp.arange(n, dtype=jnp.float32) / n)
    return row[:, None] * inv, col[:, None] * inv


def rotate_half_pairs(x, ang):
    n = x.shape[-1] // 2
    cos = jnp.cos(ang)[:, None, :].astype(x.dtype)
    sin = jnp.sin(ang)[:, None, :].astype(x.dtype)
    x1, x2 = x[..., :n], x[..., n:]
    return jnp.concatenate([x1 * cos - x2 * sin, x2 * cos + x1 * sin], axis=-1)


def rope2d(x, ang_r, ang_c):
    half = MLA_ROPE // 2
    return jnp.concatenate([rotate_half_pairs(x[..., :half], ang_r),
                            rotate_half_pairs(x[..., half:], ang_c)], axis=-1)


def mla_attend(q_nope, q_rope, k_nope, k_rope, v):
    scale = (MLA_NOPE + MLA_ROPE) ** -0.5
    s = (jnp.einsum('bqhd,bkhd->bhqk', q_nope, k_nope)
         + jnp.einsum('bqhr,bkr->bhqk', q_rope, k_rope))
    p = jax.nn.softmax(s.astype(jnp.float32) * scale, axis=-1).astype(v.dtype)
    return jnp.einsum('bhqk,bkhd->bqhd', p, v)


def gated_chunk_scan(q, log_f, k, v, s0):
    B, T, H, _ = q.shape
    DV = v.shape[-1]
    n = T // CHUNK

    def to_chunks(a):
        return a.astype(jnp.float32).reshape(B, n, CHUNK, H, a.shape[-1]).transpose(1, 0, 3, 2, 4)

    mask = jnp.tril(jnp.ones((CHUNK, CHUNK), dtype=bool))[:, :, None]

    def step(S, inp):
        qc, lfc, kc, vc = inp
        b = jnp.cumsum(lfc, axis=2)
        diff = b[:, :, :, None, :] - b[:, :, None, :, :]
        decay = jnp.exp(jnp.where(mask, diff, -jnp.inf))
        a = jnp.einsum('bhtk,bhtsk,bhsk->bhts', qc, decay, kc)
        o = (jnp.einsum('bhts,bhsv->bhtv', a, vc)
             + jnp.einsum('bhtk,bhkv->bhtv', qc * jnp.exp(b), S))
        b_last = b[:, :, -1:, :]
        S = (jnp.exp(b_last[:, :, 0, :])[..., None] * S
             + jnp.einsum('bhsk,bhsv->bhkv', kc * jnp.exp(b_last - b), vc))
        return S, o

    S, o = lax.scan(step, s0, (to_chunks(q), to_chunks(log_f), to_chunks(k), to_chunks(v)))
    o = o.transpose(1, 0, 3, 2, 4).reshape(B, T, H, DV)
    return o, S


def hgrn2_direction(q, zf, i, qc, zfc, ic, lb, reverse):
    def gates(z):
        sig_pos = jax.nn.sigmoid(z.astype(jnp.float32))
        log_f = jnp.log(lb + (1 - lb) * sig_pos)
        k = (1 - lb) * jax.nn.sigmoid(-z.astype(jnp.float32))
        return log_f, k

    lf, k = gates(zf)
    lfc, kc = gates(zfc)
    if reverse:
        q, lf, k, i = [jnp.flip(a, axis=1) for a in (q, lf, k, i)]
        qc, lfc, kc, ic = [jnp.flip(a, axis=1) for a in (qc, lfc, kc, ic)]
    B = q.shape[0]
    s0 = jnp.zeros((B, HG_HEADS, HG_DK, HG_DV), jnp.float32)
    o_c, s_c = gated_chunk_scan(qc, lfc, kc, ic, s0)
    o, _ = gated_chunk_scan(q, lf, k, i, s_c)
    if reverse:
        o, o_c = jnp.flip(o, axis=1), jnp.flip(o_c, axis=1)
    return o, o_c


def mixer(h, hc, ang_r, ang_c, w_in, q_norm, w_uq, kv_norm, w_ukv, lb, hg_norm, w_out, need_ctx):
    B, T, _ = h.shape
    L = hc.shape[1]
    cq, ckv, kr, hq, hf, hb, hi, hg = split_cols(h @ w_in)
    ccq, cckv, ckr, chq, chf, chb, chi, chg = split_cols(hc @ w_in)

    def mla_qkv(cq_, ckv_, kr_, S):
        q = (rmsnorm(cq_, q_norm) @ w_uq).reshape(B, S, MLA_HEADS, MLA_NOPE + MLA_ROPE)
        kv = (rmsnorm(ckv_, kv_norm) @ w_ukv).reshape(B, S, MLA_HEADS, MLA_NOPE + MLA_V)
        return q[..., :MLA_NOPE], q[..., MLA_NOPE:], kv[..., :MLA_NOPE], kv[..., MLA_NOPE:], kr_

    q_nope, q_rope, k_nope, v, k_rope = mla_qkv(cq, ckv, kr, T)
    q_rope = rope2d(q_rope, ang_r, ang_c)
    k_rope = rope2d(k_rope[:, :, None, :], ang_r, ang_c)[:, :, 0, :]
    cq_nope, cq_rope, ck_nope, cv, ck_rope = mla_qkv(ccq, cckv, ckr, L)

    k_all = jnp.concatenate([k_nope, ck_nope], axis=1)
    kr_all = jnp.concatenate([k_rope, ck_rope], axis=1)
    v_all = jnp.concatenate([v, cv], axis=1)

    nb = T // Q_BLOCK

    def blocks(a):
        return a.reshape(B, nb, Q_BLOCK, *a.shape[2:]).swapaxes(0, 1)

    o_mla = lax.map(lambda qs: mla_attend(qs[0], qs[1], k_all, kr_all, v_all),
                    (blocks(q_nope), blocks(q_rope)))
    o_mla = o_mla.swapaxes(0, 1).reshape(B, T, MLA_WIDTH)

    def hg_heads(a, S, d):
        return a.reshape(B, S, HG_HEADS, d)

    q_h, cq_h = jax.nn.silu(hg_heads(hq, T, HG_DK)), jax.nn.silu(hg_heads(chq, L, HG_DK))
    i_h, ci_h = hg_heads(hi, T, HG_DV), hg_heads(chi, L, HG_DV)
    lb_f = lb[0].reshape(HG_HEADS, HG_DK)
    lb_b = lb[1].reshape(HG_HEADS, HG_DK)
    o_f, oc_f = hgrn2_direction(q_h, hg_heads(hf, T, HG_DK), i_h, cq_h, hg_heads(chf, L, HG_DK), ci_h, lb_f, False)
    o_b, oc_b = hgrn2_direction(q_h, hg_heads(hb, T, HG_DK), i_h, cq_h, hg_heads(chb, L, HG_DK), ci_h, lb_b, True)
    o_hg = rmsnorm(o_f + o_b, hg_norm) * jax.nn.silu(hg_heads(hg, T, HG_DV).astype(jnp.float32))
    o_hg = o_hg.reshape(B, T, HG_WIDTH).astype(h.dtype)

    y = jnp.concatenate([o_mla, o_hg], axis=-1) @ w_out

    if not need_ctx:
        return y, None
    oc_mla = mla_attend(cq_nope, cq_rope, ck_nope, ck_rope, cv).reshape(B, L, MLA_WIDTH)
    oc_hg = rmsnorm(oc_f + oc_b, hg_norm) * jax.nn.silu(hg_heads(chg, L, HG_DV).astype(jnp.float32))
    oc_hg = oc_hg.reshape(B, L, HG_WIDTH).astype(hc.dtype)
    yc = jnp.concatenate([oc_mla, oc_hg], axis=-1) @ w_out
    return y, yc


def sq_relu_mlp(h, w1, w2):
    return jnp.square(jax.nn.relu(h @ w1)) @ w2


def setup_inputs(seed: int = 0) -> dict:
    key = jax.random.key(seed)
    ks = jax.random.split(key, 20)
    f32 = jnp.float32

    def nrm(k, shape, scale):
        return jax.random.normal(k, shape, f32) * scale

    def gain(k, shape):
        return 1.0 + 0.02 * jax.random.normal(k, shape, f32)

    return {
        "x": nrm(ks[0], (BATCH, SEQ, D_MODEL), 1.0),
        "c": nrm(ks[1], (BATCH, D_MODEL), 1.0),
        "ctx": nrm(ks[2], (BATCH, CTX_LEN, D_MODEL), 1.0),
        "c_ctx": nrm(ks[3], (D_MODEL,), 1.0),
        "w_ada": nrm(ks[4], (DEPTH, D_MODEL, 6 * D_MODEL), 0.01),
        "b_ada": nrm(ks[5], (DEPTH, 6 * D_MODEL), 0.01),
        "norm_mix": gain(ks[6], (DEPTH, D_MODEL)),
        "w_in": nrm(ks[7], (DEPTH, D_MODEL, IN_WIDTH), D_MODEL ** -0.5),
        "q_norm": gain(ks[8], (DEPTH, MLA_Q_RANK)),
        "w_uq": nrm(ks[9], (DEPTH, MLA_Q_RANK, MLA_HEADS * (MLA_NOPE + MLA_ROPE)), MLA_Q_RANK ** -0.5),
        "kv_norm": gain(ks[10], (DEPTH, MLA_KV_RANK)),
        "w_ukv": nrm(ks[11], (DEPTH, MLA_KV_RANK, MLA_HEADS * (MLA_NOPE + MLA_V)), MLA_KV_RANK ** -0.5),
        "hgrn_lb": nrm(ks[12], (DEPTH + 1, 2, HG_KW), 0.1),
        "hgrn_norm": gain(ks[13], (DEPTH, HG_DV)),
        "w_out": nrm(ks[14], (DEPTH, MIX_WIDTH, D_MODEL), MIX_WIDTH ** -0.5),
        "norm_mlp": gain(ks[15], (DEPTH, D_MODEL)),
        "w_mlp_in": nrm(ks[16], (DEPTH, D_MODEL, D_FF), D_MODEL ** -0.5),
        "w_mlp_out": nrm(ks[17], (DEPTH, D_FF, D_MODEL), D_FF ** -0.5),
        "final_norm": gain(ks[18], (D_MODEL,)),
    }


def reference(x, c, ctx, c_ctx, w_ada, b_ada, norm_mix, w_in, q_norm, w_uq, kv_norm, w_ukv,
              hgrn_lb, hgrn_norm, w_out, norm_mlp, w_mlp_in, w_mlp_out, final_norm):
    T = x.shape[1]
    ang_r, ang_c = axial_angles(T)
    lb_all = jnp.cumsum(jax.nn.softmax(hgrn_lb.astype(jnp.float32), axis=0), axis=0)
    s_lat = jax.nn.silu(c)
    s_ctx = jax.nn.silu(c_ctx)
    z = ctx
    for l in range(DEPTH):
        need_ctx = l < DEPTH - 1
        mod = s_lat @ w_ada[l] + b_ada[l]
        mod_c = s_ctx @ w_ada[l] + b_ada[l]
        sh1, sc1, g1, sh2, sc2, g2 = [m[:, None, :] for m in jnp.split(mod, 6, axis=-1)]
        csh1, csc1, cg1, csh2, csc2, cg2 = jnp.split(mod_c, 6, axis=-1)

        h = modulate(rmsnorm(x, norm_mix[l]), sh1, sc1)
        hc = modulate(rmsnorm(z, norm_mix[l]), csh1, csc1)
        y, yc = mixer(h, hc, ang_r, ang_c, w_in[l], q_norm[l], w_uq[l], kv_norm[l], w_ukv[l],
                      lb_all[l], hgrn_norm[l], w_out[l], need_ctx)
        x = x + g1 * y
        x = x + g2 * sq_relu_mlp(modulate(rmsnorm(x, norm_mlp[l]), sh2, sc2), w_mlp_in[l], w_mlp_out[l])
        if need_ctx:
            z = z + cg1 * yc
            z = z + cg2 * sq_relu_mlp(modulate(rmsnorm(z, norm_mlp[l]), csh2, csc2), w_mlp_in[l], w_mlp_out[l])
    return rmsnorm(x, final_norm)
```

```python
import numpy as np
import ml_dtypes
import concourse.bass as bass
import concourse.mybir as mybir
from concourse.bass_utils import run_bass_kernel_spmd

F32 = mybir.dt.float32
BF16 = mybir.dt.bfloat16
AF = mybir.ActivationFunctionType
ALU = mybir.AluOpType

NB = 2
T = 2048
L = 256
TE = T + L
D = 1024
NT = TE // 128
DFF = 4096
EPS = 1e-6
BLKS = [(0, 256)] + [(256 + 512 * j, 512) for j in range(4)]
ATT_SCALE = float(96 ** -0.5)
DMA_K = 8


class _Op:
    __slots__ = ("idx", "eng", "fn", "dma", "deps", "signal", "tick", "sem_i", "sem_v")

    def __init__(self, idx, eng, fn, dma, deps):
        self.idx, self.eng, self.fn, self.dma, self.deps = idx, eng, fn, dma, deps
        self.signal = False
        self.tick = 0
        self.sem_i = 0
        self.sem_v = 0


class Sched:
    ENGS = ("pe", "act", "dve", "pool", "sp")

    def __init__(self, nc):
        self.nc = nc
        self.ops = []
        self.last_w = {}
        self.readers = {}
        self.dma_ops = {"sp": [], "pool": [], "act": []}

    def add(self, eng, fn, r=(), w=(), dma=False):
        idx = len(self.ops)
        deps = {}
        for k in r:
            p = self.last_w.get(k)
            if p is not None:
                deps[p] = "raw"
        for k in w:
            p = self.last_w.get(k)
            if p is not None and p not in deps:
                deps[p] = "waw"
            for q in self.readers.get(k, ()):
                if q not in deps:
                    deps[q] = "war"
        if dma:
            lst = self.dma_ops[eng]
            if len(lst) >= DMA_K:
                deps[lst[-DMA_K]] = "raw"
            lst.append(idx)
        op = _Op(idx, eng, fn, dma, deps)
        for k in w:
            self.last_w[k] = idx
            self.readers[k] = []
        for k in r:
            self.readers.setdefault(k, []).append(idx)
        self.ops.append(op)
        return op

    def barrier(self):
        last = {}
        for op in self.ops:
            if not op.dma and op.fn is not None:
                last[op.eng] = op.idx
        dmas = [i for q in self.dma_ops.values() for i in q[-DMA_K:]]
        for e in self.ENGS:
            deps = {i: "raw" for ee, i in last.items()}
            for i in dmas:
                deps[i] = "raw"
            idx = len(self.ops)
            op = _Op(idx, e, None, False, deps)
            op.deps = {k: ("bar") for k in deps}
            self.ops.append(op)

    def finalize(self):
        ops = self.ops
        for q, lst in self.dma_ops.items():
            for n, i in enumerate(lst):
                ops[i].sem_i = n % DMA_K
                ops[i].sem_v = 16 * (n // DMA_K + 1)
        self.waits = {}
        for op in ops:
            best = {}
            dma_w = {}
            for p, kind in op.deps.items():
                po = ops[p]
                if po.dma:
                    key = (po.eng, po.sem_i)
                    dma_w[key] = max(dma_w.get(key, 0), po.sem_v)
                    continue
                if po.fn is None:
                    continue
                if po.eng == op.eng and not op.dma and kind != "bar":
                    if po.eng == "pe":
                        continue
                if p > best.get(po.eng, -1):
                    best[po.eng] = p
            for e, p in best.items():
                ops[p].signal = True
            self.waits[op.idx] = (best, dma_w)
        cnt = {e: 0 for e in self.ENGS}
        for op in ops:
            if op.signal:
                cnt[op.eng] += 1
                op.tick = cnt[op.eng]

    def emit(self, final_wait_eng="sp"):
        nc = self.nc
        ops = self.ops
        from contextlib import ExitStack
        with ExitStack() as st:
            esem = {e: st.enter_context(nc.semaphore("cs_" + e)) for e in self.ENGS}
            dsem = {q: [st.enter_context(nc.semaphore("ds_%s%d" % (q, i))) for i in range(DMA_K)]
                    for q in self.dma_ops}
            block = st.enter_context(nc.Block())
            per_eng = {e: [op for op in ops if op.eng == e] for e in self.ENGS}
            waits = self.waits

            def body(ename, eng):
                seen = {}
                for op in per_eng[ename]:
                    best, dma_w = waits[op.idx]
                    for pe_, p in best.items():
                        t = ops[p].tick
                        key = ("c", pe_)
                        if seen.get(key, 0) < t:
                            eng.wait_ge(esem[pe_], t)
                            seen[key] = t
                    for (q, si), v in dma_w.items():
                        key = ("d", q, si)
                        if seen.get(key, 0) < v:
                            eng.wait_ge(dsem[q][si], v)
                            seen[key] = v
                    if op.fn is None:
                        continue
                    inst = op.fn(eng)
                    if op.dma:
                        inst.then_inc(dsem[op.eng][op.sem_i], 16)
                    elif op.signal:
                        inst.then_inc(esem[ename], 1)
                if ename == final_wait_eng:
                    for q, lst in self.dma_ops.items():
                        for i in lst[-DMA_K:]:
                            eng.wait_ge(dsem[q][ops[i].sem_i], ops[i].sem_v)

            @block.tensor
            def _(e):
                body("pe", e)

            @block.scalar
            def _(e):
                body("act", e)

            @block.vector
            def _(e):
                body("dve", e)

            @block.gpsimd
            def _(e):
                body("pool", e)

            @block.sync
            def _(e):
                body("sp", e)


class Arena:
    def __init__(self, nc, lo=16896, hi=229376):
        self.nc, self.lo, self.hi, self.off = nc, lo, hi, lo
        self.n = 0

    def alloc(self, shape, dt):
        nbytes = int(np.prod(shape[1:])) * (2 if dt == BF16 else 4)
        nbytes = (nbytes + 63) // 64 * 64
        assert self.off + nbytes <= self.hi, ("SBUF overflow", self.off, nbytes)
        self.n += 1
        t = self.nc.alloc_sbuf_tensor_at("sb%d" % self.n, list(shape), dt, offset=self.off)
        self.off += nbytes
        return t.ap()

    def mark(self):
        return self.off

    def release(self, m):
        self.off = m


def build_program(dbg=None, stop=None):
    nc = bass.Bass("TRN2", target_bir_lowering=False)
    S = Sched(nc)
    A = Arena(nc)

    def din(name, shape, dt=F32):
        return nc.dram_tensor(name, list(shape), dt, kind="ExternalInput").ap()

    x = din("x", [NB, T, D])
    ctx = din("ctx", [NB, L, D])
    cvecT = din("cvecT", [128, 8, 3])
    w_ada = din("w_ada", [128, 8, 6144])
    b_adaT = din("b_adaT", [128, 48])
    b_ada = din("b_ada", [6144])
    nmixT = din("nmixT", [128, 8])
    nmlpT = din("nmlpT", [128, 8])
    w_in = din("w_in", [128, 8, 2976])
    w_krs = din("w_krs", [128, 8, 96])
    qnT = din("qnT", [128, 2])
    kvnT = din("kvnT", [128, 1])
    w_uq = din("w_uq", [128, 2, 768])
    w_uqs = din("w_uqs", [128, 2, 768])
    w_kn = din("w_kn", [128, 512])
    w_v = din("w_v", [128, 512])
    lbT = din("lbT", [128, 2, 8])
    hgn = din("hgn", [128])
    w_out = din("w_out", [128, 8, 1024])
    w1 = din("w1", [8, 128, 4096])
    w2 = din("w2", [8, 128, 4096])
    fnorm = din("fnorm", [D])
    c_ident = din("c_ident", [128, 128], BF16)
    c_identf = din("c_identf", [128, 128], F32)
    c_mask = din("c_mask", [128, 256], BF16)
    c_rmask = din("c_rmask", [128, 2, 512], BF16)
    c_cs = din("c_cs", [128, 2, T], BF16)
    out = nc.dram_tensor("out", [NB, T, D], F32, kind="ExternalOutput").ap()
    w1s = nc.dram_tensor("w1s", [8, 128, 4096], BF16).ap()
    w2s = nc.dram_tensor("w2s", [8, 128, 4096], BF16).ap()
    dbg_out = {}
    if dbg:
        for name, shape, dt in dbg:
            dbg_out[name] = nc.dram_tensor("dbg_" + name, list(shape), dt, kind="ExternalOutput").ap()

    ps2 = [nc.alloc_psum_tensor("pp%d" % i, [128, 1024], F32).ap() for i in range(4)]
    ps = [ps2[i // 2][:, (i % 2) * 512:(i % 2) * 512 + 512] for i in range(8)]
    psb = [p.bitcast(BF16) for p in ps]
    PS = ["ps%d" % i for i in range(8)]

    def ACT(out_, in_, func, r, w, scale=1.0, bias=0.0, accum=None):
        kw = {}
        if accum is not None:
            kw["accum_out"] = accum
        S.add("act", lambda e: e.activation(out=out_, in_=in_, func=func, bias=bias, scale=scale, **kw), r, w)

    def TT(eng, out_, a, b, op, r, w):
        S.add(eng, lambda e: e.tensor_tensor(out_, a, b, op), r, w)

    def TS(eng, out_, a, s1, s2, op0, op1, r, w):
        if s2 is None:
            S.add(eng, lambda e: e.tensor_scalar(out_, a, s1, None, op0), r, w)
        else:
            S.add(eng, lambda e: e.tensor_scalar(out_, a, s1, s2, op0, op1), r, w)

    def STT(eng, out_, a, sc, b, op0, op1, r, w):
        S.add(eng, lambda e: e.scalar_tensor_tensor(out_, a, sc, b, op0, op1), r, w)

    def CP(eng, out_, in_, r, w):
        if eng == "act":
            S.add("act", lambda e: e.copy(out_, in_), r, w)
        else:
            S.add(eng, lambda e: e.tensor_copy(out_, in_), r, w)

    def MSET(eng, ap, val, w):
        S.add(eng, lambda e: e.memset(ap, val), (), w)

    def MMG(lst, r, w):
        lst = list(lst)

        def fn(e):
            ins = None
            for (o, l, rr, st, sp) in lst:
                ins = e.matmul(o, lhsT=l, rhs=rr, start=st, stop=sp)
            return ins
        S.add("pe", fn, r, w)

    def TRG(lst, ident_ap, r, w):
        lst = list(lst)

        def fn(e):
            ins = None
            for (o, i_) in lst:
                ins = e.transpose(o, i_, ident_ap)
            return ins
        S.add("pe", fn, r, w)

    def DMA(q, out_, in_, r, w):
        S.add(q, lambda e: e.dma_start(out=out_, in_=in_), r, w, dma=True)

    def RECIP(out_, in_, r, w):
        S.add("dve", lambda e: e.reciprocal(out_, in_), r, w)

    def SCAN(out_, d0, d1, r, w):
        S.add("dve", lambda e: e.tensor_tensor_scan(out_, d0, d1, 0.0, ALU.mult, ALU.add), r, w)

    def dump(name, src_ap, r):
        if name in dbg_out:
            DMA("sp", dbg_out[name], src_ap, r, ["dbg_" + name])

    def rstd_chain(ss, tmp, rs, n_inv, r, w_tmp, w_rs):
        ACT(tmp, ss, AF.Ln, r, w_tmp, scale=n_inv, bias=epsc[:, 0:1])
        ACT(rs, tmp, AF.Exp, w_tmp, w_rs, scale=-0.5)

    ident = A.alloc([128, 128], BF16)
    ones_bf = A.alloc([128, 128], BF16)
    maskfb2 = A.alloc([128, 2, 256], BF16)
    rmask = A.alloc([128, 2, 512], BF16)
    epsc = A.alloc([128, 2], F32)
    fn_bc = A.alloc([128, D], F32)
    gn_bc = A.alloc([128, 512], F32)
    G1 = A.alloc([128, NB, D], F32)
    modT = A.alloc([128, 48, 3], F32)
    A1 = A.alloc([128, 3, 8], F32)
    A2 = A.alloc([128, 3, 8], F32)
    lb_t = A.alloc([128, 8], F32)
    w_uq_bf = A.alloc([128, 2, 768], BF16)
    w_uqs_bf = A.alloc([128, 2, 768], BF16)
    w_kn_bf = A.alloc([128, 512], BF16)
    w_v_bf = A.alloc([128, 512], BF16)

    DMA("sp", ident, c_ident, [], ["ident"])
    DMA("sp", maskfb2[:, 0, :], c_mask, [], ["maskfb"])
    DMA("sp", maskfb2[:, 1, :], c_mask, [], ["maskfb"])
    DMA("sp", rmask, c_rmask, [], ["rmask"])
    DMA("sp", fn_bc, fnorm.partition_broadcast(128), [], ["fn_bc"])
    for i in range(4):
        DMA("sp", gn_bc[:, i * 128:(i + 1) * 128], hgn.partition_broadcast(128), [], ["gn_bc%d" % i])
    GN = ["gn_bc%d" % i for i in range(4)]
    MSET("dve", ones_bf, 1.0, ["ones"])
    MSET("dve", epsc, EPS, ["epsc"])

    m_setup = A.mark()
    wa = A.alloc([128, 8, 6144], BF16)
    cT = A.alloc([128, 8, 3], F32)
    sT = A.alloc([128, 8, 3], BF16)
    sTb = A.alloc([128, NB, 8, 128], BF16)
    badaT = A.alloc([128, 48], F32)
    bada_g1 = A.alloc([128, D], F32)
    nmix_t = A.alloc([128, 8], F32)
    nmlp_t = A.alloc([128, 8], F32)
    lbraw = A.alloc([128, 2, 8], F32)
    lbtmp = A.alloc([128, 8], F32)
    qn_t = A.alloc([128, 2], F32)
    kvn_t = A.alloc([128, 1], F32)
    wst = A.alloc([128, 2, 768], F32)
    wst2 = A.alloc([128, 2, 768], F32)
    wst3 = A.alloc([128, 1024], F32)

    for kc in range(8):
        DMA("pool", wa[:, kc, 0:2048], w_ada[:, kc, 0:2048], [], ["wa%d_a" % kc])
    for kc in range(8):
        DMA("pool", wa[:, kc, 2048:6144], w_ada[:, kc, 2048:6144], [], ["wa%d_b" % kc])
    DMA("sp", cT, cvecT, [], ["cT"])
    DMA("sp", badaT, b_adaT, [], ["badaT"])
    DMA("sp", bada_g1, b_ada[2048:3072].partition_broadcast(128), [], ["bada_g1"])
    DMA("sp", nmix_t, nmixT, [], ["nmix"])
    DMA("sp", nmlp_t, nmlpT, [], ["nmlp"])
    DMA("sp", lbraw, lbT, [], ["lbraw"])
    DMA("sp", qn_t, qnT, [], ["qn"])
    DMA("sp", kvn_t, kvnT, [], ["kvn"])
    DMA("sp", wst, w_uq, [], ["wst"])
    DMA("sp", wst2, w_uqs, [], ["wst2"])
    DMA("sp", wst3[:, 0:512], w_kn, [], ["wst3a"])
    DMA("sp", wst3[:, 512:1024], w_v, [], ["wst3b"])

    TT("dve", lbtmp, lbraw[:, 1, :], lbraw[:, 0, :], ALU.subtract, ["lbraw"], ["lbtmp"])
    ACT(lbtmp, lbtmp, AF.Exp, ["lbtmp"], ["lbtmp"])
    TS("dve", lbtmp, lbtmp, 1.0, None, ALU.add, None, ["lbtmp"], ["lbtmp"])
    RECIP(lb_t, lbtmp, ["lbtmp"], ["lb_t"])
    for c in range(2):
        TS("dve", w_uq_bf[:, c, :], wst[:, c, :], qn_t[:, c:c + 1], None, ALU.mult, None, ["wst", "qn"], ["w_uq_bf%d" % c])
        TS("dve", w_uqs_bf[:, c, :], wst2[:, c, :], qn_t[:, c:c + 1], None, ALU.mult, None, ["wst2", "qn"], ["w_uqs_bf%d" % c])
    TS("dve", w_kn_bf, wst3[:, 0:512], kvn_t[:, 0:1], None, ALU.mult, None, ["wst3a", "kvn"], ["w_kn_bf"])
    TS("dve", w_v_bf, wst3[:, 512:1024], kvn_t[:, 0:1], None, ALU.mult, None, ["wst3b", "kvn"], ["w_v_bf"])
    WUQ = ["w_uq_bf0", "w_uq_bf1"]
    WUQS = ["w_uqs_bf0", "w_uqs_bf1"]

    ACT(sT, cT, AF.Silu, ["cT"], ["sT"])
    WAa = ["wa%d_a" % k for k in range(8)]
    WA = ["wa%d_b" % k for k in range(8)]
    psM = ps[0][:, 0:144]
    psM2 = ps[3][:, 0:144]
    MMG([(psM[:, j * 3:(j + 1) * 3], wa[:, kc, j * 128:(j + 1) * 128], sT[:, kc, :], kc == 0, kc == 7)
         for j in range(16) for kc in range(8)], WAa + ["sT"], [PS[0]])
    for b in range(3):
        TT("dve", modT[:, 0:16, b], ps[0][:, b:48:3], badaT[:, 0:16], ALU.add, [PS[0], "badaT"], ["modT%d" % b])
    MODT = ["modT0", "modT1", "modT2"]
    MMG([(psM2[:, j * 3:(j + 1) * 3], wa[:, kc, j * 128:(j + 1) * 128], sT[:, kc, :], kc == 0, kc == 7)
         for j in range(16, 48) for kc in range(8)], WA + ["sT"], [PS[3]])
    MODT2 = ["modTb0", "modTb1", "modTb2"]
    for b in range(3):
        TT("dve", modT[:, 16:48, b], ps[3][:, 48 + b:144:3], badaT[:, 16:48], ALU.add, [PS[3], "badaT"], ["modTb%d" % b])
    for b in range(3):
        STT("dve", A1[:, b, :], modT[:, 8:16, b], 1.0, nmix_t, ALU.add, ALU.mult, MODT + ["nmix"], ["A1_%d" % b])
        STT("dve", A2[:, b, :], modT[:, 32:40, b], 1.0, nmlp_t, ALU.add, ALU.mult, MODT2 + ["nmlp"], ["A2_%d" % b])
    for b in range(NB):
        for kc in range(8):
            CP("dve", sTb[:, b, kc, :], sT[:, kc, b:b + 1].to_broadcast([128, 128]), ["sT"], ["sTb%d_%d" % (b, kc)])
        for half in range(2):
            pg = ps[1 + half]
            MMG([(pg, sTb[:, b, kc, :], wa[:, kc, 2048 + half * 512:2048 + (half + 1) * 512], kc == 0, kc == 7)
                 for kc in range(8)], WA + ["sTb%d_%d" % (b, kc) for kc in range(8)], [PS[1 + half]])
            TT("dve", G1[:, b, half * 512:(half + 1) * 512], pg, bada_g1[:, half * 512:(half + 1) * 512], ALU.add,
               [PS[1 + half], "bada_g1"], ["G1_%d_%d" % (b, half)])
    dump("modT", modT, MODT + MODT2)
    dump("G1", G1, ["G1_%d_%d" % (b, h) for b in range(NB) for h in range(2)])
    dump("lb", lb_t, ["lb_t"])
    S.barrier()
    if stop == 'setup':
        S.finalize(); S.emit(); return nc
    A.release(m_setup)

    o_mlaT = A.alloc([128, 4, T], BF16)
    o_hgT = A.alloc([128, 4, T], BF16)
    m_fin = A.mark()
    hT = A.alloc([128, 8, TE], BF16)
    m_batch = A.mark()

    for b in range(NB):
        bn = "b%d_" % b

        def HK(g):
            return [bn + "hT%d_%d" % (g, c) for c in range(8)]

        def HKB(t0, n):
            r = []
            for g in range(t0 // 128, (t0 + n) // 128):
                r += HK(g)
            return r

        A.release(m_batch)
        xsb = [A.alloc([128, D], BF16) for _ in range(2)]
        junk = A.alloc([128, D], BF16)
        ssA = A.alloc([128, NT], F32)
        lnA = A.alloc([128, NT], F32)
        rsA = A.alloc([128, NT], F32)
        NXB = 3
        xtb = [A.alloc([128, D], F32) for _ in range(NXB)]

        def pa1(g):
            src = ctx[b, g * 128:(g + 1) * 128, :] if g < 2 else x[b, (g - 2) * 128:(g - 1) * 128, :]
            k = g % NXB
            xk = bn + "xt%d" % k
            DMA("sp", xtb[k], src, [], [xk])
            ACT(junk, xtb[k], AF.Square, [xk], [bn + "junk", bn + "ssA%d" % g], accum=ssA[:, g:g + 1])
            rstd_chain(ssA[:, g:g + 1], lnA[:, g:g + 1], rsA[:, g:g + 1], 1.0 / D,
                       [bn + "ssA%d" % g, "epsc"], [bn + "lnA%d" % g], [bn + "rsA%d" % g])

        def pa2(g):
            k = g % NXB
            xk, sk = bn + "xt%d" % k, bn + "xs%d" % (g % 2)
            TS("dve", xsb[g % 2], xtb[k], rsA[:, g:g + 1], None, ALU.mult, None, [xk, bn + "rsA%d" % g], [sk])
            pt = psb[g % 2]
            TRG([(pt[:, c * 128:(c + 1) * 128], xsb[g % 2][:, c * 128:(c + 1) * 128]) for c in range(8)], ident,
                [sk, "ident"], [PS[g % 2]])

        def pa3(g):
            bb = 2 if g < 2 else b
            pt = psb[g % 2]
            for c in range(8):
                dst = hT[:, c, g * 128:(g + 1) * 128]
                if g % 2 == 0:
                    ACT(dst, pt[:, c * 128:(c + 1) * 128], AF.Identity, [PS[g % 2], "A1_%d" % bb] + MODT, [bn + "hT%d_%d" % (g, c)],
                        scale=A1[:, bb, c:c + 1], bias=modT[:, c, bb:bb + 1])
                else:
                    TS("dve", dst, pt[:, c * 128:(c + 1) * 128], A1[:, bb, c:c + 1], modT[:, c, bb:bb + 1], ALU.mult, ALU.add,
                       [PS[g % 2], "A1_%d" % bb] + MODT, [bn + "hT%d_%d" % (g, c)])

        for tt_ in range(NT + 2):
            if tt_ - 2 >= 0:
                pa3(tt_ - 2)
            if 0 <= tt_ - 1 < NT:
                pa2(tt_ - 1)
            if tt_ < NT:
                pa1(tt_)
        if b == 0:
            dump("hT", hT, [k for g in range(NT) for k in HK(g)])

        if stop == 'A' and b == 0:
            S.barrier(); S.finalize(); S.emit(); return nc
        S.barrier()
        A.release(m_batch)
        m_mla = A.mark()
        cs = A.alloc([128, 2, T], BF16)
        w_mla = A.alloc([128, 8, 416], BF16)
        w_krs_bf = A.alloc([128, 8, 96], BF16)
        cqnT = A.alloc([128, 2, TE], BF16)
        ckvnT = A.alloc([128, TE], BF16)
        krotT = A.alloc([128, TE], BF16)
        Vaug = A.alloc([128, NT, 8, 128], BF16)
        KTb = [A.alloc([128, TE], BF16) for _ in range(2)]
        QTb = [A.alloc([128, T], BF16) for _ in range(2)]
        PT2 = [A.alloc([128, 1024], BF16) for _ in range(3)]
        sq = A.alloc([128, 3, 512], BF16)
        tA = A.alloc([128, 512], F32)
        tB = A.alloc([128, 512], F32)
        rq_bc = A.alloc([128, 512], F32)
        rkv_bc = A.alloc([128, 512], F32)
        rp1 = [A.alloc([128, 512], F32) for _ in range(1)]
        rp2 = [A.alloc([128, 512], F32) for _ in range(1)]
        rden = [A.alloc([128, 512], F32) for _ in range(2)]

        DMA("sp", cs, c_cs, [], [bn + "cs"])
        MSET("dve", Vaug, 1.0, [bn + "Vp%d" % g for g in range(NT)])
        DMA("pool", w_mla, w_in[:, :, 0:416], [], [bn + "w_mla"])
        DMA("pool", w_krs_bf, w_krs, [], [bn + "w_krs"])
        WM = [bn + "w_mla"]

        for bi, (t0, n) in enumerate(BLKS):
            hk = HKB(t0, n)
            tok = slice(t0, t0 + n)
            for c in range(2):
                MMG([(ps[c][:, 0:n], w_mla[:, kc, c * 128:(c + 1) * 128], hT[:, kc, tok], kc == 0, kc == 7) for kc in range(8)],
                    WM + hk, [PS[c]])
            MMG([(ps[2][:, 0:n], w_mla[:, kc, 256:384], hT[:, kc, tok], kc == 0, kc == 7) for kc in range(8)], WM + hk, [PS[2]])
            MMG([(ps[3][0:96, 0:n], w_mla[:, kc, 320:416], hT[:, kc, tok], kc == 0, kc == 7) for kc in range(8)], WM + hk, [PS[3]])
            if bi > 0:
                MMG([(ps[4][0:96, 0:n], w_krs_bf[:, kc, :], hT[:, kc, tok], kc == 0, kc == 7) for kc in range(8)],
                    [bn + "w_krs"] + hk, [PS[4]])
            for c in range(3):
                ACT(sq[:, c, 0:n], ps[c][:, 0:n], AF.Square, [PS[c]], [bn + "sq%d" % c])
            MMG([(ps[5][:, 0:n], ones_bf, sq[:, 0, 0:n], True, False), (ps[5][:, 0:n], ones_bf, sq[:, 1, 0:n], False, True)],
                ["ones", bn + "sq0", bn + "sq1"], [PS[5]])
            MMG([(ps[6][:, 0:n], ones_bf, sq[:, 2, 0:n], True, True)], ["ones", bn + "sq2"], [PS[6]])
            rstd_chain(ps[5][:, 0:n], tA[:, 0:n], rq_bc[:, 0:n], 1.0 / 256, [PS[5], "epsc"], [bn + "tA"], [bn + "rq_bc"])
            rstd_chain(ps[6][:, 0:n], tB[:, 0:n], rkv_bc[:, 0:n], 1.0 / 128, [PS[6], "epsc"], [bn + "tB"], [bn + "rkv_bc"])
            for c in range(2):
                TT("dve", cqnT[:, c, tok], ps[c][:, 0:n], rq_bc[:, 0:n], ALU.mult, [PS[c], bn + "rq_bc"], [bn + "cqnT%d_%d" % (bi, c)])
            TT("dve", ckvnT[:, tok], ps[2][:, 0:n], rkv_bc[:, 0:n], ALU.mult, [PS[2], bn + "rkv_bc"], [bn + "ckvnT%d" % bi])
            if bi == 0:
                CP("dve", krotT[64:96, tok], ps[3][64:96, 0:n], [PS[3]], [bn + "krotT%d" % bi])
            else:
                lt = slice(t0 - L, t0 - L + n)
                TT("dve", rp1[0][64:96, 0:n], ps[3][64:96, 0:n], cs[64:96, 0, lt], ALU.mult, [PS[3], bn + "cs"], [bn + "rp1_0"])
                TT("dve", rp2[0][64:96, 0:n], ps[4][64:96, 0:n], cs[64:96, 1, lt], ALU.mult, [PS[4], bn + "cs"], [bn + "rp2_0"])
                TT("dve", krotT[64:96, tok], rp1[0][64:96, 0:n], rp2[0][64:96, 0:n], ALU.add, [bn + "rp1_0", bn + "rp2_0"],
                   [bn + "krotT%d" % bi])
            for g in range(t0 // 128, (t0 + n) // 128):
                MMG([(ps[7], ckvnT[:, g * 128:(g + 1) * 128], w_v_bf, True, True)], [bn + "ckvnT%d" % bi, "w_v_bf"], [PS[7]])
                pv4 = ps[7].rearrange("p (j e c) -> p j e c", j=4, e=2)
                veng = "act" if g % 2 else "dve"
                CP(veng, Vaug[:, g, 0::2, 0:64], pv4[:, :, 0, :], [PS[7]], [bn + "Vp%d" % g])
                CP(veng, Vaug[:, g, 1::2, 64:128], pv4[:, :, 1, :], [PS[7]], [bn + "Vp%d" % g])
        CQ = [bn + "cqnT%d_%d" % (bi, c) for bi in range(5) for c in range(2)]
        CKV = [bn + "ckvnT%d" % bi for bi in range(5)]
        KROT = [bn + "krotT%d" % bi for bi in range(5)]
        if b == 0:
            dump("cqnT", cqnT, CQ)
            dump("ckvnT", ckvnT, CKV)
            dump("krotT", krotT[64:96, :], KROT)

        if stop == 'mlaproj' and b == 0:
            S.barrier(); S.finalize(); S.emit(); return nc
        def proj_head(h):
            kb = h % 2
            KT, QT = KTb[kb], QTb[kb]
            kk, qk = bn + "KT%d" % kb, bn + "QT%d" % kb
            pc = 0
            for bi, (t0, n) in enumerate(BLKS):
                tok = slice(t0, t0 + n)
                pb, pk = ps[7 - pc % 2], PS[7 - pc % 2]
                pc += 1
                MMG([(pb[0:64, 0:n], w_kn_bf[:, h * 64:(h + 1) * 64], ckvnT[:, tok], True, True)],
                    ["w_kn_bf", bn + "ckvnT%d" % bi], [pk])
                CP("dve", KT[0:64, tok], pb[0:64, 0:n], [pk], [kk + "_n%d" % bi])
                yield
            CP("dve", KT[64:96, :], krotT[64:96, :], KROT, [kk + "_r"])
            for j in range(4):
                q0 = j * 512
                et = slice(L + q0, L + q0 + 512)
                lt = slice(q0, q0 + 512)
                cqk = [bn + "cqnT%d_%d" % (j + 1, c) for c in range(2)]
                pb, pk = ps[7 - pc % 2], PS[7 - pc % 2]
                pc += 1
                MMG([(pb[0:96, :], w_uq_bf[:, c, h * 96:(h + 1) * 96], cqnT[:, c, et], c == 0, c == 1) for c in range(2)],
                    WUQ + cqk, [pk])
                CP("dve", QT[0:64, lt], pb[0:64, :], [pk], [qk + "_n%d" % j])
                TT("dve", rp1[0][64:96, :], pb[64:96, :], cs[64:96, 0, lt], ALU.mult, [pk, bn + "cs"], [bn + "rp1_0"])
                yield
                pb, pk = ps[7 - pc % 2], PS[7 - pc % 2]
                pc += 1
                MMG([(pb[0:96, :], w_uqs_bf[:, c, h * 96:(h + 1) * 96], cqnT[:, c, et], c == 0, c == 1) for c in range(2)],
                    WUQS + cqk, [pk])
                TT("dve", rp2[0][64:96, :], pb[64:96, :], cs[64:96, 1, lt], ALU.mult, [pk, bn + "cs"], [bn + "rp2_0"])
                TT("dve", QT[64:96, lt], rp1[0][64:96, :], rp2[0][64:96, :], ALU.add, [bn + "rp1_0", bn + "rp2_0"], [qk + "_r%d" % j])
                yield

        def KTK(h):
            kk = bn + "KT%d" % (h % 2)
            return [kk + "_n%d" % bi for bi in range(5)] + [kk + "_r"]

        def QTK(h, j):
            qk = bn + "QT%d" % (h % 2)
            return [qk + "_n%d" % j, qk + "_r%d" % j]

        steps = [(h, qb, kt) for h in range(8) for qb in range(4) for kt in range(NT)]
        for _ in proj_head(0):
            pass
        pgen = None
        if b == 0:
            dump("KT0", KTb[0][0:96, :], KTK(0))
            dump("QT0", QTb[0][0:96, :], [k for j in range(4) for k in QTK(0, j)])

        npairs = len(steps) // 2

        def reg_qk_pair(p):
            for e_ in range(2):
                h, qb, kt = steps[2 * p + e_]
                KT, QT = KTb[h % 2], QTb[h % 2]
                bank = 2 * (p % 2) + e_
                MMG([(ps[bank], KT[0:96, kt * 128:(kt + 1) * 128], QT[0:96, qb * 512:(qb + 1) * 512], True, True)],
                    KTK(h) + QTK(h, qb), [PS[bank]])

        reg_qk_pair(0)
        for p in range(npairs):
            h, qb, kt0 = steps[2 * p]
            u = (h * 4 + qb) % 2
            pO = ps[4 + u]
            ACT(PT2[p % 3], ps2[p % 2], AF.Exp, [PS[2 * (p % 2)], PS[2 * (p % 2) + 1]],
                [bn + "PT%d" % (p % 3)] + (["convgate"] if (b == 0 and p == 0) else []), scale=ATT_SCALE)
            if p + 1 < npairs:
                reg_qk_pair(p + 1)
            for e_ in range(2):
                kt = kt0 + e_
                MMG([(pO, Vaug[:, kt, h, :], PT2[p % 3][:, e_ * 512:(e_ + 1) * 512], kt == 0, kt == NT - 1)],
                    [bn + "Vp%d" % kt, bn + "PT%d" % (p % 3)], [PS[4 + u]])
            if kt0 + 1 == NT - 1:
                orow = slice((h % 2) * 64, (h % 2) * 64 + 64)
                drow = slice(64 - (h % 2) * 64, 128 - (h % 2) * 64)
                RECIP(rden[u][orow, :], pO[drow, :], [PS[4 + u]], [bn + "rden%d" % u])
                TT("dve", o_mlaT[orow, h // 2, qb * 512:(qb + 1) * 512], pO[orow, :], rden[u][orow, :], ALU.mult,
                   [PS[4 + u], bn + "rden%d" % u], [bn + "omla%d_%d" % (h, qb)])
            if b == 0 and p == 0:
                for p_ in range(8):
                    DMA("pool", w1s[p_], w1[p_], ["convgate"], ["w1s%d" % p_])
                for p_ in range(8):
                    DMA("pool", w2s[p_], w2[p_], [], ["w2s%d" % p_])
            if qb == 0 and kt0 == 2 and h + 1 < 8:
                pgen = proj_head(h + 1)
            if pgen is not None and p % 2 == 0:
                try:
                    next(pgen)
                except StopIteration:
                    pgen = None
        OM = [bn + "omla%d_%d" % (h, qb) for h in range(8) for qb in range(4)]
        if b == 0:
            dump("o_mlaT", o_mlaT, OM)
        if stop == 'attn' and b == 0:
            S.barrier(); S.finalize(); S.emit(); return nc
        S.barrier()
        A.release(m_mla)

        v_tm = A.alloc([128, NT, 512], BF16)
        gate = A.alloc([128, 16, 512], BF16)
        eb_tab = A.alloc([128, 4, 2, NT], F32)
        m_h0 = A.mark()
        w_hig = A.alloc([128, 8, 1024], BF16)
        gtmp = [A.alloc([128, 512], F32) for _ in range(2)]
        DMA("pool", w_hig[:, :, 0:512], w_in[:, :, 1952:2464], [], [bn + "w_hig_v"])
        DMA("pool", w_hig[:, :, 512:1024], w_in[:, :, 2464:2976], [], [bn + "w_hig_g"])
        for g in range(NT):
            k = g % 2
            MMG([(ps[k], hT[:, kc, g * 128:(g + 1) * 128], w_hig[:, kc, 0:512], kc == 0, kc == 7) for kc in range(8)],
                [bn + "w_hig_v"] + HK(g), [PS[k]])
            CP("dve" if g % 2 else "act", v_tm[:, g, :], ps[k], [PS[k]], [bn + "v_tm%d" % g])
            if g >= 2:
                MMG([(ps[2 + k], hT[:, kc, g * 128:(g + 1) * 128], w_hig[:, kc, 512:1024], kc == 0, kc == 7) for kc in range(8)],
                    [bn + "w_hig_g"] + HK(g), [PS[2 + k]])
                ACT(gtmp[k], ps[2 + k], AF.Silu, [PS[2 + k]], [bn + "gtmp%d" % k])
                TT("dve", gate[:, g - 2, :], gtmp[k], gn_bc, ALU.mult, [bn + "gtmp%d" % k] + GN, [bn + "gate%d" % (g - 2)])
        if b == 0:
            dump("v_tm", v_tm, [bn + "v_tm%d" % g for g in range(NT)])
            dump("gate", gate, [bn + "gate%d" % j for j in range(16)])
        if stop == 'hg0' and b == 0:
            S.barrier(); S.finalize(); S.emit(); return nc
        S.barrier()
        A.release(m_h0)

        NB_H = 9
        NSET = 4
        whb = [A.alloc([128, 8, 384], BF16) for _ in range(1)]
        qTs = [A.alloc([128, T], BF16) for _ in range(2)]
        kTs = [A.alloc([128, T], BF16) for _ in range(2)]
        khat = A.alloc([128, NT, 2, 128], BF16)
        S_st = A.alloc([128, 2, 17, 128], BF16)
        khT = [A.alloc([128, 256], BF16) for _ in range(2)]
        ktmp = [A.alloc([128, 256], BF16) for _ in range(1)]
        TS1 = [A.alloc([128, 256], F32) for _ in range(NSET)]
        TS2 = [A.alloc([128, 256], F32) for _ in range(NSET)]
        TS3 = [A.alloc([128, 256], F32) for _ in range(NSET)]
        TSq = [A.alloc([128, 256], F32) for _ in range(NSET)]
        TSz = [A.alloc([128, 256], BF16) for _ in range(NSET)]
        ATb = A.alloc([128, 4, 256], BF16)
        o_sb = A.alloc([128, 4, 128], F32)
        og = A.alloc([128, 4, 128], BF16)
        ssh = A.alloc([128, 4], F32)
        lnh = A.alloc([128, 4], F32)
        rsh = A.alloc([128, 4], F32)
        junk2 = A.alloc([128, 128], BF16)

        units = []
        for h in range(4):
            fo = [(k_, 0) for k_ in range(NB_H)]
            bo = [(k_, 1) for k_ in [0] + list(range(NB_H - 1, 0, -1))]
            seq = (bo + fo) if h % 2 == 0 else (fo + bo)
            for (k_, d_) in seq:
                units.append((h, k_, d_))
        NU = len(units)

        def hkeys(h):
            return bn + "h%d_" % h

        def load_wh(h):
            wh = whb[0]
            whk = bn + "wh0"
            for i, c0 in enumerate((416, 928, 1440)):
                DMA("pool", wh[:, :, i * 128:(i + 1) * 128], w_in[:, :, c0 + h * 128:c0 + (h + 1) * 128], [], [whk + "_%d" % i])

        def uinfo(ui):
            h, k, d = units[ui]
            return h, k, d, whb[0], [bn + "wh0_%d" % i for i in range(3)], hkeys(h), ui % NSET, k > 0

        def st0(ui):
            h, k, d, wh, WHK, hn, s, lat = uinfo(ui)
            tok = slice(k * 256, k * 256 + 256)
            hk = HKB(k * 256, 256)
            pz = ps[ui % 2]
            MMG([(pz[:, 0:256], wh[:, kc, (1 + d) * 128:(2 + d) * 128], hT[:, kc, tok], kc == 0, kc == 7) for kc in range(8)],
                WHK + hk, [PS[ui % 2]])
            if lat:
                pq = ps[2]
                MMG([(pq[:, 0:256], wh[:, kc, 0:128], hT[:, kc, tok], kc == 0, kc == 7) for kc in range(8)], WHK + hk, [PS[2]])

        def st1(ui):
            h, k, d, wh, WHK, hn, s, lat = uinfo(ui)
            sk = bn + "ts%d_" % s
            pz = ps[ui % 2]
            ACT(TS1[s], pz[:, 0:256], AF.Exp, [PS[ui % 2]], [sk + "T1"], scale=-1.0)
            ACT(TS2[s], TS1[s], AF.Ln, [sk + "T1", "lb_t"], [sk + "T2"], scale=lb_t[:, d * 4 + h:d * 4 + h + 1], bias=1.0)
            ACT(TS1[s], TS1[s], AF.Ln, [sk + "T1"], [sk + "T1"], bias=1.0)
            if lat:
                pq = ps[2]
                ACT(TSq[s], pq[:, 0:256], AF.Exp, [PS[2]], [sk + "Tq"], scale=-1.0)
                ACT(TSz[s], pq[:, 0:256], AF.Copy, [PS[2]], [sk + "Tz"])
                ACT(TSq[s], TSq[s], AF.Ln, [sk + "Tq"], [sk + "Tq"], bias=1.0)

        def st2(ui):
            h, k, d, wh, WHK, hn, s, lat = uinfo(ui)
            sk = bn + "ts%d_" % s
            TT("dve", TS2[s], TS2[s], TS1[s], ALU.subtract, [sk + "T1", sk + "T2"], [sk + "T2"])
            if d == 0:
                SCAN(TS3[s], rmask[:, 0, 0:256], TS2[s], ["rmask", sk + "T2"], [sk + "T3"])
            else:
                SCAN(TS3[s][:, ::-1], rmask[:, 1, 0:256][:, ::-1], TS2[s][:, ::-1], ["rmask", sk + "T2"], [sk + "T3"])
            if lat:
                TT("dve", TSq[s], TS3[s], TSq[s], ALU.subtract, [sk + "T3", sk + "Tq"], [sk + "Tq"])

        def st3(ui):
            h, k, d, wh, WHK, hn, s, lat = uinfo(ui)
            sk = bn + "ts%d_" % s
            ACT(TS1[s], TS2[s], AF.Exp, [sk + "T2"], [sk + "T1"])
            lastcol = 127 if d == 0 else 0
            ACT(eb_tab[:, h, d, 2 * k:2 * k + 2], TS3[s][:, lastcol:256:128], AF.Exp, [sk + "T3"], [hn + "eb%d_%d" % (d, k)])
            if lat:
                ACT(TSq[s], TSq[s], AF.Exp, [sk + "Tq"], [sk + "Tq"])
            ACT(TS3[s], TS3[s], AF.Exp, [sk + "T3"], [sk + "T3"], scale=-1.0)

        def st4(ui):
            h, k, d, wh, WHK, hn, s, lat = uinfo(ui)
            sk = bn + "ts%d_" % s
            if lat:
                lt = slice((k - 1) * 256, k * 256)
                STT("dve", qTs[d][:, lt], TSz[s], -1.0, TSq[s], ALU.mult, ALU.mult, [sk + "Tz", sk + "Tq"], [bn + "qT%d_%d" % (d, k)])
                kdst = kTs[d][:, lt]
                kkey = bn + "kT%d_%d" % (d, k)
            else:
                kdst = ktmp[0]
                kkey = bn + "ktmp0"
            STT("dve", kdst, TS1[s], 1.0, TS3[s], ALU.subtract, ALU.mult, [sk + "T1", sk + "T3"], [kkey])
            kh = khT[ui % 2]
            for c in range(2):
                TS("dve", kh[:, c * 128:(c + 1) * 128], kdst[:, c * 128:(c + 1) * 128], eb_tab[:, h, d, 2 * k + c:2 * k + c + 1], None,
                   ALU.mult, None, [kkey, hn + "eb%d_%d" % (d, k)], [bn + "khT%d_%d" % (ui % 2, c)])

        def st5(ui):
            h, k, d, wh, WHK, hn, s, lat = uinfo(ui)
            kh = khT[ui % 2]
            ptk = psb[4 + ui % 2]
            TRG([(ptk[:, c * 128:(c + 1) * 128], kh[:, c * 128:(c + 1) * 128]) for c in range(2)], ident,
                [bn + "khT%d_%d" % (ui % 2, c) for c in range(2)] + ["ident"], [PS[4 + ui % 2]])

        def st6(ui):
            h, k, d, wh, WHK, hn, s, lat = uinfo(ui)
            ptk = psb[4 + ui % 2]
            CP("act" if ui % 2 else "dve", khat[:, 2 * k:2 * k + 2, d, :], ptk[:, 0:256].rearrange("p (c k) -> p c k", c=2),
               [PS[4 + ui % 2]], [bn + "khat%d_%d" % (d, k)])

        def scan_mm(h, g, d, k):
            hn = hkeys(h)
            pS = ps[6][:, d * 128:(d + 1) * 128]
            MMG([(pS, khat[:, g, d, :], v_tm[:, g, h * 128:(h + 1) * 128], True, True)],
                [bn + "khat%d_%d" % (d, k), bn + "v_tm%d" % g], [PS[6]])

        def scan_upd(h, g, d, k, p):
            hn = hkeys(h)
            pS = ps[6][:, d * 128:(d + 1) * 128]
            if p == 0:
                CP("dve", S_st[:, d, 0, :], pS, [PS[6]], [bn + "S%d_%d" % (d, 0)])
            else:
                STT("dve", S_st[:, d, p, :], S_st[:, d, p - 1, :], eb_tab[:, h, d, g:g + 1], pS, ALU.mult, ALU.add,
                    [bn + "S%d_%d" % (d, p - 1), hn + "eb%d_%d" % (d, k), PS[6]], [bn + "S%d_%d" % (d, p)])

        def grp_piece(h, gq, step):
            hn = hkeys(h)
            pO = ps[7]

            def pa_mm(i):
                j = 4 * gq + i
                tl = slice(j * 128, (j + 1) * 128)
                kb = j // 2 + 1
                pA = ps[3][:, (i % 2) * 256:(i % 2) * 256 + 256]
                MMG([(pA[:, 0:128], kTs[0][:, tl], qTs[0][:, tl], True, True),
                     (pA[:, 128:256], kTs[1][:, tl], qTs[1][:, tl], True, True)],
                    [bn + "kT%d_%d" % (d, kb) for d in range(2)] + [bn + "qT%d_%d" % (d, kb) for d in range(2)], [PS[3]])

            def mask2(i0):
                TT("dve", ATb[:, i0:i0 + 2, :], ps[3].rearrange("p (i c) -> p i c", i=2), maskfb2, ALU.mult, [PS[3], "maskfb"],
                   [bn + "AT%d" % i0, bn + "AT%d" % (i0 + 1)])

            def po_mm(i):
                j = 4 * gq + i
                g = j + 2
                kb = j // 2 + 1
                tl = slice(j * 128, (j + 1) * 128)
                vv = v_tm[:, g, h * 128:(h + 1) * 128]
                MMG([(pO[:, i * 128:(i + 1) * 128], ATb[:, i, 0:128], vv, True, False),
                     (pO[:, i * 128:(i + 1) * 128], ATb[:, i, 128:256], vv, False, False),
                     (pO[:, i * 128:(i + 1) * 128], qTs[0][:, tl], S_st[:, 0, j + 1, :], False, False),
                     (pO[:, i * 128:(i + 1) * 128], qTs[1][:, tl], S_st[:, 1, 16 - j, :], False, True)],
                    [bn + "AT%d" % i, bn + "v_tm%d" % g, bn + "qT0_%d" % kb, bn + "qT1_%d" % kb,
                     bn + "S0_%d" % (j + 1), bn + "S1_%d" % (16 - j)], [PS[7]])

            if step == 1:
                pa_mm(0)
                pa_mm(1)
                mask2(0)
            elif step == 2:
                po_mm(0)
                po_mm(1)
                pa_mm(2)
                pa_mm(3)
                mask2(2)
            elif step == 3:
                po_mm(2)
                po_mm(3)
                CP("dve", o_sb, pO.rearrange("p (i c) -> p i c", i=4), [PS[7]], [bn + "o_sb"])
                for i in range(4):
                    ACT(junk2, o_sb[:, i, :], AF.Square, [bn + "o_sb"], [bn + "junk2", bn + "ssh%d" % i], accum=ssh[:, i:i + 1])
                rstd_chain(ssh, lnh, rsh, 1.0 / 128, [bn + "ssh%d" % i for i in range(4)] + ["epsc"], [bn + "lnh"], [bn + "rsh"])
            elif step == 4:
                for i in range(4):
                    j = 4 * gq + i
                    STT("dve", og[:, i, :], o_sb[:, i, :], rsh[:, i:i + 1], gate[:, j, h * 128:(h + 1) * 128],
                        ALU.mult, ALU.mult, [bn + "o_sb", bn + "rsh", bn + "gate%d" % j], [bn + "og%d" % i])
            elif step == 5:
                pT = psb[6][:, 512:1024]
                TRG([(pT[:, i * 128:(i + 1) * 128], og[:, i, :]) for i in range(4)], ident,
                    [bn + "og%d" % i for i in range(4)] + ["ident"], [PS[6]])
                CP("dve", o_hgT[:, h, gq * 512:(gq + 1) * 512], pT, [PS[6]], [bn + "ohg%d_%d" % (h, gq)])

        load_wh(0)
        stages = [st0, st1, st2, st3, st4, st5, st6]
        from collections import deque
        scan_q = [deque(), deque()]
        blk_scanned = {}
        grp_todo = deque((h, gq) for h in range(4) for gq in (range(4) if h % 2 == 0 else range(3, -1, -1)))
        front = None
        back = None
        grp_done = {}
        tau = 0
        tc = 0

        def chain_hazard(tc_):
            for si in (4, 6):
                ui = tc_ - si
                if 0 <= ui < NU:
                    h, k, d = units[ui]
                    if si == 4 and k >= 1:
                        for hp in range(h):
                            if grp_done.get((hp, (k - 1) // 2), 0) < 3:
                                return True
                    if si == 6:
                        for dq_ in scan_q:
                            for ent in dq_:
                                if ent[0] < h:
                                    return True
            return False

        def scan_hazard(h, d, p):
            j = (p - 1) if d == 0 else (16 - p)
            if 0 <= j <= 15:
                for hp in range(h):
                    if grp_done.get((hp, j // 4), 0) < 3:
                        return True
            return False

        while True:
            active = False
            if tc < NU + len(stages) and not chain_hazard(tc):
                for si in (1, 2, 3, 4, 5, 6, 0):
                    ui = tc - si
                    if 0 <= ui < NU:
                        stages[si](ui)
                        h, k, d = units[ui]
                        if si == 0 and ui + 1 < NU and units[ui + 1][0] != h:
                            load_wh(h + 1)
                        if si == 6:
                            tiles = [2 * k, 2 * k + 1] if d == 0 else [2 * k + 1, 2 * k]
                            todo = []
                            for g in tiles:
                                p = g if d == 0 else (1 - g if g < 2 else 19 - g)
                                if p <= 16:
                                    todo.append((g, p))
                            for n_, (g, p) in enumerate(todo):
                                scan_q[d].append((h, g, k, p, tau + 1, n_ == len(todo) - 1))
                            if not todo:
                                blk_scanned[(h, k, d)] = tau
                tc += 1
                active = True
            steps_now = []
            for d in range(2):
                if scan_q[d] and scan_q[d][0][4] <= tau and not scan_hazard(scan_q[d][0][0], d, scan_q[d][0][3]):
                    steps_now.append((d,) + scan_q[d].popleft())
            for (d, h, g, k, p, rt, last) in steps_now:
                scan_mm(h, g, d, k)
            for (d, h, g, k, p, rt, last) in steps_now:
                scan_upd(h, g, d, k, p)
                if last:
                    blk_scanned[(h, k, d)] = tau
                active = True
            if front is not None:
                h, gq, stp = front
                grp_piece(h, gq, stp)
                grp_done[(h, gq)] = stp
                front = (h, gq, stp + 1) if stp < 5 else None
                active = True
            pending_front = None
            if back is not None:
                h, gq, stp = back
                grp_piece(h, gq, stp)
                grp_done[(h, gq)] = stp
                active = True
                if stp == 3:
                    back = None
                    pending_front = (h, gq, 4)
                else:
                    back = (h, gq, stp + 1)
            if back is None and pending_front is None and grp_todo:
                h, gq = grp_todo[0]
                need = [(h, 2 * gq + 1, 0), (h, 2 * gq + 2, 0), (h, 2 * gq + 1, 1), (h, 2 * gq + 2, 1), (h, 0, 0), (h, 0, 1)]
                if all((kk in blk_scanned and blk_scanned[kk] < tau) for kk in need):
                    grp_todo.popleft()
                    back = (h, gq, 1)
            if pending_front is not None:
                assert front is None
                front = pending_front
            tau += 1
            if not active and not grp_todo and back is None and front is None and not scan_q[0] and not scan_q[1] and tc >= NU + len(stages):
                break
            assert tau < NU + 600, "side work did not drain"
        OH = [bn + "ohg%d_%d" % (h, gq) for h in range(4) for gq in range(4)]
        if b == 0:
            dump("o_hgT", o_hgT, OH)
        S.barrier()

        if stop == 'hgrn' and b == 0:
            S.barrier(); S.finalize(); S.emit(); return nc
        A.release(m_fin)
        w_out_bf = A.alloc([128, 8, D], BF16)
        identf = A.alloc([128, 128], F32)
        DMA("sp", identf, c_identf, [], ["identf"])
        x1b = [A.alloc([128, 4, D], F32) for _ in range(2)]
        h2Tb = [A.alloc([128, 8, 512], BF16) for _ in range(2)]
        uT = A.alloc([128, 32, 512], BF16)
        xt2 = [A.alloc([128, D], F32) for _ in range(1)]
        xs2b = [A.alloc([128, D], BF16) for _ in range(2)]
        w1p = [A.alloc([128, 8, 512], BF16) for _ in range(2)]
        w2p = [A.alloc([128, 32, 128], BF16) for _ in range(2)]
        rbuf = [A.alloc([128, 512], F32) for _ in range(2)]
        yT = [A.alloc([128, 512], F32) for _ in range(2)]
        ot = [A.alloc([128, D], F32) for _ in range(1)]
        ss2 = A.alloc([128, 16], F32)
        ln2 = A.alloc([128, 16], F32)
        rs2 = A.alloc([128, 16], F32)
        ss3 = A.alloc([128, 16], F32)
        ln3 = A.alloc([128, 16], F32)
        rs3 = A.alloc([128, 16], F32)
        DMA("pool", w_out_bf, w_out, [], [bn + "w_out"])
        fn_ = bn + "f_"

        def X1K(jb, i):
            return fn_ + "x1_%d_%d" % (jb % 2, i)

        def stage_a(jb, i):
            x1 = x1b[jb % 2]
            j = jb * 4 + i
            tl = slice(j * 128, (j + 1) * 128)
            DMA("sp", xt2[0], x[b, tl, :], [], [fn_ + "xt0"])
            for half in range(2):
                MMG([(ps[half], (o_mlaT[:, c, tl] if c < 4 else o_hgT[:, c - 4, tl]), w_out_bf[:, c, half * 512:(half + 1) * 512],
                      c == 0, c == 7) for c in range(8)], OM + OH + [bn + "w_out"], [PS[half]])
                TT("dve", x1[:, i, half * 512:(half + 1) * 512], ps[half], G1[:, b, half * 512:(half + 1) * 512], ALU.mult,
                   [PS[half], "G1_%d_%d" % (b, half)], [fn_ + "x1h_%d_%d_%d" % (jb % 2, i, half), X1K(jb, i)])
            TT("dve", x1[:, i, :], x1[:, i, :], xt2[0], ALU.add,
               [fn_ + "x1h_%d_%d_0" % (jb % 2, i), fn_ + "x1h_%d_%d_1" % (jb % 2, i), fn_ + "xt0"], [X1K(jb, i)])

        def stage_b1(jb, i):
            x1 = x1b[jb % 2]
            j = jb * 4 + i
            xs2k = xs2b[i % 2]
            xsk = fn_ + "xs2_%d" % (i % 2)
            ACT(xs2k, x1[:, i, :], AF.Square, [X1K(jb, i)], [xsk, fn_ + "ss2_%d" % j], accum=ss2[:, j:j + 1])
            rstd_chain(ss2[:, j:j + 1], ln2[:, j:j + 1], rs2[:, j:j + 1], 1.0 / D, [fn_ + "ss2_%d" % j, "epsc"],
                       [fn_ + "ln2_%d" % j], [fn_ + "rs2_%d" % j])
            TS("dve", xs2k, x1[:, i, :], rs2[:, j:j + 1], None, ALU.mult, None, [X1K(jb, i), fn_ + "rs2_%d" % j], [xsk])

        def stage_b2(jb, i):
            h2T = h2Tb[jb % 2]
            xs2k = xs2b[i % 2]
            xsk = fn_ + "xs2_%d" % (i % 2)
            pt = psb[2]
            TRG([(pt[:, c * 128:(c + 1) * 128], xs2k[:, c * 128:(c + 1) * 128]) for c in range(8)], ident,
                [xsk, "ident"], [PS[2]])
            for c in range(8):
                dst = h2T[:, c, i * 128:(i + 1) * 128]
                hk_ = fn_ + "h2T%d_%d_%d" % (jb % 2, i, c)
                if i % 2 == 0:
                    ACT(dst, pt[:, c * 128:(c + 1) * 128], AF.Identity, [PS[2], "A2_%d" % b] + MODT2, [hk_],
                        scale=A2[:, b, c:c + 1], bias=modT[:, 24 + c, b:b + 1])
                else:
                    TS("dve", dst, pt[:, c * 128:(c + 1) * 128], A2[:, b, c:c + 1], modT[:, 24 + c, b:b + 1], ALU.mult, ALU.add,
                       [PS[2], "A2_%d" % b] + MODT2, [hk_])

        def prep_pieces(jb):
            return [
                [lambda: stage_a(jb, 0)],
                [lambda: stage_a(jb, 1)],
                [lambda: stage_b1(jb, 0)],
                [lambda: stage_a(jb, 2), lambda: stage_b2(jb, 0)],
                [lambda: stage_b1(jb, 1)],
                [lambda: stage_a(jb, 3), lambda: stage_b2(jb, 1)],
                [lambda: stage_b1(jb, 2)],
                [lambda: stage_b2(jb, 2), lambda: stage_b1(jb, 3)],
                [lambda: stage_b2(jb, 3)],
            ]

        otb = [ot[0], xt2[0]]
        otk = [fn_ + "ot0", fn_ + "xt0"]

        def final_norm(jb_, i):
            x1_ = x1b[jb_ % 2]
            j = jb_ * 4 + i
            xs2k = xs2b[i % 2]
            ACT(xs2k, x1_[:, i, :], AF.Square, [X1K(jb_, i)], [fn_ + "xs2_%d" % (i % 2), fn_ + "ss3_%d" % j], accum=ss3[:, j:j + 1])
            rstd_chain(ss3[:, j:j + 1], ln3[:, j:j + 1], rs3[:, j:j + 1], 1.0 / D, [fn_ + "ss3_%d" % j, "epsc"],
                       [fn_ + "ln3_%d" % j], [fn_ + "rs3_%d" % j])
            STT("dve", otb[i % 2], x1_[:, i, :], rs3[:, j:j + 1], fn_bc, ALU.mult, ALU.mult, [X1K(jb_, i), fn_ + "rs3_%d" % j, "fn_bc"],
                [otk[i % 2]])
            DMA("pool", out[b, j * 128:(j + 1) * 128, :], otb[i % 2], [otk[i % 2]], [fn_ + "out%d" % j])

        for grp in prep_pieces(0):
            for f_ in grp:
                f_()
        for jb in range(4):
            x1 = x1b[jb % 2]
            h2T = h2Tb[jb % 2]
            H2 = [fn_ + "h2T%d_%d_%d" % (jb % 2, i, c) for i in range(4) for c in range(8)]
            X1 = [X1K(jb, i) for i in range(4)]
            if b == 0 and jb == 0:
                dump("x1", x1, X1)
                dump("h2T", h2T, H2)
            for p in range(8):
                wp = w1p[p % 2]
                wk = fn_ + "w1p%d" % (p % 2)
                DMA("sp", wp, w1s[p].rearrange("q (k c) -> q k c", k=8), ["w1s%d" % p], [wk])
                for q4 in range(4):
                    jj = 4 * p + q4
                    pu = ps[3 + jj % 2]
                    MMG([(pu, wp[:, kc, q4 * 128:(q4 + 1) * 128], h2T[:, kc, :], kc == 0, kc == 7) for kc in range(8)],
                        [wk] + H2, [PS[3 + jj % 2]])
                    rb = rbuf[jj % 2]
                    ACT(rb, pu, AF.Relu, [PS[3 + jj % 2]], [fn_ + "rb%d" % (jj % 2)])
                    TT("dve", uT[:, jj, :], rb, rb, ALU.mult, [fn_ + "rb%d" % (jj % 2)], [fn_ + "uT%d" % jj])
                if jb >= 1 and p < 4:
                    final_norm(jb - 1, p)
            UT = [fn_ + "uT%d" % jj for jj in range(32)]
            if b == 0 and jb == 0:
                dump("uT", uT, UT)

            def mlp2_tail(dq):
                yk = fn_ + "yT%d" % (dq % 2)
                S.add("pe", (lambda src, dstp: (lambda e: [e.transpose(dstp[:, i * 128:(i + 1) * 128], src[:, i * 128:(i + 1) * 128], identf)
                                                         for i in range(4)][-1]))(yT[dq % 2], ps[7]),
                      [yk, "identf"], [PS[7]])
                xv = x1[:, :, dq * 128:(dq + 1) * 128]
                TT("dve", xv, xv, ps[7].rearrange("p (i c) -> p i c", i=4), ALU.add, [PS[7]] + X1, X1)

            nxt = prep_pieces(jb + 1) if jb + 1 < 4 else []
            for dq in range(8):
                wp = w2p[dq % 2]
                wk = fn_ + "w2p%d" % (dq % 2)
                DMA("sp", wp, w2s[dq].rearrange("q (j c) -> q j c", j=32), ["w2s%d" % dq], [wk])
                pv = ps[5 + dq % 2]
                MMG([(pv, wp[:, jj, :], uT[:, jj, :], jj == 0, jj == 31) for jj in range(32)], [wk] + UT, [PS[5 + dq % 2]])
                yk = fn_ + "yT%d" % (dq % 2)
                ACT(yT[dq % 2], pv, AF.Identity, [PS[5 + dq % 2]] + MODT2, [yk], scale=modT[:, 40 + dq, b:b + 1])
                if dq >= 1:
                    mlp2_tail(dq - 1)
                if nxt:
                    for f_ in nxt.pop(0):
                        f_()
            mlp2_tail(7)
            while nxt:
                for f_ in nxt.pop(0):
                    f_()
            if jb == 3:
                for i in range(4):
                    final_norm(jb, i)
        S.barrier()
        if stop == 'b0' and b == 0:
            S.barrier(); S.finalize(); S.emit(); return nc

    S.finalize()
    S.emit()
    return nc


def _consts():
    bf = ml_dtypes.bfloat16
    ident = np.eye(128, dtype=np.float32)
    s = np.arange(128)[:, None]
    t = np.arange(128)[None, :]
    mask = np.concatenate([(s <= t), (s >= t)], axis=1).astype(np.float32)
    rm = np.ones((128, 2, 512), np.float32)
    rm[:, 0, 0::128] = 0.0
    rm[:, 1, 127::128] = 0.0
    tok = np.arange(T)
    row = (tok // 64).astype(np.float32)
    col = (tok % 64).astype(np.float32)
    nfreq = 8
    inv = (np.float32(10000.0) ** (-np.arange(nfreq, dtype=np.float32) / np.float32(nfreq))).astype(np.float32)
    cs = np.zeros((128, 2, T), np.float32)
    for dmm in range(32):
        grp, i = dmm // 16, dmm % 16
        f = i % 8
        ang = (row if grp == 0 else col) * inv[f]
        c, sn = np.cos(ang.astype(np.float32)), np.sin(ang.astype(np.float32))
        sign = -1.0 if i < 8 else 1.0
        for q in range(4):
            cs[q * 32 + dmm, 0] = c
            cs[q * 32 + dmm, 1] = sign * sn
    return dict(c_ident=ident.astype(bf), c_identf=ident, c_mask=mask.astype(bf), c_rmask=rm.astype(bf), c_cs=cs.astype(bf))


def _swap_rope(w, base, period):
    w = w.copy()
    ncol = w.shape[-1]
    for h0 in range(0, ncol, period):
        r0 = h0 + base
        blk = w[..., r0:r0 + 32].copy()
        new = blk.copy()
        for grp in range(2):
            o = grp * 16
            new[..., o:o + 8] = blk[..., o + 8:o + 16]
            new[..., o + 8:o + 16] = blk[..., o:o + 8]
        w[..., r0:r0 + 32] = new
    return w


def _kp(w, nk):
    return np.ascontiguousarray(w.reshape(nk, 128, -1).transpose(1, 0, 2))


def _shared_inputs(inp):
    f = np.float32
    w_in = inp["w_in"][0]
    w_uq = inp["w_uq"][0]
    w_ukv = inp["w_ukv"][0].reshape(128, 8, 128)
    kr = np.zeros((1024, 96), f)
    kr[:, 64:96] = w_in[:, 384:416]
    kr = _swap_rope(kr, 64, 96)
    w1 = inp["w_mlp_in"][0]
    w1r = np.ascontiguousarray(w1.reshape(8, 128, 8, 512).transpose(2, 1, 0, 3)).reshape(8, 128, 4096)
    w2 = inp["w_mlp_out"][0]
    w2r = np.ascontiguousarray(w2.reshape(32, 128, 8, 128).transpose(2, 1, 0, 3)).reshape(8, 128, 4096)
    sh = dict(
        w_ada=_kp(inp["w_ada"][0], 8),
        b_adaT=np.ascontiguousarray(inp["b_ada"][0].reshape(48, 128).T),
        b_ada=np.ascontiguousarray(inp["b_ada"][0]),
        nmixT=np.ascontiguousarray(inp["norm_mix"][0].reshape(8, 128).T),
        nmlpT=np.ascontiguousarray(inp["norm_mlp"][0].reshape(8, 128).T),
        w_in=_kp(w_in, 8),
        w_krs=_kp(kr, 8),
        qnT=np.ascontiguousarray(inp["q_norm"][0].reshape(2, 128).T),
        kvnT=np.ascontiguousarray(inp["kv_norm"][0].reshape(128, 1)),
        w_uq=_kp(w_uq, 2),
        w_uqs=_kp(_swap_rope(w_uq, 64, 96), 2),
        w_kn=np.ascontiguousarray(w_ukv[:, :, 0:64].reshape(128, 512)),
        w_v=np.ascontiguousarray(w_ukv[:, :, 64:128].reshape(128, 512)),
        lbT=np.ascontiguousarray(inp["hgrn_lb"].reshape(2, 8, 128).transpose(2, 0, 1)),
        hgn=np.ascontiguousarray(inp["hgrn_norm"][0]),
        w_out=_kp(inp["w_out"][0], 8),
        w1=w1r, w2=w2r,
        fnorm=np.ascontiguousarray(inp["final_norm"]),
    )
    sh = {k: np.ascontiguousarray(v, dtype=f) for k, v in sh.items()}
    sh.update(_consts())
    return sh


_NC_CACHE = {}


def kernel(**inputs):
    inp = {k: np.asarray(v) for k, v in inputs.items()}
    shared = _shared_inputs(inp)
    in_maps = []
    for c in range(8):
        b0 = c * NB
        cv = np.stack([inp["c"][b0], inp["c"][b0 + 1], inp["c_ctx"]], axis=0)
        m = dict(shared)
        m["x"] = np.ascontiguousarray(inp["x"][b0:b0 + NB], dtype=np.float32)
        m["ctx"] = np.ascontiguousarray(inp["ctx"][b0:b0 + NB], dtype=np.float32)
        m["cvecT"] = np.ascontiguousarray(cv.reshape(3, 8, 128).transpose(2, 1, 0), dtype=np.float32)
        in_maps.append(m)
    if "nc" not in _NC_CACHE:
        _NC_CACHE["nc"] = build_program()
    res = run_bass_kernel_spmd(_NC_CACHE["nc"], in_maps, core_ids=list(range(8)))
    return np.concatenate([np.asarray(r["out"]) for r in res.results], axis=0).astype(np.float32)
```

```python
import numpy as np
import ml_dtypes
import concourse.bass as bass
import concourse.mybir as mybir
from concourse.bass_utils import run_bass_kernel_spmd

F32 = mybir.dt.float32
BF16 = mybir.dt.bfloat16
AF = mybir.ActivationFunctionType
ALU = mybir.AluOpType

NB = 2
T = 2048
L = 256
TE = T + L
D = 1024
NT = TE // 128
DFF = 4096
EPS = 1e-6
BLKS = [(0, 256)] + [(256 + 512 * j, 512) for j in range(4)]
ATT_SCALE = float(96 ** -0.5)
DMA_K = 8


class _Op:
    __slots__ = ("idx", "eng", "fn", "dma", "deps", "signal", "tick", "sem_i", "sem_v")

    def __init__(self, idx, eng, fn, dma, deps):
        self.idx, self.eng, self.fn, self.dma, self.deps = idx, eng, fn, dma, deps
        self.signal = False
        self.tick = 0
        self.sem_i = 0
        self.sem_v = 0


class Sched:
    ENGS = ("pe", "act", "dve", "pool", "sp")

    def __init__(self, nc):
        self.nc = nc
        self.ops = []
        self.last_w = {}
        self.readers = {}
        self.dma_ops = {"sp": [], "pool": [], "act": []}

    def add(self, eng, fn, r=(), w=(), dma=False):
        idx = len(self.ops)
        deps = {}
        for k in r:
            p = self.last_w.get(k)
            if p is not None:
                deps[p] = "raw"
        for k in w:
            p = self.last_w.get(k)
            if p is not None and p not in deps:
                deps[p] = "waw"
            for q in self.readers.get(k, ()):
                if q not in deps:
                    deps[q] = "war"
        if dma:
            lst = self.dma_ops[eng]
            if len(lst) >= DMA_K:
                deps[lst[-DMA_K]] = "raw"
            lst.append(idx)
        op = _Op(idx, eng, fn, dma, deps)
        for k in w:
            self.last_w[k] = idx
            self.readers[k] = []
        for k in r:
            self.readers.setdefault(k, []).append(idx)
        self.ops.append(op)
        return op

    def barrier(self):
        last = {}
        for op in self.ops:
            if not op.dma and op.fn is not None:
                last[op.eng] = op.idx
        dmas = [i for q in self.dma_ops.values() for i in q[-DMA_K:]]
        for e in self.ENGS:
            deps = {i: "raw" for ee, i in last.items()}
            for i in dmas:
                deps[i] = "raw"
            idx = len(self.ops)
            op = _Op(idx, e, None, False, deps)
            op.deps = {k: ("bar") for k in deps}
            self.ops.append(op)

    def finalize(self):
        ops = self.ops
        for q, lst in self.dma_ops.items():
            for n, i in enumerate(lst):
                ops[i].sem_i = n % DMA_K
                ops[i].sem_v = 16 * (n // DMA_K + 1)
        self.waits = {}
        for op in ops:
            best = {}
            dma_w = {}
            for p, kind in op.deps.items():
                po = ops[p]
                if po.dma:
                    key = (po.eng, po.sem_i)
                    dma_w[key] = max(dma_w.get(key, 0), po.sem_v)
                    continue
                if po.fn is None:
                    continue
                if po.eng == op.eng and not op.dma and kind != "bar":
                    if po.eng == "pe":
                        continue
                if p > best.get(po.eng, -1):
                    best[po.eng] = p
            for e, p in best.items():
                ops[p].signal = True
            self.waits[op.idx] = (best, dma_w)
        cnt = {e: 0 for e in self.ENGS}
        for op in ops:
            if op.signal:
                cnt[op.eng] += 1
                op.tick = cnt[op.eng]

    def emit(self, final_wait_eng="sp"):
        nc = self.nc
        ops = self.ops
        from contextlib import ExitStack
        with ExitStack() as st:
            esem = {e: st.enter_context(nc.semaphore("cs_" + e)) for e in self.ENGS}
            dsem = {q: [st.enter_context(nc.semaphore("ds_%s%d" % (q, i))) for i in range(DMA_K)]
                    for q in self.dma_ops}
            block = st.enter_context(nc.Block())
            per_eng = {e: [op for op in ops if op.eng == e] for e in self.ENGS}
            waits = self.waits

            def body(ename, eng):
                seen = {}
                for op in per_eng[ename]:
                    best, dma_w = waits[op.idx]
                    for pe_, p in best.items():
                        t = ops[p].tick
                        key = ("c", pe_)
                        if seen.get(key, 0) < t:
                            eng.wait_ge(esem[pe_], t)
                            seen[key] = t
                    for (q, si), v in dma_w.items():
                        key = ("d", q, si)
                        if seen.get(key, 0) < v:
                            eng.wait_ge(dsem[q][si], v)
                            seen[key] = v
                    if op.fn is None:
                        continue
                    inst = op.fn(eng)
                    if op.dma:
                        inst.then_inc(dsem[op.eng][op.sem_i], 16)
                    elif op.signal:
                        inst.then_inc(esem[ename], 1)
                if ename == final_wait_eng:
                    for q, lst in self.dma_ops.items():
                        for i in lst[-DMA_K:]:
                            eng.wait_ge(dsem[q][ops[i].sem_i], ops[i].sem_v)

            @block.tensor
            def _(e):
                body("pe", e)

            @block.scalar
            def _(e):
                body("act", e)

            @block.vector
            def _(e):
                body("dve", e)

            @block.gpsimd
            def _(e):
                body("pool", e)

            @block.sync
            def _(e):
                body("sp", e)


class Arena:
    def __init__(self, nc, lo=16896, hi=229376):
        self.nc, self.lo, self.hi, self.off = nc, lo, hi, lo
        self.n = 0

    def alloc(self, shape, dt):
        nbytes = int(np.prod(shape[1:])) * (2 if dt == BF16 else 4)
        nbytes = (nbytes + 63) // 64 * 64
        assert self.off + nbytes <= self.hi, ("SBUF overflow", self.off, nbytes)
        self.n += 1
        t = self.nc.alloc_sbuf_tensor_at("sb%d" % self.n, list(shape), dt, offset=self.off)
        self.off += nbytes
        return t.ap()

    def mark(self):
        return self.off

    def release(self, m):
        self.off = m


def build_program(dbg=None, stop=None):
    nc = bass.Bass("TRN2", target_bir_lowering=False)
    S = Sched(nc)
    A = Arena(nc)

    def din(name, shape, dt=F32):
        return nc.dram_tensor(name, list(shape), dt, kind="ExternalInput").ap()

    x = din("x", [NB, T, D])
    ctx = din("ctx", [NB, L, D])
    cvecT = din("cvecT", [128, 8, 3])
    w_ada = din("w_ada", [128, 8, 6144])
    b_adaT = din("b_adaT", [128, 48])
    b_ada = din("b_ada", [6144])
    nmixT = din("nmixT", [128, 8])
    nmlpT = din("nmlpT", [128, 8])
    w_in = din("w_in", [128, 8, 2976])
    w_krs = din("w_krs", [128, 8, 96])
    qnT = din("qnT", [128, 2])
    kvnT = din("kvnT", [128, 1])
    w_uq = din("w_uq", [128, 2, 768])
    w_uqs = din("w_uqs", [128, 2, 768])
    w_kn = din("w_kn", [128, 512])
    w_v = din("w_v", [128, 512])
    lbT = din("lbT", [128, 2, 8])
    hgn = din("hgn", [128])
    w_out = din("w_out", [128, 8, 1024])
    w1 = din("w1", [8, 128, 4096])
    w2 = din("w2", [8, 128, 4096])
    fnorm = din("fnorm", [D])
    c_ident = din("c_ident", [128, 128], BF16)
    c_identf = din("c_identf", [128, 128], F32)
    c_mask = din("c_mask", [128, 256], BF16)
    c_rmask = din("c_rmask", [128, 2, 512], BF16)
    c_cs = din("c_cs", [128, 2, T], BF16)
    out = nc.dram_tensor("out", [NB, T, D], F32, kind="ExternalOutput").ap()
    w1s = nc.dram_tensor("w1s", [8, 128, 4096], BF16).ap()
    w2s = nc.dram_tensor("w2s", [8, 128, 4096], BF16).ap()
    dbg_out = {}
    if dbg:
        for name, shape, dt in dbg:
            dbg_out[name] = nc.dram_tensor("dbg_" + name, list(shape), dt, kind="ExternalOutput").ap()

    ps = [nc.alloc_psum_tensor("ps%d" % i, [128, 512], F32).ap() for i in range(8)]
    psb = [p.bitcast(BF16) for p in ps]
    PS = ["ps%d" % i for i in range(8)]

    def ACT(out_, in_, func, r, w, scale=1.0, bias=0.0, accum=None):
        kw = {}
        if accum is not None:
            kw["accum_out"] = accum
        S.add("act", lambda e: e.activation(out=out_, in_=in_, func=func, bias=bias, scale=scale, **kw), r, w)

    def TT(eng, out_, a, b, op, r, w):
        S.add(eng, lambda e: e.tensor_tensor(out_, a, b, op), r, w)

    def TS(eng, out_, a, s1, s2, op0, op1, r, w):
        if s2 is None:
            S.add(eng, lambda e: e.tensor_scalar(out_, a, s1, None, op0), r, w)
        else:
            S.add(eng, lambda e: e.tensor_scalar(out_, a, s1, s2, op0, op1), r, w)

    def STT(eng, out_, a, sc, b, op0, op1, r, w):
        S.add(eng, lambda e: e.scalar_tensor_tensor(out_, a, sc, b, op0, op1), r, w)

    def CP(eng, out_, in_, r, w):
        if eng == "act":
            S.add("act", lambda e: e.copy(out_, in_), r, w)
        else:
            S.add(eng, lambda e: e.tensor_copy(out_, in_), r, w)

    def MSET(eng, ap, val, w):
        S.add(eng, lambda e: e.memset(ap, val), (), w)

    def MMG(lst, r, w):
        lst = list(lst)

        def fn(e):
            ins = None
            for (o, l, rr, st, sp) in lst:
                ins = e.matmul(o, lhsT=l, rhs=rr, start=st, stop=sp)
            return ins
        S.add("pe", fn, r, w)

    def TRG(lst, ident_ap, r, w):
        lst = list(lst)

        def fn(e):
            ins = None
            for (o, i_) in lst:
                ins = e.transpose(o, i_, ident_ap)
            return ins
        S.add("pe", fn, r, w)

    def DMA(q, out_, in_, r, w):
        S.add(q, lambda e: e.dma_start(out=out_, in_=in_), r, w, dma=True)

    def RECIP(out_, in_, r, w):
        S.add("dve", lambda e: e.reciprocal(out_, in_), r, w)

    def SCAN(out_, d0, d1, r, w):
        S.add("dve", lambda e: e.tensor_tensor_scan(out_, d0, d1, 0.0, ALU.mult, ALU.add), r, w)

    def dump(name, src_ap, r):
        if name in dbg_out:
            DMA("sp", dbg_out[name], src_ap, r, ["dbg_" + name])

    def rstd_chain(ss, tmp, rs, n_inv, r, w_tmp, w_rs):
        ACT(tmp, ss, AF.Ln, r, w_tmp, scale=n_inv, bias=epsc[:, 0:1])
        ACT(rs, tmp, AF.Exp, w_tmp, w_rs, scale=-0.5)

    ident = A.alloc([128, 128], BF16)
    ones_bf = A.alloc([128, 128], BF16)
    maskfb2 = A.alloc([128, 2, 256], BF16)
    rmask = A.alloc([128, 2, 512], BF16)
    epsc = A.alloc([128, 2], F32)
    fn_bc = A.alloc([128, D], F32)
    gn_bc = A.alloc([128, 512], F32)
    G1 = A.alloc([128, NB, D], F32)
    modT = A.alloc([128, 48, 3], F32)
    A1 = A.alloc([128, 3, 8], F32)
    A2 = A.alloc([128, 3, 8], F32)
    lb_t = A.alloc([128, 8], F32)
    w_uq_bf = A.alloc([128, 2, 768], BF16)
    w_uqs_bf = A.alloc([128, 2, 768], BF16)
    w_kn_bf = A.alloc([128, 512], BF16)
    w_v_bf = A.alloc([128, 512], BF16)

    DMA("sp", ident, c_ident, [], ["ident"])
    DMA("sp", maskfb2[:, 0, :], c_mask, [], ["maskfb"])
    DMA("sp", maskfb2[:, 1, :], c_mask, [], ["maskfb"])
    DMA("sp", rmask, c_rmask, [], ["rmask"])
    DMA("sp", fn_bc, fnorm.partition_broadcast(128), [], ["fn_bc"])
    for i in range(4):
        DMA("sp", gn_bc[:, i * 128:(i + 1) * 128], hgn.partition_broadcast(128), [], ["gn_bc%d" % i])
    GN = ["gn_bc%d" % i for i in range(4)]
    MSET("dve", ones_bf, 1.0, ["ones"])
    MSET("dve", epsc, EPS, ["epsc"])

    m_setup = A.mark()
    wa = A.alloc([128, 8, 6144], BF16)
    cT = A.alloc([128, 8, 3], F32)
    sT = A.alloc([128, 8, 3], BF16)
    sTb = A.alloc([128, NB, 8, 128], BF16)
    badaT = A.alloc([128, 48], F32)
    bada_g1 = A.alloc([128, D], F32)
    nmix_t = A.alloc([128, 8], F32)
    nmlp_t = A.alloc([128, 8], F32)
    lbraw = A.alloc([128, 2, 8], F32)
    lbtmp = A.alloc([128, 8], F32)
    qn_t = A.alloc([128, 2], F32)
    kvn_t = A.alloc([128, 1], F32)
    wst = A.alloc([128, 2, 768], F32)
    wst2 = A.alloc([128, 2, 768], F32)
    wst3 = A.alloc([128, 1024], F32)

    for kc in range(8):
        DMA("pool", wa[:, kc, 0:2048], w_ada[:, kc, 0:2048], [], ["wa%d_a" % kc])
    for kc in range(8):
        DMA("pool", wa[:, kc, 2048:6144], w_ada[:, kc, 2048:6144], [], ["wa%d_b" % kc])
    DMA("sp", cT, cvecT, [], ["cT"])
    DMA("sp", badaT, b_adaT, [], ["badaT"])
    DMA("sp", bada_g1, b_ada[2048:3072].partition_broadcast(128), [], ["bada_g1"])
    DMA("sp", nmix_t, nmixT, [], ["nmix"])
    DMA("sp", nmlp_t, nmlpT, [], ["nmlp"])
    DMA("sp", lbraw, lbT, [], ["lbraw"])
    DMA("sp", qn_t, qnT, [], ["qn"])
    DMA("sp", kvn_t, kvnT, [], ["kvn"])
    DMA("sp", wst, w_uq, [], ["wst"])
    DMA("sp", wst2, w_uqs, [], ["wst2"])
    DMA("sp", wst3[:, 0:512], w_kn, [], ["wst3a"])
    DMA("sp", wst3[:, 512:1024], w_v, [], ["wst3b"])

    TT("dve", lbtmp, lbraw[:, 1, :], lbraw[:, 0, :], ALU.subtract, ["lbraw"], ["lbtmp"])
    ACT(lbtmp, lbtmp, AF.Exp, ["lbtmp"], ["lbtmp"])
    TS("dve", lbtmp, lbtmp, 1.0, None, ALU.add, None, ["lbtmp"], ["lbtmp"])
    RECIP(lb_t, lbtmp, ["lbtmp"], ["lb_t"])
    for c in range(2):
        TS("dve", w_uq_bf[:, c, :], wst[:, c, :], qn_t[:, c:c + 1], None, ALU.mult, None, ["wst", "qn"], ["w_uq_bf%d" % c])
        TS("dve", w_uqs_bf[:, c, :], wst2[:, c, :], qn_t[:, c:c + 1], None, ALU.mult, None, ["wst2", "qn"], ["w_uqs_bf%d" % c])
    TS("dve", w_kn_bf, wst3[:, 0:512], kvn_t[:, 0:1], None, ALU.mult, None, ["wst3a", "kvn"], ["w_kn_bf"])
    TS("dve", w_v_bf, wst3[:, 512:1024], kvn_t[:, 0:1], None, ALU.mult, None, ["wst3b", "kvn"], ["w_v_bf"])
    WUQ = ["w_uq_bf0", "w_uq_bf1"]
    WUQS = ["w_uqs_bf0", "w_uqs_bf1"]

    ACT(sT, cT, AF.Silu, ["cT"], ["sT"])
    WAa = ["wa%d_a" % k for k in range(8)]
    WA = ["wa%d_b" % k for k in range(8)]
    psM = ps[0][:, 0:144]
    psM2 = ps[3][:, 0:144]
    MMG([(psM[:, j * 3:(j + 1) * 3], wa[:, kc, j * 128:(j + 1) * 128], sT[:, kc, :], kc == 0, kc == 7)
         for j in range(16) for kc in range(8)], WAa + ["sT"], [PS[0]])
    for b in range(3):
        TT("dve", modT[:, 0:16, b], ps[0][:, b:48:3], badaT[:, 0:16], ALU.add, [PS[0], "badaT"], ["modT%d" % b])
    MODT = ["modT0", "modT1", "modT2"]
    MMG([(psM2[:, j * 3:(j + 1) * 3], wa[:, kc, j * 128:(j + 1) * 128], sT[:, kc, :], kc == 0, kc == 7)
         for j in range(16, 48) for kc in range(8)], WA + ["sT"], [PS[3]])
    MODT2 = ["modTb0", "modTb1", "modTb2"]
    for b in range(3):
        TT("dve", modT[:, 16:48, b], ps[3][:, 48 + b:144:3], badaT[:, 16:48], ALU.add, [PS[3], "badaT"], ["modTb%d" % b])
    for b in range(3):
        STT("dve", A1[:, b, :], modT[:, 8:16, b], 1.0, nmix_t, ALU.add, ALU.mult, MODT + ["nmix"], ["A1_%d" % b])
        STT("dve", A2[:, b, :], modT[:, 32:40, b], 1.0, nmlp_t, ALU.add, ALU.mult, MODT2 + ["nmlp"], ["A2_%d" % b])
    for b in range(NB):
        for kc in range(8):
            CP("dve", sTb[:, b, kc, :], sT[:, kc, b:b + 1].to_broadcast([128, 128]), ["sT"], ["sTb%d_%d" % (b, kc)])
        for half in range(2):
            pg = ps[1 + half]
            MMG([(pg, sTb[:, b, kc, :], wa[:, kc, 2048 + half * 512:2048 + (half + 1) * 512], kc == 0, kc == 7)
                 for kc in range(8)], WA + ["sTb%d_%d" % (b, kc) for kc in range(8)], [PS[1 + half]])
            TT("dve", G1[:, b, half * 512:(half + 1) * 512], pg, bada_g1[:, half * 512:(half + 1) * 512], ALU.add,
               [PS[1 + half], "bada_g1"], ["G1_%d_%d" % (b, half)])
    dump("modT", modT, MODT + MODT2)
    dump("G1", G1, ["G1_%d_%d" % (b, h) for b in range(NB) for h in range(2)])
    dump("lb", lb_t, ["lb_t"])
    S.barrier()
    if stop == 'setup':
        S.finalize(); S.emit(); return nc
    A.release(m_setup)

    o_mlaT = A.alloc([128, 4, T], BF16)
    o_hgT = A.alloc([128, 4, T], BF16)
    m_fin = A.mark()
    hT = A.alloc([128, 8, TE], BF16)
    m_batch = A.mark()

    for b in range(NB):
        bn = "b%d_" % b

        def HK(g):
            return [bn + "hT%d_%d" % (g, c) for c in range(8)]

        def HKB(t0, n):
            r = []
            for g in range(t0 // 128, (t0 + n) // 128):
                r += HK(g)
            return r

        A.release(m_batch)
        xsb = [A.alloc([128, D], BF16) for _ in range(2)]
        junk = A.alloc([128, D], BF16)
        ssA = A.alloc([128, NT], F32)
        lnA = A.alloc([128, NT], F32)
        rsA = A.alloc([128, NT], F32)
        NXB = 3
        xtb = [A.alloc([128, D], F32) for _ in range(NXB)]

        def pa1(g):
            src = ctx[b, g * 128:(g + 1) * 128, :] if g < 2 else x[b, (g - 2) * 128:(g - 1) * 128, :]
            k = g % NXB
            xk = bn + "xt%d" % k
            DMA("sp", xtb[k], src, [], [xk])
            ACT(junk, xtb[k], AF.Square, [xk], [bn + "junk", bn + "ssA%d" % g], accum=ssA[:, g:g + 1])
            rstd_chain(ssA[:, g:g + 1], lnA[:, g:g + 1], rsA[:, g:g + 1], 1.0 / D,
                       [bn + "ssA%d" % g, "epsc"], [bn + "lnA%d" % g], [bn + "rsA%d" % g])

        def pa2(g):
            k = g % NXB
            xk, sk = bn + "xt%d" % k, bn + "xs%d" % (g % 2)
            TS("dve", xsb[g % 2], xtb[k], rsA[:, g:g + 1], None, ALU.mult, None, [xk, bn + "rsA%d" % g], [sk])
            pt = psb[g % 2]
            TRG([(pt[:, c * 128:(c + 1) * 128], xsb[g % 2][:, c * 128:(c + 1) * 128]) for c in range(8)], ident,
                [sk, "ident"], [PS[g % 2]])

        def pa3(g):
            bb = 2 if g < 2 else b
            pt = psb[g % 2]
            for c in range(8):
                dst = hT[:, c, g * 128:(g + 1) * 128]
                if g % 4 == 0:
                    ACT(dst, pt[:, c * 128:(c + 1) * 128], AF.Identity, [PS[g % 2], "A1_%d" % bb] + MODT, [bn + "hT%d_%d" % (g, c)],
                        scale=A1[:, bb, c:c + 1], bias=modT[:, c, bb:bb + 1])
                else:
                    TS("dve", dst, pt[:, c * 128:(c + 1) * 128], A1[:, bb, c:c + 1], modT[:, c, bb:bb + 1], ALU.mult, ALU.add,
                       [PS[g % 2], "A1_%d" % bb] + MODT, [bn + "hT%d_%d" % (g, c)])

        for tt_ in range(NT + 2):
            if tt_ - 2 >= 0:
                pa3(tt_ - 2)
            if 0 <= tt_ - 1 < NT:
                pa2(tt_ - 1)
            if tt_ < NT:
                pa1(tt_)
        if b == 0:
            dump("hT", hT, [k for g in range(NT) for k in HK(g)])

        if stop == 'A' and b == 0:
            S.barrier(); S.finalize(); S.emit(); return nc
        S.barrier()
        A.release(m_batch)
        m_mla = A.mark()
        cs = A.alloc([128, 2, T], BF16)
        w_mla = A.alloc([128, 8, 416], BF16)
        w_krs_bf = A.alloc([128, 8, 96], BF16)
        cqnT = A.alloc([128, 2, TE], BF16)
        ckvnT = A.alloc([128, TE], BF16)
        krotT = A.alloc([128, TE], BF16)
        Vaug = A.alloc([128, NT, 8, 128], BF16)
        KTb = [A.alloc([128, TE], BF16) for _ in range(2)]
        QTb = [A.alloc([128, T], BF16) for _ in range(2)]
        PTb = [A.alloc([128, 512], BF16) for _ in range(3)]
        sq = A.alloc([128, 3, 512], BF16)
        tA = A.alloc([128, 512], F32)
        tB = A.alloc([128, 512], F32)
        rq_bc = A.alloc([128, 512], F32)
        rkv_bc = A.alloc([128, 512], F32)
        rp1 = [A.alloc([128, 512], F32) for _ in range(1)]
        rp2 = [A.alloc([128, 512], F32) for _ in range(1)]
        rden = [A.alloc([128, 512], F32) for _ in range(2)]

        DMA("sp", cs, c_cs, [], [bn + "cs"])
        MSET("dve", Vaug, 1.0, [bn + "Vp%d" % g for g in range(NT)])
        DMA("pool", w_mla, w_in[:, :, 0:416], [], [bn + "w_mla"])
        DMA("pool", w_krs_bf, w_krs, [], [bn + "w_krs"])
        WM = [bn + "w_mla"]

        v_defer = []
        for bi, (t0, n) in enumerate(BLKS):
            hk = HKB(t0, n)
            tok = slice(t0, t0 + n)
            for c in range(2):
                MMG([(ps[c][:, 0:n], w_mla[:, kc, c * 128:(c + 1) * 128], hT[:, kc, tok], kc == 0, kc == 7) for kc in range(8)],
                    WM + hk, [PS[c]])
            MMG([(ps[2][:, 0:n], w_mla[:, kc, 256:384], hT[:, kc, tok], kc == 0, kc == 7) for kc in range(8)], WM + hk, [PS[2]])
            MMG([(ps[3][0:96, 0:n], w_mla[:, kc, 320:416], hT[:, kc, tok], kc == 0, kc == 7) for kc in range(8)], WM + hk, [PS[3]])
            if bi > 0:
                MMG([(ps[4][0:96, 0:n], w_krs_bf[:, kc, :], hT[:, kc, tok], kc == 0, kc == 7) for kc in range(8)],
                    [bn + "w_krs"] + hk, [PS[4]])
            while v_defer:
                v_defer.pop(0)()
            for c in range(3):
                ACT(sq[:, c, 0:n], ps[c][:, 0:n], AF.Square, [PS[c]], [bn + "sq%d" % c])
            MMG([(ps[5][:, 0:n], ones_bf, sq[:, 0, 0:n], True, False), (ps[5][:, 0:n], ones_bf, sq[:, 1, 0:n], False, True)],
                ["ones", bn + "sq0", bn + "sq1"], [PS[5]])
            MMG([(ps[6][:, 0:n], ones_bf, sq[:, 2, 0:n], True, True)], ["ones", bn + "sq2"], [PS[6]])
            rstd_chain(ps[5][:, 0:n], tA[:, 0:n], rq_bc[:, 0:n], 1.0 / 256, [PS[5], "epsc"], [bn + "tA"], [bn + "rq_bc"])
            rstd_chain(ps[6][:, 0:n], tB[:, 0:n], rkv_bc[:, 0:n], 1.0 / 128, [PS[6], "epsc"], [bn + "tB"], [bn + "rkv_bc"])
            for c in range(2):
                TT("dve", cqnT[:, c, tok], ps[c][:, 0:n], rq_bc[:, 0:n], ALU.mult, [PS[c], bn + "rq_bc"], [bn + "cqnT%d_%d" % (bi, c)])
            TT("dve", ckvnT[:, tok], ps[2][:, 0:n], rkv_bc[:, 0:n], ALU.mult, [PS[2], bn + "rkv_bc"], [bn + "ckvnT%d" % bi])
            if bi == 0:
                CP("dve", krotT[64:96, tok], ps[3][64:96, 0:n], [PS[3]], [bn + "krotT%d" % bi])
            else:
                lt = slice(t0 - L, t0 - L + n)
                TT("dve", rp1[0][64:96, 0:n], ps[3][64:96, 0:n], cs[64:96, 0, lt], ALU.mult, [PS[3], bn + "cs"], [bn + "rp1_0"])
                TT("dve", rp2[0][64:96, 0:n], ps[4][64:96, 0:n], cs[64:96, 1, lt], ALU.mult, [PS[4], bn + "cs"], [bn + "rp2_0"])
                TT("dve", krotT[64:96, tok], rp1[0][64:96, 0:n], rp2[0][64:96, 0:n], ALU.add, [bn + "rp1_0", bn + "rp2_0"],
                   [bn + "krotT%d" % bi])
            def v_part(bi=bi, t0=t0, n=n):
                for g in range(t0 // 128, (t0 + n) // 128):
                    MMG([(ps[7], ckvnT[:, g * 128:(g + 1) * 128], w_v_bf, True, True)], [bn + "ckvnT%d" % bi, "w_v_bf"], [PS[7]])
                    pv4 = ps[7].rearrange("p (j e c) -> p j e c", j=4, e=2)
                    veng = "act" if g % 2 else "dve"
                    CP(veng, Vaug[:, g, 0::2, 0:64], pv4[:, :, 0, :], [PS[7]], [bn + "Vp%d" % g])
                    CP(veng, Vaug[:, g, 1::2, 64:128], pv4[:, :, 1, :], [PS[7]], [bn + "Vp%d" % g])
            v_defer.append(v_part)
        while v_defer:
            v_defer.pop(0)()
        CQ = [bn + "cqnT%d_%d" % (bi, c) for bi in range(5) for c in range(2)]
        CKV = [bn + "ckvnT%d" % bi for bi in range(5)]
        KROT = [bn + "krotT%d" % bi for bi in range(5)]
        if b == 0:
            dump("cqnT", cqnT, CQ)
            dump("ckvnT", ckvnT, CKV)
            dump("krotT", krotT[64:96, :], KROT)

        if stop == 'mlaproj' and b == 0:
            S.barrier(); S.finalize(); S.emit(); return nc
        def proj_head(h):
            kb = h % 2
            KT, QT = KTb[kb], QTb[kb]
            kk, qk = bn + "KT%d" % kb, bn + "QT%d" % kb
            pc = 0
            for bi, (t0, n) in enumerate(BLKS):
                tok = slice(t0, t0 + n)
                pb, pk = ps[7 - pc % 2], PS[7 - pc % 2]
                pc += 1
                MMG([(pb[0:64, 0:n], w_kn_bf[:, h * 64:(h + 1) * 64], ckvnT[:, tok], True, True)],
                    ["w_kn_bf", bn + "ckvnT%d" % bi], [pk])
                CP("dve", KT[0:64, tok], pb[0:64, 0:n], [pk], [kk + "_n%d" % bi])
                yield
            CP("dve", KT[64:96, :], krotT[64:96, :], KROT, [kk + "_r"])
            for j in range(4):
                q0 = j * 512
                et = slice(L + q0, L + q0 + 512)
                lt = slice(q0, q0 + 512)
                cqk = [bn + "cqnT%d_%d" % (j + 1, c) for c in range(2)]
                pb, pk = ps[7 - pc % 2], PS[7 - pc % 2]
                pc += 1
                MMG([(pb[0:96, :], w_uq_bf[:, c, h * 96:(h + 1) * 96], cqnT[:, c, et], c == 0, c == 1) for c in range(2)],
                    WUQ + cqk, [pk])
                CP("dve", QT[0:64, lt], pb[0:64, :], [pk], [qk + "_n%d" % j])
                TT("dve", rp1[0][64:96, :], pb[64:96, :], cs[64:96, 0, lt], ALU.mult, [pk, bn + "cs"], [bn + "rp1_0"])
                yield
                pb, pk = ps[7 - pc % 2], PS[7 - pc % 2]
                pc += 1
                MMG([(pb[0:96, :], w_uqs_bf[:, c, h * 96:(h + 1) * 96], cqnT[:, c, et], c == 0, c == 1) for c in range(2)],
                    WUQS + cqk, [pk])
                TT("dve", rp2[0][64:96, :], pb[64:96, :], cs[64:96, 1, lt], ALU.mult, [pk, bn + "cs"], [bn + "rp2_0"])
                TT("dve", QT[64:96, lt], rp1[0][64:96, :], rp2[0][64:96, :], ALU.add, [bn + "rp1_0", bn + "rp2_0"], [qk + "_r%d" % j])
                yield

        def KTK(h):
            kk = bn + "KT%d" % (h % 2)
            return [kk + "_n%d" % bi for bi in range(5)] + [kk + "_r"]

        def QTK(h, j):
            qk = bn + "QT%d" % (h % 2)
            return [qk + "_n%d" % j, qk + "_r%d" % j]

        steps = [(h, qb, kt) for h in range(8) for qb in range(4) for kt in range(NT)]
        for _ in proj_head(0):
            pass
        pgen = None
        if b == 0:
            dump("KT0", KTb[0][0:96, :], KTK(0))
            dump("QT0", QTb[0][0:96, :], [k for j in range(4) for k in QTK(0, j)])

        SB = [0, 1, 2, 5]

        def reg_qk(si):
            h, qb, kt = steps[si]
            KT, QT = KTb[h % 2], QTb[h % 2]
            sb_ = SB[si % 4]
            MMG([(ps[sb_], KT[0:96, kt * 128:(kt + 1) * 128], QT[0:96, qb * 512:(qb + 1) * 512], True, True)],
                KTK(h) + QTK(h, qb), [PS[sb_]])

        reg_qk(0)
        reg_qk(1)
        reg_qk(2)
        for si, (h, qb, kt) in enumerate(steps):
            u = (h * 4 + qb) % 2
            pO = ps[3 + u]
            sb_ = SB[si % 4]
            ACT(PTb[si % 3], ps[sb_], AF.Exp, [PS[sb_]], [bn + "PT%d" % (si % 3)] + (["convgate"] if (b == 0 and si == 0) else []),
                scale=ATT_SCALE)
            if si + 3 < len(steps):
                reg_qk(si + 3)
            MMG([(pO, Vaug[:, kt, h, :], PTb[si % 3], kt == 0, kt == NT - 1)],
                [bn + "Vp%d" % kt, bn + "PT%d" % (si % 3)], [PS[3 + u]])
            if kt == NT - 1:
                orow = slice((h % 2) * 64, (h % 2) * 64 + 64)
                drow = slice(64 - (h % 2) * 64, 128 - (h % 2) * 64)
                RECIP(rden[u][orow, :], pO[drow, :], [PS[3 + u]], [bn + "rden%d" % u])
                TT("dve", o_mlaT[orow, h // 2, qb * 512:(qb + 1) * 512], pO[orow, :], rden[u][orow, :], ALU.mult,
                   [PS[3 + u], bn + "rden%d" % u], [bn + "omla%d_%d" % (h, qb)])
            if b == 0 and kt == 0 and qb == 0 and h == 0:
                for p in range(8):
                    DMA("pool", w1s[p], w1[p], ["convgate"], ["w1s%d" % p])
                for p in range(8):
                    DMA("pool", w2s[p], w2[p], [], ["w2s%d" % p])
            if qb == 0 and kt == 2 and h + 1 < 8:
                pgen = proj_head(h + 1)
            if pgen is not None and si % 4 == 0:
                try:
                    next(pgen)
                except StopIteration:
                    pgen = None
        OM = [bn + "omla%d_%d" % (h, qb) for h in range(8) for qb in range(4)]
        if b == 0:
            dump("o_mlaT", o_mlaT, OM)
        if stop == 'attn' and b == 0:
            S.barrier(); S.finalize(); S.emit(); return nc
        S.barrier()
        A.release(m_mla)

        v_tm = A.alloc([128, NT, 512], BF16)
        gate = A.alloc([128, 16, 512], BF16)
        eb_tab = A.alloc([128, 4, 2, NT], F32)
        m_h0 = A.mark()
        w_hig = A.alloc([128, 8, 1024], BF16)
        gtmp = [A.alloc([128, 512], F32) for _ in range(2)]
        DMA("pool", w_hig[:, :, 0:512], w_in[:, :, 1952:2464], [], [bn + "w_hig_v"])
        DMA("pool", w_hig[:, :, 512:1024], w_in[:, :, 2464:2976], [], [bn + "w_hig_g"])
        for g in range(NT):
            k = g % 2
            MMG([(ps[k], hT[:, kc, g * 128:(g + 1) * 128], w_hig[:, kc, 0:512], kc == 0, kc == 7) for kc in range(8)],
                [bn + "w_hig_v"] + HK(g), [PS[k]])
            CP("dve" if g % 2 else "act", v_tm[:, g, :], ps[k], [PS[k]], [bn + "v_tm%d" % g])
            if g >= 2:
                MMG([(ps[2 + k], hT[:, kc, g * 128:(g + 1) * 128], w_hig[:, kc, 512:1024], kc == 0, kc == 7) for kc in range(8)],
                    [bn + "w_hig_g"] + HK(g), [PS[2 + k]])
                ACT(gtmp[k], ps[2 + k], AF.Silu, [PS[2 + k]], [bn + "gtmp%d" % k])
                TT("dve", gate[:, g - 2, :], gtmp[k], gn_bc, ALU.mult, [bn + "gtmp%d" % k] + GN, [bn + "gate%d" % (g - 2)])
        if b == 0:
            dump("v_tm", v_tm, [bn + "v_tm%d" % g for g in range(NT)])
            dump("gate", gate, [bn + "gate%d" % j for j in range(16)])
        if stop == 'hg0' and b == 0:
            S.barrier(); S.finalize(); S.emit(); return nc
        S.barrier()
        A.release(m_h0)

        NB_H = 9
        NSET = 4
        whb = [A.alloc([128, 8, 384], BF16) for _ in range(1)]
        qTs = [A.alloc([128, T], BF16) for _ in range(2)]
        kTs = [A.alloc([128, T], BF16) for _ in range(2)]
        khat = A.alloc([128, NT, 2, 128], BF16)
        S_st = A.alloc([128, 2, 17, 128], BF16)
        khT = [A.alloc([128, 256], BF16) for _ in range(2)]
        ktmp = [A.alloc([128, 256], BF16) for _ in range(1)]
        TS1 = [A.alloc([128, 256], F32) for _ in range(NSET)]
        TS2 = [A.alloc([128, 256], F32) for _ in range(NSET)]
        TS3 = [A.alloc([128, 256], F32) for _ in range(NSET)]
        TSq = [A.alloc([128, 256], F32) for _ in range(NSET)]
        TSz = [A.alloc([128, 256], BF16) for _ in range(NSET)]
        ATb = A.alloc([128, 4, 256], BF16)
        o_sb = A.alloc([128, 4, 128], F32)
        og = A.alloc([128, 4, 128], BF16)
        ssh = A.alloc([128, 4], F32)
        lnh = A.alloc([128, 4], F32)
        rsh = A.alloc([128, 4], F32)
        junk2 = A.alloc([128, 128], BF16)

        units = []
        for h in range(4):
            fo = [(k_, 0) for k_ in range(NB_H)]
            bo = [(k_, 1) for k_ in [0] + list(range(NB_H - 1, 0, -1))]
            seq = (bo + fo) if h % 2 == 0 else (fo + bo)
            for (k_, d_) in seq:
                units.append((h, k_, d_))
        NU = len(units)

        def hkeys(h):
            return bn + "h%d_" % h

        def load_wh(h):
            wh = whb[0]
            whk = bn + "wh0"
            for i, c0 in enumerate((416, 928, 1440)):
                DMA("pool", wh[:, :, i * 128:(i + 1) * 128], w_in[:, :, c0 + h * 128:c0 + (h + 1) * 128], [], [whk + "_%d" % i])

        def uinfo(ui):
            h, k, d = units[ui]
            return h, k, d, whb[0], [bn + "wh0_%d" % i for i in range(3)], hkeys(h), ui % NSET, k > 0

        def st0(ui):
            h, k, d, wh, WHK, hn, s, lat = uinfo(ui)
            tok = slice(k * 256, k * 256 + 256)
            hk = HKB(k * 256, 256)
            pz = ps[ui % 2]
            MMG([(pz[:, 0:256], wh[:, kc, (1 + d) * 128:(2 + d) * 128], hT[:, kc, tok], kc == 0, kc == 7) for kc in range(8)],
                WHK + hk, [PS[ui % 2]])
            if lat:
                pq = ps[2]
                MMG([(pq[:, 0:256], wh[:, kc, 0:128], hT[:, kc, tok], kc == 0, kc == 7) for kc in range(8)], WHK + hk, [PS[2]])

        def st1(ui):
            h, k, d, wh, WHK, hn, s, lat = uinfo(ui)
            sk = bn + "ts%d_" % s
            pz = ps[ui % 2]
            ACT(TS1[s], pz[:, 0:256], AF.Exp, [PS[ui % 2]], [sk + "T1"], scale=-1.0)
            ACT(TS2[s], TS1[s], AF.Ln, [sk + "T1", "lb_t"], [sk + "T2"], scale=lb_t[:, d * 4 + h:d * 4 + h + 1], bias=1.0)
            ACT(TS1[s], TS1[s], AF.Ln, [sk + "T1"], [sk + "T1"], bias=1.0)
            if lat:
                pq = ps[2]
                ACT(TSq[s], pq[:, 0:256], AF.Exp, [PS[2]], [sk + "Tq"], scale=-1.0)
                ACT(TSz[s], pq[:, 0:256], AF.Copy, [PS[2]], [sk + "Tz"])
                ACT(TSq[s], TSq[s], AF.Ln, [sk + "Tq"], [sk + "Tq"], bias=1.0)

        def st2(ui):
            h, k, d, wh, WHK, hn, s, lat = uinfo(ui)
            sk = bn + "ts%d_" % s
            TT("dve", TS2[s], TS2[s], TS1[s], ALU.subtract, [sk + "T1", sk + "T2"], [sk + "T2"])
            if d == 0:
                SCAN(TS3[s], rmask[:, 0, 0:256], TS2[s], ["rmask", sk + "T2"], [sk + "T3"])
            else:
                SCAN(TS3[s][:, ::-1], rmask[:, 1, 0:256][:, ::-1], TS2[s][:, ::-1], ["rmask", sk + "T2"], [sk + "T3"])
            if lat:
                TT("dve", TSq[s], TS3[s], TSq[s], ALU.subtract, [sk + "T3", sk + "Tq"], [sk + "Tq"])

        def st3(ui):
            h, k, d, wh, WHK, hn, s, lat = uinfo(ui)
            sk = bn + "ts%d_" % s
            ACT(TS1[s], TS2[s], AF.Exp, [sk + "T2"], [sk + "T1"])
            lastcol = 127 if d == 0 else 0
            ACT(eb_tab[:, h, d, 2 * k:2 * k + 2], TS3[s][:, lastcol:256:128], AF.Exp, [sk + "T3"], [hn + "eb%d_%d" % (d, k)])
            if lat:
                ACT(TSq[s], TSq[s], AF.Exp, [sk + "Tq"], [sk + "Tq"])
            ACT(TS3[s], TS3[s], AF.Exp, [sk + "T3"], [sk + "T3"], scale=-1.0)

        def st4(ui):
            h, k, d, wh, WHK, hn, s, lat = uinfo(ui)
            sk = bn + "ts%d_" % s
            if lat:
                lt = slice((k - 1) * 256, k * 256)
                STT("dve", qTs[d][:, lt], TSz[s], -1.0, TSq[s], ALU.mult, ALU.mult, [sk + "Tz", sk + "Tq"], [bn + "qT%d_%d" % (d, k)])
                kdst = kTs[d][:, lt]
                kkey = bn + "kT%d_%d" % (d, k)
            else:
                kdst = ktmp[0]
                kkey = bn + "ktmp0"
            STT("dve", kdst, TS1[s], 1.0, TS3[s], ALU.subtract, ALU.mult, [sk + "T1", sk + "T3"], [kkey])
            kh = khT[ui % 2]
            for c in range(2):
                TS("dve", kh[:, c * 128:(c + 1) * 128], kdst[:, c * 128:(c + 1) * 128], eb_tab[:, h, d, 2 * k + c:2 * k + c + 1], None,
                   ALU.mult, None, [kkey, hn + "eb%d_%d" % (d, k)], [bn + "khT%d_%d" % (ui % 2, c)])

        def st5(ui):
            h, k, d, wh, WHK, hn, s, lat = uinfo(ui)
            kh = khT[ui % 2]
            ptk = psb[4 + ui % 2]
            TRG([(ptk[:, c * 128:(c + 1) * 128], kh[:, c * 128:(c + 1) * 128]) for c in range(2)], ident,
                [bn + "khT%d_%d" % (ui % 2, c) for c in range(2)] + ["ident"], [PS[4 + ui % 2]])

        def st6(ui):
            h, k, d, wh, WHK, hn, s, lat = uinfo(ui)
            ptk = psb[4 + ui % 2]
            CP("act" if ui % 2 else "dve", khat[:, 2 * k:2 * k + 2, d, :], ptk[:, 0:256].rearrange("p (c k) -> p c k", c=2),
               [PS[4 + ui % 2]], [bn + "khat%d_%d" % (d, k)])

        def scan_mm(h, g, d, k):
            hn = hkeys(h)
            pS = ps[6][:, d * 128:(d + 1) * 128]
            MMG([(pS, khat[:, g, d, :], v_tm[:, g, h * 128:(h + 1) * 128], True, True)],
                [bn + "khat%d_%d" % (d, k), bn + "v_tm%d" % g], [PS[6]])

        def scan_upd(h, g, d, k, p):
            hn = hkeys(h)
            pS = ps[6][:, d * 128:(d + 1) * 128]
            if p == 0:
                CP("dve", S_st[:, d, 0, :], pS, [PS[6]], [bn + "S%d_%d" % (d, 0)])
            else:
                STT("dve", S_st[:, d, p, :], S_st[:, d, p - 1, :], eb_tab[:, h, d, g:g + 1], pS, ALU.mult, ALU.add,
                    [bn + "S%d_%d" % (d, p - 1), hn + "eb%d_%d" % (d, k), PS[6]], [bn + "S%d_%d" % (d, p)])

        def grp_piece(h, gq, step):
            hn = hkeys(h)
            pO = ps[7]

            def pa_mm(i):
                j = 4 * gq + i
                tl = slice(j * 128, (j + 1) * 128)
                kb = j // 2 + 1
                pA = ps[3][:, (i % 2) * 256:(i % 2) * 256 + 256]
                MMG([(pA[:, 0:128], kTs[0][:, tl], qTs[0][:, tl], True, True),
                     (pA[:, 128:256], kTs[1][:, tl], qTs[1][:, tl], True, True)],
                    [bn + "kT%d_%d" % (d, kb) for d in range(2)] + [bn + "qT%d_%d" % (d, kb) for d in range(2)], [PS[3]])

            def mask2(i0):
                TT("dve", ATb[:, i0:i0 + 2, :], ps[3].rearrange("p (i c) -> p i c", i=2), maskfb2, ALU.mult, [PS[3], "maskfb"],
                   [bn + "AT%d" % i0, bn + "AT%d" % (i0 + 1)])

            def po_mm(i):
                j = 4 * gq + i
                g = j + 2
                kb = j // 2 + 1
                tl = slice(j * 128, (j + 1) * 128)
                vv = v_tm[:, g, h * 128:(h + 1) * 128]
                MMG([(pO[:, i * 128:(i + 1) * 128], ATb[:, i, 0:128], vv, True, False),
                     (pO[:, i * 128:(i + 1) * 128], ATb[:, i, 128:256], vv, False, False),
                     (pO[:, i * 128:(i + 1) * 128], qTs[0][:, tl], S_st[:, 0, j + 1, :], False, False),
                     (pO[:, i * 128:(i + 1) * 128], qTs[1][:, tl], S_st[:, 1, 16 - j, :], False, True)],
                    [bn + "AT%d" % i, bn + "v_tm%d" % g, bn + "qT0_%d" % kb, bn + "qT1_%d" % kb,
                     bn + "S0_%d" % (j + 1), bn + "S1_%d" % (16 - j)], [PS[7]])

            if step == 1:
                pa_mm(0)
                pa_mm(1)
                mask2(0)
            elif step == 2:
                po_mm(0)
                po_mm(1)
                pa_mm(2)
                pa_mm(3)
                mask2(2)
            elif step == 3:
                po_mm(2)
                po_mm(3)
                CP("dve", o_sb, pO.rearrange("p (i c) -> p i c", i=4), [PS[7]], [bn + "o_sb"])
                for i in range(4):
                    ACT(junk2, o_sb[:, i, :], AF.Square, [bn + "o_sb"], [bn + "junk2", bn + "ssh%d" % i], accum=ssh[:, i:i + 1])
                rstd_chain(ssh, lnh, rsh, 1.0 / 128, [bn + "ssh%d" % i for i in range(4)] + ["epsc"], [bn + "lnh"], [bn + "rsh"])
            elif step == 4:
                for i in range(4):
                    j = 4 * gq + i
                    STT("dve", og[:, i, :], o_sb[:, i, :], rsh[:, i:i + 1], gate[:, j, h * 128:(h + 1) * 128],
                        ALU.mult, ALU.mult, [bn + "o_sb", bn + "rsh", bn + "gate%d" % j], [bn + "og%d" % i])
            elif step == 5:
                pT = psb[6][:, 512:1024]
                TRG([(pT[:, i * 128:(i + 1) * 128], og[:, i, :]) for i in range(4)], ident,
                    [bn + "og%d" % i for i in range(4)] + ["ident"], [PS[6]])
                CP("dve", o_hgT[:, h, gq * 512:(gq + 1) * 512], pT, [PS[6]], [bn + "ohg%d_%d" % (h, gq)])

        load_wh(0)
        stages = [st0, st1, st2, st3, st4, st5, st6]
        from collections import deque
        scan_q = [deque(), deque()]
        blk_scanned = {}
        grp_todo = deque((h, gq) for h in range(4) for gq in (range(4) if h % 2 == 0 else range(3, -1, -1)))
        front = None
        back = None
        grp_done = {}
        tau = 0
        tc = 0

        def chain_hazard(tc_):
            for si in (4, 6):
                ui = tc_ - si
                if 0 <= ui < NU:
                    h, k, d = units[ui]
                    if si == 4 and k >= 1:
                        for hp in range(h):
                            if grp_done.get((hp, (k - 1) // 2), 0) < 3:
                                return True
                    if si == 6:
                        for dq_ in scan_q:
                            for ent in dq_:
                                if ent[0] < h:
                                    return True
            return False

        def scan_hazard(h, d, p):
            j = (p - 1) if d == 0 else (16 - p)
            if 0 <= j <= 15:
                for hp in range(h):
                    if grp_done.get((hp, j // 4), 0) < 3:
                        return True
            return False

        while True:
            active = False
            if tc < NU + len(stages) and not chain_hazard(tc):
                for si in (1, 2, 3, 4, 5, 6, 0):
                    ui = tc - si
                    if 0 <= ui < NU:
                        stages[si](ui)
                        h, k, d = units[ui]
                        if si == 0 and ui + 1 < NU and units[ui + 1][0] != h:
                            load_wh(h + 1)
                        if si == 6:
                            tiles = [2 * k, 2 * k + 1] if d == 0 else [2 * k + 1, 2 * k]
                            todo = []
                            for g in tiles:
                                p = g if d == 0 else (1 - g if g < 2 else 19 - g)
                                if p <= 16:
                                    todo.append((g, p))
                            for n_, (g, p) in enumerate(todo):
                                scan_q[d].append((h, g, k, p, tau + 1, n_ == len(todo) - 1))
                            if not todo:
                                blk_scanned[(h, k, d)] = tau
                tc += 1
                active = True
            steps_now = []
            for d in range(2):
                if scan_q[d] and scan_q[d][0][4] <= tau and not scan_hazard(scan_q[d][0][0], d, scan_q[d][0][3]):
                    steps_now.append((d,) + scan_q[d].popleft())
            for (d, h, g, k, p, rt, last) in steps_now:
                scan_mm(h, g, d, k)
            for (d, h, g, k, p, rt, last) in steps_now:
                scan_upd(h, g, d, k, p)
                if last:
                    blk_scanned[(h, k, d)] = tau
                active = True
            if front is not None:
                h, gq, stp = front
                grp_piece(h, gq, stp)
                grp_done[(h, gq)] = stp
                front = (h, gq, stp + 1) if stp < 5 else None
                active = True
            pending_front = None
            if back is not None:
                h, gq, stp = back
                grp_piece(h, gq, stp)
                grp_done[(h, gq)] = stp
                active = True
                if stp == 3:
                    back = None
                    pending_front = (h, gq, 4)
                else:
                    back = (h, gq, stp + 1)
            if back is None and pending_front is None and grp_todo:
                h, gq = grp_todo[0]
                need = [(h, 2 * gq + 1, 0), (h, 2 * gq + 2, 0), (h, 2 * gq + 1, 1), (h, 2 * gq + 2, 1), (h, 0, 0), (h, 0, 1)]
                if all((kk in blk_scanned and blk_scanned[kk] < tau) for kk in need):
                    grp_todo.popleft()
                    back = (h, gq, 1)
            if pending_front is not None:
                assert front is None
                front = pending_front
            tau += 1
            if not active and not grp_todo and back is None and front is None and not scan_q[0] and not scan_q[1] and tc >= NU + len(stages):
                break
            assert tau < NU + 600, "side work did not drain"
        OH = [bn + "ohg%d_%d" % (h, gq) for h in range(4) for gq in range(4)]
        if b == 0:
            dump("o_hgT", o_hgT, OH)
        S.barrier()

        if stop == 'hgrn' and b == 0:
            S.barrier(); S.finalize(); S.emit(); return nc
        A.release(m_fin)
        w_out_bf = A.alloc([128, 8, D], BF16)
        identf = A.alloc([128, 128], F32)
        DMA("sp", identf, c_identf, [], ["identf"])
        x1b = [A.alloc([128, 4, D], F32) for _ in range(2)]
        h2Tb = [A.alloc([128, 8, 512], BF16) for _ in range(2)]
        uT = A.alloc([128, 32, 512], BF16)
        xt2 = [A.alloc([128, D], F32) for _ in range(1)]
        xs2b = [A.alloc([128, D], BF16) for _ in range(2)]
        w1p = [A.alloc([128, 8, 512], BF16) for _ in range(2)]
        w2p = [A.alloc([128, 32, 128], BF16) for _ in range(2)]
        rbuf = [A.alloc([128, 512], F32) for _ in range(2)]
        yT = [A.alloc([128, 512], F32) for _ in range(2)]
        ot = [A.alloc([128, D], F32) for _ in range(1)]
        ss2 = A.alloc([128, 16], F32)
        ln2 = A.alloc([128, 16], F32)
        rs2 = A.alloc([128, 16], F32)
        ss3 = A.alloc([128, 16], F32)
        ln3 = A.alloc([128, 16], F32)
        rs3 = A.alloc([128, 16], F32)
        DMA("pool", w_out_bf, w_out, [], [bn + "w_out"])
        fn_ = bn + "f_"

        def X1K(jb, i):
            return fn_ + "x1_%d_%d" % (jb % 2, i)

        def stage_a(jb, i):
            x1 = x1b[jb % 2]
            j = jb * 4 + i
            tl = slice(j * 128, (j + 1) * 128)
            DMA("sp", xt2[0], x[b, tl, :], [], [fn_ + "xt0"])
            for half in range(2):
                MMG([(ps[half], (o_mlaT[:, c, tl] if c < 4 else o_hgT[:, c - 4, tl]), w_out_bf[:, c, half * 512:(half + 1) * 512],
                      c == 0, c == 7) for c in range(8)], OM + OH + [bn + "w_out"], [PS[half]])
                TT("dve", x1[:, i, half * 512:(half + 1) * 512], ps[half], G1[:, b, half * 512:(half + 1) * 512], ALU.mult,
                   [PS[half], "G1_%d_%d" % (b, half)], [fn_ + "x1h_%d_%d_%d" % (jb % 2, i, half), X1K(jb, i)])
            TT("dve", x1[:, i, :], x1[:, i, :], xt2[0], ALU.add,
               [fn_ + "x1h_%d_%d_0" % (jb % 2, i), fn_ + "x1h_%d_%d_1" % (jb % 2, i), fn_ + "xt0"], [X1K(jb, i)])

        def stage_b1(jb, i):
            x1 = x1b[jb % 2]
            j = jb * 4 + i
            xs2k = xs2b[i % 2]
            xsk = fn_ + "xs2_%d" % (i % 2)
            ACT(xs2k, x1[:, i, :], AF.Square, [X1K(jb, i)], [xsk, fn_ + "ss2_%d" % j], accum=ss2[:, j:j + 1])
            rstd_chain(ss2[:, j:j + 1], ln2[:, j:j + 1], rs2[:, j:j + 1], 1.0 / D, [fn_ + "ss2_%d" % j, "epsc"],
                       [fn_ + "ln2_%d" % j], [fn_ + "rs2_%d" % j])
            TS("dve", xs2k, x1[:, i, :], rs2[:, j:j + 1], None, ALU.mult, None, [X1K(jb, i), fn_ + "rs2_%d" % j], [xsk])

        def stage_b2(jb, i):
            h2T = h2Tb[jb % 2]
            xs2k = xs2b[i % 2]
            xsk = fn_ + "xs2_%d" % (i % 2)
            pt = psb[2]
            TRG([(pt[:, c * 128:(c + 1) * 128], xs2k[:, c * 128:(c + 1) * 128]) for c in range(8)], ident,
                [xsk, "ident"], [PS[2]])
            for c in range(8):
                dst = h2T[:, c, i * 128:(i + 1) * 128]
                hk_ = fn_ + "h2T%d_%d_%d" % (jb % 2, i, c)
                if i % 2 == 0:
                    ACT(dst, pt[:, c * 128:(c + 1) * 128], AF.Identity, [PS[2], "A2_%d" % b] + MODT2, [hk_],
                        scale=A2[:, b, c:c + 1], bias=modT[:, 24 + c, b:b + 1])
                else:
                    TS("dve", dst, pt[:, c * 128:(c + 1) * 128], A2[:, b, c:c + 1], modT[:, 24 + c, b:b + 1], ALU.mult, ALU.add,
                       [PS[2], "A2_%d" % b] + MODT2, [hk_])

        def prep_pieces(jb):
            return [
                [lambda: stage_a(jb, 0)],
                [lambda: stage_a(jb, 1)],
                [lambda: stage_b1(jb, 0)],
                [lambda: stage_a(jb, 2), lambda: stage_b2(jb, 0)],
                [lambda: stage_b1(jb, 1)],
                [lambda: stage_a(jb, 3), lambda: stage_b2(jb, 1)],
                [lambda: stage_b1(jb, 2)],
                [lambda: stage_b2(jb, 2), lambda: stage_b1(jb, 3)],
                [lambda: stage_b2(jb, 3)],
            ]

        otb = [ot[0], xt2[0]]
        otk = [fn_ + "ot0", fn_ + "xt0"]

        def final_norm(jb_, i):
            x1_ = x1b[jb_ % 2]
            j = jb_ * 4 + i
            xs2k = xs2b[i % 2]
            ACT(xs2k, x1_[:, i, :], AF.Square, [X1K(jb_, i)], [fn_ + "xs2_%d" % (i % 2), fn_ + "ss3_%d" % j], accum=ss3[:, j:j + 1])
            rstd_chain(ss3[:, j:j + 1], ln3[:, j:j + 1], rs3[:, j:j + 1], 1.0 / D, [fn_ + "ss3_%d" % j, "epsc"],
                       [fn_ + "ln3_%d" % j], [fn_ + "rs3_%d" % j])
            STT("dve", otb[i % 2], x1_[:, i, :], rs3[:, j:j + 1], fn_bc, ALU.mult, ALU.mult, [X1K(jb_, i), fn_ + "rs3_%d" % j, "fn_bc"],
                [otk[i % 2]])
            DMA("pool", out[b, j * 128:(j + 1) * 128, :], otb[i % 2], [otk[i % 2]], [fn_ + "out%d" % j])

        for grp in prep_pieces(0):
            for f_ in grp:
                f_()
        for jb in range(4):
            x1 = x1b[jb % 2]
            h2T = h2Tb[jb % 2]
            H2 = [fn_ + "h2T%d_%d_%d" % (jb % 2, i, c) for i in range(4) for c in range(8)]
            X1 = [X1K(jb, i) for i in range(4)]
            if b == 0 and jb == 0:
                dump("x1", x1, X1)
                dump("h2T", h2T, H2)
            for p in range(8):
                wp = w1p[p % 2]
                wk = fn_ + "w1p%d" % (p % 2)
                DMA("sp", wp, w1s[p].rearrange("q (k c) -> q k c", k=8), ["w1s%d" % p], [wk])
                for q4 in range(4):
                    jj = 4 * p + q4
                    pu = ps[3 + jj % 2]
                    MMG([(pu, wp[:, kc, q4 * 128:(q4 + 1) * 128], h2T[:, kc, :], kc == 0, kc == 7) for kc in range(8)],
                        [wk] + H2, [PS[3 + jj % 2]])
                    rb = rbuf[jj % 2]
                    ACT(rb, pu, AF.Relu, [PS[3 + jj % 2]], [fn_ + "rb%d" % (jj % 2)])
                    TT("dve", uT[:, jj, :], rb, rb, ALU.mult, [fn_ + "rb%d" % (jj % 2)], [fn_ + "uT%d" % jj])
                if jb >= 1 and p < 4:
                    final_norm(jb - 1, p)
            UT = [fn_ + "uT%d" % jj for jj in range(32)]
            if b == 0 and jb == 0:
                dump("uT", uT, UT)

            def mlp2_tail(dq):
                yk = fn_ + "yT%d" % (dq % 2)
                S.add("pe", (lambda src, dstp: (lambda e: [e.transpose(dstp[:, i * 128:(i + 1) * 128], src[:, i * 128:(i + 1) * 128], identf)
                                                         for i in range(4)][-1]))(yT[dq % 2], ps[7]),
                      [yk, "identf"], [PS[7]])
                xv = x1[:, :, dq * 128:(dq + 1) * 128]
                TT("dve", xv, xv, ps[7].rearrange("p (i c) -> p i c", i=4), ALU.add, [PS[7]] + X1, X1)

            nxt = prep_pieces(jb + 1) if jb + 1 < 4 else []
            for dq in range(8):
                wp = w2p[dq % 2]
                wk = fn_ + "w2p%d" % (dq % 2)
                DMA("sp", wp, w2s[dq].rearrange("q (j c) -> q j c", j=32), ["w2s%d" % dq], [wk])
                pv = ps[5 + dq % 2]
                MMG([(pv, wp[:, jj, :], uT[:, jj, :], jj == 0, jj == 31) for jj in range(32)], [wk] + UT, [PS[5 + dq % 2]])
                yk = fn_ + "yT%d" % (dq % 2)
                ACT(yT[dq % 2], pv, AF.Identity, [PS[5 + dq % 2]] + MODT2, [yk], scale=modT[:, 40 + dq, b:b + 1])
                if dq >= 1:
                    mlp2_tail(dq - 1)
                if nxt:
                    for f_ in nxt.pop(0):
                        f_()
            mlp2_tail(7)
            while nxt:
                for f_ in nxt.pop(0):
                    f_()
            if jb == 3:
                for i in range(4):
                    final_norm(jb, i)
        S.barrier()
        if stop == 'b0' and b == 0:
            S.barrier(); S.finalize(); S.emit(); return nc

    S.finalize()
    S.emit()
    return nc


def _consts():
    bf = ml_dtypes.bfloat16
    ident = np.eye(128, dtype=np.float32)
    s = np.arange(128)[:, None]
    t = np.arange(128)[None, :]
    mask = np.concatenate([(s <= t), (s >= t)], axis=1).astype(np.float32)
    rm = np.ones((128, 2, 512), np.float32)
    rm[:, 0, 0::128] = 0.0
    rm[:, 1, 127::128] = 0.0
    tok = np.arange(T)
    row = (tok // 64).astype(np.float32)
    col = (tok % 64).astype(np.float32)
    nfreq = 8
    inv = (np.float32(10000.0) ** (-np.arange(nfreq, dtype=np.float32) / np.float32(nfreq))).astype(np.float32)
    cs = np.zeros((128, 2, T), np.float32)
    for dmm in range(32):
        grp, i = dmm // 16, dmm % 16
        f = i % 8
        ang = (row if grp == 0 else col) * inv[f]
        c, sn = np.cos(ang.astype(np.float32)), np.sin(ang.astype(np.float32))
        sign = -1.0 if i < 8 else 1.0
        for q in range(4):
            cs[q * 32 + dmm, 0] = c
            cs[q * 32 + dmm, 1] = sign * sn
    return dict(c_ident=ident.astype(bf), c_identf=ident, c_mask=mask.astype(bf), c_rmask=rm.astype(bf), c_cs=cs.astype(bf))


def _swap_rope(w, base, period):
    w = w.copy()
    ncol = w.shape[-1]
    for h0 in range(0, ncol, period):
        r0 = h0 + base
        blk = w[..., r0:r0 + 32].copy()
        new = blk.copy()
        for grp in range(2):
            o = grp * 16
            new[..., o:o + 8] = blk[..., o + 8:o + 16]
            new[..., o + 8:o + 16] = blk[..., o:o + 8]
        w[..., r0:r0 + 32] = new
    return w


def _kp(w, nk):
    return np.ascontiguousarray(w.reshape(nk, 128, -1).transpose(1, 0, 2))


def _shared_inputs(inp):
    f = np.float32
    w_in = inp["w_in"][0]
    w_uq = inp["w_uq"][0]
    w_ukv = inp["w_ukv"][0].reshape(128, 8, 128)
    kr = np.zeros((1024, 96), f)
    kr[:, 64:96] = w_in[:, 384:416]
    kr = _swap_rope(kr, 64, 96)
    w1 = inp["w_mlp_in"][0]
    w1r = np.ascontiguousarray(w1.reshape(8, 128, 8, 512).transpose(2, 1, 0, 3)).reshape(8, 128, 4096)
    w2 = inp["w_mlp_out"][0]
    w2r = np.ascontiguousarray(w2.reshape(32, 128, 8, 128).transpose(2, 1, 0, 3)).reshape(8, 128, 4096)
    sh = dict(
        w_ada=_kp(inp["w_ada"][0], 8),
        b_adaT=np.ascontiguousarray(inp["b_ada"][0].reshape(48, 128).T),
        b_ada=np.ascontiguousarray(inp["b_ada"][0]),
        nmixT=np.ascontiguousarray(inp["norm_mix"][0].reshape(8, 128).T),
        nmlpT=np.ascontiguousarray(inp["norm_mlp"][0].reshape(8, 128).T),
        w_in=_kp(w_in, 8),
        w_krs=_kp(kr, 8),
        qnT=np.ascontiguousarray(inp["q_norm"][0].reshape(2, 128).T),
        kvnT=np.ascontiguousarray(inp["kv_norm"][0].reshape(128, 1)),
        w_uq=_kp(w_uq, 2),
        w_uqs=_kp(_swap_rope(w_uq, 64, 96), 2),
        w_kn=np.ascontiguousarray(w_ukv[:, :, 0:64].reshape(128, 512)),
        w_v=np.ascontiguousarray(w_ukv[:, :, 64:128].reshape(128, 512)),
        lbT=np.ascontiguousarray(inp["hgrn_lb"].reshape(2, 8, 128).transpose(2, 0, 1)),
        hgn=np.ascontiguousarray(inp["hgrn_norm"][0]),
        w_out=_kp(inp["w_out"][0], 8),
        w1=w1r, w2=w2r,
        fnorm=np.ascontiguousarray(inp["final_norm"]),
    )
    sh = {k: np.ascontiguousarray(v, dtype=f) for k, v in sh.items()}
    sh.update(_consts())
    return sh


_NC_CACHE = {}


def kernel(**inputs):
    inp = {k: np.asarray(v) for k, v in inputs.items()}
    shared = _shared_inputs(inp)
    in_maps = []
    for c in range(8):
        b0 = c * NB
        cv = np.stack([inp["c"][b0], inp["c"][b0 + 1], inp["c_ctx"]], axis=0)
        m = dict(shared)
        m["x"] = np.ascontiguousarray(inp["x"][b0:b0 + NB], dtype=np.float32)
        m["ctx"] = np.ascontiguousarray(inp["ctx"][b0:b0 + NB], dtype=np.float32)
        m["cvecT"] = np.ascontiguousarray(cv.reshape(3, 8, 128).transpose(2, 1, 0), dtype=np.float32)
        in_maps.append(m)
    if "nc" not in _NC_CACHE:
        _NC_CACHE["nc"] = build_program()
    res = run_bass_kernel_spmd(_NC_CACHE["nc"], in_maps, core_ids=list(range(8)))
    return np.concatenate([np.asarray(r["out"]) for r in res.results], axis=0).astype(np.float32)
```

```python
import numpy as np
import ml_dtypes
import concourse.bass as bass
import concourse.mybir as mybir
from concourse.bass_utils import run_bass_kernel_spmd

F32 = mybir.dt.float32
BF16 = mybir.dt.bfloat16
AF = mybir.ActivationFunctionType
ALU = mybir.AluOpType

NB = 2
T = 2048
L = 256
TE = T + L
D = 1024
NT = TE // 128
DFF = 4096
EPS = 1e-6
BLKS = [(0, 256)] + [(256 + 512 * j, 512) for j in range(4)]
ATT_SCALE = float(96 ** -0.5)
DMA_K = 8


class _Op:
    __slots__ = ("idx", "eng", "fn", "dma", "deps", "signal", "tick", "sem_i", "sem_v")

    def __init__(self, idx, eng, fn, dma, deps):
        self.idx, self.eng, self.fn, self.dma, self.deps = idx, eng, fn, dma, deps
        self.signal = False
        self.tick = 0
        self.sem_i = 0
        self.sem_v = 0


class Sched:
    ENGS = ("pe", "act", "dve", "pool", "sp")

    def __init__(self, nc):
        self.nc = nc
        self.ops = []
        self.last_w = {}
        self.readers = {}
        self.dma_ops = {"sp": [], "pool": [], "act": []}

    def add(self, eng, fn, r=(), w=(), dma=False):
        idx = len(self.ops)
        deps = {}
        for k in r:
            p = self.last_w.get(k)
            if p is not None:
                deps[p] = "raw"
        for k in w:
            p = self.last_w.get(k)
            if p is not None and p not in deps:
                deps[p] = "waw"
            for q in self.readers.get(k, ()):
                if q not in deps:
                    deps[q] = "war"
        if dma:
            lst = self.dma_ops[eng]
            if len(lst) >= DMA_K:
                deps[lst[-DMA_K]] = "raw"
            lst.append(idx)
        op = _Op(idx, eng, fn, dma, deps)
        for k in w:
            self.last_w[k] = idx
            self.readers[k] = []
        for k in r:
            self.readers.setdefault(k, []).append(idx)
        self.ops.append(op)
        return op

    def barrier(self):
        last = {}
        for op in self.ops:
            if not op.dma and op.fn is not None:
                last[op.eng] = op.idx
        dmas = [i for q in self.dma_ops.values() for i in q[-DMA_K:]]
        for e in self.ENGS:
            deps = {i: "raw" for ee, i in last.items()}
            for i in dmas:
                deps[i] = "raw"
            idx = len(self.ops)
            op = _Op(idx, e, None, False, deps)
            op.deps = {k: ("bar") for k in deps}
            self.ops.append(op)

    def finalize(self):
        ops = self.ops
        for q, lst in self.dma_ops.items():
            for n, i in enumerate(lst):
                ops[i].sem_i = n % DMA_K
                ops[i].sem_v = 16 * (n // DMA_K + 1)
        self.waits = {}
        for op in ops:
            best = {}
            dma_w = {}
            for p, kind in op.deps.items():
                po = ops[p]
                if po.dma:
                    key = (po.eng, po.sem_i)
                    dma_w[key] = max(dma_w.get(key, 0), po.sem_v)
                    continue
                if po.fn is None:
                    continue
                if po.eng == op.eng and not op.dma and kind != "bar":
                    if po.eng == "pe":
                        continue
                if p > best.get(po.eng, -1):
                    best[po.eng] = p
            for e, p in best.items():
                ops[p].signal = True
            self.waits[op.idx] = (best, dma_w)
        cnt = {e: 0 for e in self.ENGS}
        for op in ops:
            if op.signal:
                cnt[op.eng] += 1
                op.tick = cnt[op.eng]

    def emit(self, final_wait_eng="sp"):
        nc = self.nc
        ops = self.ops
        from contextlib import ExitStack
        with ExitStack() as st:
            esem = {e: st.enter_context(nc.semaphore("cs_" + e)) for e in self.ENGS}
            dsem = {q: [st.enter_context(nc.semaphore("ds_%s%d" % (q, i))) for i in range(DMA_K)]
                    for q in self.dma_ops}
            block = st.enter_context(nc.Block())
            per_eng = {e: [op for op in ops if op.eng == e] for e in self.ENGS}
            waits = self.waits

            def body(ename, eng):
                seen = {}
                for op in per_eng[ename]:
                    best, dma_w = waits[op.idx]
                    for pe_, p in best.items():
                        t = ops[p].tick
                        key = ("c", pe_)
                        if seen.get(key, 0) < t:
                            eng.wait_ge(esem[pe_], t)
                            seen[key] = t
                    for (q, si), v in dma_w.items():
                        key = ("d", q, si)
                        if seen.get(key, 0) < v:
                            eng.wait_ge(dsem[q][si], v)
                            seen[key] = v
                    if op.fn is None:
                        continue
                    inst = op.fn(eng)
                    if op.dma:
                        inst.then_inc(dsem[op.eng][op.sem_i], 16)
                    elif op.signal:
                        inst.then_inc(esem[ename], 1)
                if ename == final_wait_eng:
                    for q, lst in self.dma_ops.items():
                        for i in lst[-DMA_K:]:
                            eng.wait_ge(dsem[q][ops[i].sem_i], ops[i].sem_v)

            @block.tensor
            def _(e):
                body("pe", e)

            @block.scalar
            def _(e):
                body("act", e)

            @block.vector
            def _(e):
                body("dve", e)

            @block.gpsimd
            def _(e):
                body("pool", e)

            @block.sync
            def _(e):
                body("sp", e)


class Arena:
    def __init__(self, nc, lo=16896, hi=229376):
        self.nc, self.lo, self.hi, self.off = nc, lo, hi, lo
        self.n = 0

    def alloc(self, shape, dt):
        nbytes = int(np.prod(shape[1:])) * (2 if dt == BF16 else 4)
        nbytes = (nbytes + 63) // 64 * 64
        assert self.off + nbytes <= self.hi, ("SBUF overflow", self.off, nbytes)
        self.n += 1
        t = self.nc.alloc_sbuf_tensor_at("sb%d" % self.n, list(shape), dt, offset=self.off)
        self.off += nbytes
        return t.ap()

    def mark(self):
        return self.off

    def release(self, m):
        self.off = m


def build_program(dbg=None, stop=None):
    nc = bass.Bass("TRN2", target_bir_lowering=False)
    S = Sched(nc)
    A = Arena(nc)

    def din(name, shape, dt=F32):
        return nc.dram_tensor(name, list(shape), dt, kind="ExternalInput").ap()

    x = din("x", [NB, T, D])
    ctx = din("ctx", [NB, L, D])
    cvecT = din("cvecT", [128, 8, 3])
    w_ada = din("w_ada", [128, 8, 6144])
    b_adaT = din("b_adaT", [128, 48])
    b_ada = din("b_ada", [6144])
    nmixT = din("nmixT", [128, 8])
    nmlpT = din("nmlpT", [128, 8])
    w_in = din("w_in", [128, 8, 2976])
    w_krs = din("w_krs", [128, 8, 96])
    qnT = din("qnT", [128, 2])
    kvnT = din("kvnT", [128, 1])
    w_uq = din("w_uq", [128, 2, 768])
    w_uqs = din("w_uqs", [128, 2, 768])
    w_kn = din("w_kn", [128, 512])
    w_v = din("w_v", [128, 512])
    lbT = din("lbT", [128, 2, 8])
    hgn = din("hgn", [128])
    w_out = din("w_out", [128, 8, 1024])
    w1 = din("w1", [8, 128, 4096])
    w2 = din("w2", [8, 128, 4096])
    fnorm = din("fnorm", [D])
    c_ident = din("c_ident", [128, 128], BF16)
    c_identf = din("c_identf", [128, 128], F32)
    c_mask = din("c_mask", [128, 256], BF16)
    c_rmask = din("c_rmask", [128, 2, 512], BF16)
    c_cs = din("c_cs", [128, 2, T], BF16)
    out = nc.dram_tensor("out", [NB, T, D], F32, kind="ExternalOutput").ap()
    w1s = nc.dram_tensor("w1s", [8, 128, 4096], BF16).ap()
    w2s = nc.dram_tensor("w2s", [8, 128, 4096], BF16).ap()
    dbg_out = {}
    if dbg:
        for name, shape, dt in dbg:
            dbg_out[name] = nc.dram_tensor("dbg_" + name, list(shape), dt, kind="ExternalOutput").ap()

    ps = [nc.alloc_psum_tensor("ps%d" % i, [128, 512], F32).ap() for i in range(8)]
    psb = [p.bitcast(BF16) for p in ps]
    PS = ["ps%d" % i for i in range(8)]

    def ACT(out_, in_, func, r, w, scale=1.0, bias=0.0, accum=None):
        kw = {}
        if accum is not None:
            kw["accum_out"] = accum
        S.add("act", lambda e: e.activation(out=out_, in_=in_, func=func, bias=bias, scale=scale, **kw), r, w)

    def TT(eng, out_, a, b, op, r, w):
        S.add(eng, lambda e: e.tensor_tensor(out_, a, b, op), r, w)

    def TS(eng, out_, a, s1, s2, op0, op1, r, w):
        if s2 is None:
            S.add(eng, lambda e: e.tensor_scalar(out_, a, s1, None, op0), r, w)
        else:
            S.add(eng, lambda e: e.tensor_scalar(out_, a, s1, s2, op0, op1), r, w)

    def STT(eng, out_, a, sc, b, op0, op1, r, w):
        S.add(eng, lambda e: e.scalar_tensor_tensor(out_, a, sc, b, op0, op1), r, w)

    def CP(eng, out_, in_, r, w):
        if eng == "act":
            S.add("act", lambda e: e.copy(out_, in_), r, w)
        else:
            S.add(eng, lambda e: e.tensor_copy(out_, in_), r, w)

    def MSET(eng, ap, val, w):
        S.add(eng, lambda e: e.memset(ap, val), (), w)

    def MMG(lst, r, w):
        lst = list(lst)

        def fn(e):
            ins = None
            for (o, l, rr, st, sp) in lst:
                ins = e.matmul(o, lhsT=l, rhs=rr, start=st, stop=sp)
            return ins
        S.add("pe", fn, r, w)

    def TRG(lst, ident_ap, r, w):
        lst = list(lst)

        def fn(e):
            ins = None
            for (o, i_) in lst:
                ins = e.transpose(o, i_, ident_ap)
            return ins
        S.add("pe", fn, r, w)

    def DMA(q, out_, in_, r, w):
        S.add(q, lambda e: e.dma_start(out=out_, in_=in_), r, w, dma=True)

    def RECIP(out_, in_, r, w):
        S.add("dve", lambda e: e.reciprocal(out_, in_), r, w)

    def SCAN(out_, d0, d1, r, w):
        S.add("dve", lambda e: e.tensor_tensor_scan(out_, d0, d1, 0.0, ALU.mult, ALU.add), r, w)

    def dump(name, src_ap, r):
        if name in dbg_out:
            DMA("sp", dbg_out[name], src_ap, r, ["dbg_" + name])

    def rstd_chain(ss, tmp, rs, n_inv, r, w_tmp, w_rs):
        ACT(tmp, ss, AF.Ln, r, w_tmp, scale=n_inv, bias=epsc[:, 0:1])
        ACT(rs, tmp, AF.Exp, w_tmp, w_rs, scale=-0.5)

    ident = A.alloc([128, 128], BF16)
    ones_bf = A.alloc([128, 128], BF16)
    maskfb2 = A.alloc([128, 2, 256], BF16)
    rmask = A.alloc([128, 2, 512], BF16)
    epsc = A.alloc([128, 2], F32)
    fn_bc = A.alloc([128, D], F32)
    gn_bc = A.alloc([128, 512], F32)
    G1 = A.alloc([128, NB, D], F32)
    modT = A.alloc([128, 48, 3], F32)
    A1 = A.alloc([128, 3, 8], F32)
    A2 = A.alloc([128, 3, 8], F32)
    lb_t = A.alloc([128, 8], F32)
    w_uq_bf = A.alloc([128, 2, 768], BF16)
    w_uqs_bf = A.alloc([128, 2, 768], BF16)
    w_kn_bf = A.alloc([128, 512], BF16)
    w_v_bf = A.alloc([128, 512], BF16)

    DMA("sp", ident, c_ident, [], ["ident"])
    DMA("sp", maskfb2[:, 0, :], c_mask, [], ["maskfb"])
    DMA("sp", maskfb2[:, 1, :], c_mask, [], ["maskfb"])
    DMA("sp", rmask, c_rmask, [], ["rmask"])
    DMA("sp", fn_bc, fnorm.partition_broadcast(128), [], ["fn_bc"])
    for i in range(4):
        DMA("sp", gn_bc[:, i * 128:(i + 1) * 128], hgn.partition_broadcast(128), [], ["gn_bc%d" % i])
    GN = ["gn_bc%d" % i for i in range(4)]
    MSET("dve", ones_bf, 1.0, ["ones"])
    MSET("dve", epsc, EPS, ["epsc"])

    m_setup = A.mark()
    wa = A.alloc([128, 8, 6144], BF16)
    cT = A.alloc([128, 8, 3], F32)
    sT = A.alloc([128, 8, 3], BF16)
    sTb = A.alloc([128, NB, 8, 128], BF16)
    badaT = A.alloc([128, 48], F32)
    bada_g1 = A.alloc([128, D], F32)
    nmix_t = A.alloc([128, 8], F32)
    nmlp_t = A.alloc([128, 8], F32)
    lbraw = A.alloc([128, 2, 8], F32)
    lbtmp = A.alloc([128, 8], F32)
    qn_t = A.alloc([128, 2], F32)
    kvn_t = A.alloc([128, 1], F32)
    wst = A.alloc([128, 2, 768], F32)
    wst2 = A.alloc([128, 2, 768], F32)
    wst3 = A.alloc([128, 1024], F32)

    for kc in range(8):
        DMA("pool", wa[:, kc, 0:2048], w_ada[:, kc, 0:2048], [], ["wa%d_a" % kc])
    for kc in range(8):
        DMA("pool", wa[:, kc, 2048:6144], w_ada[:, kc, 2048:6144], [], ["wa%d_b" % kc])
    DMA("sp", cT, cvecT, [], ["cT"])
    DMA("sp", badaT, b_adaT, [], ["badaT"])
    DMA("sp", bada_g1, b_ada[2048:3072].partition_broadcast(128), [], ["bada_g1"])
    DMA("sp", nmix_t, nmixT, [], ["nmix"])
    DMA("sp", nmlp_t, nmlpT, [], ["nmlp"])
    DMA("sp", lbraw, lbT, [], ["lbraw"])
    DMA("sp", qn_t, qnT, [], ["qn"])
    DMA("sp", kvn_t, kvnT, [], ["kvn"])
    DMA("sp", wst, w_uq, [], ["wst"])
    DMA("sp", wst2, w_uqs, [], ["wst2"])
    DMA("sp", wst3[:, 0:512], w_kn, [], ["wst3a"])
    DMA("sp", wst3[:, 512:1024], w_v, [], ["wst3b"])

    TT("dve", lbtmp, lbraw[:, 1, :], lbraw[:, 0, :], ALU.subtract, ["lbraw"], ["lbtmp"])
    ACT(lbtmp, lbtmp, AF.Exp, ["lbtmp"], ["lbtmp"])
    TS("dve", lbtmp, lbtmp, 1.0, None, ALU.add, None, ["lbtmp"], ["lbtmp"])
    RECIP(lb_t, lbtmp, ["lbtmp"], ["lb_t"])
    for c in range(2):
        TS("dve", w_uq_bf[:, c, :], wst[:, c, :], qn_t[:, c:c + 1], None, ALU.mult, None, ["wst", "qn"], ["w_uq_bf%d" % c])
        TS("dve", w_uqs_bf[:, c, :], wst2[:, c, :], qn_t[:, c:c + 1], None, ALU.mult, None, ["wst2", "qn"], ["w_uqs_bf%d" % c])
    TS("dve", w_kn_bf, wst3[:, 0:512], kvn_t[:, 0:1], None, ALU.mult, None, ["wst3a", "kvn"], ["w_kn_bf"])
    TS("dve", w_v_bf, wst3[:, 512:1024], kvn_t[:, 0:1], None, ALU.mult, None, ["wst3b", "kvn"], ["w_v_bf"])
    WUQ = ["w_uq_bf0", "w_uq_bf1"]
    WUQS = ["w_uqs_bf0", "w_uqs_bf1"]

    ACT(sT, cT, AF.Silu, ["cT"], ["sT"])
    WAa = ["wa%d_a" % k for k in range(8)]
    WA = ["wa%d_b" % k for k in range(8)]
    psM = ps[0][:, 0:144]
    psM2 = ps[3][:, 0:144]
    MMG([(psM[:, j * 3:(j + 1) * 3], wa[:, kc, j * 128:(j + 1) * 128], sT[:, kc, :], kc == 0, kc == 7)
         for j in range(16) for kc in range(8)], WAa + ["sT"], [PS[0]])
    for b in range(3):
        TT("dve", modT[:, 0:16, b], ps[0][:, b:48:3], badaT[:, 0:16], ALU.add, [PS[0], "badaT"], ["modT%d" % b])
    MODT = ["modT0", "modT1", "modT2"]
    MMG([(psM2[:, j * 3:(j + 1) * 3], wa[:, kc, j * 128:(j + 1) * 128], sT[:, kc, :], kc == 0, kc == 7)
         for j in range(16, 48) for kc in range(8)], WA + ["sT"], [PS[3]])
    MODT2 = ["modTb0", "modTb1", "modTb2"]
    for b in range(3):
        TT("dve", modT[:, 16:48, b], ps[3][:, 48 + b:144:3], badaT[:, 16:48], ALU.add, [PS[3], "badaT"], ["modTb%d" % b])
    for b in range(3):
        STT("dve", A1[:, b, :], modT[:, 8:16, b], 1.0, nmix_t, ALU.add, ALU.mult, MODT + ["nmix"], ["A1_%d" % b])
        STT("dve", A2[:, b, :], modT[:, 32:40, b], 1.0, nmlp_t, ALU.add, ALU.mult, MODT2 + ["nmlp"], ["A2_%d" % b])
    for b in range(NB):
        for kc in range(8):
            CP("dve", sTb[:, b, kc, :], sT[:, kc, b:b + 1].to_broadcast([128, 128]), ["sT"], ["sTb%d_%d" % (b, kc)])
        for half in range(2):
            pg = ps[1 + half]
            MMG([(pg, sTb[:, b, kc, :], wa[:, kc, 2048 + half * 512:2048 + (half + 1) * 512], kc == 0, kc == 7)
                 for kc in range(8)], WA + ["sTb%d_%d" % (b, kc) for kc in range(8)], [PS[1 + half]])
            TT("dve", G1[:, b, half * 512:(half + 1) * 512], pg, bada_g1[:, half * 512:(half + 1) * 512], ALU.add,
               [PS[1 + half], "bada_g1"], ["G1_%d_%d" % (b, half)])
    dump("modT", modT, MODT + MODT2)
    dump("G1", G1, ["G1_%d_%d" % (b, h) for b in range(NB) for h in range(2)])
    dump("lb", lb_t, ["lb_t"])
    S.barrier()
    if stop == 'setup':
        S.finalize(); S.emit(); return nc
    A.release(m_setup)

    o_mlaT = A.alloc([128, 4, T], BF16)
    o_hgT = A.alloc([128, 4, T], BF16)
    m_fin = A.mark()
    hT = A.alloc([128, 8, TE], BF16)
    m_batch = A.mark()

    for b in range(NB):
        bn = "b%d_" % b

        def HK(g):
            return [bn + "hT%d_%d" % (g, c) for c in range(8)]

        def HKB(t0, n):
            r = []
            for g in range(t0 // 128, (t0 + n) // 128):
                r += HK(g)
            return r

        A.release(m_batch)
        xsb = [A.alloc([128, D], BF16) for _ in range(2)]
        junk = A.alloc([128, D], BF16)
        ssA = A.alloc([128, NT], F32)
        lnA = A.alloc([128, NT], F32)
        rsA = A.alloc([128, NT], F32)
        NXB = 3
        xtb = [A.alloc([128, D], F32) for _ in range(NXB)]

        def pa1(g):
            src = ctx[b, g * 128:(g + 1) * 128, :] if g < 2 else x[b, (g - 2) * 128:(g - 1) * 128, :]
            k = g % NXB
            xk = bn + "xt%d" % k
            DMA("sp", xtb[k], src, [], [xk])
            ACT(junk, xtb[k], AF.Square, [xk], [bn + "junk", bn + "ssA%d" % g], accum=ssA[:, g:g + 1])
            rstd_chain(ssA[:, g:g + 1], lnA[:, g:g + 1], rsA[:, g:g + 1], 1.0 / D,
                       [bn + "ssA%d" % g, "epsc"], [bn + "lnA%d" % g], [bn + "rsA%d" % g])

        def pa2(g):
            k = g % NXB
            xk, sk = bn + "xt%d" % k, bn + "xs%d" % (g % 2)
            TS("dve", xsb[g % 2], xtb[k], rsA[:, g:g + 1], None, ALU.mult, None, [xk, bn + "rsA%d" % g], [sk])
            pt = psb[g % 2]
            TRG([(pt[:, c * 128:(c + 1) * 128], xsb[g % 2][:, c * 128:(c + 1) * 128]) for c in range(8)], ident,
                [sk, "ident"], [PS[g % 2]])

        def pa3(g):
            bb = 2 if g < 2 else b
            pt = psb[g % 2]
            for c in range(8):
                dst = hT[:, c, g * 128:(g + 1) * 128]
                if g % 4 == 0:
                    ACT(dst, pt[:, c * 128:(c + 1) * 128], AF.Identity, [PS[g % 2], "A1_%d" % bb] + MODT, [bn + "hT%d_%d" % (g, c)],
                        scale=A1[:, bb, c:c + 1], bias=modT[:, c, bb:bb + 1])
                else:
                    TS("dve", dst, pt[:, c * 128:(c + 1) * 128], A1[:, bb, c:c + 1], modT[:, c, bb:bb + 1], ALU.mult, ALU.add,
                       [PS[g % 2], "A1_%d" % bb] + MODT, [bn + "hT%d_%d" % (g, c)])

        for tt_ in range(NT + 2):
            if tt_ - 2 >= 0:
                pa3(tt_ - 2)
            if 0 <= tt_ - 1 < NT:
                pa2(tt_ - 1)
            if tt_ < NT:
                pa1(tt_)
        if b == 0:
            dump("hT", hT, [k for g in range(NT) for k in HK(g)])

        if stop == 'A' and b == 0:
            S.barrier(); S.finalize(); S.emit(); return nc
        S.barrier()
        A.release(m_batch)
        m_mla = A.mark()
        cs = A.alloc([128, 2, T], BF16)
        w_mla = A.alloc([128, 8, 416], BF16)
        w_krs_bf = A.alloc([128, 8, 96], BF16)
        cqnT = A.alloc([128, 2, TE], BF16)
        ckvnT = A.alloc([128, TE], BF16)
        krotT = A.alloc([128, TE], BF16)
        Vaug = A.alloc([128, NT, 8, 128], BF16)
        KTb = [A.alloc([128, TE], BF16) for _ in range(2)]
        QTb = [A.alloc([128, T], BF16) for _ in range(2)]
        PTb = [A.alloc([128, 512], BF16) for _ in range(3)]
        sq = A.alloc([128, 3, 512], BF16)
        tA = A.alloc([128, 512], F32)
        tB = A.alloc([128, 512], F32)
        rq_bc = A.alloc([128, 512], F32)
        rkv_bc = A.alloc([128, 512], F32)
        rp1 = [A.alloc([128, 512], F32) for _ in range(1)]
        rp2 = [A.alloc([128, 512], F32) for _ in range(1)]
        rden = [A.alloc([128, 512], F32) for _ in range(2)]

        DMA("sp", cs, c_cs, [], [bn + "cs"])
        MSET("dve", Vaug, 1.0, [bn + "Vp%d" % g for g in range(NT)])
        DMA("pool", w_mla, w_in[:, :, 0:416], [], [bn + "w_mla"])
        DMA("pool", w_krs_bf, w_krs, [], [bn + "w_krs"])
        WM = [bn + "w_mla"]

        v_defer = []
        for bi, (t0, n) in enumerate(BLKS):
            hk = HKB(t0, n)
            tok = slice(t0, t0 + n)
            for c in range(2):
                MMG([(ps[c][:, 0:n], w_mla[:, kc, c * 128:(c + 1) * 128], hT[:, kc, tok], kc == 0, kc == 7) for kc in range(8)],
                    WM + hk, [PS[c]])
            MMG([(ps[2][:, 0:n], w_mla[:, kc, 256:384], hT[:, kc, tok], kc == 0, kc == 7) for kc in range(8)], WM + hk, [PS[2]])
            MMG([(ps[3][0:96, 0:n], w_mla[:, kc, 320:416], hT[:, kc, tok], kc == 0, kc == 7) for kc in range(8)], WM + hk, [PS[3]])
            if bi > 0:
                MMG([(ps[4][0:96, 0:n], w_krs_bf[:, kc, :], hT[:, kc, tok], kc == 0, kc == 7) for kc in range(8)],
                    [bn + "w_krs"] + hk, [PS[4]])
            while v_defer:
                v_defer.pop(0)()
            for c in range(3):
                ACT(sq[:, c, 0:n], ps[c][:, 0:n], AF.Square, [PS[c]], [bn + "sq%d" % c])
            MMG([(ps[5][:, 0:n], ones_bf, sq[:, 0, 0:n], True, False), (ps[5][:, 0:n], ones_bf, sq[:, 1, 0:n], False, True)],
                ["ones", bn + "sq0", bn + "sq1"], [PS[5]])
            MMG([(ps[6][:, 0:n], ones_bf, sq[:, 2, 0:n], True, True)], ["ones", bn + "sq2"], [PS[6]])
            rstd_chain(ps[5][:, 0:n], tA[:, 0:n], rq_bc[:, 0:n], 1.0 / 256, [PS[5], "epsc"], [bn + "tA"], [bn + "rq_bc"])
            rstd_chain(ps[6][:, 0:n], tB[:, 0:n], rkv_bc[:, 0:n], 1.0 / 128, [PS[6], "epsc"], [bn + "tB"], [bn + "rkv_bc"])
            for c in range(2):
                TT("dve", cqnT[:, c, tok], ps[c][:, 0:n], rq_bc[:, 0:n], ALU.mult, [PS[c], bn + "rq_bc"], [bn + "cqnT%d_%d" % (bi, c)])
            TT("dve", ckvnT[:, tok], ps[2][:, 0:n], rkv_bc[:, 0:n], ALU.mult, [PS[2], bn + "rkv_bc"], [bn + "ckvnT%d" % bi])
            if bi == 0:
                CP("dve", krotT[64:96, tok], ps[3][64:96, 0:n], [PS[3]], [bn + "krotT%d" % bi])
            else:
                lt = slice(t0 - L, t0 - L + n)
                TT("dve", rp1[0][64:96, 0:n], ps[3][64:96, 0:n], cs[64:96, 0, lt], ALU.mult, [PS[3], bn + "cs"], [bn + "rp1_0"])
                TT("dve", rp2[0][64:96, 0:n], ps[4][64:96, 0:n], cs[64:96, 1, lt], ALU.mult, [PS[4], bn + "cs"], [bn + "rp2_0"])
                TT("dve", krotT[64:96, tok], rp1[0][64:96, 0:n], rp2[0][64:96, 0:n], ALU.add, [bn + "rp1_0", bn + "rp2_0"],
                   [bn + "krotT%d" % bi])
            def v_part(bi=bi, t0=t0, n=n):
                for g in range(t0 // 128, (t0 + n) // 128):
                    MMG([(ps[7], ckvnT[:, g * 128:(g + 1) * 128], w_v_bf, True, True)], [bn + "ckvnT%d" % bi, "w_v_bf"], [PS[7]])
                    pv4 = ps[7].rearrange("p (j e c) -> p j e c", j=4, e=2)
                    veng = "act" if g % 2 else "dve"
                    CP(veng, Vaug[:, g, 0::2, 0:64], pv4[:, :, 0, :], [PS[7]], [bn + "Vp%d" % g])
                    CP(veng, Vaug[:, g, 1::2, 64:128], pv4[:, :, 1, :], [PS[7]], [bn + "Vp%d" % g])
            v_defer.append(v_part)
        while v_defer:
            v_defer.pop(0)()
        CQ = [bn + "cqnT%d_%d" % (bi, c) for bi in range(5) for c in range(2)]
        CKV = [bn + "ckvnT%d" % bi for bi in range(5)]
        KROT = [bn + "krotT%d" % bi for bi in range(5)]
        if b == 0:
            dump("cqnT", cqnT, CQ)
            dump("ckvnT", ckvnT, CKV)
            dump("krotT", krotT[64:96, :], KROT)

        if stop == 'mlaproj' and b == 0:
            S.barrier(); S.finalize(); S.emit(); return nc
        def proj_head(h):
            kb = h % 2
            KT, QT = KTb[kb], QTb[kb]
            kk, qk = bn + "KT%d" % kb, bn + "QT%d" % kb
            pc = 0
            for bi, (t0, n) in enumerate(BLKS):
                tok = slice(t0, t0 + n)
                pb, pk = ps[7 - pc % 2], PS[7 - pc % 2]
                pc += 1
                MMG([(pb[0:64, 0:n], w_kn_bf[:, h * 64:(h + 1) * 64], ckvnT[:, tok], True, True)],
                    ["w_kn_bf", bn + "ckvnT%d" % bi], [pk])
                CP("dve", KT[0:64, tok], pb[0:64, 0:n], [pk], [kk + "_n%d" % bi])
                yield
            CP("dve", KT[64:96, :], krotT[64:96, :], KROT, [kk + "_r"])
            for j in range(4):
                q0 = j * 512
                et = slice(L + q0, L + q0 + 512)
                lt = slice(q0, q0 + 512)
                cqk = [bn + "cqnT%d_%d" % (j + 1, c) for c in range(2)]
                pb, pk = ps[7 - pc % 2], PS[7 - pc % 2]
                pc += 1
                MMG([(pb[0:96, :], w_uq_bf[:, c, h * 96:(h + 1) * 96], cqnT[:, c, et], c == 0, c == 1) for c in range(2)],
                    WUQ + cqk, [pk])
                CP("dve", QT[0:64, lt], pb[0:64, :], [pk], [qk + "_n%d" % j])
                TT("dve", rp1[0][64:96, :], pb[64:96, :], cs[64:96, 0, lt], ALU.mult, [pk, bn + "cs"], [bn + "rp1_0"])
                yield
                pb, pk = ps[7 - pc % 2], PS[7 - pc % 2]
                pc += 1
                MMG([(pb[0:96, :], w_uqs_bf[:, c, h * 96:(h + 1) * 96], cqnT[:, c, et], c == 0, c == 1) for c in range(2)],
                    WUQS + cqk, [pk])
                TT("dve", rp2[0][64:96, :], pb[64:96, :], cs[64:96, 1, lt], ALU.mult, [pk, bn + "cs"], [bn + "rp2_0"])
                TT("dve", QT[64:96, lt], rp1[0][64:96, :], rp2[0][64:96, :], ALU.add, [bn + "rp1_0", bn + "rp2_0"], [qk + "_r%d" % j])
                yield

        def KTK(h):
            kk = bn + "KT%d" % (h % 2)
            return [kk + "_n%d" % bi for bi in range(5)] + [kk + "_r"]

        def QTK(h, j):
            qk = bn + "QT%d" % (h % 2)
            return [qk + "_n%d" % j, qk + "_r%d" % j]

        steps = [(h, qb, kt) for h in range(8) for qb in range(4) for kt in range(NT)]
        for _ in proj_head(0):
            pass
        pgen = None
        if b == 0:
            dump("KT0", KTb[0][0:96, :], KTK(0))
            dump("QT0", QTb[0][0:96, :], [k for j in range(4) for k in QTK(0, j)])

        SB = [0, 1, 2, 5]

        def reg_qk(si):
            h, qb, kt = steps[si]
            KT, QT = KTb[h % 2], QTb[h % 2]
            sb_ = SB[si % 4]
            MMG([(ps[sb_], KT[0:96, kt * 128:(kt + 1) * 128], QT[0:96, qb * 512:(qb + 1) * 512], True, True)],
                KTK(h) + QTK(h, qb), [PS[sb_]])

        reg_qk(0)
        reg_qk(1)
        reg_qk(2)
        for si, (h, qb, kt) in enumerate(steps):
            u = (h * 4 + qb) % 2
            pO = ps[3 + u]
            sb_ = SB[si % 4]
            ACT(PTb[si % 3], ps[sb_], AF.Exp, [PS[sb_]], [bn + "PT%d" % (si % 3)] + (["convgate"] if (b == 0 and si == 0) else []),
                scale=ATT_SCALE)
            if si + 3 < len(steps):
                reg_qk(si + 3)
            MMG([(pO, Vaug[:, kt, h, :], PTb[si % 3], kt == 0, kt == NT - 1)],
                [bn + "Vp%d" % kt, bn + "PT%d" % (si % 3)], [PS[3 + u]])
            if kt == NT - 1:
                orow = slice((h % 2) * 64, (h % 2) * 64 + 64)
                drow = slice(64 - (h % 2) * 64, 128 - (h % 2) * 64)
                RECIP(rden[u][orow, :], pO[drow, :], [PS[3 + u]], [bn + "rden%d" % u])
                TT("dve", o_mlaT[orow, h // 2, qb * 512:(qb + 1) * 512], pO[orow, :], rden[u][orow, :], ALU.mult,
                   [PS[3 + u], bn + "rden%d" % u], [bn + "omla%d_%d" % (h, qb)])
            if b == 0 and kt == 0 and qb == 0 and h == 0:
                for p in range(8):
                    DMA("pool", w1s[p], w1[p], ["convgate"], ["w1s%d" % p])
                for p in range(8):
                    DMA("pool", w2s[p], w2[p], [], ["w2s%d" % p])
            if qb == 0 and kt == 2 and h + 1 < 8:
                pgen = proj_head(h + 1)
            if pgen is not None and si % 4 == 0:
                try:
                    next(pgen)
                except StopIteration:
                    pgen = None
        OM = [bn + "omla%d_%d" % (h, qb) for h in range(8) for qb in range(4)]
        if b == 0:
            dump("o_mlaT", o_mlaT, OM)
        if stop == 'attn' and b == 0:
            S.barrier(); S.finalize(); S.emit(); return nc
        S.barrier()
        A.release(m_mla)

        v_tm = A.alloc([128, NT, 512], BF16)
        gate = A.alloc([128, 16, 512], BF16)
        eb_tab = A.alloc([128, 4, 2, NT], F32)
        m_h0 = A.mark()
        w_hig = A.alloc([128, 8, 1024], BF16)
        gtmp = [A.alloc([128, 512], F32) for _ in range(2)]
        DMA("pool", w_hig[:, :, 0:512], w_in[:, :, 1952:2464], [], [bn + "w_hig_v"])
        DMA("pool", w_hig[:, :, 512:1024], w_in[:, :, 2464:2976], [], [bn + "w_hig_g"])
        for g in range(NT):
            k = g % 2
            MMG([(ps[k], hT[:, kc, g * 128:(g + 1) * 128], w_hig[:, kc, 0:512], kc == 0, kc == 7) for kc in range(8)],
                [bn + "w_hig_v"] + HK(g), [PS[k]])
            CP("dve" if g % 2 else "act", v_tm[:, g, :], ps[k], [PS[k]], [bn + "v_tm%d" % g])
            if g >= 2:
                MMG([(ps[2 + k], hT[:, kc, g * 128:(g + 1) * 128], w_hig[:, kc, 512:1024], kc == 0, kc == 7) for kc in range(8)],
                    [bn + "w_hig_g"] + HK(g), [PS[2 + k]])
                ACT(gtmp[k], ps[2 + k], AF.Silu, [PS[2 + k]], [bn + "gtmp%d" % k])
                TT("dve", gate[:, g - 2, :], gtmp[k], gn_bc, ALU.mult, [bn + "gtmp%d" % k] + GN, [bn + "gate%d" % (g - 2)])
        if b == 0:
            dump("v_tm", v_tm, [bn + "v_tm%d" % g for g in range(NT)])
            dump("gate", gate, [bn + "gate%d" % j for j in range(16)])
        if stop == 'hg0' and b == 0:
            S.barrier(); S.finalize(); S.emit(); return nc
        S.barrier()
        A.release(m_h0)

        NB_H = 9
        NSET = 4
        whb = [A.alloc([128, 8, 384], BF16) for _ in range(1)]
        qTs = [A.alloc([128, T], BF16) for _ in range(2)]
        kTs = [A.alloc([128, T], BF16) for _ in range(2)]
        khat = A.alloc([128, NT, 2, 128], BF16)
        S_st = A.alloc([128, 2, 17, 128], BF16)
        khT = [A.alloc([128, 256], BF16) for _ in range(2)]
        ktmp = [A.alloc([128, 256], BF16) for _ in range(1)]
        TS1 = [A.alloc([128, 256], F32) for _ in range(NSET)]
        TS2 = [A.alloc([128, 256], F32) for _ in range(NSET)]
        TS3 = [A.alloc([128, 256], F32) for _ in range(NSET)]
        TSq = [A.alloc([128, 256], F32) for _ in range(NSET)]
        TSz = [A.alloc([128, 256], BF16) for _ in range(NSET)]
        ATb = A.alloc([128, 4, 256], BF16)
        o_sb = A.alloc([128, 4, 128], F32)
        og = A.alloc([128, 4, 128], BF16)
        ssh = A.alloc([128, 4], F32)
        lnh = A.alloc([128, 4], F32)
        rsh = A.alloc([128, 4], F32)
        junk2 = A.alloc([128, 128], BF16)

        units = []
        for h in range(4):
            fo = [(k_, 0) for k_ in range(NB_H)]
            bo = [(k_, 1) for k_ in [0] + list(range(NB_H - 1, 0, -1))]
            seq = (bo + fo) if h % 2 == 0 else (fo + bo)
            for (k_, d_) in seq:
                units.append((h, k_, d_))
        NU = len(units)

        def hkeys(h):
            return bn + "h%d_" % h

        def load_wh(h):
            wh = whb[0]
            whk = bn + "wh0"
            for i, c0 in enumerate((416, 928, 1440)):
                DMA("pool", wh[:, :, i * 128:(i + 1) * 128], w_in[:, :, c0 + h * 128:c0 + (h + 1) * 128], [], [whk + "_%d" % i])

        def uinfo(ui):
            h, k, d = units[ui]
            return h, k, d, whb[0], [bn + "wh0_%d" % i for i in range(3)], hkeys(h), ui % NSET, k > 0

        def st0(ui):
            h, k, d, wh, WHK, hn, s, lat = uinfo(ui)
            tok = slice(k * 256, k * 256 + 256)
            hk = HKB(k * 256, 256)
            pz = ps[ui % 2]
            MMG([(pz[:, 0:256], wh[:, kc, (1 + d) * 128:(2 + d) * 128], hT[:, kc, tok], kc == 0, kc == 7) for kc in range(8)],
                WHK + hk, [PS[ui % 2]])
            if lat:
                pq = ps[2]
                MMG([(pq[:, 0:256], wh[:, kc, 0:128], hT[:, kc, tok], kc == 0, kc == 7) for kc in range(8)], WHK + hk, [PS[2]])

        def st1(ui):
            h, k, d, wh, WHK, hn, s, lat = uinfo(ui)
            sk = bn + "ts%d_" % s
            pz = ps[ui % 2]
            ACT(TS1[s], pz[:, 0:256], AF.Exp, [PS[ui % 2]], [sk + "T1"], scale=-1.0)
            ACT(TS2[s], TS1[s], AF.Ln, [sk + "T1", "lb_t"], [sk + "T2"], scale=lb_t[:, d * 4 + h:d * 4 + h + 1], bias=1.0)
            ACT(TS1[s], TS1[s], AF.Ln, [sk + "T1"], [sk + "T1"], bias=1.0)
            if lat:
                pq = ps[2]
                ACT(TSq[s], pq[:, 0:256], AF.Exp, [PS[2]], [sk + "Tq"], scale=-1.0)
                ACT(TSz[s], pq[:, 0:256], AF.Copy, [PS[2]], [sk + "Tz"])
                ACT(TSq[s], TSq[s], AF.Ln, [sk + "Tq"], [sk + "Tq"], bias=1.0)

        def st2(ui):
            h, k, d, wh, WHK, hn, s, lat = uinfo(ui)
            sk = bn + "ts%d_" % s
            TT("dve", TS2[s], TS2[s], TS1[s], ALU.subtract, [sk + "T1", sk + "T2"], [sk + "T2"])
            if d == 0:
                SCAN(TS3[s], rmask[:, 0, 0:256], TS2[s], ["rmask", sk + "T2"], [sk + "T3"])
            else:
                SCAN(TS3[s][:, ::-1], rmask[:, 1, 0:256][:, ::-1], TS2[s][:, ::-1], ["rmask", sk + "T2"], [sk + "T3"])
            if lat:
                TT("dve", TSq[s], TS3[s], TSq[s], ALU.subtract, [sk + "T3", sk + "Tq"], [sk + "Tq"])

        def st3(ui):
            h, k, d, wh, WHK, hn, s, lat = uinfo(ui)
            sk = bn + "ts%d_" % s
            ACT(TS1[s], TS2[s], AF.Exp, [sk + "T2"], [sk + "T1"])
            lastcol = 127 if d == 0 else 0
            ACT(eb_tab[:, h, d, 2 * k:2 * k + 2], TS3[s][:, lastcol:256:128], AF.Exp, [sk + "T3"], [hn + "eb%d_%d" % (d, k)])
            if lat:
                ACT(TSq[s], TSq[s], AF.Exp, [sk + "Tq"], [sk + "Tq"])
            ACT(TS3[s], TS3[s], AF.Exp, [sk + "T3"], [sk + "T3"], scale=-1.0)

        def st4(ui):
            h, k, d, wh, WHK, hn, s, lat = uinfo(ui)
            sk = bn + "ts%d_" % s
            if lat:
                lt = slice((k - 1) * 256, k * 256)
                STT("dve", qTs[d][:, lt], TSz[s], -1.0, TSq[s], ALU.mult, ALU.mult, [sk + "Tz", sk + "Tq"], [bn + "qT%d_%d" % (d, k)])
                kdst = kTs[d][:, lt]
                kkey = bn + "kT%d_%d" % (d, k)
            else:
                kdst = ktmp[0]
                kkey = bn + "ktmp0"
            STT("dve", kdst, TS1[s], 1.0, TS3[s], ALU.subtract, ALU.mult, [sk + "T1", sk + "T3"], [kkey])
            kh = khT[ui % 2]
            for c in range(2):
                TS("dve", kh[:, c * 128:(c + 1) * 128], kdst[:, c * 128:(c + 1) * 128], eb_tab[:, h, d, 2 * k + c:2 * k + c + 1], None,
                   ALU.mult, None, [kkey, hn + "eb%d_%d" % (d, k)], [bn + "khT%d_%d" % (ui % 2, c)])

        def st5(ui):
            h, k, d, wh, WHK, hn, s, lat = uinfo(ui)
            kh = khT[ui % 2]
            ptk = psb[4 + ui % 2]
            TRG([(ptk[:, c * 128:(c + 1) * 128], kh[:, c * 128:(c + 1) * 128]) for c in range(2)], ident,
                [bn + "khT%d_%d" % (ui % 2, c) for c in range(2)] + ["ident"], [PS[4 + ui % 2]])

        def st6(ui):
            h, k, d, wh, WHK, hn, s, lat = uinfo(ui)
            ptk = psb[4 + ui % 2]
            CP("act" if ui % 2 else "dve", khat[:, 2 * k:2 * k + 2, d, :], ptk[:, 0:256].rearrange("p (c k) -> p c k", c=2),
               [PS[4 + ui % 2]], [bn + "khat%d_%d" % (d, k)])

        def scan_mm(h, g, d, k):
            hn = hkeys(h)
            pS = ps[6][:, d * 128:(d + 1) * 128]
            MMG([(pS, khat[:, g, d, :], v_tm[:, g, h * 128:(h + 1) * 128], True, True)],
                [bn + "khat%d_%d" % (d, k), bn + "v_tm%d" % g], [PS[6]])

        def scan_upd(h, g, d, k, p):
            hn = hkeys(h)
            pS = ps[6][:, d * 128:(d + 1) * 128]
            if p == 0:
                CP("dve", S_st[:, d, 0, :], pS, [PS[6]], [bn + "S%d_%d" % (d, 0)])
            else:
                STT("dve", S_st[:, d, p, :], S_st[:, d, p - 1, :], eb_tab[:, h, d, g:g + 1], pS, ALU.mult, ALU.add,
                    [bn + "S%d_%d" % (d, p - 1), hn + "eb%d_%d" % (d, k), PS[6]], [bn + "S%d_%d" % (d, p)])

        def grp_piece(h, gq, step):
            hn = hkeys(h)
            pO = ps[7]

            def pa_mm(i):
                j = 4 * gq + i
                tl = slice(j * 128, (j + 1) * 128)
                kb = j // 2 + 1
                pA = ps[3][:, (i % 2) * 256:(i % 2) * 256 + 256]
                MMG([(pA[:, 0:128], kTs[0][:, tl], qTs[0][:, tl], True, True),
                     (pA[:, 128:256], kTs[1][:, tl], qTs[1][:, tl], True, True)],
                    [bn + "kT%d_%d" % (d, kb) for d in range(2)] + [bn + "qT%d_%d" % (d, kb) for d in range(2)], [PS[3]])

            def mask2(i0):
                TT("dve", ATb[:, i0:i0 + 2, :], ps[3].rearrange("p (i c) -> p i c", i=2), maskfb2, ALU.mult, [PS[3], "maskfb"],
                   [bn + "AT%d" % i0, bn + "AT%d" % (i0 + 1)])

            def po_mm(i):
                j = 4 * gq + i
                g = j + 2
                kb = j // 2 + 1
                tl = slice(j * 128, (j + 1) * 128)
                vv = v_tm[:, g, h * 128:(h + 1) * 128]
                MMG([(pO[:, i * 128:(i + 1) * 128], ATb[:, i, 0:128], vv, True, False),
                     (pO[:, i * 128:(i + 1) * 128], ATb[:, i, 128:256], vv, False, False),
                     (pO[:, i * 128:(i + 1) * 128], qTs[0][:, tl], S_st[:, 0, j + 1, :], False, False),
                     (pO[:, i * 128:(i + 1) * 128], qTs[1][:, tl], S_st[:, 1, 16 - j, :], False, True)],
                    [bn + "AT%d" % i, bn + "v_tm%d" % g, bn + "qT0_%d" % kb, bn + "qT1_%d" % kb,
                     bn + "S0_%d" % (j + 1), bn + "S1_%d" % (16 - j)], [PS[7]])

            if step == 1:
                pa_mm(0)
                pa_mm(1)
                mask2(0)
            elif step == 2:
                po_mm(0)
                po_mm(1)
                pa_mm(2)
                pa_mm(3)
                mask2(2)
            elif step == 3:
                po_mm(2)
                po_mm(3)
                CP("dve", o_sb, pO.rearrange("p (i c) -> p i c", i=4), [PS[7]], [bn + "o_sb"])
                for i in range(4):
                    ACT(junk2, o_sb[:, i, :], AF.Square, [bn + "o_sb"], [bn + "junk2", bn + "ssh%d" % i], accum=ssh[:, i:i + 1])
                rstd_chain(ssh, lnh, rsh, 1.0 / 128, [bn + "ssh%d" % i for i in range(4)] + ["epsc"], [bn + "lnh"], [bn + "rsh"])
            elif step == 4:
                for i in range(4):
                    j = 4 * gq + i
                    STT("dve", og[:, i, :], o_sb[:, i, :], rsh[:, i:i + 1], gate[:, j, h * 128:(h + 1) * 128],
                        ALU.mult, ALU.mult, [bn + "o_sb", bn + "rsh", bn + "gate%d" % j], [bn + "og%d" % i])
            elif step == 5:
                pT = psb[6][:, 512:1024]
                TRG([(pT[:, i * 128:(i + 1) * 128], og[:, i, :]) for i in range(4)], ident,
                    [bn + "og%d" % i for i in range(4)] + ["ident"], [PS[6]])
                CP("dve", o_hgT[:, h, gq * 512:(gq + 1) * 512], pT, [PS[6]], [bn + "ohg%d_%d" % (h, gq)])

        load_wh(0)
        stages = [st0, st1, st2, st3, st4, st5, st6]
        from collections import deque
        scan_q = [deque(), deque()]
        blk_scanned = {}
        grp_todo = deque((h, gq) for h in range(4) for gq in (range(4) if h % 2 == 0 else range(3, -1, -1)))
        front = None
        back = None
        grp_done = {}
        tau = 0
        tc = 0

        def chain_hazard(tc_):
            for si in (4, 6):
                ui = tc_ - si
                if 0 <= ui < NU:
                    h, k, d = units[ui]
                    if si == 4 and k >= 1:
                        for hp in range(h):
                            if grp_done.get((hp, (k - 1) // 2), 0) < 3:
                                return True
                    if si == 6:
                        for dq_ in scan_q:
                            for ent in dq_:
                                if ent[0] < h:
                                    return True
            return False

        def scan_hazard(h, d, p):
            j = (p - 1) if d == 0 else (16 - p)
            if 0 <= j <= 15:
                for hp in range(h):
                    if grp_done.get((hp, j // 4), 0) < 3:
                        return True
            return False

        while True:
            active = False
            steps_now = []
            for d in range(2):
                if scan_q[d] and scan_q[d][0][4] <= tau and not scan_hazard(scan_q[d][0][0], d, scan_q[d][0][3]):
                    steps_now.append((d,) + scan_q[d].popleft())
            for (d, h, g, k, p, rt, last) in steps_now:
                scan_mm(h, g, d, k)
            for (d, h, g, k, p, rt, last) in steps_now:
                scan_upd(h, g, d, k, p)
                if last:
                    blk_scanned[(h, k, d)] = tau
                active = True
            if front is not None:
                h, gq, stp = front
                grp_piece(h, gq, stp)
                grp_done[(h, gq)] = stp
                front = (h, gq, stp + 1) if stp < 5 else None
                active = True
            pending_front = None
            if back is not None:
                h, gq, stp = back
                grp_piece(h, gq, stp)
                grp_done[(h, gq)] = stp
                active = True
                if stp == 3:
                    back = None
                    pending_front = (h, gq, 4)
                else:
                    back = (h, gq, stp + 1)
            if tc < NU + len(stages) and not chain_hazard(tc):
                for si in (1, 2, 3, 4, 5, 6, 0):
                    ui = tc - si
                    if 0 <= ui < NU:
                        stages[si](ui)
                        h, k, d = units[ui]
                        if si == 0 and ui + 1 < NU and units[ui + 1][0] != h:
                            load_wh(h + 1)
                        if si == 6:
                            tiles = [2 * k, 2 * k + 1] if d == 0 else [2 * k + 1, 2 * k]
                            todo = []
                            for g in tiles:
                                p = g if d == 0 else (1 - g if g < 2 else 19 - g)
                                if p <= 16:
                                    todo.append((g, p))
                            for n_, (g, p) in enumerate(todo):
                                scan_q[d].append((h, g, k, p, tau + 1, n_ == len(todo) - 1))
                            if not todo:
                                blk_scanned[(h, k, d)] = tau
                tc += 1
                active = True
            if back is None and pending_front is None and grp_todo:
                h, gq = grp_todo[0]
                need = [(h, 2 * gq + 1, 0), (h, 2 * gq + 2, 0), (h, 2 * gq + 1, 1), (h, 2 * gq + 2, 1), (h, 0, 0), (h, 0, 1)]
                if all((kk in blk_scanned and blk_scanned[kk] < tau) for kk in need):
                    grp_todo.popleft()
                    back = (h, gq, 1)
            if pending_front is not None:
                assert front is None
                front = pending_front
            tau += 1
            if not active and not grp_todo and back is None and front is None and not scan_q[0] and not scan_q[1] and tc >= NU + len(stages):
                break
            assert tau < NU + 600, "side work did not drain"
        OH = [bn + "ohg%d_%d" % (h, gq) for h in range(4) for gq in range(4)]
        if b == 0:
            dump("o_hgT", o_hgT, OH)
        S.barrier()

        if stop == 'hgrn' and b == 0:
            S.barrier(); S.finalize(); S.emit(); return nc
        A.release(m_fin)
        w_out_bf = A.alloc([128, 8, D], BF16)
        identf = A.alloc([128, 128], F32)
        DMA("sp", identf, c_identf, [], ["identf"])
        x1b = [A.alloc([128, 4, D], F32) for _ in range(2)]
        h2Tb = [A.alloc([128, 8, 512], BF16) for _ in range(2)]
        uT = A.alloc([128, 32, 512], BF16)
        xt2 = [A.alloc([128, D], F32) for _ in range(1)]
        xs2b = [A.alloc([128, D], BF16) for _ in range(2)]
        w1p = [A.alloc([128, 8, 512], BF16) for _ in range(2)]
        w2p = [A.alloc([128, 32, 128], BF16) for _ in range(2)]
        rbuf = [A.alloc([128, 512], F32) for _ in range(2)]
        yT = [A.alloc([128, 512], F32) for _ in range(2)]
        ot = [A.alloc([128, D], F32) for _ in range(1)]
        ss2 = A.alloc([128, 16], F32)
        ln2 = A.alloc([128, 16], F32)
        rs2 = A.alloc([128, 16], F32)
        ss3 = A.alloc([128, 16], F32)
        ln3 = A.alloc([128, 16], F32)
        rs3 = A.alloc([128, 16], F32)
        DMA("pool", w_out_bf, w_out, [], [bn + "w_out"])
        fn_ = bn + "f_"

        def X1K(jb, i):
            return fn_ + "x1_%d_%d" % (jb % 2, i)

        def stage_a(jb, i):
            x1 = x1b[jb % 2]
            j = jb * 4 + i
            tl = slice(j * 128, (j + 1) * 128)
            DMA("sp", xt2[0], x[b, tl, :], [], [fn_ + "xt0"])
            for half in range(2):
                MMG([(ps[half], (o_mlaT[:, c, tl] if c < 4 else o_hgT[:, c - 4, tl]), w_out_bf[:, c, half * 512:(half + 1) * 512],
                      c == 0, c == 7) for c in range(8)], OM + OH + [bn + "w_out"], [PS[half]])
                TT("dve", x1[:, i, half * 512:(half + 1) * 512], ps[half], G1[:, b, half * 512:(half + 1) * 512], ALU.mult,
                   [PS[half], "G1_%d_%d" % (b, half)], [fn_ + "x1h_%d_%d_%d" % (jb % 2, i, half), X1K(jb, i)])
            TT("dve", x1[:, i, :], x1[:, i, :], xt2[0], ALU.add,
               [fn_ + "x1h_%d_%d_0" % (jb % 2, i), fn_ + "x1h_%d_%d_1" % (jb % 2, i), fn_ + "xt0"], [X1K(jb, i)])

        def stage_b1(jb, i):
            x1 = x1b[jb % 2]
            j = jb * 4 + i
            xs2k = xs2b[i % 2]
            xsk = fn_ + "xs2_%d" % (i % 2)
            ACT(xs2k, x1[:, i, :], AF.Square, [X1K(jb, i)], [xsk, fn_ + "ss2_%d" % j], accum=ss2[:, j:j + 1])
            rstd_chain(ss2[:, j:j + 1], ln2[:, j:j + 1], rs2[:, j:j + 1], 1.0 / D, [fn_ + "ss2_%d" % j, "epsc"],
                       [fn_ + "ln2_%d" % j], [fn_ + "rs2_%d" % j])
            TS("dve", xs2k, x1[:, i, :], rs2[:, j:j + 1], None, ALU.mult, None, [X1K(jb, i), fn_ + "rs2_%d" % j], [xsk])

        def stage_b2(jb, i):
            h2T = h2Tb[jb % 2]
            xs2k = xs2b[i % 2]
            xsk = fn_ + "xs2_%d" % (i % 2)
            pt = psb[2]
            TRG([(pt[:, c * 128:(c + 1) * 128], xs2k[:, c * 128:(c + 1) * 128]) for c in range(8)], ident,
                [xsk, "ident"], [PS[2]])
            for c in range(8):
                dst = h2T[:, c, i * 128:(i + 1) * 128]
                hk_ = fn_ + "h2T%d_%d_%d" % (jb % 2, i, c)
                if i % 2 == 0:
                    ACT(dst, pt[:, c * 128:(c + 1) * 128], AF.Identity, [PS[2], "A2_%d" % b] + MODT2, [hk_],
                        scale=A2[:, b, c:c + 1], bias=modT[:, 24 + c, b:b + 1])
                else:
                    TS("dve", dst, pt[:, c * 128:(c + 1) * 128], A2[:, b, c:c + 1], modT[:, 24 + c, b:b + 1], ALU.mult, ALU.add,
                       [PS[2], "A2_%d" % b] + MODT2, [hk_])

        def prep_pieces(jb):
            return [
                [lambda: stage_a(jb, 0)],
                [lambda: stage_a(jb, 1)],
                [lambda: stage_b1(jb, 0)],
                [lambda: stage_a(jb, 2), lambda: stage_b2(jb, 0)],
                [lambda: stage_b1(jb, 1)],
                [lambda: stage_a(jb, 3), lambda: stage_b2(jb, 1)],
                [lambda: stage_b1(jb, 2)],
                [lambda: stage_b2(jb, 2), lambda: stage_b1(jb, 3)],
                [lambda: stage_b2(jb, 3)],
            ]

        otb = [ot[0], xt2[0]]
        otk = [fn_ + "ot0", fn_ + "xt0"]

        def final_norm(jb_, i):
            x1_ = x1b[jb_ % 2]
            j = jb_ * 4 + i
            xs2k = xs2b[i % 2]
            ACT(xs2k, x1_[:, i, :], AF.Square, [X1K(jb_, i)], [fn_ + "xs2_%d" % (i % 2), fn_ + "ss3_%d" % j], accum=ss3[:, j:j + 1])
            rstd_chain(ss3[:, j:j + 1], ln3[:, j:j + 1], rs3[:, j:j + 1], 1.0 / D, [fn_ + "ss3_%d" % j, "epsc"],
                       [fn_ + "ln3_%d" % j], [fn_ + "rs3_%d" % j])
            STT("dve", otb[i % 2], x1_[:, i, :], rs3[:, j:j + 1], fn_bc, ALU.mult, ALU.mult, [X1K(jb_, i), fn_ + "rs3_%d" % j, "fn_bc"],
                [otk[i % 2]])
            DMA("pool", out[b, j * 128:(j + 1) * 128, :], otb[i % 2], [otk[i % 2]], [fn_ + "out%d" % j])

        for grp in prep_pieces(0):
            for f_ in grp:
                f_()
        for jb in range(4):
            x1 = x1b[jb % 2]
            h2T = h2Tb[jb % 2]
            H2 = [fn_ + "h2T%d_%d_%d" % (jb % 2, i, c) for i in range(4) for c in range(8)]
            X1 = [X1K(jb, i) for i in range(4)]
            if b == 0 and jb == 0:
                dump("x1", x1, X1)
                dump("h2T", h2T, H2)
            for p in range(8):
                wp = w1p[p % 2]
                wk = fn_ + "w1p%d" % (p % 2)
                DMA("sp", wp, w1s[p].rearrange("q (k c) -> q k c", k=8), ["w1s%d" % p], [wk])
                for q4 in range(4):
                    jj = 4 * p + q4
                    pu = ps[3 + jj % 2]
                    MMG([(pu, wp[:, kc, q4 * 128:(q4 + 1) * 128], h2T[:, kc, :], kc == 0, kc == 7) for kc in range(8)],
                        [wk] + H2, [PS[3 + jj % 2]])
                    rb = rbuf[jj % 2]
                    ACT(rb, pu, AF.Relu, [PS[3 + jj % 2]], [fn_ + "rb%d" % (jj % 2)])
                    TT("dve", uT[:, jj, :], rb, rb, ALU.mult, [fn_ + "rb%d" % (jj % 2)], [fn_ + "uT%d" % jj])
                if jb >= 1 and p < 4:
                    final_norm(jb - 1, p)
            UT = [fn_ + "uT%d" % jj for jj in range(32)]
            if b == 0 and jb == 0:
                dump("uT", uT, UT)

            def mlp2_tail(dq):
                yk = fn_ + "yT%d" % (dq % 2)
                S.add("pe", (lambda src, dstp: (lambda e: [e.transpose(dstp[:, i * 128:(i + 1) * 128], src[:, i * 128:(i + 1) * 128], identf)
                                                         for i in range(4)][-1]))(yT[dq % 2], ps[7]),
                      [yk, "identf"], [PS[7]])
                xv = x1[:, :, dq * 128:(dq + 1) * 128]
                TT("dve", xv, xv, ps[7].rearrange("p (i c) -> p i c", i=4), ALU.add, [PS[7]] + X1, X1)

            nxt = prep_pieces(jb + 1) if jb + 1 < 4 else []
            for dq in range(8):
                wp = w2p[dq % 2]
                wk = fn_ + "w2p%d" % (dq % 2)
                DMA("sp", wp, w2s[dq].rearrange("q (j c) -> q j c", j=32), ["w2s%d" % dq], [wk])
                pv = ps[5 + dq % 2]
                MMG([(pv, wp[:, jj, :], uT[:, jj, :], jj == 0, jj == 31) for jj in range(32)], [wk] + UT, [PS[5 + dq % 2]])
                yk = fn_ + "yT%d" % (dq % 2)
                ACT(yT[dq % 2], pv, AF.Identity, [PS[5 + dq % 2]] + MODT2, [yk], scale=modT[:, 40 + dq, b:b + 1])
                if dq >= 1:
                    mlp2_tail(dq - 1)
                if nxt:
                    for f_ in nxt.pop(0):
                        f_()
            mlp2_tail(7)
            while nxt:
                for f_ in nxt.pop(0):
                    f_()
            if jb == 3:
                for i in range(4):
                    final_norm(jb, i)
        S.barrier()
        if stop == 'b0' and b == 0:
            S.barrier(); S.finalize(); S.emit(); return nc

    S.finalize()
    S.emit()
    return nc


def _consts():
    bf = ml_dtypes.bfloat16
    ident = np.eye(128, dtype=np.float32)
    s = np.arange(128)[:, None]
    t = np.arange(128)[None, :]
    mask = np.concatenate([(s <= t), (s >= t)], axis=1).astype(np.float32)
    rm = np.ones((128, 2, 512), np.float32)
    rm[:, 0, 0::128] = 0.0
    rm[:, 1, 127::128] = 0.0
    tok = np.arange(T)
    row = (tok // 64).astype(np.float32)
    col = (tok % 64).astype(np.float32)
    nfreq = 8
    inv = (np.float32(10000.0) ** (-np.arange(nfreq, dtype=np.float32) / np.float32(nfreq))).astype(np.float32)
    cs = np.zeros((128, 2, T), np.float32)
    for dmm in range(32):
        grp, i = dmm // 16, dmm % 16
        f = i % 8
        ang = (row if grp == 0 else col) * inv[f]
        c, sn = np.cos(ang.astype(np.float32)), np.sin(ang.astype(np.float32))
        sign = -1.0 if i < 8 else 1.0
        for q in range(4):
            cs[q * 32 + dmm, 0] = c
            cs[q * 32 + dmm, 1] = sign * sn
    return dict(c_ident=ident.astype(bf), c_identf=ident, c_mask=mask.astype(bf), c_rmask=rm.astype(bf), c_cs=cs.astype(bf))


def _swap_rope(w, base, period):
    w = w.copy()
    ncol = w.shape[-1]
    for h0 in range(0, ncol, period):
        r0 = h0 + base
        blk = w[..., r0:r0 + 32].copy()
        new = blk.copy()
        for grp in range(2):
            o = grp * 16
            new[..., o:o + 8] = blk[..., o + 8:o + 16]
            new[..., o + 8:o + 16] = blk[..., o:o + 8]
        w[..., r0:r0 + 32] = new
    return w


def _kp(w, nk):
    return np.ascontiguousarray(w.reshape(nk, 128, -1).transpose(1, 0, 2))


def _shared_inputs(inp):
    f = np.float32
    w_in = inp["w_in"][0]
    w_uq = inp["w_uq"][0]
    w_ukv = inp["w_ukv"][0].reshape(128, 8, 128)
    kr = np.zeros((1024, 96), f)
    kr[:, 64:96] = w_in[:, 384:416]
    kr = _swap_rope(kr, 64, 96)
    w1 = inp["w_mlp_in"][0]
    w1r = np.ascontiguousarray(w1.reshape(8, 128, 8, 512).transpose(2, 1, 0, 3)).reshape(8, 128, 4096)
    w2 = inp["w_mlp_out"][0]
    w2r = np.ascontiguousarray(w2.reshape(32, 128, 8, 128).transpose(2, 1, 0, 3)).reshape(8, 128, 4096)
    sh = dict(
        w_ada=_kp(inp["w_ada"][0], 8),
        b_adaT=np.ascontiguousarray(inp["b_ada"][0].reshape(48, 128).T),
        b_ada=np.ascontiguousarray(inp["b_ada"][0]),
        nmixT=np.ascontiguousarray(inp["norm_mix"][0].reshape(8, 128).T),
        nmlpT=np.ascontiguousarray(inp["norm_mlp"][0].reshape(8, 128).T),
        w_in=_kp(w_in, 8),
        w_krs=_kp(kr, 8),
        qnT=np.ascontiguousarray(inp["q_norm"][0].reshape(2, 128).T),
        kvnT=np.ascontiguousarray(inp["kv_norm"][0].reshape(128, 1)),
        w_uq=_kp(w_uq, 2),
        w_uqs=_kp(_swap_rope(w_uq, 64, 96), 2),
        w_kn=np.ascontiguousarray(w_ukv[:, :, 0:64].reshape(128, 512)),
        w_v=np.ascontiguousarray(w_ukv[:, :, 64:128].reshape(128, 512)),
        lbT=np.ascontiguousarray(inp["hgrn_lb"].reshape(2, 8, 128).transpose(2, 0, 1)),
        hgn=np.ascontiguousarray(inp["hgrn_norm"][0]),
        w_out=_kp(inp["w_out"][0], 8),
        w1=w1r, w2=w2r,
        fnorm=np.ascontiguousarray(inp["final_norm"]),
    )
    sh = {k: np.ascontiguousarray(v, dtype=f) for k, v in sh.items()}
    sh.update(_consts())
    return sh


_NC_CACHE = {}


def kernel(**inputs):
    inp = {k: np.asarray(v) for k, v in inputs.items()}
    shared = _shared_inputs(inp)
    in_maps = []
    for c in range(8):
        b0 = c * NB
        cv = np.stack([inp["c"][b0], inp["c"][b0 + 1], inp["c_ctx"]], axis=0)
        m = dict(shared)
        m["x"] = np.ascontiguousarray(inp["x"][b0:b0 + NB], dtype=np.float32)
        m["ctx"] = np.ascontiguousarray(inp["ctx"][b0:b0 + NB], dtype=np.float32)
        m["cvecT"] = np.ascontiguousarray(cv.reshape(3, 8, 128).transpose(2, 1, 0), dtype=np.float32)
        in_maps.append(m)
    if "nc" not in _NC_CACHE:
        _NC_CACHE["nc"] = build_program()
    res = run_bass_kernel_spmd(_NC_CACHE["nc"], in_maps, core_ids=list(range(8)))
    return np.concatenate([np.asarray(r["out"]) for r in res.results], axis=0).astype(np.float32)
```

```python
import numpy as np
import ml_dtypes
import concourse.bass as bass
import concourse.mybir as mybir
from concourse.bass_utils import run_bass_kernel_spmd

F32 = mybir.dt.float32
BF16 = mybir.dt.bfloat16
AF = mybir.ActivationFunctionType
ALU = mybir.AluOpType

NB = 2
T = 2048
L = 256
TE = T + L
D = 1024
NT = TE // 128
DFF = 4096
EPS = 1e-6
BLKS = [(0, 256)] + [(256 + 512 * j, 512) for j in range(4)]
ATT_SCALE = float(96 ** -0.5)
DMA_K = 8


class _Op:
    __slots__ = ("idx", "eng", "fn", "dma", "deps", "signal", "tick", "sem_i", "sem_v")

    def __init__(self, idx, eng, fn, dma, deps):
        self.idx, self.eng, self.fn, self.dma, self.deps = idx, eng, fn, dma, deps
        self.signal = False
        self.tick = 0
        self.sem_i = 0
        self.sem_v = 0


class Sched:
    ENGS = ("pe", "act", "dve", "pool", "sp")

    def __init__(self, nc):
        self.nc = nc
        self.ops = []
        self.last_w = {}
        self.readers = {}
        self.dma_ops = {"sp": [], "pool": [], "act": []}

    def add(self, eng, fn, r=(), w=(), dma=False):
        idx = len(self.ops)
        deps = {}
        for k in r:
            p = self.last_w.get(k)
            if p is not None:
                deps[p] = "raw"
        for k in w:
            p = self.last_w.get(k)
            if p is not None and p not in deps:
                deps[p] = "waw"
            for q in self.readers.get(k, ()):
                if q not in deps:
                    deps[q] = "war"
        if dma:
            lst = self.dma_ops[eng]
            if len(lst) >= DMA_K:
                deps[lst[-DMA_K]] = "raw"
            lst.append(idx)
        op = _Op(idx, eng, fn, dma, deps)
        for k in w:
            self.last_w[k] = idx
            self.readers[k] = []
        for k in r:
            self.readers.setdefault(k, []).append(idx)
        self.ops.append(op)
        return op

    def barrier(self):
        last = {}
        for op in self.ops:
            if not op.dma and op.fn is not None:
                last[op.eng] = op.idx
        dmas = [i for q in self.dma_ops.values() for i in q[-DMA_K:]]
        for e in self.ENGS:
            deps = {i: "raw" for ee, i in last.items()}
            for i in dmas:
                deps[i] = "raw"
            idx = len(self.ops)
            op = _Op(idx, e, None, False, deps)
            op.deps = {k: ("bar") for k in deps}
            self.ops.append(op)

    def finalize(self):
        ops = self.ops
        for q, lst in self.dma_ops.items():
            for n, i in enumerate(lst):
                ops[i].sem_i = n % DMA_K
                ops[i].sem_v = 16 * (n // DMA_K + 1)
        self.waits = {}
        for op in ops:
            best = {}
            dma_w = {}
            for p, kind in op.deps.items():
                po = ops[p]
                if po.dma:
                    key = (po.eng, po.sem_i)
                    dma_w[key] = max(dma_w.get(key, 0), po.sem_v)
                    continue
                if po.fn is None:
                    continue
                if po.eng == op.eng and not op.dma and kind != "bar":
                    if po.eng == "pe":
                        continue
                if p > best.get(po.eng, -1):
                    best[po.eng] = p
            for e, p in best.items():
                ops[p].signal = True
            self.waits[op.idx] = (best, dma_w)
        cnt = {e: 0 for e in self.ENGS}
        for op in ops:
            if op.signal:
                cnt[op.eng] += 1
                op.tick = cnt[op.eng]

    def emit(self, final_wait_eng="sp"):
        nc = self.nc
        ops = self.ops
        from contextlib import ExitStack
        with ExitStack() as st:
            esem = {e: st.enter_context(nc.semaphore("cs_" + e)) for e in self.ENGS}
            dsem = {q: [st.enter_context(nc.semaphore("ds_%s%d" % (q, i))) for i in range(DMA_K)]
                    for q in self.dma_ops}
            block = st.enter_context(nc.Block())
            per_eng = {e: [op for op in ops if op.eng == e] for e in self.ENGS}
            waits = self.waits

            def body(ename, eng):
                seen = {}
                for op in per_eng[ename]:
                    best, dma_w = waits[op.idx]
                    for pe_, p in best.items():
                        t = ops[p].tick
                        key = ("c", pe_)
                        if seen.get(key, 0) < t:
                            eng.wait_ge(esem[pe_], t)
                            seen[key] = t
                    for (q, si), v in dma_w.items():
                        key = ("d", q, si)
                        if seen.get(key, 0) < v:
                            eng.wait_ge(dsem[q][si], v)
                            seen[key] = v
                    if op.fn is None:
                        continue
                    inst = op.fn(eng)
                    if op.dma:
                        inst.then_inc(dsem[op.eng][op.sem_i], 16)
                    elif op.signal:
                        inst.then_inc(esem[ename], 1)
                if ename == final_wait_eng:
                    for q, lst in self.dma_ops.items():
                        for i in lst[-DMA_K:]:
                            eng.wait_ge(dsem[q][ops[i].sem_i], ops[i].sem_v)

            @block.tensor
            def _(e):
                body("pe", e)

            @block.scalar
            def _(e):
                body("act", e)

            @block.vector
            def _(e):
                body("dve", e)

            @block.gpsimd
            def _(e):
                body("pool", e)

            @block.sync
            def _(e):
                body("sp", e)


class Arena:
    def __init__(self, nc, lo=16896, hi=229376):
        self.nc, self.lo, self.hi, self.off = nc, lo, hi, lo
        self.n = 0

    def alloc(self, shape, dt):
        nbytes = int(np.prod(shape[1:])) * (2 if dt == BF16 else 4)
        nbytes = (nbytes + 63) // 64 * 64
        assert self.off + nbytes <= self.hi, ("SBUF overflow", self.off, nbytes)
        self.n += 1
        t = self.nc.alloc_sbuf_tensor_at("sb%d" % self.n, list(shape), dt, offset=self.off)
        self.off += nbytes
        return t.ap()

    def mark(self):
        return self.off

    def release(self, m):
        self.off = m


def build_program(dbg=None, stop=None):
    nc = bass.Bass("TRN2", target_bir_lowering=False)
    S = Sched(nc)
    A = Arena(nc)

    def din(name, shape, dt=F32):
        return nc.dram_tensor(name, list(shape), dt, kind="ExternalInput").ap()

    x = din("x", [NB, T, D])
    ctx = din("ctx", [NB, L, D])
    cvecT = din("cvecT", [128, 8, 3])
    w_ada = din("w_ada", [128, 8, 6144])
    b_adaT = din("b_adaT", [128, 48])
    b_ada = din("b_ada", [6144])
    nmixT = din("nmixT", [128, 8])
    nmlpT = din("nmlpT", [128, 8])
    w_in = din("w_in", [128, 8, 2976])
    w_krs = din("w_krs", [128, 8, 96])
    qnT = din("qnT", [128, 2])
    kvnT = din("kvnT", [128, 1])
    w_uq = din("w_uq", [128, 2, 768])
    w_uqs = din("w_uqs", [128, 2, 768])
    w_kn = din("w_kn", [128, 512])
    w_v = din("w_v", [128, 512])
    lbT = din("lbT", [128, 2, 8])
    hgn = din("hgn", [128])
    w_out = din("w_out", [128, 8, 1024])
    w1 = din("w1", [8, 128, 4096])
    w2 = din("w2", [8, 128, 4096])
    fnorm = din("fnorm", [D])
    c_ident = din("c_ident", [128, 128], BF16)
    c_identf = din("c_identf", [128, 128], F32)
    c_mask = din("c_mask", [128, 256], BF16)
    c_rmask = din("c_rmask", [128, 2, 512], BF16)
    c_cs = din("c_cs", [128, 2, T], BF16)
    out = nc.dram_tensor("out", [NB, T, D], F32, kind="ExternalOutput").ap()
    w1s = nc.dram_tensor("w1s", [8, 128, 4096], BF16).ap()
    w2s = nc.dram_tensor("w2s", [8, 128, 4096], BF16).ap()
    dbg_out = {}
    if dbg:
        for name, shape, dt in dbg:
            dbg_out[name] = nc.dram_tensor("dbg_" + name, list(shape), dt, kind="ExternalOutput").ap()

    ps = [nc.alloc_psum_tensor("ps%d" % i, [128, 512], F32).ap() for i in range(8)]
    psb = [p.bitcast(BF16) for p in ps]
    PS = ["ps%d" % i for i in range(8)]

    def ACT(out_, in_, func, r, w, scale=1.0, bias=0.0, accum=None):
        kw = {}
        if accum is not None:
            kw["accum_out"] = accum
        S.add("act", lambda e: e.activation(out=out_, in_=in_, func=func, bias=bias, scale=scale, **kw), r, w)

    def TT(eng, out_, a, b, op, r, w):
        S.add(eng, lambda e: e.tensor_tensor(out_, a, b, op), r, w)

    def TS(eng, out_, a, s1, s2, op0, op1, r, w):
        if s2 is None:
            S.add(eng, lambda e: e.tensor_scalar(out_, a, s1, None, op0), r, w)
        else:
            S.add(eng, lambda e: e.tensor_scalar(out_, a, s1, s2, op0, op1), r, w)

    def STT(eng, out_, a, sc, b, op0, op1, r, w):
        S.add(eng, lambda e: e.scalar_tensor_tensor(out_, a, sc, b, op0, op1), r, w)

    def CP(eng, out_, in_, r, w):
        if eng == "act":
            S.add("act", lambda e: e.copy(out_, in_), r, w)
        else:
            S.add(eng, lambda e: e.tensor_copy(out_, in_), r, w)

    def MSET(eng, ap, val, w):
        S.add(eng, lambda e: e.memset(ap, val), (), w)

    def MMG(lst, r, w):
        lst = list(lst)

        def fn(e):
            ins = None
            for (o, l, rr, st, sp) in lst:
                ins = e.matmul(o, lhsT=l, rhs=rr, start=st, stop=sp)
            return ins
        S.add("pe", fn, r, w)

    def TRG(lst, ident_ap, r, w):
        lst = list(lst)

        def fn(e):
            ins = None
            for (o, i_) in lst:
                ins = e.transpose(o, i_, ident_ap)
            return ins
        S.add("pe", fn, r, w)

    def DMA(q, out_, in_, r, w):
        S.add(q, lambda e: e.dma_start(out=out_, in_=in_), r, w, dma=True)

    def RECIP(out_, in_, r, w):
        S.add("dve", lambda e: e.reciprocal(out_, in_), r, w)

    def SCAN(out_, d0, d1, r, w):
        S.add("dve", lambda e: e.tensor_tensor_scan(out_, d0, d1, 0.0, ALU.mult, ALU.add), r, w)

    def dump(name, src_ap, r):
        if name in dbg_out:
            DMA("sp", dbg_out[name], src_ap, r, ["dbg_" + name])

    def rstd_chain(ss, tmp, rs, n_inv, r, w_tmp, w_rs):
        ACT(tmp, ss, AF.Ln, r, w_tmp, scale=n_inv, bias=epsc[:, 0:1])
        ACT(rs, tmp, AF.Exp, w_tmp, w_rs, scale=-0.5)

    ident = A.alloc([128, 128], BF16)
    ones_bf = A.alloc([128, 128], BF16)
    maskfb2 = A.alloc([128, 2, 256], BF16)
    rmask = A.alloc([128, 2, 512], BF16)
    epsc = A.alloc([128, 2], F32)
    fn_bc = A.alloc([128, D], F32)
    gn_bc = A.alloc([128, 512], F32)
    G1 = A.alloc([128, NB, D], F32)
    modT = A.alloc([128, 48, 3], F32)
    A1 = A.alloc([128, 3, 8], F32)
    A2 = A.alloc([128, 3, 8], F32)
    lb_t = A.alloc([128, 8], F32)
    w_uq_bf = A.alloc([128, 2, 768], BF16)
    w_uqs_bf = A.alloc([128, 2, 768], BF16)
    w_kn_bf = A.alloc([128, 512], BF16)
    w_v_bf = A.alloc([128, 512], BF16)

    DMA("sp", ident, c_ident, [], ["ident"])
    DMA("sp", maskfb2[:, 0, :], c_mask, [], ["maskfb"])
    DMA("sp", maskfb2[:, 1, :], c_mask, [], ["maskfb"])
    DMA("sp", rmask, c_rmask, [], ["rmask"])
    DMA("sp", fn_bc, fnorm.partition_broadcast(128), [], ["fn_bc"])
    for i in range(4):
        DMA("sp", gn_bc[:, i * 128:(i + 1) * 128], hgn.partition_broadcast(128), [], ["gn_bc%d" % i])
    GN = ["gn_bc%d" % i for i in range(4)]
    MSET("dve", ones_bf, 1.0, ["ones"])
    MSET("dve", epsc, EPS, ["epsc"])

    m_setup = A.mark()
    wa = A.alloc([128, 8, 6144], BF16)
    cT = A.alloc([128, 8, 3], F32)
    sT = A.alloc([128, 8, 3], BF16)
    sTb = A.alloc([128, NB, 8, 128], BF16)
    badaT = A.alloc([128, 48], F32)
    bada_g1 = A.alloc([128, D], F32)
    nmix_t = A.alloc([128, 8], F32)
    nmlp_t = A.alloc([128, 8], F32)
    lbraw = A.alloc([128, 2, 8], F32)
    lbtmp = A.alloc([128, 8], F32)
    qn_t = A.alloc([128, 2], F32)
    kvn_t = A.alloc([128, 1], F32)
    wst = A.alloc([128, 2, 768], F32)
    wst2 = A.alloc([128, 2, 768], F32)
    wst3 = A.alloc([128, 1024], F32)

    for kc in range(8):
        DMA("pool", wa[:, kc, 0:2048], w_ada[:, kc, 0:2048], [], ["wa%d_a" % kc])
    for kc in range(8):
        DMA("pool", wa[:, kc, 2048:6144], w_ada[:, kc, 2048:6144], [], ["wa%d_b" % kc])
    DMA("sp", cT, cvecT, [], ["cT"])
    DMA("sp", badaT, b_adaT, [], ["badaT"])
    DMA("sp", bada_g1, b_ada[2048:3072].partition_broadcast(128), [], ["bada_g1"])
    DMA("sp", nmix_t, nmixT, [], ["nmix"])
    DMA("sp", nmlp_t, nmlpT, [], ["nmlp"])
    DMA("sp", lbraw, lbT, [], ["lbraw"])
    DMA("sp", qn_t, qnT, [], ["qn"])
    DMA("sp", kvn_t, kvnT, [], ["kvn"])
    DMA("sp", wst, w_uq, [], ["wst"])
    DMA("sp", wst2, w_uqs, [], ["wst2"])
    DMA("sp", wst3[:, 0:512], w_kn, [], ["wst3a"])
    DMA("sp", wst3[:, 512:1024], w_v, [], ["wst3b"])

    TT("dve", lbtmp, lbraw[:, 1, :], lbraw[:, 0, :], ALU.subtract, ["lbraw"], ["lbtmp"])
    ACT(lbtmp, lbtmp, AF.Exp, ["lbtmp"], ["lbtmp"])
    TS("dve", lbtmp, lbtmp, 1.0, None, ALU.add, None, ["lbtmp"], ["lbtmp"])
    RECIP(lb_t, lbtmp, ["lbtmp"], ["lb_t"])
    for c in range(2):
        TS("dve", w_uq_bf[:, c, :], wst[:, c, :], qn_t[:, c:c + 1], None, ALU.mult, None, ["wst", "qn"], ["w_uq_bf%d" % c])
        TS("dve", w_uqs_bf[:, c, :], wst2[:, c, :], qn_t[:, c:c + 1], None, ALU.mult, None, ["wst2", "qn"], ["w_uqs_bf%d" % c])
    TS("dve", w_kn_bf, wst3[:, 0:512], kvn_t[:, 0:1], None, ALU.mult, None, ["wst3a", "kvn"], ["w_kn_bf"])
    TS("dve", w_v_bf, wst3[:, 512:1024], kvn_t[:, 0:1], None, ALU.mult, None, ["wst3b", "kvn"], ["w_v_bf"])
    WUQ = ["w_uq_bf0", "w_uq_bf1"]
    WUQS = ["w_uqs_bf0", "w_uqs_bf1"]

    ACT(sT, cT, AF.Silu, ["cT"], ["sT"])
    WAa = ["wa%d_a" % k for k in range(8)]
    WA = ["wa%d_b" % k for k in range(8)]
    psM = ps[0][:, 0:144]
    psM2 = ps[3][:, 0:144]
    MMG([(psM[:, j * 3:(j + 1) * 3], wa[:, kc, j * 128:(j + 1) * 128], sT[:, kc, :], kc == 0, kc == 7)
         for j in range(16) for kc in range(8)], WAa + ["sT"], [PS[0]])
    for b in range(3):
        TT("dve", modT[:, 0:16, b], ps[0][:, b:48:3], badaT[:, 0:16], ALU.add, [PS[0], "badaT"], ["modT%d" % b])
    MODT = ["modT0", "modT1", "modT2"]
    MMG([(psM2[:, j * 3:(j + 1) * 3], wa[:, kc, j * 128:(j + 1) * 128], sT[:, kc, :], kc == 0, kc == 7)
         for j in range(16, 48) for kc in range(8)], WA + ["sT"], [PS[3]])
    MODT2 = ["modTb0", "modTb1", "modTb2"]
    for b in range(3):
        TT("dve", modT[:, 16:48, b], ps[3][:, 48 + b:144:3], badaT[:, 16:48], ALU.add, [PS[3], "badaT"], ["modTb%d" % b])
    for b in range(3):
        STT("dve", A1[:, b, :], modT[:, 8:16, b], 1.0, nmix_t, ALU.add, ALU.mult, MODT + ["nmix"], ["A1_%d" % b])
        STT("dve", A2[:, b, :], modT[:, 32:40, b], 1.0, nmlp_t, ALU.add, ALU.mult, MODT2 + ["nmlp"], ["A2_%d" % b])
    for b in range(NB):
        for kc in range(8):
            CP("dve", sTb[:, b, kc, :], sT[:, kc, b:b + 1].to_broadcast([128, 128]), ["sT"], ["sTb%d_%d" % (b, kc)])
        for half in range(2):
            pg = ps[1 + half]
            MMG([(pg, sTb[:, b, kc, :], wa[:, kc, 2048 + half * 512:2048 + (half + 1) * 512], kc == 0, kc == 7)
                 for kc in range(8)], WA + ["sTb%d_%d" % (b, kc) for kc in range(8)], [PS[1 + half]])
            TT("dve", G1[:, b, half * 512:(half + 1) * 512], pg, bada_g1[:, half * 512:(half + 1) * 512], ALU.add,
               [PS[1 + half], "bada_g1"], ["G1_%d_%d" % (b, half)])
    dump("modT", modT, MODT + MODT2)
    dump("G1", G1, ["G1_%d_%d" % (b, h) for b in range(NB) for h in range(2)])
    dump("lb", lb_t, ["lb_t"])
    S.barrier()
    if stop == 'setup':
        S.finalize(); S.emit(); return nc
    A.release(m_setup)

    o_mlaT = A.alloc([128, 4, T], BF16)
    o_hgT = A.alloc([128, 4, T], BF16)
    m_fin = A.mark()
    hT = A.alloc([128, 8, TE], BF16)
    m_batch = A.mark()

    for b in range(NB):
        bn = "b%d_" % b

        def HK(g):
            return [bn + "hT%d_%d" % (g, c) for c in range(8)]

        def HKB(t0, n):
            r = []
            for g in range(t0 // 128, (t0 + n) // 128):
                r += HK(g)
            return r

        A.release(m_batch)
        xsb = [A.alloc([128, D], BF16) for _ in range(2)]
        junk = A.alloc([128, D], BF16)
        ssA = A.alloc([128, NT], F32)
        lnA = A.alloc([128, NT], F32)
        rsA = A.alloc([128, NT], F32)
        NXB = 3
        xtb = [A.alloc([128, D], F32) for _ in range(NXB)]

        def pa1(g):
            src = ctx[b, g * 128:(g + 1) * 128, :] if g < 2 else x[b, (g - 2) * 128:(g - 1) * 128, :]
            k = g % NXB
            xk = bn + "xt%d" % k
            DMA("sp", xtb[k], src, [], [xk])
            ACT(junk, xtb[k], AF.Square, [xk], [bn + "junk", bn + "ssA%d" % g], accum=ssA[:, g:g + 1])
            rstd_chain(ssA[:, g:g + 1], lnA[:, g:g + 1], rsA[:, g:g + 1], 1.0 / D,
                       [bn + "ssA%d" % g, "epsc"], [bn + "lnA%d" % g], [bn + "rsA%d" % g])

        def pa2(g):
            k = g % NXB
            xk, sk = bn + "xt%d" % k, bn + "xs%d" % (g % 2)
            TS("dve", xsb[g % 2], xtb[k], rsA[:, g:g + 1], None, ALU.mult, None, [xk, bn + "rsA%d" % g], [sk])
            pt = psb[g % 2]
            TRG([(pt[:, c * 128:(c + 1) * 128], xsb[g % 2][:, c * 128:(c + 1) * 128]) for c in range(8)], ident,
                [sk, "ident"], [PS[g % 2]])

        def pa3(g):
            bb = 2 if g < 2 else b
            pt = psb[g % 2]
            for c in range(8):
                dst = hT[:, c, g * 128:(g + 1) * 128]
                if g % 4 == 0:
                    ACT(dst, pt[:, c * 128:(c + 1) * 128], AF.Identity, [PS[g % 2], "A1_%d" % bb] + MODT, [bn + "hT%d_%d" % (g, c)],
                        scale=A1[:, bb, c:c + 1], bias=modT[:, c, bb:bb + 1])
                else:
                    TS("dve", dst, pt[:, c * 128:(c + 1) * 128], A1[:, bb, c:c + 1], modT[:, c, bb:bb + 1], ALU.mult, ALU.add,
                       [PS[g % 2], "A1_%d" % bb] + MODT, [bn + "hT%d_%d" % (g, c)])

        for tt_ in range(NT + 2):
            if tt_ - 2 >= 0:
                pa3(tt_ - 2)
            if 0 <= tt_ - 1 < NT:
                pa2(tt_ - 1)
            if tt_ < NT:
                pa1(tt_)
        if b == 0:
            dump("hT", hT, [k for g in range(NT) for k in HK(g)])

        if stop == 'A' and b == 0:
            S.barrier(); S.finalize(); S.emit(); return nc
        S.barrier()
        A.release(m_batch)
        m_mla = A.mark()
        cs = A.alloc([128, 2, T], BF16)
        w_mla = A.alloc([128, 8, 416], BF16)
        w_krs_bf = A.alloc([128, 8, 96], BF16)
        cqnT = A.alloc([128, 2, TE], BF16)
        ckvnT = A.alloc([128, TE], BF16)
        krotT = A.alloc([128, TE], BF16)
        Vaug = A.alloc([128, NT, 8, 128], BF16)
        KTb = [A.alloc([128, TE], BF16) for _ in range(2)]
        QTb = [A.alloc([128, T], BF16) for _ in range(2)]
        PTb = [A.alloc([128, 512], BF16) for _ in range(3)]
        sq = A.alloc([128, 3, 512], BF16)
        tA = A.alloc([128, 512], F32)
        tB = A.alloc([128, 512], F32)
        rq_bc = A.alloc([128, 512], F32)
        rkv_bc = A.alloc([128, 512], F32)
        rp1 = [A.alloc([128, 512], F32) for _ in range(1)]
        rp2 = [A.alloc([128, 512], F32) for _ in range(1)]
        rden = [A.alloc([128, 512], F32) for _ in range(2)]

        DMA("sp", cs, c_cs, [], [bn + "cs"])
        MSET("dve", Vaug, 1.0, [bn + "Vp%d" % g for g in range(NT)])
        DMA("pool", w_mla, w_in[:, :, 0:416], [], [bn + "w_mla"])
        DMA("pool", w_krs_bf, w_krs, [], [bn + "w_krs"])
        WM = [bn + "w_mla"]

        v_defer = []
        for bi, (t0, n) in enumerate(BLKS):
            hk = HKB(t0, n)
            tok = slice(t0, t0 + n)
            for c in range(2):
                MMG([(ps[c][:, 0:n], w_mla[:, kc, c * 128:(c + 1) * 128], hT[:, kc, tok], kc == 0, kc == 7) for kc in range(8)],
                    WM + hk, [PS[c]])
            MMG([(ps[2][:, 0:n], w_mla[:, kc, 256:384], hT[:, kc, tok], kc == 0, kc == 7) for kc in range(8)], WM + hk, [PS[2]])
            MMG([(ps[3][0:96, 0:n], w_mla[:, kc, 320:416], hT[:, kc, tok], kc == 0, kc == 7) for kc in range(8)], WM + hk, [PS[3]])
            if bi > 0:
                MMG([(ps[4][0:96, 0:n], w_krs_bf[:, kc, :], hT[:, kc, tok], kc == 0, kc == 7) for kc in range(8)],
                    [bn + "w_krs"] + hk, [PS[4]])
            while v_defer:
                v_defer.pop(0)()
            for c in range(3):
                ACT(sq[:, c, 0:n], ps[c][:, 0:n], AF.Square, [PS[c]], [bn + "sq%d" % c])
            MMG([(ps[5][:, 0:n], ones_bf, sq[:, 0, 0:n], True, False), (ps[5][:, 0:n], ones_bf, sq[:, 1, 0:n], False, True)],
                ["ones", bn + "sq0", bn + "sq1"], [PS[5]])
            MMG([(ps[6][:, 0:n], ones_bf, sq[:, 2, 0:n], True, True)], ["ones", bn + "sq2"], [PS[6]])
            rstd_chain(ps[5][:, 0:n], tA[:, 0:n], rq_bc[:, 0:n], 1.0 / 256, [PS[5], "epsc"], [bn + "tA"], [bn + "rq_bc"])
            rstd_chain(ps[6][:, 0:n], tB[:, 0:n], rkv_bc[:, 0:n], 1.0 / 128, [PS[6], "epsc"], [bn + "tB"], [bn + "rkv_bc"])
            for c in range(2):
                TT("dve", cqnT[:, c, tok], ps[c][:, 0:n], rq_bc[:, 0:n], ALU.mult, [PS[c], bn + "rq_bc"], [bn + "cqnT%d_%d" % (bi, c)])
            TT("dve", ckvnT[:, tok], ps[2][:, 0:n], rkv_bc[:, 0:n], ALU.mult, [PS[2], bn + "rkv_bc"], [bn + "ckvnT%d" % bi])
            if bi == 0:
                CP("dve", krotT[64:96, tok], ps[3][64:96, 0:n], [PS[3]], [bn + "krotT%d" % bi])
            else:
                lt = slice(t0 - L, t0 - L + n)
                TT("dve", rp1[0][64:96, 0:n], ps[3][64:96, 0:n], cs[64:96, 0, lt], ALU.mult, [PS[3], bn + "cs"], [bn + "rp1_0"])
                TT("dve", rp2[0][64:96, 0:n], ps[4][64:96, 0:n], cs[64:96, 1, lt], ALU.mult, [PS[4], bn + "cs"], [bn + "rp2_0"])
                TT("dve", krotT[64:96, tok], rp1[0][64:96, 0:n], rp2[0][64:96, 0:n], ALU.add, [bn + "rp1_0", bn + "rp2_0"],
                   [bn + "krotT%d" % bi])
            def v_part(bi=bi, t0=t0, n=n):
                for g in range(t0 // 128, (t0 + n) // 128):
                    MMG([(ps[7], ckvnT[:, g * 128:(g + 1) * 128], w_v_bf, True, True)], [bn + "ckvnT%d" % bi, "w_v_bf"], [PS[7]])
                    pv4 = ps[7].rearrange("p (j e c) -> p j e c", j=4, e=2)
                    veng = "act" if g % 2 else "dve"
                    CP(veng, Vaug[:, g, 0::2, 0:64], pv4[:, :, 0, :], [PS[7]], [bn + "Vp%d" % g])
                    CP(veng, Vaug[:, g, 1::2, 64:128], pv4[:, :, 1, :], [PS[7]], [bn + "Vp%d" % g])
            v_defer.append(v_part)
        while v_defer:
            v_defer.pop(0)()
        CQ = [bn + "cqnT%d_%d" % (bi, c) for bi in range(5) for c in range(2)]
        CKV = [bn + "ckvnT%d" % bi for bi in range(5)]
        KROT = [bn + "krotT%d" % bi for bi in range(5)]
        if b == 0:
            dump("cqnT", cqnT, CQ)
            dump("ckvnT", ckvnT, CKV)
            dump("krotT", krotT[64:96, :], KROT)

        if stop == 'mlaproj' and b == 0:
            S.barrier(); S.finalize(); S.emit(); return nc
        def proj_head(h):
            kb = h % 2
            KT, QT = KTb[kb], QTb[kb]
            kk, qk = bn + "KT%d" % kb, bn + "QT%d" % kb
            pc = 0
            for bi, (t0, n) in enumerate(BLKS):
                tok = slice(t0, t0 + n)
                pb, pk = ps[7 - pc % 2], PS[7 - pc % 2]
                pc += 1
                MMG([(pb[0:64, 0:n], w_kn_bf[:, h * 64:(h + 1) * 64], ckvnT[:, tok], True, True)],
                    ["w_kn_bf", bn + "ckvnT%d" % bi], [pk])
                CP("dve", KT[0:64, tok], pb[0:64, 0:n], [pk], [kk + "_n%d" % bi])
                yield
            CP("dve", KT[64:96, :], krotT[64:96, :], KROT, [kk + "_r"])
            for j in range(4):
                q0 = j * 512
                et = slice(L + q0, L + q0 + 512)
                lt = slice(q0, q0 + 512)
                cqk = [bn + "cqnT%d_%d" % (j + 1, c) for c in range(2)]
                pb, pk = ps[7 - pc % 2], PS[7 - pc % 2]
                pc += 1
                MMG([(pb[0:96, :], w_uq_bf[:, c, h * 96:(h + 1) * 96], cqnT[:, c, et], c == 0, c == 1) for c in range(2)],
                    WUQ + cqk, [pk])
                CP("dve", QT[0:64, lt], pb[0:64, :], [pk], [qk + "_n%d" % j])
                TT("dve", rp1[0][64:96, :], pb[64:96, :], cs[64:96, 0, lt], ALU.mult, [pk, bn + "cs"], [bn + "rp1_0"])
                yield
                pb, pk = ps[7 - pc % 2], PS[7 - pc % 2]
                pc += 1
                MMG([(pb[0:96, :], w_uqs_bf[:, c, h * 96:(h + 1) * 96], cqnT[:, c, et], c == 0, c == 1) for c in range(2)],
                    WUQS + cqk, [pk])
                TT("dve", rp2[0][64:96, :], pb[64:96, :], cs[64:96, 1, lt], ALU.mult, [pk, bn + "cs"], [bn + "rp2_0"])
                TT("dve", QT[64:96, lt], rp1[0][64:96, :], rp2[0][64:96, :], ALU.add, [bn + "rp1_0", bn + "rp2_0"], [qk + "_r%d" % j])
                yield

        def KTK(h):
            kk = bn + "KT%d" % (h % 2)
            return [kk + "_n%d" % bi for bi in range(5)] + [kk + "_r"]

        def QTK(h, j):
            qk = bn + "QT%d" % (h % 2)
            return [qk + "_n%d" % j, qk + "_r%d" % j]

        steps = [(h, qb, kt) for h in range(8) for qb in range(4) for kt in range(NT)]
        for _ in proj_head(0):
            pass
        pgen = None
        if b == 0:
            dump("KT0", KTb[0][0:96, :], KTK(0))
            dump("QT0", QTb[0][0:96, :], [k for j in range(4) for k in QTK(0, j)])

        SB = [0, 1, 2, 5]

        def reg_qk(si):
            h, qb, kt = steps[si]
            KT, QT = KTb[h % 2], QTb[h % 2]
            sb_ = SB[si % 4]
            MMG([(ps[sb_], KT[0:96, kt * 128:(kt + 1) * 128], QT[0:96, qb * 512:(qb + 1) * 512], True, True)],
                KTK(h) + QTK(h, qb), [PS[sb_]])

        reg_qk(0)
        reg_qk(1)
        reg_qk(2)
        for si, (h, qb, kt) in enumerate(steps):
            u = (h * 4 + qb) % 2
            pO = ps[3 + u]
            sb_ = SB[si % 4]
            ACT(PTb[si % 3], ps[sb_], AF.Exp, [PS[sb_]], [bn + "PT%d" % (si % 3)] + (["convgate"] if (b == 0 and si == 0) else []),
                scale=ATT_SCALE)
            if si + 3 < len(steps):
                reg_qk(si + 3)
            MMG([(pO, Vaug[:, kt, h, :], PTb[si % 3], kt == 0, kt == NT - 1)],
                [bn + "Vp%d" % kt, bn + "PT%d" % (si % 3)], [PS[3 + u]])
            if kt == NT - 1:
                orow = slice((h % 2) * 64, (h % 2) * 64 + 64)
                drow = slice(64 - (h % 2) * 64, 128 - (h % 2) * 64)
                RECIP(rden[u][orow, :], pO[drow, :], [PS[3 + u]], [bn + "rden%d" % u])
                TT("dve", o_mlaT[orow, h // 2, qb * 512:(qb + 1) * 512], pO[orow, :], rden[u][orow, :], ALU.mult,
                   [PS[3 + u], bn + "rden%d" % u], [bn + "omla%d_%d" % (h, qb)])
            if b == 0 and kt == 0 and qb == 0 and h == 0:
                for p in range(8):
                    DMA("pool", w1s[p], w1[p], ["convgate"], ["w1s%d" % p])
                for p in range(8):
                    DMA("pool", w2s[p], w2[p], [], ["w2s%d" % p])
            if qb == 0 and kt == 2 and h + 1 < 8:
                pgen = proj_head(h + 1)
            if pgen is not None and si % 4 == 0:
                try:
                    next(pgen)
                except StopIteration:
                    pgen = None
        OM = [bn + "omla%d_%d" % (h, qb) for h in range(8) for qb in range(4)]
        if b == 0:
            dump("o_mlaT", o_mlaT, OM)
        if stop == 'attn' and b == 0:
            S.barrier(); S.finalize(); S.emit(); return nc
        S.barrier()
        A.release(m_mla)

        v_tm = A.alloc([128, NT, 512], BF16)
        gate = A.alloc([128, 16, 512], BF16)
        eb_tab = A.alloc([128, 4, 2, NT], F32)
        m_h0 = A.mark()
        w_hig = A.alloc([128, 8, 1024], BF16)
        gtmp = [A.alloc([128, 512], F32) for _ in range(2)]
        DMA("pool", w_hig[:, :, 0:512], w_in[:, :, 1952:2464], [], [bn + "w_hig_v"])
        DMA("pool", w_hig[:, :, 512:1024], w_in[:, :, 2464:2976], [], [bn + "w_hig_g"])
        for g in range(NT):
            k = g % 2
            MMG([(ps[k], hT[:, kc, g * 128:(g + 1) * 128], w_hig[:, kc, 0:512], kc == 0, kc == 7) for kc in range(8)],
                [bn + "w_hig_v"] + HK(g), [PS[k]])
            CP("dve" if g % 2 else "act", v_tm[:, g, :], ps[k], [PS[k]], [bn + "v_tm%d" % g])
            if g >= 2:
                MMG([(ps[2 + k], hT[:, kc, g * 128:(g + 1) * 128], w_hig[:, kc, 512:1024], kc == 0, kc == 7) for kc in range(8)],
                    [bn + "w_hig_g"] + HK(g), [PS[2 + k]])
                ACT(gtmp[k], ps[2 + k], AF.Silu, [PS[2 + k]], [bn + "gtmp%d" % k])
                TT("dve", gate[:, g - 2, :], gtmp[k], gn_bc, ALU.mult, [bn + "gtmp%d" % k] + GN, [bn + "gate%d" % (g - 2)])
        if b == 0:
            dump("v_tm", v_tm, [bn + "v_tm%d" % g for g in range(NT)])
            dump("gate", gate, [bn + "gate%d" % j for j in range(16)])
        if stop == 'hg0' and b == 0:
            S.barrier(); S.finalize(); S.emit(); return nc
        S.barrier()
        A.release(m_h0)

        NB_H = 9
        NSET = 4
        whb = [A.alloc([128, 8, 384], BF16) for _ in range(1)]
        qTs = [A.alloc([128, T], BF16) for _ in range(2)]
        kTs = [A.alloc([128, T], BF16) for _ in range(2)]
        khat = A.alloc([128, NT, 2, 128], BF16)
        S_st = A.alloc([128, 2, 17, 128], BF16)
        khT = [A.alloc([128, 256], BF16) for _ in range(2)]
        ktmp = [A.alloc([128, 256], BF16) for _ in range(1)]
        TS1 = [A.alloc([128, 256], F32) for _ in range(NSET)]
        TS2 = [A.alloc([128, 256], F32) for _ in range(NSET)]
        TS3 = [A.alloc([128, 256], F32) for _ in range(NSET)]
        TSq = [A.alloc([128, 256], F32) for _ in range(NSET)]
        TSz = [A.alloc([128, 256], BF16) for _ in range(NSET)]
        ATb = A.alloc([128, 4, 256], BF16)
        o_sb = A.alloc([128, 4, 128], F32)
        og = A.alloc([128, 4, 128], BF16)
        ssh = A.alloc([128, 4], F32)
        lnh = A.alloc([128, 4], F32)
        rsh = A.alloc([128, 4], F32)
        junk2 = A.alloc([128, 128], BF16)

        units = []
        for h in range(4):
            fo = [(k_, 0) for k_ in range(NB_H)]
            bo = [(k_, 1) for k_ in [0] + list(range(NB_H - 1, 0, -1))]
            seq = (bo + fo) if h % 2 == 0 else (fo + bo)
            for (k_, d_) in seq:
                units.append((h, k_, d_))
        NU = len(units)

        def hkeys(h):
            return bn + "h%d_" % h

        def load_wh(h):
            wh = whb[0]
            whk = bn + "wh0"
            for i, c0 in enumerate((416, 928, 1440)):
                DMA("pool", wh[:, :, i * 128:(i + 1) * 128], w_in[:, :, c0 + h * 128:c0 + (h + 1) * 128], [], [whk + "_%d" % i])

        def uinfo(ui):
            h, k, d = units[ui]
            return h, k, d, whb[0], [bn + "wh0_%d" % i for i in range(3)], hkeys(h), ui % NSET, k > 0

        def st0(ui):
            h, k, d, wh, WHK, hn, s, lat = uinfo(ui)
            tok = slice(k * 256, k * 256 + 256)
            hk = HKB(k * 256, 256)
            pz = ps[ui % 2]
            MMG([(pz[:, 0:256], wh[:, kc, (1 + d) * 128:(2 + d) * 128], hT[:, kc, tok], kc == 0, kc == 7) for kc in range(8)],
                WHK + hk, [PS[ui % 2]])
            if lat:
                pq = ps[2]
                MMG([(pq[:, 0:256], wh[:, kc, 0:128], hT[:, kc, tok], kc == 0, kc == 7) for kc in range(8)], WHK + hk, [PS[2]])

        def st1(ui):
            h, k, d, wh, WHK, hn, s, lat = uinfo(ui)
            sk = bn + "ts%d_" % s
            pz = ps[ui % 2]
            ACT(TS1[s], pz[:, 0:256], AF.Exp, [PS[ui % 2]], [sk + "T1"], scale=-1.0)
            ACT(TS2[s], TS1[s], AF.Ln, [sk + "T1", "lb_t"], [sk + "T2"], scale=lb_t[:, d * 4 + h:d * 4 + h + 1], bias=1.0)
            ACT(TS1[s], TS1[s], AF.Ln, [sk + "T1"], [sk + "T1"], bias=1.0)
            if lat:
                pq = ps[2]
                ACT(TSq[s], pq[:, 0:256], AF.Exp, [PS[2]], [sk + "Tq"], scale=-1.0)
                ACT(TSz[s], pq[:, 0:256], AF.Copy, [PS[2]], [sk + "Tz"])
                ACT(TSq[s], TSq[s], AF.Ln, [sk + "Tq"], [sk + "Tq"], bias=1.0)

        def st2(ui):
            h, k, d, wh, WHK, hn, s, lat = uinfo(ui)
            sk = bn + "ts%d_" % s
            TT("dve", TS2[s], TS2[s], TS1[s], ALU.subtract, [sk + "T1", sk + "T2"], [sk + "T2"])
            if d == 0:
                SCAN(TS3[s], rmask[:, 0, 0:256], TS2[s], ["rmask", sk + "T2"], [sk + "T3"])
            else:
                SCAN(TS3[s][:, ::-1], rmask[:, 1, 0:256][:, ::-1], TS2[s][:, ::-1], ["rmask", sk + "T2"], [sk + "T3"])
            if lat:
                TT("dve", TSq[s], TS3[s], TSq[s], ALU.subtract, [sk + "T3", sk + "Tq"], [sk + "Tq"])

        def st3(ui):
            h, k, d, wh, WHK, hn, s, lat = uinfo(ui)
            sk = bn + "ts%d_" % s
            ACT(TS1[s], TS2[s], AF.Exp, [sk + "T2"], [sk + "T1"])
            lastcol = 127 if d == 0 else 0
            ACT(eb_tab[:, h, d, 2 * k:2 * k + 2], TS3[s][:, lastcol:256:128], AF.Exp, [sk + "T3"], [hn + "eb%d_%d" % (d, k)])
            if lat:
                ACT(TSq[s], TSq[s], AF.Exp, [sk + "Tq"], [sk + "Tq"])
            ACT(TS3[s], TS3[s], AF.Exp, [sk + "T3"], [sk + "T3"], scale=-1.0)

        def st4(ui):
            h, k, d, wh, WHK, hn, s, lat = uinfo(ui)
            sk = bn + "ts%d_" % s
            if lat:
                lt = slice((k - 1) * 256, k * 256)
                STT("dve", qTs[d][:, lt], TSz[s], -1.0, TSq[s], ALU.mult, ALU.mult, [sk + "Tz", sk + "Tq"], [bn + "qT%d_%d" % (d, k)])
                kdst = kTs[d][:, lt]
                kkey = bn + "kT%d_%d" % (d, k)
            else:
                kdst = ktmp[0]
                kkey = bn + "ktmp0"
            STT("dve", kdst, TS1[s], 1.0, TS3[s], ALU.subtract, ALU.mult, [sk + "T1", sk + "T3"], [kkey])
            kh = khT[ui % 2]
            for c in range(2):
                TS("dve", kh[:, c * 128:(c + 1) * 128], kdst[:, c * 128:(c + 1) * 128], eb_tab[:, h, d, 2 * k + c:2 * k + c + 1], None,
                   ALU.mult, None, [kkey, hn + "eb%d_%d" % (d, k)], [bn + "khT%d_%d" % (ui % 2, c)])

        def st5(ui):
            h, k, d, wh, WHK, hn, s, lat = uinfo(ui)
            kh = khT[ui % 2]
            ptk = psb[4 + ui % 2]
            TRG([(ptk[:, c * 128:(c + 1) * 128], kh[:, c * 128:(c + 1) * 128]) for c in range(2)], ident,
                [bn + "khT%d_%d" % (ui % 2, c) for c in range(2)] + ["ident"], [PS[4 + ui % 2]])

        def st6(ui):
            h, k, d, wh, WHK, hn, s, lat = uinfo(ui)
            ptk = psb[4 + ui % 2]
            CP("act" if ui % 2 else "dve", khat[:, 2 * k:2 * k + 2, d, :], ptk[:, 0:256].rearrange("p (c k) -> p c k", c=2),
               [PS[4 + ui % 2]], [bn + "khat%d_%d" % (d, k)])

        def scan_mm(h, g, d, k):
            hn = hkeys(h)
            pS = ps[6][:, d * 128:(d + 1) * 128]
            MMG([(pS, khat[:, g, d, :], v_tm[:, g, h * 128:(h + 1) * 128], True, True)],
                [bn + "khat%d_%d" % (d, k), bn + "v_tm%d" % g], [PS[6]])

        def scan_upd(h, g, d, k, p):
            hn = hkeys(h)
            pS = ps[6][:, d * 128:(d + 1) * 128]
            if p == 0:
                CP("dve", S_st[:, d, 0, :], pS, [PS[6]], [bn + "S%d_%d" % (d, 0)])
            else:
                STT("dve", S_st[:, d, p, :], S_st[:, d, p - 1, :], eb_tab[:, h, d, g:g + 1], pS, ALU.mult, ALU.add,
                    [bn + "S%d_%d" % (d, p - 1), hn + "eb%d_%d" % (d, k), PS[6]], [bn + "S%d_%d" % (d, p)])

        def grp_piece(h, gq, step):
            hn = hkeys(h)
            pO = ps[7]

            def pa_mm(i):
                j = 4 * gq + i
                tl = slice(j * 128, (j + 1) * 128)
                kb = j // 2 + 1
                pA = ps[3][:, (i % 2) * 256:(i % 2) * 256 + 256]
                MMG([(pA[:, 0:128], kTs[0][:, tl], qTs[0][:, tl], True, True),
                     (pA[:, 128:256], kTs[1][:, tl], qTs[1][:, tl], True, True)],
                    [bn + "kT%d_%d" % (d, kb) for d in range(2)] + [bn + "qT%d_%d" % (d, kb) for d in range(2)], [PS[3]])

            def mask2(i0):
                TT("dve", ATb[:, i0:i0 + 2, :], ps[3].rearrange("p (i c) -> p i c", i=2), maskfb2, ALU.mult, [PS[3], "maskfb"],
                   [bn + "AT%d" % i0, bn + "AT%d" % (i0 + 1)])

            def po_mm(i):
                j = 4 * gq + i
                g = j + 2
                kb = j // 2 + 1
                tl = slice(j * 128, (j + 1) * 128)
                vv = v_tm[:, g, h * 128:(h + 1) * 128]
                MMG([(pO[:, i * 128:(i + 1) * 128], ATb[:, i, 0:128], vv, True, False),
                     (pO[:, i * 128:(i + 1) * 128], ATb[:, i, 128:256], vv, False, False),
                     (pO[:, i * 128:(i + 1) * 128], qTs[0][:, tl], S_st[:, 0, j + 1, :], False, False),
                     (pO[:, i * 128:(i + 1) * 128], qTs[1][:, tl], S_st[:, 1, 16 - j, :], False, True)],
                    [bn + "AT%d" % i, bn + "v_tm%d" % g, bn + "qT0_%d" % kb, bn + "qT1_%d" % kb,
                     bn + "S0_%d" % (j + 1), bn + "S1_%d" % (16 - j)], [PS[7]])

            if step == 1:
                pa_mm(0)
                pa_mm(1)
                mask2(0)
            elif step == 2:
                po_mm(0)
                po_mm(1)
                pa_mm(2)
                pa_mm(3)
                mask2(2)
            elif step == 3:
                po_mm(2)
                po_mm(3)
                CP("dve", o_sb, pO.rearrange("p (i c) -> p i c", i=4), [PS[7]], [bn + "o_sb"])
                for i in range(4):
                    ACT(junk2, o_sb[:, i, :], AF.Square, [bn + "o_sb"], [bn + "junk2", bn + "ssh%d" % i], accum=ssh[:, i:i + 1])
                rstd_chain(ssh, lnh, rsh, 1.0 / 128, [bn + "ssh%d" % i for i in range(4)] + ["epsc"], [bn + "lnh"], [bn + "rsh"])
            elif step == 4:
                for i in range(4):
                    j = 4 * gq + i
                    STT("dve", og[:, i, :], o_sb[:, i, :], rsh[:, i:i + 1], gate[:, j, h * 128:(h + 1) * 128],
                        ALU.mult, ALU.mult, [bn + "o_sb", bn + "rsh", bn + "gate%d" % j], [bn + "og%d" % i])
            elif step == 5:
                pT = psb[6][:, 512:1024]
                TRG([(pT[:, i * 128:(i + 1) * 128], og[:, i, :]) for i in range(4)], ident,
                    [bn + "og%d" % i for i in range(4)] + ["ident"], [PS[6]])
                CP("dve", o_hgT[:, h, gq * 512:(gq + 1) * 512], pT, [PS[6]], [bn + "ohg%d_%d" % (h, gq)])

        load_wh(0)
        stages = [st0, st1, st2, st3, st4, st5, st6]
        from collections import deque
        scan_q = [deque(), deque()]
        blk_scanned = {}
        grp_todo = deque((h, gq) for h in range(4) for gq in (range(4) if h % 2 == 0 else range(3, -1, -1)))
        front = None
        back = None
        grp_done = {}
        tau = 0
        tc = 0

        def chain_hazard(tc_):
            for si in (4, 6):
                ui = tc_ - si
                if 0 <= ui < NU:
                    h, k, d = units[ui]
                    if si == 4 and k >= 1:
                        for hp in range(h):
                            if grp_done.get((hp, (k - 1) // 2), 0) < 3:
                                return True
                    if si == 6:
                        for dq_ in scan_q:
                            for ent in dq_:
                                if ent[0] < h:
                                    return True
            return False

        def scan_hazard(h, d, p):
            j = (p - 1) if d == 0 else (16 - p)
            if 0 <= j <= 15:
                for hp in range(h):
                    if grp_done.get((hp, j // 4), 0) < 3:
                        return True
            return False

        while True:
            active = False
            if tc < NU + len(stages) and not chain_hazard(tc):
                for si in (1, 2, 3, 4, 5, 6, 0):
                    ui = tc - si
                    if 0 <= ui < NU:
                        stages[si](ui)
                        h, k, d = units[ui]
                        if si == 0 and ui + 1 < NU and units[ui + 1][0] != h:
                            load_wh(h + 1)
                        if si == 6:
                            tiles = [2 * k, 2 * k + 1] if d == 0 else [2 * k + 1, 2 * k]
                            todo = []
                            for g in tiles:
                                p = g if d == 0 else (1 - g if g < 2 else 19 - g)
                                if p <= 16:
                                    todo.append((g, p))
                            for n_, (g, p) in enumerate(todo):
                                scan_q[d].append((h, g, k, p, tau + 1, n_ == len(todo) - 1))
                            if not todo:
                                blk_scanned[(h, k, d)] = tau
                tc += 1
                active = True
            steps_now = []
            for d in range(2):
                if scan_q[d] and scan_q[d][0][4] <= tau and not scan_hazard(scan_q[d][0][0], d, scan_q[d][0][3]):
                    steps_now.append((d,) + scan_q[d].popleft())
            for (d, h, g, k, p, rt, last) in steps_now:
                scan_mm(h, g, d, k)
            for (d, h, g, k, p, rt, last) in steps_now:
                scan_upd(h, g, d, k, p)
                if last:
                    blk_scanned[(h, k, d)] = tau
                active = True
            if front is not None:
                h, gq, stp = front
                grp_piece(h, gq, stp)
                grp_done[(h, gq)] = stp
                front = (h, gq, stp + 1) if stp < 5 else None
                active = True
            pending_front = None
            if back is not None:
                h, gq, stp = back
                grp_piece(h, gq, stp)
                grp_done[(h, gq)] = stp
                active = True
                if stp == 3:
                    back = None
                    pending_front = (h, gq, 4)
                else:
                    back = (h, gq, stp + 1)
            if back is None and pending_front is None and grp_todo:
                h, gq = grp_todo[0]
                need = [(h, 2 * gq + 1, 0), (h, 2 * gq + 2, 0), (h, 2 * gq + 1, 1), (h, 2 * gq + 2, 1), (h, 0, 0), (h, 0, 1)]
                if all((kk in blk_scanned and blk_scanned[kk] < tau) for kk in need):
                    grp_todo.popleft()
                    back = (h, gq, 1)
            if pending_front is not None:
                assert front is None
                front = pending_front
            tau += 1
            if not active and not grp_todo and back is None and front is None and not scan_q[0] and not scan_q[1] and tc >= NU + len(stages):
                break
            assert tau < NU + 600, "side work did not drain"
        OH = [bn + "ohg%d_%d" % (h, gq) for h in range(4) for gq in range(4)]
        if b == 0:
            dump("o_hgT", o_hgT, OH)
        _saved_off = A.off
        A.release(m_fin)
        w_out_bf = A.alloc([128, 8, D], BF16)
        _wout_end = A.off
        A.off = _saved_off
        DMA("pool", w_out_bf, w_out, [], [bn + "w_out"] + [kk_ for g_ in range(NT) for kk_ in HK(g_)])
        S.barrier()

        if stop == 'hgrn' and b == 0:
            S.barrier(); S.finalize(); S.emit(); return nc
        A.release(m_fin)
        A.off = _wout_end
        identf = A.alloc([128, 128], F32)
        DMA("sp", identf, c_identf, [], ["identf"])
        x1b = [A.alloc([128, 4, D], F32) for _ in range(2)]
        h2Tb = [A.alloc([128, 8, 512], BF16) for _ in range(2)]
        uT = A.alloc([128, 32, 512], BF16)
        xt2 = [A.alloc([128, D], F32) for _ in range(1)]
        xs2b = [A.alloc([128, D], BF16) for _ in range(2)]
        w1p = [A.alloc([128, 8, 512], BF16) for _ in range(2)]
        w2p = [A.alloc([128, 32, 128], BF16) for _ in range(2)]
        rbuf = [A.alloc([128, 512], F32) for _ in range(2)]
        yT = [A.alloc([128, 512], F32) for _ in range(2)]
        ot = [A.alloc([128, D], F32) for _ in range(1)]
        ss2 = A.alloc([128, 16], F32)
        ln2 = A.alloc([128, 16], F32)
        rs2 = A.alloc([128, 16], F32)
        ss3 = A.alloc([128, 16], F32)
        ln3 = A.alloc([128, 16], F32)
        rs3 = A.alloc([128, 16], F32)
        fn_ = bn + "f_"

        def X1K(jb, i):
            return fn_ + "x1_%d_%d" % (jb % 2, i)

        def stage_a(jb, i):
            x1 = x1b[jb % 2]
            j = jb * 4 + i
            tl = slice(j * 128, (j + 1) * 128)
            DMA("sp", xt2[0], x[b, tl, :], [], [fn_ + "xt0"])
            for half in range(2):
                MMG([(ps[half], (o_mlaT[:, c, tl] if c < 4 else o_hgT[:, c - 4, tl]), w_out_bf[:, c, half * 512:(half + 1) * 512],
                      c == 0, c == 7) for c in range(8)], OM + OH + [bn + "w_out"], [PS[half]])
                TT("dve", x1[:, i, half * 512:(half + 1) * 512], ps[half], G1[:, b, half * 512:(half + 1) * 512], ALU.mult,
                   [PS[half], "G1_%d_%d" % (b, half)], [fn_ + "x1h_%d_%d_%d" % (jb % 2, i, half), X1K(jb, i)])
            TT("dve", x1[:, i, :], x1[:, i, :], xt2[0], ALU.add,
               [fn_ + "x1h_%d_%d_0" % (jb % 2, i), fn_ + "x1h_%d_%d_1" % (jb % 2, i), fn_ + "xt0"], [X1K(jb, i)])

        def stage_b1(jb, i):
            x1 = x1b[jb % 2]
            j = jb * 4 + i
            xs2k = xs2b[i % 2]
            xsk = fn_ + "xs2_%d" % (i % 2)
            ACT(xs2k, x1[:, i, :], AF.Square, [X1K(jb, i)], [xsk, fn_ + "ss2_%d" % j], accum=ss2[:, j:j + 1])
            rstd_chain(ss2[:, j:j + 1], ln2[:, j:j + 1], rs2[:, j:j + 1], 1.0 / D, [fn_ + "ss2_%d" % j, "epsc"],
                       [fn_ + "ln2_%d" % j], [fn_ + "rs2_%d" % j])
            TS("dve", xs2k, x1[:, i, :], rs2[:, j:j + 1], None, ALU.mult, None, [X1K(jb, i), fn_ + "rs2_%d" % j], [xsk])

        def stage_b2(jb, i):
            h2T = h2Tb[jb % 2]
            xs2k = xs2b[i % 2]
            xsk = fn_ + "xs2_%d" % (i % 2)
            pt = psb[2]
            TRG([(pt[:, c * 128:(c + 1) * 128], xs2k[:, c * 128:(c + 1) * 128]) for c in range(8)], ident,
                [xsk, "ident"], [PS[2]])
            for c in range(8):
                dst = h2T[:, c, i * 128:(i + 1) * 128]
                hk_ = fn_ + "h2T%d_%d_%d" % (jb % 2, i, c)
                if i % 2 == 0:
                    ACT(dst, pt[:, c * 128:(c + 1) * 128], AF.Identity, [PS[2], "A2_%d" % b] + MODT2, [hk_],
                        scale=A2[:, b, c:c + 1], bias=modT[:, 24 + c, b:b + 1])
                else:
                    TS("dve", dst, pt[:, c * 128:(c + 1) * 128], A2[:, b, c:c + 1], modT[:, 24 + c, b:b + 1], ALU.mult, ALU.add,
                       [PS[2], "A2_%d" % b] + MODT2, [hk_])

        def prep_pieces(jb):
            return [
                [lambda: stage_a(jb, 0)],
                [lambda: stage_a(jb, 1)],
                [lambda: stage_b1(jb, 0)],
                [lambda: stage_a(jb, 2), lambda: stage_b2(jb, 0)],
                [lambda: stage_b1(jb, 1)],
                [lambda: stage_a(jb, 3), lambda: stage_b2(jb, 1)],
                [lambda: stage_b1(jb, 2)],
                [lambda: stage_b2(jb, 2), lambda: stage_b1(jb, 3)],
                [lambda: stage_b2(jb, 3)],
            ]

        otb = [ot[0], xt2[0]]
        otk = [fn_ + "ot0", fn_ + "xt0"]

        def final_norm(jb_, i):
            x1_ = x1b[jb_ % 2]
            j = jb_ * 4 + i
            xs2k = xs2b[i % 2]
            ACT(xs2k, x1_[:, i, :], AF.Square, [X1K(jb_, i)], [fn_ + "xs2_%d" % (i % 2), fn_ + "ss3_%d" % j], accum=ss3[:, j:j + 1])
            rstd_chain(ss3[:, j:j + 1], ln3[:, j:j + 1], rs3[:, j:j + 1], 1.0 / D, [fn_ + "ss3_%d" % j, "epsc"],
                       [fn_ + "ln3_%d" % j], [fn_ + "rs3_%d" % j])
            STT("dve", otb[i % 2], x1_[:, i, :], rs3[:, j:j + 1], fn_bc, ALU.mult, ALU.mult, [X1K(jb_, i), fn_ + "rs3_%d" % j, "fn_bc"],
                [otk[i % 2]])
            DMA("pool", out[b, j * 128:(j + 1) * 128, :], otb[i % 2], [otk[i % 2]], [fn_ + "out%d" % j])

        for grp in prep_pieces(0):
            for f_ in grp:
                f_()
        for jb in range(4):
            x1 = x1b[jb % 2]
            h2T = h2Tb[jb % 2]
            H2 = [fn_ + "h2T%d_%d_%d" % (jb % 2, i, c) for i in range(4) for c in range(8)]
            X1 = [X1K(jb, i) for i in range(4)]
            if b == 0 and jb == 0:
                dump("x1", x1, X1)
                dump("h2T", h2T, H2)
            for p in range(8):
                wp = w1p[p % 2]
                wk = fn_ + "w1p%d" % (p % 2)
                DMA("sp", wp, w1s[p].rearrange("q (k c) -> q k c", k=8), ["w1s%d" % p], [wk])
                for q4 in range(4):
                    jj = 4 * p + q4
                    pu = ps[3 + jj % 2]
                    MMG([(pu, wp[:, kc, q4 * 128:(q4 + 1) * 128], h2T[:, kc, :], kc == 0, kc == 7) for kc in range(8)],
                        [wk] + H2, [PS[3 + jj % 2]])
                    rb = rbuf[jj % 2]
                    ACT(rb, pu, AF.Relu, [PS[3 + jj % 2]], [fn_ + "rb%d" % (jj % 2)])
                    TT("dve", uT[:, jj, :], rb, rb, ALU.mult, [fn_ + "rb%d" % (jj % 2)], [fn_ + "uT%d" % jj])
                if jb >= 1 and p < 4:
                    final_norm(jb - 1, p)
            UT = [fn_ + "uT%d" % jj for jj in range(32)]
            if b == 0 and jb == 0:
                dump("uT", uT, UT)

            def mlp2_tail(dq):
                yk = fn_ + "yT%d" % (dq % 2)
                S.add("pe", (lambda src, dstp: (lambda e: [e.transpose(dstp[:, i * 128:(i + 1) * 128], src[:, i * 128:(i + 1) * 128], identf)
                                                         for i in range(4)][-1]))(yT[dq % 2], ps[7]),
                      [yk, "identf"], [PS[7]])
                xv = x1[:, :, dq * 128:(dq + 1) * 128]
                TT("dve", xv, xv, ps[7].rearrange("p (i c) -> p i c", i=4), ALU.add, [PS[7]] + X1, X1)

            nxt = prep_pieces(jb + 1) if jb + 1 < 4 else []
            for dq in range(8):
                wp = w2p[dq % 2]
                wk = fn_ + "w2p%d" % (dq % 2)
                DMA("sp", wp, w2s[dq].rearrange("q (j c) -> q j c", j=32), ["w2s%d" % dq], [wk])
                pv = ps[5 + dq % 2]
                MMG([(pv, wp[:, jj, :], uT[:, jj, :], jj == 0, jj == 31) for jj in range(32)], [wk] + UT, [PS[5 + dq % 2]])
                yk = fn_ + "yT%d" % (dq % 2)
                ACT(yT[dq % 2], pv, AF.Identity, [PS[5 + dq % 2]] + MODT2, [yk], scale=modT[:, 40 + dq, b:b + 1])
                if dq >= 1:
                    mlp2_tail(dq - 1)
                if nxt:
                    for f_ in nxt.pop(0):
                        f_()
            mlp2_tail(7)
            while nxt:
                for f_ in nxt.pop(0):
                    f_()
            if jb == 3:
                for i in range(4):
                    final_norm(jb, i)
        S.barrier()
        if stop == 'b0' and b == 0:
            S.barrier(); S.finalize(); S.emit(); return nc

    S.finalize()
    S.emit()
    return nc


def _consts():
    bf = ml_dtypes.bfloat16
    ident = np.eye(128, dtype=np.float32)
    s = np.arange(128)[:, None]
    t = np.arange(128)[None, :]
    mask = np.concatenate([(s <= t), (s >= t)], axis=1).astype(np.float32)
    rm = np.ones((128, 2, 512), np.float32)
    rm[:, 0, 0::128] = 0.0
    rm[:, 1, 127::128] = 0.0
    tok = np.arange(T)
    row = (tok // 64).astype(np.float32)
    col = (tok % 64).astype(np.float32)
    nfreq = 8
    inv = (np.float32(10000.0) ** (-np.arange(nfreq, dtype=np.float32) / np.float32(nfreq))).astype(np.float32)
    cs = np.zeros((128, 2, T), np.float32)
    for dmm in range(32):
        grp, i = dmm // 16, dmm % 16
        f = i % 8
        ang = (row if grp == 0 else col) * inv[f]
        c, sn = np.cos(ang.astype(np.float32)), np.sin(ang.astype(np.float32))
        sign = -1.0 if i < 8 else 1.0
        for q in range(4):
            cs[q * 32 + dmm, 0] = c
            cs[q * 32 + dmm, 1] = sign * sn
    return dict(c_ident=ident.astype(bf), c_identf=ident, c_mask=mask.astype(bf), c_rmask=rm.astype(bf), c_cs=cs.astype(bf))


def _swap_rope(w, base, period):
    w = w.copy()
    ncol = w.shape[-1]
    for h0 in range(0, ncol, period):
        r0 = h0 + base
        blk = w[..., r0:r0 + 32].copy()
        new = blk.copy()
        for grp in range(2):
            o = grp * 16
            new[..., o:o + 8] = blk[..., o + 8:o + 16]
            new[..., o + 8:o + 16] = blk[..., o:o + 8]
        w[..., r0:r0 + 32] = new
    return w


def _kp(w, nk):
    return np.ascontiguousarray(w.reshape(nk, 128, -1).transpose(1, 0, 2))


def _shared_inputs(inp):
    f = np.float32
    w_in = inp["w_in"][0]
    w_uq = inp["w_uq"][0]
    w_ukv = inp["w_ukv"][0].reshape(128, 8, 128)
    kr = np.zeros((1024, 96), f)
    kr[:, 64:96] = w_in[:, 384:416]
    kr = _swap_rope(kr, 64, 96)
    w1 = inp["w_mlp_in"][0]
    w1r = np.ascontiguousarray(w1.reshape(8, 128, 8, 512).transpose(2, 1, 0, 3)).reshape(8, 128, 4096)
    w2 = inp["w_mlp_out"][0]
    w2r = np.ascontiguousarray(w2.reshape(32, 128, 8, 128).transpose(2, 1, 0, 3)).reshape(8, 128, 4096)
    sh = dict(
        w_ada=_kp(inp["w_ada"][0], 8),
        b_adaT=np.ascontiguousarray(inp["b_ada"][0].reshape(48, 128).T),
        b_ada=np.ascontiguousarray(inp["b_ada"][0]),
        nmixT=np.ascontiguousarray(inp["norm_mix"][0].reshape(8, 128).T),
        nmlpT=np.ascontiguousarray(inp["norm_mlp"][0].reshape(8, 128).T),
        w_in=_kp(w_in, 8),
        w_krs=_kp(kr, 8),
        qnT=np.ascontiguousarray(inp["q_norm"][0].reshape(2, 128).T),
        kvnT=np.ascontiguousarray(inp["kv_norm"][0].reshape(128, 1)),
        w_uq=_kp(w_uq, 2),
        w_uqs=_kp(_swap_rope(w_uq, 64, 96), 2),
        w_kn=np.ascontiguousarray(w_ukv[:, :, 0:64].reshape(128, 512)),
        w_v=np.ascontiguousarray(w_ukv[:, :, 64:128].reshape(128, 512)),
        lbT=np.ascontiguousarray(inp["hgrn_lb"].reshape(2, 8, 128).transpose(2, 0, 1)),
        hgn=np.ascontiguousarray(inp["hgrn_norm"][0]),
        w_out=_kp(inp["w_out"][0], 8),
        w1=w1r, w2=w2r,
        fnorm=np.ascontiguousarray(inp["final_norm"]),
    )
    sh = {k: np.ascontiguousarray(v, dtype=f) for k, v in sh.items()}
    sh.update(_consts())
    return sh


_NC_CACHE = {}


def kernel(**inputs):
    inp = {k: np.asarray(v) for k, v in inputs.items()}
    shared = _shared_inputs(inp)
    in_maps = []
    for c in range(8):
        b0 = c * NB
        cv = np.stack([inp["c"][b0], inp["c"][b0 + 1], inp["c_ctx"]], axis=0)
        m = dict(shared)
        m["x"] = np.ascontiguousarray(inp["x"][b0:b0 + NB], dtype=np.float32)
        m["ctx"] = np.ascontiguousarray(inp["ctx"][b0:b0 + NB], dtype=np.float32)
        m["cvecT"] = np.ascontiguousarray(cv.reshape(3, 8, 128).transpose(2, 1, 0), dtype=np.float32)
        in_maps.append(m)
    if "nc" not in _NC_CACHE:
        _NC_CACHE["nc"] = build_program()
    res = run_bass_kernel_spmd(_NC_CACHE["nc"], in_maps, core_ids=list(range(8)))
    return np.concatenate([np.asarray(r["out"]) for r in res.results], axis=0).astype(np.float32)
```

```python
import numpy as np
import ml_dtypes
import concourse.bass as bass
import concourse.mybir as mybir
from concourse.bass_utils import run_bass_kernel_spmd

F32 = mybir.dt.float32
BF16 = mybir.dt.bfloat16
AF = mybir.ActivationFunctionType
ALU = mybir.AluOpType

NB = 2
T = 2048
L = 256
TE = T + L
D = 1024
NT = TE // 128
DFF = 4096
EPS = 1e-6
BLKS = [(0, 256)] + [(256 + 512 * j, 512) for j in range(4)]
ATT_SCALE = float(96 ** -0.5)
DMA_K = 8


class _Op:
    __slots__ = ("idx", "eng", "fn", "dma", "deps", "signal", "tick", "sem_i", "sem_v")

    def __init__(self, idx, eng, fn, dma, deps):
        self.idx, self.eng, self.fn, self.dma, self.deps = idx, eng, fn, dma, deps
        self.signal = False
        self.tick = 0
        self.sem_i = 0
        self.sem_v = 0


class Sched:
    ENGS = ("pe", "act", "dve", "pool", "sp")

    def __init__(self, nc):
        self.nc = nc
        self.ops = []
        self.last_w = {}
        self.readers = {}
        self.dma_ops = {"sp": [], "pool": [], "act": []}

    def add(self, eng, fn, r=(), w=(), dma=False):
        idx = len(self.ops)
        deps = {}
        for k in r:
            p = self.last_w.get(k)
            if p is not None:
                deps[p] = "raw"
        for k in w:
            p = self.last_w.get(k)
            if p is not None and p not in deps:
                deps[p] = "waw"
            for q in self.readers.get(k, ()):
                if q not in deps:
                    deps[q] = "war"
        if dma:
            lst = self.dma_ops[eng]
            if len(lst) >= DMA_K:
                deps[lst[-DMA_K]] = "raw"
            lst.append(idx)
        op = _Op(idx, eng, fn, dma, deps)
        for k in w:
            self.last_w[k] = idx
            self.readers[k] = []
        for k in r:
            self.readers.setdefault(k, []).append(idx)
        self.ops.append(op)
        return op

    def barrier(self):
        last = {}
        for op in self.ops:
            if not op.dma and op.fn is not None:
                last[op.eng] = op.idx
        dmas = [i for q in self.dma_ops.values() for i in q[-DMA_K:]]
        for e in self.ENGS:
            deps = {i: "raw" for ee, i in last.items()}
            for i in dmas:
                deps[i] = "raw"
            idx = len(self.ops)
            op = _Op(idx, e, None, False, deps)
            op.deps = {k: ("bar") for k in deps}
            self.ops.append(op)

    def finalize(self):
        ops = self.ops
        for q, lst in self.dma_ops.items():
            for n, i in enumerate(lst):
                ops[i].sem_i = n % DMA_K
                ops[i].sem_v = 16 * (n // DMA_K + 1)
        self.waits = {}
        for op in ops:
            best = {}
            dma_w = {}
            for p, kind in op.deps.items():
                po = ops[p]
                if po.dma:
                    key = (po.eng, po.sem_i)
                    dma_w[key] = max(dma_w.get(key, 0), po.sem_v)
                    continue
                if po.fn is None:
                    continue
                if po.eng == op.eng and not op.dma and kind != "bar":
                    if po.eng == "pe":
                        continue
                if p > best.get(po.eng, -1):
                    best[po.eng] = p
            for e, p in best.items():
                ops[p].signal = True
            self.waits[op.idx] = (best, dma_w)
        cnt = {e: 0 for e in self.ENGS}
        for op in ops:
            if op.signal:
                cnt[op.eng] += 1
                op.tick = cnt[op.eng]

    def emit(self, final_wait_eng="sp"):
        nc = self.nc
        ops = self.ops
        from contextlib import ExitStack
        with ExitStack() as st:
            esem = {e: st.enter_context(nc.semaphore("cs_" + e)) for e in self.ENGS}
            dsem = {q: [st.enter_context(nc.semaphore("ds_%s%d" % (q, i))) for i in range(DMA_K)]
                    for q in self.dma_ops}
            block = st.enter_context(nc.Block())
            per_eng = {e: [op for op in ops if op.eng == e] for e in self.ENGS}
            waits = self.waits

            def body(ename, eng):
                seen = {}
                for op in per_eng[ename]:
                    best, dma_w = waits[op.idx]
                    for pe_, p in best.items():
                        t = ops[p].tick
                        key = ("c", pe_)
                        if seen.get(key, 0) < t:
                            eng.wait_ge(esem[pe_], t)
                            seen[key] = t
                    for (q, si), v in dma_w.items():
                        key = ("d", q, si)
                        if seen.get(key, 0) < v:
                            eng.wait_ge(dsem[q][si], v)
                            seen[key] = v
                    if op.fn is None:
                        continue
                    inst = op.fn(eng)
                    if op.dma:
                        inst.then_inc(dsem[op.eng][op.sem_i], 16)
                    elif op.signal:
                        inst.then_inc(esem[ename], 1)
                if ename == final_wait_eng:
                    for q, lst in self.dma_ops.items():
                        for i in lst[-DMA_K:]:
                            eng.wait_ge(dsem[q][ops[i].sem_i], ops[i].sem_v)

            @block.tensor
            def _(e):
                body("pe", e)

            @block.scalar
            def _(e):
                body("act", e)

            @block.vector
            def _(e):
                body("dve", e)

            @block.gpsimd
            def _(e):
                body("pool", e)

            @block.sync
            def _(e):
                body("sp", e)


class Arena:
    def __init__(self, nc, lo=16896, hi=229376):
        self.nc, self.lo, self.hi, self.off = nc, lo, hi, lo
        self.n = 0

    def alloc(self, shape, dt):
        nbytes = int(np.prod(shape[1:])) * (2 if dt == BF16 else 4)
        nbytes = (nbytes + 63) // 64 * 64
        assert self.off + nbytes <= self.hi, ("SBUF overflow", self.off, nbytes)
        self.n += 1
        t = self.nc.alloc_sbuf_tensor_at("sb%d" % self.n, list(shape), dt, offset=self.off)
        self.off += nbytes
        return t.ap()

    def mark(self):
        return self.off

    def release(self, m):
        self.off = m


def build_program(dbg=None, stop=None):
    nc = bass.Bass("TRN2", target_bir_lowering=False)
    S = Sched(nc)
    A = Arena(nc)

    def din(name, shape, dt=F32):
        return nc.dram_tensor(name, list(shape), dt, kind="ExternalInput").ap()

    x = din("x", [NB, T, D])
    ctx = din("ctx", [NB, L, D])
    cvecT = din("cvecT", [128, 8, 3])
    w_ada = din("w_ada", [128, 8, 6144])
    b_adaT = din("b_adaT", [128, 48])
    b_ada = din("b_ada", [6144])
    nmixT = din("nmixT", [128, 8])
    nmlpT = din("nmlpT", [128, 8])
    w_in = din("w_in", [128, 8, 2976])
    w_krs = din("w_krs", [128, 8, 96])
    qnT = din("qnT", [128, 2])
    kvnT = din("kvnT", [128, 1])
    w_uq = din("w_uq", [128, 2, 768])
    w_uqs = din("w_uqs", [128, 2, 768])
    w_kn = din("w_kn", [128, 512])
    w_v = din("w_v", [128, 512])
    lbT = din("lbT", [128, 2, 8])
    hgn = din("hgn", [128])
    w_out = din("w_out", [128, 8, 1024])
    w1 = din("w1", [8, 128, 4096])
    w2 = din("w2", [8, 128, 4096])
    fnorm = din("fnorm", [D])
    c_ident = din("c_ident", [128, 128], BF16)
    c_identf = din("c_identf", [128, 128], F32)
    c_mask = din("c_mask", [128, 256], BF16)
    c_rmask = din("c_rmask", [128, 2, 512], BF16)
    c_cs = din("c_cs", [128, 2, T], BF16)
    out = nc.dram_tensor("out", [NB, T, D], F32, kind="ExternalOutput").ap()
    w1s = nc.dram_tensor("w1s", [8, 128, 4096], BF16).ap()
    w2s = nc.dram_tensor("w2s", [8, 128, 4096], BF16).ap()
    dbg_out = {}
    if dbg:
        for name, shape, dt in dbg:
            dbg_out[name] = nc.dram_tensor("dbg_" + name, list(shape), dt, kind="ExternalOutput").ap()

    ps = [nc.alloc_psum_tensor("ps%d" % i, [128, 512], F32).ap() for i in range(8)]
    psb = [p.bitcast(BF16) for p in ps]
    PS = ["ps%d" % i for i in range(8)]

    def ACT(out_, in_, func, r, w, scale=1.0, bias=0.0, accum=None):
        kw = {}
        if accum is not None:
            kw["accum_out"] = accum
        S.add("act", lambda e: e.activation(out=out_, in_=in_, func=func, bias=bias, scale=scale, **kw), r, w)

    def TT(eng, out_, a, b, op, r, w):
        S.add(eng, lambda e: e.tensor_tensor(out_, a, b, op), r, w)

    def TS(eng, out_, a, s1, s2, op0, op1, r, w):
        if s2 is None:
            S.add(eng, lambda e: e.tensor_scalar(out_, a, s1, None, op0), r, w)
        else:
            S.add(eng, lambda e: e.tensor_scalar(out_, a, s1, s2, op0, op1), r, w)

    def STT(eng, out_, a, sc, b, op0, op1, r, w):
        S.add(eng, lambda e: e.scalar_tensor_tensor(out_, a, sc, b, op0, op1), r, w)

    def CP(eng, out_, in_, r, w):
        if eng == "act":
            S.add("act", lambda e: e.copy(out_, in_), r, w)
        else:
            S.add(eng, lambda e: e.tensor_copy(out_, in_), r, w)

    def MSET(eng, ap, val, w):
        S.add(eng, lambda e: e.memset(ap, val), (), w)

    def MMG(lst, r, w):
        lst = list(lst)

        def fn(e):
            ins = None
            for (o, l, rr, st, sp) in lst:
                ins = e.matmul(o, lhsT=l, rhs=rr, start=st, stop=sp)
            return ins
        S.add("pe", fn, r, w)

    def TRG(lst, ident_ap, r, w):
        lst = list(lst)

        def fn(e):
            ins = None
            for (o, i_) in lst:
                ins = e.transpose(o, i_, ident_ap)
            return ins
        S.add("pe", fn, r, w)

    def DMA(q, out_, in_, r, w):
        S.add(q, lambda e: e.dma_start(out=out_, in_=in_), r, w, dma=True)

    def RECIP(out_, in_, r, w):
        S.add("dve", lambda e: e.reciprocal(out_, in_), r, w)

    def SCAN(out_, d0, d1, r, w):
        S.add("dve", lambda e: e.tensor_tensor_scan(out_, d0, d1, 0.0, ALU.mult, ALU.add), r, w)

    def dump(name, src_ap, r):
        if name in dbg_out:
            DMA("sp", dbg_out[name], src_ap, r, ["dbg_" + name])

    def rstd_chain(ss, tmp, rs, n_inv, r, w_tmp, w_rs):
        ACT(tmp, ss, AF.Ln, r, w_tmp, scale=n_inv, bias=epsc[:, 0:1])
        ACT(rs, tmp, AF.Exp, w_tmp, w_rs, scale=-0.5)

    ident = A.alloc([128, 128], BF16)
    ones_bf = A.alloc([128, 128], BF16)
    maskfb2 = A.alloc([128, 2, 256], BF16)
    rmask = A.alloc([128, 2, 512], BF16)
    epsc = A.alloc([128, 2], F32)
    fn_bc = A.alloc([128, D], F32)
    gn_bc = A.alloc([128, 512], F32)
    G1 = A.alloc([128, NB, D], F32)
    modT = A.alloc([128, 48, 3], F32)
    A1 = A.alloc([128, 3, 8], F32)
    A2 = A.alloc([128, 3, 8], F32)
    lb_t = A.alloc([128, 8], F32)
    w_uq_bf = A.alloc([128, 2, 768], BF16)
    w_uqs_bf = A.alloc([128, 2, 768], BF16)
    w_kn_bf = A.alloc([128, 512], BF16)
    w_v_bf = A.alloc([128, 512], BF16)

    DMA("sp", ident, c_ident, [], ["ident"])
    DMA("sp", maskfb2[:, 0, :], c_mask, [], ["maskfb"])
    DMA("sp", maskfb2[:, 1, :], c_mask, [], ["maskfb"])
    DMA("sp", rmask, c_rmask, [], ["rmask"])
    DMA("sp", fn_bc, fnorm.partition_broadcast(128), [], ["fn_bc"])
    for i in range(4):
        DMA("sp", gn_bc[:, i * 128:(i + 1) * 128], hgn.partition_broadcast(128), [], ["gn_bc%d" % i])
    GN = ["gn_bc%d" % i for i in range(4)]
    MSET("dve", ones_bf, 1.0, ["ones"])
    MSET("dve", epsc, EPS, ["epsc"])

    m_setup = A.mark()
    wa = A.alloc([128, 8, 6144], BF16)
    cT = A.alloc([128, 8, 3], F32)
    sT = A.alloc([128, 8, 3], BF16)
    sTb = A.alloc([128, NB, 8, 128], BF16)
    badaT = A.alloc([128, 48], F32)
    bada_g1 = A.alloc([128, D], F32)
    nmix_t = A.alloc([128, 8], F32)
    nmlp_t = A.alloc([128, 8], F32)
    lbraw = A.alloc([128, 2, 8], F32)
    lbtmp = A.alloc([128, 8], F32)
    qn_t = A.alloc([128, 2], F32)
    kvn_t = A.alloc([128, 1], F32)
    wst = A.alloc([128, 2, 768], F32)
    wst2 = A.alloc([128, 2, 768], F32)
    wst3 = A.alloc([128, 1024], F32)

    for kc in range(8):
        DMA("pool", wa[:, kc, 0:2048], w_ada[:, kc, 0:2048], [], ["wa%d_a" % kc])
    for kc in range(8):
        DMA("pool", wa[:, kc, 2048:6144], w_ada[:, kc, 2048:6144], [], ["wa%d_b" % kc])
    DMA("sp", cT, cvecT, [], ["cT"])
    DMA("sp", badaT, b_adaT, [], ["badaT"])
    DMA("sp", bada_g1, b_ada[2048:3072].partition_broadcast(128), [], ["bada_g1"])
    DMA("sp", nmix_t, nmixT, [], ["nmix"])
    DMA("sp", nmlp_t, nmlpT, [], ["nmlp"])
    DMA("sp", lbraw, lbT, [], ["lbraw"])
    DMA("sp", qn_t, qnT, [], ["qn"])
    DMA("sp", kvn_t, kvnT, [], ["kvn"])
    DMA("sp", wst, w_uq, [], ["wst"])
    DMA("sp", wst2, w_uqs, [], ["wst2"])
    DMA("sp", wst3[:, 0:512], w_kn, [], ["wst3a"])
    DMA("sp", wst3[:, 512:1024], w_v, [], ["wst3b"])

    TT("dve", lbtmp, lbraw[:, 1, :], lbraw[:, 0, :], ALU.subtract, ["lbraw"], ["lbtmp"])
    ACT(lbtmp, lbtmp, AF.Exp, ["lbtmp"], ["lbtmp"])
    TS("dve", lbtmp, lbtmp, 1.0, None, ALU.add, None, ["lbtmp"], ["lbtmp"])
    RECIP(lb_t, lbtmp, ["lbtmp"], ["lb_t"])
    for c in range(2):
        TS("dve", w_uq_bf[:, c, :], wst[:, c, :], qn_t[:, c:c + 1], None, ALU.mult, None, ["wst", "qn"], ["w_uq_bf%d" % c])
        TS("dve", w_uqs_bf[:, c, :], wst2[:, c, :], qn_t[:, c:c + 1], None, ALU.mult, None, ["wst2", "qn"], ["w_uqs_bf%d" % c])
    TS("dve", w_kn_bf, wst3[:, 0:512], kvn_t[:, 0:1], None, ALU.mult, None, ["wst3a", "kvn"], ["w_kn_bf"])
    TS("dve", w_v_bf, wst3[:, 512:1024], kvn_t[:, 0:1], None, ALU.mult, None, ["wst3b", "kvn"], ["w_v_bf"])
    WUQ = ["w_uq_bf0", "w_uq_bf1"]
    WUQS = ["w_uqs_bf0", "w_uqs_bf1"]

    ACT(sT, cT, AF.Silu, ["cT"], ["sT"])
    WAa = ["wa%d_a" % k for k in range(8)]
    WA = ["wa%d_b" % k for k in range(8)]
    psM = ps[0][:, 0:144]
    psM2 = ps[3][:, 0:144]
    MMG([(psM[:, j * 3:(j + 1) * 3], wa[:, kc, j * 128:(j + 1) * 128], sT[:, kc, :], kc == 0, kc == 7)
         for j in range(16) for kc in range(8)], WAa + ["sT"], [PS[0]])
    for b in range(3):
        TT("dve", modT[:, 0:16, b], ps[0][:, b:48:3], badaT[:, 0:16], ALU.add, [PS[0], "badaT"], ["modT%d" % b])
    MODT = ["modT0", "modT1", "modT2"]
    MMG([(psM2[:, j * 3:(j + 1) * 3], wa[:, kc, j * 128:(j + 1) * 128], sT[:, kc, :], kc == 0, kc == 7)
         for j in range(16, 48) for kc in range(8)], WA + ["sT"], [PS[3]])
    MODT2 = ["modTb0", "modTb1", "modTb2"]
    for b in range(3):
        TT("dve", modT[:, 16:48, b], ps[3][:, 48 + b:144:3], badaT[:, 16:48], ALU.add, [PS[3], "badaT"], ["modTb%d" % b])
    for b in range(3):
        STT("dve", A1[:, b, :], modT[:, 8:16, b], 1.0, nmix_t, ALU.add, ALU.mult, MODT + ["nmix"], ["A1_%d" % b])
        STT("dve", A2[:, b, :], modT[:, 32:40, b], 1.0, nmlp_t, ALU.add, ALU.mult, MODT2 + ["nmlp"], ["A2_%d" % b])
    for b in range(NB):
        for kc in range(8):
            CP("dve", sTb[:, b, kc, :], sT[:, kc, b:b + 1].to_broadcast([128, 128]), ["sT"], ["sTb%d_%d" % (b, kc)])
        for half in range(2):
            pg = ps[1 + half]
            MMG([(pg, sTb[:, b, kc, :], wa[:, kc, 2048 + half * 512:2048 + (half + 1) * 512], kc == 0, kc == 7)
                 for kc in range(8)], WA + ["sTb%d_%d" % (b, kc) for kc in range(8)], [PS[1 + half]])
            TT("dve", G1[:, b, half * 512:(half + 1) * 512], pg, bada_g1[:, half * 512:(half + 1) * 512], ALU.add,
               [PS[1 + half], "bada_g1"], ["G1_%d_%d" % (b, half)])
    dump("modT", modT, MODT + MODT2)
    dump("G1", G1, ["G1_%d_%d" % (b, h) for b in range(NB) for h in range(2)])
    dump("lb", lb_t, ["lb_t"])
    S.barrier()
    if stop == 'setup':
        S.finalize(); S.emit(); return nc
    A.release(m_setup)

    o_mlaT = A.alloc([128, 4, T], BF16)
    o_hgT = A.alloc([128, 4, T], BF16)
    m_fin = A.mark()
    hT = A.alloc([128, 8, TE], BF16)
    m_batch = A.mark()

    for b in range(NB):
        bn = "b%d_" % b

        def HK(g):
            return [bn + "hT%d_%d" % (g, c) for c in range(8)]

        def HKB(t0, n):
            r = []
            for g in range(t0 // 128, (t0 + n) // 128):
                r += HK(g)
            return r

        A.release(m_batch)
        cs = A.alloc([128, 2, T], BF16)
        w_mla = A.alloc([128, 8, 416], BF16)
        w_krs_bf = A.alloc([128, 8, 96], BF16)
        DMA("sp", cs, c_cs, [], [bn + "cs"])
        DMA("pool", w_mla, w_in[:, :, 0:416], [], [bn + "w_mla"])
        DMA("pool", w_krs_bf, w_krs, [], [bn + "w_krs"])
        m_pa = A.mark()
        xsb = [A.alloc([128, D], BF16) for _ in range(2)]
        junk = A.alloc([128, D], BF16)
        ssA = A.alloc([128, NT], F32)
        lnA = A.alloc([128, NT], F32)
        rsA = A.alloc([128, NT], F32)
        NXB = 3
        xtb = [A.alloc([128, D], F32) for _ in range(NXB)]

        def pa1(g):
            src = ctx[b, g * 128:(g + 1) * 128, :] if g < 2 else x[b, (g - 2) * 128:(g - 1) * 128, :]
            k = g % NXB
            xk = bn + "xt%d" % k
            DMA("sp", xtb[k], src, [], [xk])
            ACT(junk, xtb[k], AF.Square, [xk], [bn + "junk", bn + "ssA%d" % g], accum=ssA[:, g:g + 1])
            rstd_chain(ssA[:, g:g + 1], lnA[:, g:g + 1], rsA[:, g:g + 1], 1.0 / D,
                       [bn + "ssA%d" % g, "epsc"], [bn + "lnA%d" % g], [bn + "rsA%d" % g])

        def pa2(g):
            k = g % NXB
            xk, sk = bn + "xt%d" % k, bn + "xs%d" % (g % 2)
            TS("dve", xsb[g % 2], xtb[k], rsA[:, g:g + 1], None, ALU.mult, None, [xk, bn + "rsA%d" % g], [sk])
            pt = psb[g % 2]
            TRG([(pt[:, c * 128:(c + 1) * 128], xsb[g % 2][:, c * 128:(c + 1) * 128]) for c in range(8)], ident,
                [sk, "ident"], [PS[g % 2]])

        def pa3(g):
            bb = 2 if g < 2 else b
            pt = psb[g % 2]
            for c in range(8):
                dst = hT[:, c, g * 128:(g + 1) * 128]
                if g % 4 == 0:
                    ACT(dst, pt[:, c * 128:(c + 1) * 128], AF.Identity, [PS[g % 2], "A1_%d" % bb] + MODT, [bn + "hT%d_%d" % (g, c)],
                        scale=A1[:, bb, c:c + 1], bias=modT[:, c, bb:bb + 1])
                else:
                    TS("dve", dst, pt[:, c * 128:(c + 1) * 128], A1[:, bb, c:c + 1], modT[:, c, bb:bb + 1], ALU.mult, ALU.add,
                       [PS[g % 2], "A1_%d" % bb] + MODT, [bn + "hT%d_%d" % (g, c)])

        for tt_ in range(NT + 2):
            if tt_ - 2 >= 0:
                pa3(tt_ - 2)
            if 0 <= tt_ - 1 < NT:
                pa2(tt_ - 1)
            if tt_ < NT:
                pa1(tt_)
        if b == 0:
            dump("hT", hT, [k for g in range(NT) for k in HK(g)])

        if stop == 'A' and b == 0:
            S.barrier(); S.finalize(); S.emit(); return nc
        S.barrier()
        A.release(m_pa)
        m_mla = m_batch
        cqnT = A.alloc([128, 2, TE], BF16)
        ckvnT = A.alloc([128, TE], BF16)
        krotT = A.alloc([128, TE], BF16)
        Vaug = A.alloc([128, NT, 8, 128], BF16)
        KTb = [A.alloc([128, TE], BF16) for _ in range(2)]
        QTb = [A.alloc([128, T], BF16) for _ in range(2)]
        PTb = [A.alloc([128, 512], BF16) for _ in range(3)]
        sq = A.alloc([128, 3, 512], BF16)
        tA = A.alloc([128, 512], F32)
        tB = A.alloc([128, 512], F32)
        rq_bc = A.alloc([128, 512], F32)
        rkv_bc = A.alloc([128, 512], F32)
        rp1 = [A.alloc([128, 512], F32) for _ in range(1)]
        rp2 = [A.alloc([128, 512], F32) for _ in range(1)]
        rden = [A.alloc([128, 512], F32) for _ in range(2)]

        MSET("dve", Vaug, 1.0, [bn + "Vp%d" % g for g in range(NT)])
        WM = [bn + "w_mla"]

        v_defer = []
        for bi, (t0, n) in enumerate(BLKS):
            hk = HKB(t0, n)
            tok = slice(t0, t0 + n)
            for c in range(2):
                MMG([(ps[c][:, 0:n], w_mla[:, kc, c * 128:(c + 1) * 128], hT[:, kc, tok], kc == 0, kc == 7) for kc in range(8)],
                    WM + hk, [PS[c]])
            MMG([(ps[2][:, 0:n], w_mla[:, kc, 256:384], hT[:, kc, tok], kc == 0, kc == 7) for kc in range(8)], WM + hk, [PS[2]])
            MMG([(ps[3][0:96, 0:n], w_mla[:, kc, 320:416], hT[:, kc, tok], kc == 0, kc == 7) for kc in range(8)], WM + hk, [PS[3]])
            if bi > 0:
                MMG([(ps[4][0:96, 0:n], w_krs_bf[:, kc, :], hT[:, kc, tok], kc == 0, kc == 7) for kc in range(8)],
                    [bn + "w_krs"] + hk, [PS[4]])
            while v_defer:
                v_defer.pop(0)()
            for c in range(3):
                ACT(sq[:, c, 0:n], ps[c][:, 0:n], AF.Square, [PS[c]], [bn + "sq%d" % c])
            MMG([(ps[5][:, 0:n], ones_bf, sq[:, 0, 0:n], True, False), (ps[5][:, 0:n], ones_bf, sq[:, 1, 0:n], False, True)],
                ["ones", bn + "sq0", bn + "sq1"], [PS[5]])
            MMG([(ps[6][:, 0:n], ones_bf, sq[:, 2, 0:n], True, True)], ["ones", bn + "sq2"], [PS[6]])
            rstd_chain(ps[5][:, 0:n], tA[:, 0:n], rq_bc[:, 0:n], 1.0 / 256, [PS[5], "epsc"], [bn + "tA"], [bn + "rq_bc"])
            rstd_chain(ps[6][:, 0:n], tB[:, 0:n], rkv_bc[:, 0:n], 1.0 / 128, [PS[6], "epsc"], [bn + "tB"], [bn + "rkv_bc"])
            for c in range(2):
                TT("dve", cqnT[:, c, tok], ps[c][:, 0:n], rq_bc[:, 0:n], ALU.mult, [PS[c], bn + "rq_bc"], [bn + "cqnT%d_%d" % (bi, c)])
            TT("dve", ckvnT[:, tok], ps[2][:, 0:n], rkv_bc[:, 0:n], ALU.mult, [PS[2], bn + "rkv_bc"], [bn + "ckvnT%d" % bi])
            if bi == 0:
                CP("dve", krotT[64:96, tok], ps[3][64:96, 0:n], [PS[3]], [bn + "krotT%d" % bi])
            else:
                lt = slice(t0 - L, t0 - L + n)
                TT("dve", rp1[0][64:96, 0:n], ps[3][64:96, 0:n], cs[64:96, 0, lt], ALU.mult, [PS[3], bn + "cs"], [bn + "rp1_0"])
                TT("dve", rp2[0][64:96, 0:n], ps[4][64:96, 0:n], cs[64:96, 1, lt], ALU.mult, [PS[4], bn + "cs"], [bn + "rp2_0"])
                TT("dve", krotT[64:96, tok], rp1[0][64:96, 0:n], rp2[0][64:96, 0:n], ALU.add, [bn + "rp1_0", bn + "rp2_0"],
                   [bn + "krotT%d" % bi])
            def v_part(bi=bi, t0=t0, n=n):
                for g in range(t0 // 128, (t0 + n) // 128):
                    MMG([(ps[7], ckvnT[:, g * 128:(g + 1) * 128], w_v_bf, True, True)], [bn + "ckvnT%d" % bi, "w_v_bf"], [PS[7]])
                    pv4 = ps[7].rearrange("p (j e c) -> p j e c", j=4, e=2)
                    veng = "act" if g % 2 else "dve"
                    CP(veng, Vaug[:, g, 0::2, 0:64], pv4[:, :, 0, :], [PS[7]], [bn + "Vp%d" % g])
                    CP(veng, Vaug[:, g, 1::2, 64:128], pv4[:, :, 1, :], [PS[7]], [bn + "Vp%d" % g])
            v_defer.append(v_part)
        while v_defer:
            v_defer.pop(0)()
        CQ = [bn + "cqnT%d_%d" % (bi, c) for bi in range(5) for c in range(2)]
        CKV = [bn + "ckvnT%d" % bi for bi in range(5)]
        KROT = [bn + "krotT%d" % bi for bi in range(5)]
        if b == 0:
            dump("cqnT", cqnT, CQ)
            dump("ckvnT", ckvnT, CKV)
            dump("krotT", krotT[64:96, :], KROT)

        if stop == 'mlaproj' and b == 0:
            S.barrier(); S.finalize(); S.emit(); return nc
        def proj_head(h):
            kb = h % 2
            KT, QT = KTb[kb], QTb[kb]
            kk, qk = bn + "KT%d" % kb, bn + "QT%d" % kb
            pc = 0
            for bi, (t0, n) in enumerate(BLKS):
                tok = slice(t0, t0 + n)
                pb, pk = ps[7 - pc % 2], PS[7 - pc % 2]
                pc += 1
                MMG([(pb[0:64, 0:n], w_kn_bf[:, h * 64:(h + 1) * 64], ckvnT[:, tok], True, True)],
                    ["w_kn_bf", bn + "ckvnT%d" % bi], [pk])
                CP("dve", KT[0:64, tok], pb[0:64, 0:n], [pk], [kk + "_n%d" % bi])
                yield
            CP("dve", KT[64:96, :], krotT[64:96, :], KROT, [kk + "_r"])
            for j in range(4):
                q0 = j * 512
                et = slice(L + q0, L + q0 + 512)
                lt = slice(q0, q0 + 512)
                cqk = [bn + "cqnT%d_%d" % (j + 1, c) for c in range(2)]
                pb, pk = ps[7 - pc % 2], PS[7 - pc % 2]
                pc += 1
                MMG([(pb[0:96, :], w_uq_bf[:, c, h * 96:(h + 1) * 96], cqnT[:, c, et], c == 0, c == 1) for c in range(2)],
                    WUQ + cqk, [pk])
                CP("dve", QT[0:64, lt], pb[0:64, :], [pk], [qk + "_n%d" % j])
                TT("dve", rp1[0][64:96, :], pb[64:96, :], cs[64:96, 0, lt], ALU.mult, [pk, bn + "cs"], [bn + "rp1_0"])
                yield
                pb, pk = ps[7 - pc % 2], PS[7 - pc % 2]
                pc += 1
                MMG([(pb[0:96, :], w_uqs_bf[:, c, h * 96:(h + 1) * 96], cqnT[:, c, et], c == 0, c == 1) for c in range(2)],
                    WUQS + cqk, [pk])
                TT("dve", rp2[0][64:96, :], pb[64:96, :], cs[64:96, 1, lt], ALU.mult, [pk, bn + "cs"], [bn + "rp2_0"])
                TT("dve", QT[64:96, lt], rp1[0][64:96, :], rp2[0][64:96, :], ALU.add, [bn + "rp1_0", bn + "rp2_0"], [qk + "_r%d" % j])
                yield

        def KTK(h):
            kk = bn + "KT%d" % (h % 2)
            return [kk + "_n%d" % bi for bi in range(5)] + [kk + "_r"]

        def QTK(h, j):
            qk = bn + "QT%d" % (h % 2)
            return [qk + "_n%d" % j, qk + "_r%d" % j]

        steps = [(h, qb, kt) for h in range(8) for qb in range(4) for kt in range(NT)]
        for _ in proj_head(0):
            pass
        pgen = None
        if b == 0:
            dump("KT0", KTb[0][0:96, :], KTK(0))
            dump("QT0", QTb[0][0:96, :], [k for j in range(4) for k in QTK(0, j)])

        SB = [0, 1, 2, 5]

        def reg_qk(si):
            h, qb, kt = steps[si]
            KT, QT = KTb[h % 2], QTb[h % 2]
            sb_ = SB[si % 4]
            MMG([(ps[sb_], KT[0:96, kt * 128:(kt + 1) * 128], QT[0:96, qb * 512:(qb + 1) * 512], True, True)],
                KTK(h) + QTK(h, qb), [PS[sb_]])

        reg_qk(0)
        reg_qk(1)
        reg_qk(2)
        for si, (h, qb, kt) in enumerate(steps):
            u = (h * 4 + qb) % 2
            pO = ps[3 + u]
            sb_ = SB[si % 4]
            ACT(PTb[si % 3], ps[sb_], AF.Exp, [PS[sb_]], [bn + "PT%d" % (si % 3)] + (["convgate"] if (b == 0 and si == 0) else []),
                scale=ATT_SCALE)
            if si + 3 < len(steps):
                reg_qk(si + 3)
            MMG([(pO, Vaug[:, kt, h, :], PTb[si % 3], kt == 0, kt == NT - 1)],
                [bn + "Vp%d" % kt, bn + "PT%d" % (si % 3)], [PS[3 + u]])
            if kt == NT - 1:
                orow = slice((h % 2) * 64, (h % 2) * 64 + 64)
                drow = slice(64 - (h % 2) * 64, 128 - (h % 2) * 64)
                RECIP(rden[u][orow, :], pO[drow, :], [PS[3 + u]], [bn + "rden%d" % u])
                TT("dve", o_mlaT[orow, h // 2, qb * 512:(qb + 1) * 512], pO[orow, :], rden[u][orow, :], ALU.mult,
                   [PS[3 + u], bn + "rden%d" % u], [bn + "omla%d_%d" % (h, qb)])
            if b == 0 and kt == 0 and qb == 0 and h == 0:
                for p in range(8):
                    DMA("pool", w1s[p], w1[p], ["convgate"], ["w1s%d" % p])
                for p in range(8):
                    DMA("pool", w2s[p], w2[p], [], ["w2s%d" % p])
            if qb == 0 and kt == 2 and h + 1 < 8:
                pgen = proj_head(h + 1)
            if pgen is not None and si % 4 == 0:
                try:
                    next(pgen)
                except StopIteration:
                    pgen = None
        OM = [bn + "omla%d_%d" % (h, qb) for h in range(8) for qb in range(4)]
        if b == 0:
            dump("o_mlaT", o_mlaT, OM)
        if stop == 'attn' and b == 0:
            S.barrier(); S.finalize(); S.emit(); return nc
        S.barrier()
        A.release(m_mla)

        v_tm = A.alloc([128, NT, 512], BF16)
        gate = A.alloc([128, 16, 512], BF16)
        eb_tab = A.alloc([128, 4, 2, NT], F32)
        m_h0 = A.mark()
        w_hig = A.alloc([128, 8, 1024], BF16)
        gtmp = [A.alloc([128, 512], F32) for _ in range(2)]
        DMA("pool", w_hig[:, :, 0:512], w_in[:, :, 1952:2464], [], [bn + "w_hig_v"])
        DMA("pool", w_hig[:, :, 512:1024], w_in[:, :, 2464:2976], [], [bn + "w_hig_g"])
        for g in range(NT):
            k = g % 2
            MMG([(ps[k], hT[:, kc, g * 128:(g + 1) * 128], w_hig[:, kc, 0:512], kc == 0, kc == 7) for kc in range(8)],
                [bn + "w_hig_v"] + HK(g), [PS[k]])
            CP("dve" if g % 2 else "act", v_tm[:, g, :], ps[k], [PS[k]], [bn + "v_tm%d" % g])
            if g >= 2:
                MMG([(ps[2 + k], hT[:, kc, g * 128:(g + 1) * 128], w_hig[:, kc, 512:1024], kc == 0, kc == 7) for kc in range(8)],
                    [bn + "w_hig_g"] + HK(g), [PS[2 + k]])
                ACT(gtmp[k], ps[2 + k], AF.Silu, [PS[2 + k]], [bn + "gtmp%d" % k])
                TT("dve", gate[:, g - 2, :], gtmp[k], gn_bc, ALU.mult, [bn + "gtmp%d" % k] + GN, [bn + "gate%d" % (g - 2)])
        if b == 0:
            dump("v_tm", v_tm, [bn + "v_tm%d" % g for g in range(NT)])
            dump("gate", gate, [bn + "gate%d" % j for j in range(16)])
        if stop == 'hg0' and b == 0:
            S.barrier(); S.finalize(); S.emit(); return nc
        S.barrier()
        A.release(m_h0)

        NB_H = 9
        NSET = 4
        whb = [A.alloc([128, 8, 384], BF16) for _ in range(1)]
        qTs = [A.alloc([128, T], BF16) for _ in range(2)]
        kTs = [A.alloc([128, T], BF16) for _ in range(2)]
        khat = A.alloc([128, NT, 2, 128], BF16)
        S_st = A.alloc([128, 2, 17, 128], BF16)
        khT = [A.alloc([128, 256], BF16) for _ in range(2)]
        ktmp = [A.alloc([128, 256], BF16) for _ in range(1)]
        TS1 = [A.alloc([128, 256], F32) for _ in range(NSET)]
        TS2 = [A.alloc([128, 256], F32) for _ in range(NSET)]
        TS3 = [A.alloc([128, 256], F32) for _ in range(NSET)]
        TSq = [A.alloc([128, 256], F32) for _ in range(NSET)]
        TSz = [A.alloc([128, 256], BF16) for _ in range(NSET)]
        ATb = A.alloc([128, 4, 256], BF16)
        o_sb = A.alloc([128, 4, 128], F32)
        og = A.alloc([128, 4, 128], BF16)
        ssh = A.alloc([128, 4], F32)
        lnh = A.alloc([128, 4], F32)
        rsh = A.alloc([128, 4], F32)
        junk2 = A.alloc([128, 128], BF16)

        units = []
        for h in range(4):
            fo = [(k_, 0) for k_ in range(NB_H)]
            bo = [(k_, 1) for k_ in [0] + list(range(NB_H - 1, 0, -1))]
            seq = (bo + fo) if h % 2 == 0 else (fo + bo)
            for (k_, d_) in seq:
                units.append((h, k_, d_))
        NU = len(units)

        def hkeys(h):
            return bn + "h%d_" % h

        def load_wh(h):
            wh = whb[0]
            whk = bn + "wh0"
            for i, c0 in enumerate((416, 928, 1440)):
                DMA("pool", wh[:, :, i * 128:(i + 1) * 128], w_in[:, :, c0 + h * 128:c0 + (h + 1) * 128], [], [whk + "_%d" % i])

        def uinfo(ui):
            h, k, d = units[ui]
            return h, k, d, whb[0], [bn + "wh0_%d" % i for i in range(3)], hkeys(h), ui % NSET, k > 0

        def st0(ui):
            h, k, d, wh, WHK, hn, s, lat = uinfo(ui)
            tok = slice(k * 256, k * 256 + 256)
            hk = HKB(k * 256, 256)
            pz = ps[ui % 2]
            MMG([(pz[:, 0:256], wh[:, kc, (1 + d) * 128:(2 + d) * 128], hT[:, kc, tok], kc == 0, kc == 7) for kc in range(8)],
                WHK + hk, [PS[ui % 2]])
            if lat:
                pq = ps[2]
                MMG([(pq[:, 0:256], wh[:, kc, 0:128], hT[:, kc, tok], kc == 0, kc == 7) for kc in range(8)], WHK + hk, [PS[2]])

        def st1(ui):
            h, k, d, wh, WHK, hn, s, lat = uinfo(ui)
            sk = bn + "ts%d_" % s
            pz = ps[ui % 2]
            ACT(TS1[s], pz[:, 0:256], AF.Exp, [PS[ui % 2]], [sk + "T1"], scale=-1.0)
            ACT(TS2[s], TS1[s], AF.Ln, [sk + "T1", "lb_t"], [sk + "T2"], scale=lb_t[:, d * 4 + h:d * 4 + h + 1], bias=1.0)
            ACT(TS1[s], TS1[s], AF.Ln, [sk + "T1"], [sk + "T1"], bias=1.0)
            if lat:
                pq = ps[2]
                ACT(TSq[s], pq[:, 0:256], AF.Exp, [PS[2]], [sk + "Tq"], scale=-1.0)
                ACT(TSz[s], pq[:, 0:256], AF.Copy, [PS[2]], [sk + "Tz"])
                ACT(TSq[s], TSq[s], AF.Ln, [sk + "Tq"], [sk + "Tq"], bias=1.0)

        def st2(ui):
            h, k, d, wh, WHK, hn, s, lat = uinfo(ui)
            sk = bn + "ts%d_" % s
            TT("dve", TS2[s], TS2[s], TS1[s], ALU.subtract, [sk + "T1", sk + "T2"], [sk + "T2"])
            if d == 0:
                SCAN(TS3[s], rmask[:, 0, 0:256], TS2[s], ["rmask", sk + "T2"], [sk + "T3"])
            else:
                SCAN(TS3[s][:, ::-1], rmask[:, 1, 0:256][:, ::-1], TS2[s][:, ::-1], ["rmask", sk + "T2"], [sk + "T3"])
            if lat:
                TT("dve", TSq[s], TS3[s], TSq[s], ALU.subtract, [sk + "T3", sk + "Tq"], [sk + "Tq"])

        def st3(ui):
            h, k, d, wh, WHK, hn, s, lat = uinfo(ui)
            sk = bn + "ts%d_" % s
            ACT(TS1[s], TS2[s], AF.Exp, [sk + "T2"], [sk + "T1"])
            lastcol = 127 if d == 0 else 0
            ACT(eb_tab[:, h, d, 2 * k:2 * k + 2], TS3[s][:, lastcol:256:128], AF.Exp, [sk + "T3"], [hn + "eb%d_%d" % (d, k)])
            if lat:
                ACT(TSq[s], TSq[s], AF.Exp, [sk + "Tq"], [sk + "Tq"])
            ACT(TS3[s], TS3[s], AF.Exp, [sk + "T3"], [sk + "T3"], scale=-1.0)

        def st4(ui):
            h, k, d, wh, WHK, hn, s, lat = uinfo(ui)
            sk = bn + "ts%d_" % s
            if lat:
                lt = slice((k - 1) * 256, k * 256)
                STT("dve", qTs[d][:, lt], TSz[s], -1.0, TSq[s], ALU.mult, ALU.mult, [sk + "Tz", sk + "Tq"], [bn + "qT%d_%d" % (d, k)])
                kdst = kTs[d][:, lt]
                kkey = bn + "kT%d_%d" % (d, k)
            else:
                kdst = ktmp[0]
                kkey = bn + "ktmp0"
            STT("dve", kdst, TS1[s], 1.0, TS3[s], ALU.subtract, ALU.mult, [sk + "T1", sk + "T3"], [kkey])
            kh = khT[ui % 2]
            for c in range(2):
                TS("dve", kh[:, c * 128:(c + 1) * 128], kdst[:, c * 128:(c + 1) * 128], eb_tab[:, h, d, 2 * k + c:2 * k + c + 1], None,
                   ALU.mult, None, [kkey, hn + "eb%d_%d" % (d, k)], [bn + "khT%d_%d" % (ui % 2, c)])

        def st5(ui):
            h, k, d, wh, WHK, hn, s, lat = uinfo(ui)
            kh = khT[ui % 2]
            ptk = psb[4 + ui % 2]
            TRG([(ptk[:, c * 128:(c + 1) * 128], kh[:, c * 128:(c + 1) * 128]) for c in range(2)], ident,
                [bn + "khT%d_%d" % (ui % 2, c) for c in range(2)] + ["ident"], [PS[4 + ui % 2]])

        def st6(ui):
            h, k, d, wh, WHK, hn, s, lat = uinfo(ui)
            ptk = psb[4 + ui % 2]
            CP("act" if ui % 2 else "dve", khat[:, 2 * k:2 * k + 2, d, :], ptk[:, 0:256].rearrange("p (c k) -> p c k", c=2),
               [PS[4 + ui % 2]], [bn + "khat%d_%d" % (d, k)])

        def scan_mm(h, g, d, k):
            hn = hkeys(h)
            pS = ps[6][:, d * 128:(d + 1) * 128]
            MMG([(pS, khat[:, g, d, :], v_tm[:, g, h * 128:(h + 1) * 128], True, True)],
                [bn + "khat%d_%d" % (d, k), bn + "v_tm%d" % g], [PS[6]])

        def scan_upd(h, g, d, k, p):
            hn = hkeys(h)
            pS = ps[6][:, d * 128:(d + 1) * 128]
            if p == 0:
                CP("dve", S_st[:, d, 0, :], pS, [PS[6]], [bn + "S%d_%d" % (d, 0)])
            else:
                STT("dve", S_st[:, d, p, :], S_st[:, d, p - 1, :], eb_tab[:, h, d, g:g + 1], pS, ALU.mult, ALU.add,
                    [bn + "S%d_%d" % (d, p - 1), hn + "eb%d_%d" % (d, k), PS[6]], [bn + "S%d_%d" % (d, p)])

        def grp_piece(h, gq, step):
            hn = hkeys(h)
            pO = ps[7]

            def pa_mm(i):
                j = 4 * gq + i
                tl = slice(j * 128, (j + 1) * 128)
                kb = j // 2 + 1
                pA = ps[3][:, (i % 2) * 256:(i % 2) * 256 + 256]
                MMG([(pA[:, 0:128], kTs[0][:, tl], qTs[0][:, tl], True, True),
                     (pA[:, 128:256], kTs[1][:, tl], qTs[1][:, tl], True, True)],
                    [bn + "kT%d_%d" % (d, kb) for d in range(2)] + [bn + "qT%d_%d" % (d, kb) for d in range(2)], [PS[3]])

            def mask2(i0):
                TT("dve", ATb[:, i0:i0 + 2, :], ps[3].rearrange("p (i c) -> p i c", i=2), maskfb2, ALU.mult, [PS[3], "maskfb"],
                   [bn + "AT%d" % i0, bn + "AT%d" % (i0 + 1)])

            def po_mm(i):
                j = 4 * gq + i
                g = j + 2
                kb = j // 2 + 1
                tl = slice(j * 128, (j + 1) * 128)
                vv = v_tm[:, g, h * 128:(h + 1) * 128]
                MMG([(pO[:, i * 128:(i + 1) * 128], ATb[:, i, 0:128], vv, True, False),
                     (pO[:, i * 128:(i + 1) * 128], ATb[:, i, 128:256], vv, False, False),
                     (pO[:, i * 128:(i + 1) * 128], qTs[0][:, tl], S_st[:, 0, j + 1, :], False, False),
                     (pO[:, i * 128:(i + 1) * 128], qTs[1][:, tl], S_st[:, 1, 16 - j, :], False, True)],
                    [bn + "AT%d" % i, bn + "v_tm%d" % g, bn + "qT0_%d" % kb, bn + "qT1_%d" % kb,
                     bn + "S0_%d" % (j + 1), bn + "S1_%d" % (16 - j)], [PS[7]])

            if step == 1:
                pa_mm(0)
                pa_mm(1)
                mask2(0)
            elif step == 2:
                po_mm(0)
                po_mm(1)
                pa_mm(2)
                pa_mm(3)
                mask2(2)
            elif step == 3:
                po_mm(2)
                po_mm(3)
                CP("dve", o_sb, pO.rearrange("p (i c) -> p i c", i=4), [PS[7]], [bn + "o_sb"])
                for i in range(4):
                    ACT(junk2, o_sb[:, i, :], AF.Square, [bn + "o_sb"], [bn + "junk2", bn + "ssh%d" % i], accum=ssh[:, i:i + 1])
                rstd_chain(ssh, lnh, rsh, 1.0 / 128, [bn + "ssh%d" % i for i in range(4)] + ["epsc"], [bn + "lnh"], [bn + "rsh"])
            elif step == 4:
                for i in range(4):
                    j = 4 * gq + i
                    STT("dve", og[:, i, :], o_sb[:, i, :], rsh[:, i:i + 1], gate[:, j, h * 128:(h + 1) * 128],
                        ALU.mult, ALU.mult, [bn + "o_sb", bn + "rsh", bn + "gate%d" % j], [bn + "og%d" % i])
            elif step == 5:
                pT = psb[6][:, 512:1024]
                TRG([(pT[:, i * 128:(i + 1) * 128], og[:, i, :]) for i in range(4)], ident,
                    [bn + "og%d" % i for i in range(4)] + ["ident"], [PS[6]])
                CP("dve", o_hgT[:, h, gq * 512:(gq + 1) * 512], pT, [PS[6]], [bn + "ohg%d_%d" % (h, gq)])

        load_wh(0)
        stages = [st0, st1, st2, st3, st4, st5, st6]
        from collections import deque
        scan_q = [deque(), deque()]
        blk_scanned = {}
        grp_todo = deque((h, gq) for h in range(4) for gq in (range(4) if h % 2 == 0 else range(3, -1, -1)))
        front = None
        back = None
        grp_done = {}
        tau = 0
        tc = 0

        def chain_hazard(tc_):
            for si in (4, 6):
                ui = tc_ - si
                if 0 <= ui < NU:
                    h, k, d = units[ui]
                    if si == 4 and k >= 1:
                        for hp in range(h):
                            if grp_done.get((hp, (k - 1) // 2), 0) < 3:
                                return True
                    if si == 6:
                        for dq_ in scan_q:
                            for ent in dq_:
                                if ent[0] < h:
                                    return True
            return False

        def scan_hazard(h, d, p):
            j = (p - 1) if d == 0 else (16 - p)
            if 0 <= j <= 15:
                for hp in range(h):
                    if grp_done.get((hp, j // 4), 0) < 3:
                        return True
            return False

        while True:
            active = False
            if tc < NU + len(stages) and not chain_hazard(tc):
                for si in (1, 2, 3, 4, 5, 6, 0):
                    ui = tc - si
                    if 0 <= ui < NU:
                        stages[si](ui)
                        h, k, d = units[ui]
                        if si == 0 and ui + 1 < NU and units[ui + 1][0] != h:
                            load_wh(h + 1)
                        if si == 6:
                            tiles = [2 * k, 2 * k + 1] if d == 0 else [2 * k + 1, 2 * k]
                            todo = []
                            for g in tiles:
                                p = g if d == 0 else (1 - g if g < 2 else 19 - g)
                                if p <= 16:
                                    todo.append((g, p))
                            for n_, (g, p) in enumerate(todo):
                                scan_q[d].append((h, g, k, p, tau + 1, n_ == len(todo) - 1))
                            if not todo:
                                blk_scanned[(h, k, d)] = tau
                tc += 1
                active = True
            steps_now = []
            for d in range(2):
                if scan_q[d] and scan_q[d][0][4] <= tau and not scan_hazard(scan_q[d][0][0], d, scan_q[d][0][3]):
                    steps_now.append((d,) + scan_q[d].popleft())
            for (d, h, g, k, p, rt, last) in steps_now:
                scan_mm(h, g, d, k)
            for (d, h, g, k, p, rt, last) in steps_now:
                scan_upd(h, g, d, k, p)
                if last:
                    blk_scanned[(h, k, d)] = tau
                active = True
            if front is not None:
                h, gq, stp = front
                grp_piece(h, gq, stp)
                grp_done[(h, gq)] = stp
                front = (h, gq, stp + 1) if stp < 5 else None
                active = True
            pending_front = None
            if back is not None:
                h, gq, stp = back
                grp_piece(h, gq, stp)
                grp_done[(h, gq)] = stp
                active = True
                if stp == 3:
                    back = None
                    pending_front = (h, gq, 4)
                else:
                    back = (h, gq, stp + 1)
            if back is None and pending_front is None and grp_todo:
                h, gq = grp_todo[0]
                need = [(h, 2 * gq + 1, 0), (h, 2 * gq + 2, 0), (h, 2 * gq + 1, 1), (h, 2 * gq + 2, 1), (h, 0, 0), (h, 0, 1)]
                if all((kk in blk_scanned and blk_scanned[kk] < tau) for kk in need):
                    grp_todo.popleft()
                    back = (h, gq, 1)
            if pending_front is not None:
                assert front is None
                front = pending_front
            tau += 1
            if not active and not grp_todo and back is None and front is None and not scan_q[0] and not scan_q[1] and tc >= NU + len(stages):
                break
            assert tau < NU + 600, "side work did not drain"
        OH = [bn + "ohg%d_%d" % (h, gq) for h in range(4) for gq in range(4)]
        if b == 0:
            dump("o_hgT", o_hgT, OH)
        _saved_off = A.off
        A.release(m_fin)
        w_out_bf = A.alloc([128, 8, D], BF16)
        _wout_end = A.off
        A.off = _saved_off
        DMA("pool", w_out_bf, w_out, [], [bn + "w_out"] + [kk_ for g_ in range(NT) for kk_ in HK(g_)])
        S.barrier()

        if stop == 'hgrn' and b == 0:
            S.barrier(); S.finalize(); S.emit(); return nc
        A.release(m_fin)
        A.off = _wout_end
        identf = A.alloc([128, 128], F32)
        DMA("sp", identf, c_identf, [], ["identf"])
        x1b = [A.alloc([128, 4, D], F32) for _ in range(2)]
        h2Tb = [A.alloc([128, 8, 512], BF16) for _ in range(2)]
        uT = A.alloc([128, 32, 512], BF16)
        xt2 = [A.alloc([128, D], F32) for _ in range(1)]
        xs2b = [A.alloc([128, D], BF16) for _ in range(2)]
        w1p = [A.alloc([128, 8, 512], BF16) for _ in range(2)]
        w2p = [A.alloc([128, 32, 128], BF16) for _ in range(2)]
        rbuf = [A.alloc([128, 512], F32) for _ in range(2)]
        yT = [A.alloc([128, 512], F32) for _ in range(2)]
        ot = [A.alloc([128, D], F32) for _ in range(1)]
        ss2 = A.alloc([128, 16], F32)
        ln2 = A.alloc([128, 16], F32)
        rs2 = A.alloc([128, 16], F32)
        ss3 = A.alloc([128, 16], F32)
        ln3 = A.alloc([128, 16], F32)
        rs3 = A.alloc([128, 16], F32)
        fn_ = bn + "f_"

        def X1K(jb, i):
            return fn_ + "x1_%d_%d" % (jb % 2, i)

        def stage_a(jb, i):
            x1 = x1b[jb % 2]
            j = jb * 4 + i
            tl = slice(j * 128, (j + 1) * 128)
            DMA("sp", xt2[0], x[b, tl, :], [], [fn_ + "xt0"])
            for half in range(2):
                MMG([(ps[half], (o_mlaT[:, c, tl] if c < 4 else o_hgT[:, c - 4, tl]), w_out_bf[:, c, half * 512:(half + 1) * 512],
                      c == 0, c == 7) for c in range(8)], OM + OH + [bn + "w_out"], [PS[half]])
                TT("dve", x1[:, i, half * 512:(half + 1) * 512], ps[half], G1[:, b, half * 512:(half + 1) * 512], ALU.mult,
                   [PS[half], "G1_%d_%d" % (b, half)], [fn_ + "x1h_%d_%d_%d" % (jb % 2, i, half), X1K(jb, i)])
            TT("dve", x1[:, i, :], x1[:, i, :], xt2[0], ALU.add,
               [fn_ + "x1h_%d_%d_0" % (jb % 2, i), fn_ + "x1h_%d_%d_1" % (jb % 2, i), fn_ + "xt0"], [X1K(jb, i)])

        def stage_b1(jb, i):
            x1 = x1b[jb % 2]
            j = jb * 4 + i
            xs2k = xs2b[i % 2]
            xsk = fn_ + "xs2_%d" % (i % 2)
            ACT(xs2k, x1[:, i, :], AF.Square, [X1K(jb, i)], [xsk, fn_ + "ss2_%d" % j], accum=ss2[:, j:j + 1])
            rstd_chain(ss2[:, j:j + 1], ln2[:, j:j + 1], rs2[:, j:j + 1], 1.0 / D, [fn_ + "ss2_%d" % j, "epsc"],
                       [fn_ + "ln2_%d" % j], [fn_ + "rs2_%d" % j])
            TS("dve", xs2k, x1[:, i, :], rs2[:, j:j + 1], None, ALU.mult, None, [X1K(jb, i), fn_ + "rs2_%d" % j], [xsk])

        def stage_b2(jb, i):
            h2T = h2Tb[jb % 2]
            xs2k = xs2b[i % 2]
            xsk = fn_ + "xs2_%d" % (i % 2)
            pt = psb[2]
            TRG([(pt[:, c * 128:(c + 1) * 128], xs2k[:, c * 128:(c + 1) * 128]) for c in range(8)], ident,
                [xsk, "ident"], [PS[2]])
            for c in range(8):
                dst = h2T[:, c, i * 128:(i + 1) * 128]
                hk_ = fn_ + "h2T%d_%d_%d" % (jb % 2, i, c)
                if i % 2 == 0:
                    ACT(dst, pt[:, c * 128:(c + 1) * 128], AF.Identity, [PS[2], "A2_%d" % b] + MODT2, [hk_],
                        scale=A2[:, b, c:c + 1], bias=modT[:, 24 + c, b:b + 1])
                else:
                    TS("dve", dst, pt[:, c * 128:(c + 1) * 128], A2[:, b, c:c + 1], modT[:, 24 + c, b:b + 1], ALU.mult, ALU.add,
                       [PS[2], "A2_%d" % b] + MODT2, [hk_])

        def prep_pieces(jb):
            return [
                [lambda: stage_a(jb, 0)],
                [lambda: stage_a(jb, 1)],
                [lambda: stage_b1(jb, 0)],
                [lambda: stage_a(jb, 2), lambda: stage_b2(jb, 0)],
                [lambda: stage_b1(jb, 1)],
                [lambda: stage_a(jb, 3), lambda: stage_b2(jb, 1)],
                [lambda: stage_b1(jb, 2)],
                [lambda: stage_b2(jb, 2), lambda: stage_b1(jb, 3)],
                [lambda: stage_b2(jb, 3)],
            ]

        otb = [ot[0], xt2[0]]
        otk = [fn_ + "ot0", fn_ + "xt0"]

        def final_norm(jb_, i):
            x1_ = x1b[jb_ % 2]
            j = jb_ * 4 + i
            xs2k = xs2b[i % 2]
            ACT(xs2k, x1_[:, i, :], AF.Square, [X1K(jb_, i)], [fn_ + "xs2_%d" % (i % 2), fn_ + "ss3_%d" % j], accum=ss3[:, j:j + 1])
            rstd_chain(ss3[:, j:j + 1], ln3[:, j:j + 1], rs3[:, j:j + 1], 1.0 / D, [fn_ + "ss3_%d" % j, "epsc"],
                       [fn_ + "ln3_%d" % j], [fn_ + "rs3_%d" % j])
            STT("dve", otb[i % 2], x1_[:, i, :], rs3[:, j:j + 1], fn_bc, ALU.mult, ALU.mult, [X1K(jb_, i), fn_ + "rs3_%d" % j, "fn_bc"],
                [otk[i % 2]])
            DMA("pool", out[b, j * 128:(j + 1) * 128, :], otb[i % 2], [otk[i % 2]], [fn_ + "out%d" % j])

        for grp in prep_pieces(0):
            for f_ in grp:
                f_()
        for jb in range(4):
            x1 = x1b[jb % 2]
            h2T = h2Tb[jb % 2]
            H2 = [fn_ + "h2T%d_%d_%d" % (jb % 2, i, c) for i in range(4) for c in range(8)]
            X1 = [X1K(jb, i) for i in range(4)]
            if b == 0 and jb == 0:
                dump("x1", x1, X1)
                dump("h2T", h2T, H2)
            for p in range(8):
                wp = w1p[p % 2]
                wk = fn_ + "w1p%d" % (p % 2)
                DMA("sp", wp, w1s[p].rearrange("q (k c) -> q k c", k=8), ["w1s%d" % p], [wk])
                for q4 in range(4):
                    jj = 4 * p + q4
                    pu = ps[3 + jj % 2]
                    MMG([(pu, wp[:, kc, q4 * 128:(q4 + 1) * 128], h2T[:, kc, :], kc == 0, kc == 7) for kc in range(8)],
                        [wk] + H2, [PS[3 + jj % 2]])
                    rb = rbuf[jj % 2]
                    ACT(rb, pu, AF.Relu, [PS[3 + jj % 2]], [fn_ + "rb%d" % (jj % 2)])
                    TT("dve", uT[:, jj, :], rb, rb, ALU.mult, [fn_ + "rb%d" % (jj % 2)], [fn_ + "uT%d" % jj])
                if jb >= 1 and p < 4:
                    final_norm(jb - 1, p)
            UT = [fn_ + "uT%d" % jj for jj in range(32)]
            if b == 0 and jb == 0:
                dump("uT", uT, UT)

            def mlp2_tail(dq):
                yk = fn_ + "yT%d" % (dq % 2)
                S.add("pe", (lambda src, dstp: (lambda e: [e.transpose(dstp[:, i * 128:(i + 1) * 128], src[:, i * 128:(i + 1) * 128], identf)
                                                         for i in range(4)][-1]))(yT[dq % 2], ps[7]),
                      [yk, "identf"], [PS[7]])
                xv = x1[:, :, dq * 128:(dq + 1) * 128]
                TT("dve", xv, xv, ps[7].rearrange("p (i c) -> p i c", i=4), ALU.add, [PS[7]] + X1, X1)

            nxt = prep_pieces(jb + 1) if jb + 1 < 4 else []
            for dq in range(8):
                wp = w2p[dq % 2]
                wk = fn_ + "w2p%d" % (dq % 2)
                DMA("sp", wp, w2s[dq].rearrange("q (j c) -> q j c", j=32), ["w2s%d" % dq], [wk])
                pv = ps[5 + dq % 2]
                MMG([(pv, wp[:, jj, :], uT[:, jj, :], jj == 0, jj == 31) for jj in range(32)], [wk] + UT, [PS[5 + dq % 2]])
                yk = fn_ + "yT%d" % (dq % 2)
                ACT(yT[dq % 2], pv, AF.Identity, [PS[5 + dq % 2]] + MODT2, [yk], scale=modT[:, 40 + dq, b:b + 1])
                if dq >= 1:
                    mlp2_tail(dq - 1)
                if nxt:
                    for f_ in nxt.pop(0):
                        f_()
            mlp2_tail(7)
            while nxt:
                for f_ in nxt.pop(0):
                    f_()
            if jb == 3:
                for i in range(4):
                    final_norm(jb, i)
        S.barrier()
        if stop == 'b0' and b == 0:
            S.barrier(); S.finalize(); S.emit(); return nc

    S.finalize()
    S.emit()
    return nc


def _consts():
    bf = ml_dtypes.bfloat16
    ident = np.eye(128, dtype=np.float32)
    s = np.arange(128)[:, None]
    t = np.arange(128)[None, :]
    mask = np.concatenate([(s <= t), (s >= t)], axis=1).astype(np.float32)
    rm = np.ones((128, 2, 512), np.float32)
    rm[:, 0, 0::128] = 0.0
    rm[:, 1, 127::128] = 0.0
    tok = np.arange(T)
    row = (tok // 64).astype(np.float32)
    col = (tok % 64).astype(np.float32)
    nfreq = 8
    inv = (np.float32(10000.0) ** (-np.arange(nfreq, dtype=np.float32) / np.float32(nfreq))).astype(np.float32)
    cs = np.zeros((128, 2, T), np.float32)
    for dmm in range(32):
        grp, i = dmm // 16, dmm % 16
        f = i % 8
        ang = (row if grp == 0 else col) * inv[f]
        c, sn = np.cos(ang.astype(np.float32)), np.sin(ang.astype(np.float32))
        sign = -1.0 if i < 8 else 1.0
        for q in range(4):
            cs[q * 32 + dmm, 0] = c
            cs[q * 32 + dmm, 1] = sign * sn
    return dict(c_ident=ident.astype(bf), c_identf=ident, c_mask=mask.astype(bf), c_rmask=rm.astype(bf), c_cs=cs.astype(bf))


def _swap_rope(w, base, period):
    w = w.copy()
    ncol = w.shape[-1]
    for h0 in range(0, ncol, period):
        r0 = h0 + base
        blk = w[..., r0:r0 + 32].copy()
        new = blk.copy()
        for grp in range(2):
            o = grp * 16
            new[..., o:o + 8] = blk[..., o + 8:o + 16]
            new[..., o + 8:o + 16] = blk[..., o:o + 8]
        w[..., r0:r0 + 32] = new
    return w


def _kp(w, nk):
    return np.ascontiguousarray(w.reshape(nk, 128, -1).transpose(1, 0, 2))


def _shared_inputs(inp):
    f = np.float32
    w_in = inp["w_in"][0]
    w_uq = inp["w_uq"][0]
    w_ukv = inp["w_ukv"][0].reshape(128, 8, 128)
    kr = np.zeros((1024, 96), f)
    kr[:, 64:96] = w_in[:, 384:416]
    kr = _swap_rope(kr, 64, 96)
    w1 = inp["w_mlp_in"][0]
    w1r = np.ascontiguousarray(w1.reshape(8, 128, 8, 512).transpose(2, 1, 0, 3)).reshape(8, 128, 4096)
    w2 = inp["w_mlp_out"][0]
    w2r = np.ascontiguousarray(w2.reshape(32, 128, 8, 128).transpose(2, 1, 0, 3)).reshape(8, 128, 4096)
    sh = dict(
        w_ada=_kp(inp["w_ada"][0], 8),
        b_adaT=np.ascontiguousarray(inp["b_ada"][0].reshape(48, 128).T),
        b_ada=np.ascontiguousarray(inp["b_ada"][0]),
        nmixT=np.ascontiguousarray(inp["norm_mix"][0].reshape(8, 128).T),
        nmlpT=np.ascontiguousarray(inp["norm_mlp"][0].reshape(8, 128).T),
        w_in=_kp(w_in, 8),
        w_krs=_kp(kr, 8),
        qnT=np.ascontiguousarray(inp["q_norm"][0].reshape(2, 128).T),
        kvnT=np.ascontiguousarray(inp["kv_norm"][0].reshape(128, 1)),
        w_uq=_kp(w_uq, 2),
        w_uqs=_kp(_swap_rope(w_uq, 64, 96), 2),
        w_kn=np.ascontiguousarray(w_ukv[:, :, 0:64].reshape(128, 512)),
        w_v=np.ascontiguousarray(w_ukv[:, :, 64:128].reshape(128, 512)),
        lbT=np.ascontiguousarray(inp["hgrn_lb"].reshape(2, 8, 128).transpose(2, 0, 1)),
        hgn=np.ascontiguousarray(inp["hgrn_norm"][0]),
        w_out=_kp(inp["w_out"][0], 8),
        w1=w1r, w2=w2r,
        fnorm=np.ascontiguousarray(inp["final_norm"]),
    )
    sh = {k: np.ascontiguousarray(v, dtype=f) for k, v in sh.items()}
    sh.update(_consts())
    return sh


_NC_CACHE = {}


def kernel(**inputs):
    inp = {k: np.asarray(v) for k, v in inputs.items()}
    shared = _shared_inputs(inp)
    in_maps = []
    for c in range(8):
        b0 = c * NB
        cv = np.stack([inp["c"][b0], inp["c"][b0 + 1], inp["c_ctx"]], axis=0)
        m = dict(shared)
        m["x"] = np.ascontiguousarray(inp["x"][b0:b0 + NB], dtype=np.float32)
        m["ctx"] = np.ascontiguousarray(inp["ctx"][b0:b0 + NB], dtype=np.float32)
        m["cvecT"] = np.ascontiguousarray(cv.reshape(3, 8, 128).transpose(2, 1, 0), dtype=np.float32)
        in_maps.append(m)
    if "nc" not in _NC_CACHE:
        _NC_CACHE["nc"] = build_program()
    res = run_bass_kernel_spmd(_NC_CACHE["nc"], in_maps, core_ids=list(range(8)))
    return np.concatenate([np.asarray(r["out"]) for r in res.results], axis=0).astype(np.float32)
```

```python
import numpy as np
import ml_dtypes
import concourse.bass as bass
import concourse.mybir as mybir
from concourse.bass_utils import run_bass_kernel_spmd

F32 = mybir.dt.float32
BF16 = mybir.dt.bfloat16
AF = mybir.ActivationFunctionType
ALU = mybir.AluOpType

NB = 2
T = 2048
L = 256
TE = T + L
D = 1024
NT = TE // 128
DFF = 4096
EPS = 1e-6
BLKS = [(0, 256)] + [(256 + 512 * j, 512) for j in range(4)]
ATT_SCALE = float(96 ** -0.5)
DMA_K = 8


class _Op:
    __slots__ = ("idx", "eng", "fn", "dma", "deps", "signal", "tick", "sem_i", "sem_v")

    def __init__(self, idx, eng, fn, dma, deps):
        self.idx, self.eng, self.fn, self.dma, self.deps = idx, eng, fn, dma, deps
        self.signal = False
        self.tick = 0
        self.sem_i = 0
        self.sem_v = 0


class Sched:
    ENGS = ("pe", "act", "dve", "pool", "sp")

    def __init__(self, nc):
        self.nc = nc
        self.ops = []
        self.last_w = {}
        self.readers = {}
        self.dma_ops = {"sp": [], "pool": [], "act": []}

    def add(self, eng, fn, r=(), w=(), dma=False):
        idx = len(self.ops)
        deps = {}
        for k in r:
            p = self.last_w.get(k)
            if p is not None:
                deps[p] = "raw"
        for k in w:
            p = self.last_w.get(k)
            if p is not None and p not in deps:
                deps[p] = "waw"
            for q in self.readers.get(k, ()):
                if q not in deps:
                    deps[q] = "war"
        if dma:
            lst = self.dma_ops[eng]
            if len(lst) >= DMA_K:
                deps[lst[-DMA_K]] = "raw"
            lst.append(idx)
        op = _Op(idx, eng, fn, dma, deps)
        for k in w:
            self.last_w[k] = idx
            self.readers[k] = []
        for k in r:
            self.readers.setdefault(k, []).append(idx)
        self.ops.append(op)
        return op

    def barrier(self):
        last = {}
        for op in self.ops:
            if not op.dma and op.fn is not None:
                last[op.eng] = op.idx
        dmas = [i for q in self.dma_ops.values() for i in q[-DMA_K:]]
        for e in self.ENGS:
            deps = {i: "raw" for ee, i in last.items()}
            for i in dmas:
                deps[i] = "raw"
            idx = len(self.ops)
            op = _Op(idx, e, None, False, deps)
            op.deps = {k: ("bar") for k in deps}
            self.ops.append(op)

    def finalize(self):
        ops = self.ops
        for q, lst in self.dma_ops.items():
            for n, i in enumerate(lst):
                ops[i].sem_i = n % DMA_K
                ops[i].sem_v = 16 * (n // DMA_K + 1)
        self.waits = {}
        for op in ops:
            best = {}
            dma_w = {}
            for p, kind in op.deps.items():
                po = ops[p]
                if po.dma:
                    key = (po.eng, po.sem_i)
                    dma_w[key] = max(dma_w.get(key, 0), po.sem_v)
                    continue
                if po.fn is None:
                    continue
                if po.eng == op.eng and not op.dma and kind != "bar":
                    if po.eng == "pe":
                        continue
                if p > best.get(po.eng, -1):
                    best[po.eng] = p
            for e, p in best.items():
                ops[p].signal = True
            self.waits[op.idx] = (best, dma_w)
        cnt = {e: 0 for e in self.ENGS}
        for op in ops:
            if op.signal:
                cnt[op.eng] += 1
                op.tick = cnt[op.eng]

    def emit(self, final_wait_eng="sp"):
        nc = self.nc
        ops = self.ops
        from contextlib import ExitStack
        with ExitStack() as st:
            esem = {e: st.enter_context(nc.semaphore("cs_" + e)) for e in self.ENGS}
            dsem = {q: [st.enter_context(nc.semaphore("ds_%s%d" % (q, i))) for i in range(DMA_K)]
                    for q in self.dma_ops}
            block = st.enter_context(nc.Block())
            per_eng = {e: [op for op in ops if op.eng == e] for e in self.ENGS}
            waits = self.waits

            def body(ename, eng):
                seen = {}
                for op in per_eng[ename]:
                    best, dma_w = waits[op.idx]
                    for pe_, p in best.items():
                        t = ops[p].tick
                        key = ("c", pe_)
                        if seen.get(key, 0) < t:
                            eng.wait_ge(esem[pe_], t)
                            seen[key] = t
                    for (q, si), v in dma_w.items():
                        key = ("d", q, si)
                        if seen.get(key, 0) < v:
                            eng.wait_ge(dsem[q][si], v)
                            seen[key] = v
                    if op.fn is None:
                        continue
                    inst = op.fn(eng)
                    if op.dma:
                        inst.then_inc(dsem[op.eng][op.sem_i], 16)
                    elif op.signal:
                        inst.then_inc(esem[ename], 1)
                if ename == final_wait_eng:
                    for q, lst in self.dma_ops.items():
                        for i in lst[-DMA_K:]:
                            eng.wait_ge(dsem[q][ops[i].sem_i], ops[i].sem_v)

            @block.tensor
            def _(e):
                body("pe", e)

            @block.scalar
            def _(e):
                body("act", e)

            @block.vector
            def _(e):
                body("dve", e)

            @block.gpsimd
            def _(e):
                body("pool", e)

            @block.sync
            def _(e):
                body("sp", e)


class Arena:
    def __init__(self, nc, lo=16896, hi=229376):
        self.nc, self.lo, self.hi, self.off = nc, lo, hi, lo
        self.n = 0

    def alloc(self, shape, dt):
        nbytes = int(np.prod(shape[1:])) * (2 if dt == BF16 else 4)
        nbytes = (nbytes + 63) // 64 * 64
        assert self.off + nbytes <= self.hi, ("SBUF overflow", self.off, nbytes)
        self.n += 1
        t = self.nc.alloc_sbuf_tensor_at("sb%d" % self.n, list(shape), dt, offset=self.off)
        self.off += nbytes
        return t.ap()

    def mark(self):
        return self.off

    def release(self, m):
        self.off = m


def build_program(dbg=None, stop=None):
    nc = bass.Bass("TRN2", target_bir_lowering=False)
    S = Sched(nc)
    A = Arena(nc)

    def din(name, shape, dt=F32):
        return nc.dram_tensor(name, list(shape), dt, kind="ExternalInput").ap()

    x = din("x", [NB, T, D])
    ctx = din("ctx", [NB, L, D])
    cvecT = din("cvecT", [128, 8, 3])
    w_ada = din("w_ada", [128, 8, 6144])
    b_adaT = din("b_adaT", [128, 48])
    b_ada = din("b_ada", [6144])
    nmixT = din("nmixT", [128, 8])
    nmlpT = din("nmlpT", [128, 8])
    w_in = din("w_in", [128, 8, 2976])
    w_krs = din("w_krs", [128, 8, 96])
    qnT = din("qnT", [128, 2])
    kvnT = din("kvnT", [128, 1])
    w_uq = din("w_uq", [128, 2, 768])
    w_uqs = din("w_uqs", [128, 2, 768])
    w_kn = din("w_kn", [128, 512])
    w_v = din("w_v", [128, 512])
    lbT = din("lbT", [128, 2, 8])
    hgn = din("hgn", [128])
    w_out = din("w_out", [128, 8, 1024])
    w1 = din("w1", [8, 128, 4096])
    w2 = din("w2", [8, 128, 4096])
    fnorm = din("fnorm", [D])
    c_ident = din("c_ident", [128, 128], BF16)
    c_identf = din("c_identf", [128, 128], F32)
    c_mask = din("c_mask", [128, 256], BF16)
    c_rmask = din("c_rmask", [128, 2, 512], BF16)
    c_cs = din("c_cs", [128, 2, T], BF16)
    out = nc.dram_tensor("out", [NB, T, D], F32, kind="ExternalOutput").ap()
    w1s = nc.dram_tensor("w1s", [8, 128, 4096], BF16).ap()
    w2s = nc.dram_tensor("w2s", [8, 128, 4096], BF16).ap()
    dbg_out = {}
    if dbg:
        for name, shape, dt in dbg:
            dbg_out[name] = nc.dram_tensor("dbg_" + name, list(shape), dt, kind="ExternalOutput").ap()

    ps = [nc.alloc_psum_tensor("ps%d" % i, [128, 512], F32).ap() for i in range(8)]
    psb = [p.bitcast(BF16) for p in ps]
    PS = ["ps%d" % i for i in range(8)]

    def ACT(out_, in_, func, r, w, scale=1.0, bias=0.0, accum=None):
        kw = {}
        if accum is not None:
            kw["accum_out"] = accum
        S.add("act", lambda e: e.activation(out=out_, in_=in_, func=func, bias=bias, scale=scale, **kw), r, w)

    def TT(eng, out_, a, b, op, r, w):
        S.add(eng, lambda e: e.tensor_tensor(out_, a, b, op), r, w)

    def TS(eng, out_, a, s1, s2, op0, op1, r, w):
        if s2 is None:
            S.add(eng, lambda e: e.tensor_scalar(out_, a, s1, None, op0), r, w)
        else:
            S.add(eng, lambda e: e.tensor_scalar(out_, a, s1, s2, op0, op1), r, w)

    def STT(eng, out_, a, sc, b, op0, op1, r, w):
        S.add(eng, lambda e: e.scalar_tensor_tensor(out_, a, sc, b, op0, op1), r, w)

    def CP(eng, out_, in_, r, w):
        if eng == "act":
            S.add("act", lambda e: e.copy(out_, in_), r, w)
        else:
            S.add(eng, lambda e: e.tensor_copy(out_, in_), r, w)

    def MSET(eng, ap, val, w):
        S.add(eng, lambda e: e.memset(ap, val), (), w)

    def MMG(lst, r, w):
        lst = list(lst)

        def fn(e):
            ins = None
            for (o, l, rr, st, sp) in lst:
                ins = e.matmul(o, lhsT=l, rhs=rr, start=st, stop=sp)
            return ins
        S.add("pe", fn, r, w)

    def TRG(lst, ident_ap, r, w):
        lst = list(lst)

        def fn(e):
            ins = None
            for (o, i_) in lst:
                ins = e.transpose(o, i_, ident_ap)
            return ins
        S.add("pe", fn, r, w)

    def DMA(q, out_, in_, r, w):
        S.add(q, lambda e: e.dma_start(out=out_, in_=in_), r, w, dma=True)

    def RECIP(out_, in_, r, w):
        S.add("dve", lambda e: e.reciprocal(out_, in_), r, w)

    def SCAN(out_, d0, d1, r, w):
        S.add("dve", lambda e: e.tensor_tensor_scan(out_, d0, d1, 0.0, ALU.mult, ALU.add), r, w)

    def dump(name, src_ap, r):
        if name in dbg_out:
            DMA("sp", dbg_out[name], src_ap, r, ["dbg_" + name])

    def rstd_chain(ss, tmp, rs, n_inv, r, w_tmp, w_rs):
        ACT(tmp, ss, AF.Ln, r, w_tmp, scale=n_inv, bias=epsc[:, 0:1])
        ACT(rs, tmp, AF.Exp, w_tmp, w_rs, scale=-0.5)

    ident = A.alloc([128, 128], BF16)
    ones_bf = A.alloc([128, 128], BF16)
    maskfb2 = A.alloc([128, 2, 256], BF16)
    rmask = A.alloc([128, 2, 512], BF16)
    epsc = A.alloc([128, 2], F32)
    fn_bc = A.alloc([128, D], F32)
    gn_bc = A.alloc([128, 512], F32)
    G1 = A.alloc([128, NB, D], F32)
    modT = A.alloc([128, 48, 3], F32)
    A1 = A.alloc([128, 3, 8], F32)
    A2 = A.alloc([128, 3, 8], F32)
    lb_t = A.alloc([128, 8], F32)
    w_uq_bf = A.alloc([128, 2, 768], BF16)
    w_uqs_bf = A.alloc([128, 2, 768], BF16)
    w_kn_bf = A.alloc([128, 512], BF16)
    w_v_bf = A.alloc([128, 512], BF16)

    DMA("sp", ident, c_ident, [], ["ident"])
    DMA("sp", maskfb2[:, 0, :], c_mask, [], ["maskfb"])
    DMA("sp", maskfb2[:, 1, :], c_mask, [], ["maskfb"])
    DMA("sp", rmask, c_rmask, [], ["rmask"])
    DMA("sp", fn_bc, fnorm.partition_broadcast(128), [], ["fn_bc"])
    for i in range(4):
        DMA("sp", gn_bc[:, i * 128:(i + 1) * 128], hgn.partition_broadcast(128), [], ["gn_bc%d" % i])
    GN = ["gn_bc%d" % i for i in range(4)]
    MSET("dve", ones_bf, 1.0, ["ones"])
    MSET("dve", epsc, EPS, ["epsc"])

    m_setup = A.mark()
    wa = A.alloc([128, 8, 6144], BF16)
    cT = A.alloc([128, 8, 3], F32)
    sT = A.alloc([128, 8, 3], BF16)
    sTb = A.alloc([128, NB, 8, 128], BF16)
    badaT = A.alloc([128, 48], F32)
    bada_g1 = A.alloc([128, D], F32)
    nmix_t = A.alloc([128, 8], F32)
    nmlp_t = A.alloc([128, 8], F32)
    lbraw = A.alloc([128, 2, 8], F32)
    lbtmp = A.alloc([128, 8], F32)
    qn_t = A.alloc([128, 2], F32)
    kvn_t = A.alloc([128, 1], F32)
    wst = A.alloc([128, 2, 768], F32)
    wst2 = A.alloc([128, 2, 768], F32)
    wst3 = A.alloc([128, 1024], F32)

    for kc in range(8):
        DMA("pool", wa[:, kc, 0:2048], w_ada[:, kc, 0:2048], [], ["wa%d_a" % kc])
    for kc in range(8):
        DMA("pool", wa[:, kc, 2048:6144], w_ada[:, kc, 2048:6144], [], ["wa%d_b" % kc])
    DMA("sp", cT, cvecT, [], ["cT"])
    DMA("sp", badaT, b_adaT, [], ["badaT"])
    DMA("sp", bada_g1, b_ada[2048:3072].partition_broadcast(128), [], ["bada_g1"])
    DMA("sp", nmix_t, nmixT, [], ["nmix"])
    DMA("sp", nmlp_t, nmlpT, [], ["nmlp"])
    DMA("sp", lbraw, lbT, [], ["lbraw"])
    DMA("sp", qn_t, qnT, [], ["qn"])
    DMA("sp", kvn_t, kvnT, [], ["kvn"])
    DMA("sp", wst, w_uq, [], ["wst"])
    DMA("sp", wst2, w_uqs, [], ["wst2"])
    DMA("sp", wst3[:, 0:512], w_kn, [], ["wst3a"])
    DMA("sp", wst3[:, 512:1024], w_v, [], ["wst3b"])

    TT("dve", lbtmp, lbraw[:, 1, :], lbraw[:, 0, :], ALU.subtract, ["lbraw"], ["lbtmp"])
    ACT(lbtmp, lbtmp, AF.Exp, ["lbtmp"], ["lbtmp"])
    TS("dve", lbtmp, lbtmp, 1.0, None, ALU.add, None, ["lbtmp"], ["lbtmp"])
    RECIP(lb_t, lbtmp, ["lbtmp"], ["lb_t"])
    for c in range(2):
        TS("dve", w_uq_bf[:, c, :], wst[:, c, :], qn_t[:, c:c + 1], None, ALU.mult, None, ["wst", "qn"], ["w_uq_bf%d" % c])
        TS("dve", w_uqs_bf[:, c, :], wst2[:, c, :], qn_t[:, c:c + 1], None, ALU.mult, None, ["wst2", "qn"], ["w_uqs_bf%d" % c])
    TS("dve", w_kn_bf, wst3[:, 0:512], kvn_t[:, 0:1], None, ALU.mult, None, ["wst3a", "kvn"], ["w_kn_bf"])
    TS("dve", w_v_bf, wst3[:, 512:1024], kvn_t[:, 0:1], None, ALU.mult, None, ["wst3b", "kvn"], ["w_v_bf"])
    WUQ = ["w_uq_bf0", "w_uq_bf1"]
    WUQS = ["w_uqs_bf0", "w_uqs_bf1"]

    ACT(sT, cT, AF.Silu, ["cT"], ["sT"])
    WAa = ["wa%d_a" % k for k in range(8)]
    WA = ["wa%d_b" % k for k in range(8)]
    psM = ps[0][:, 0:144]
    psM2 = ps[3][:, 0:144]
    MMG([(psM[:, j * 3:(j + 1) * 3], wa[:, kc, j * 128:(j + 1) * 128], sT[:, kc, :], kc == 0, kc == 7)
         for j in range(16) for kc in range(8)], WAa + ["sT"], [PS[0]])
    for b in range(3):
        TT("dve", modT[:, 0:16, b], ps[0][:, b:48:3], badaT[:, 0:16], ALU.add, [PS[0], "badaT"], ["modT%d" % b])
    MODT = ["modT0", "modT1", "modT2"]
    MMG([(psM2[:, j * 3:(j + 1) * 3], wa[:, kc, j * 128:(j + 1) * 128], sT[:, kc, :], kc == 0, kc == 7)
         for j in range(16, 48) for kc in range(8)], WA + ["sT"], [PS[3]])
    MODT2 = ["modTb0", "modTb1", "modTb2"]
    for b in range(3):
        TT("dve", modT[:, 16:48, b], ps[3][:, 48 + b:144:3], badaT[:, 16:48], ALU.add, [PS[3], "badaT"], ["modTb%d" % b])
    for b in range(3):
        STT("dve", A1[:, b, :], modT[:, 8:16, b], 1.0, nmix_t, ALU.add, ALU.mult, MODT + ["nmix"], ["A1_%d" % b])
        STT("dve", A2[:, b, :], modT[:, 32:40, b], 1.0, nmlp_t, ALU.add, ALU.mult, MODT2 + ["nmlp"], ["A2_%d" % b])
    for b in range(NB):
        for kc in range(8):
            CP("dve", sTb[:, b, kc, :], sT[:, kc, b:b + 1].to_broadcast([128, 128]), ["sT"], ["sTb%d_%d" % (b, kc)])
        for half in range(2):
            pg = ps[1 + half]
            MMG([(pg, sTb[:, b, kc, :], wa[:, kc, 2048 + half * 512:2048 + (half + 1) * 512], kc == 0, kc == 7)
                 for kc in range(8)], WA + ["sTb%d_%d" % (b, kc) for kc in range(8)], [PS[1 + half]])
            TT("dve", G1[:, b, half * 512:(half + 1) * 512], pg, bada_g1[:, half * 512:(half + 1) * 512], ALU.add,
               [PS[1 + half], "bada_g1"], ["G1_%d_%d" % (b, half)])
    dump("modT", modT, MODT + MODT2)
    dump("G1", G1, ["G1_%d_%d" % (b, h) for b in range(NB) for h in range(2)])
    dump("lb", lb_t, ["lb_t"])
    S.barrier()
    if stop == 'setup':
        S.finalize(); S.emit(); return nc
    A.release(m_setup)

    o_mlaT = A.alloc([128, 4, T], BF16)
    o_hgT = A.alloc([128, 4, T], BF16)
    m_fin = A.mark()
    hT = A.alloc([128, 8, TE], BF16)
    m_batch = A.mark()

    for b in range(NB):
        bn = "b%d_" % b

        def HK(g):
            return [bn + "hT%d_%d" % (g, c) for c in range(8)]

        def HKB(t0, n):
            r = []
            for g in range(t0 // 128, (t0 + n) // 128):
                r += HK(g)
            return r

        A.release(m_batch)
        cs = A.alloc([128, 2, T], BF16)
        w_mla = A.alloc([128, 8, 416], BF16)
        w_krs_bf = A.alloc([128, 8, 96], BF16)
        if b > 0:
            DMA("sp", cs, c_cs, [], [bn + "cs"])
            DMA("pool", w_mla, w_in[:, :, 0:416], [], [bn + "w_mla"])
            DMA("pool", w_krs_bf, w_krs, [], [bn + "w_krs"])
        m_pa = A.mark()
        xsb = [A.alloc([128, D], BF16) for _ in range(2)]
        junk = A.alloc([128, D], BF16)
        ssA = A.alloc([128, NT], F32)
        lnA = A.alloc([128, NT], F32)
        rsA = A.alloc([128, NT], F32)
        NXB = 3
        xtb = [A.alloc([128, D], F32) for _ in range(NXB)]

        def pa1(g):
            src = ctx[b, g * 128:(g + 1) * 128, :] if g < 2 else x[b, (g - 2) * 128:(g - 1) * 128, :]
            k = g % NXB
            xk = bn + "xt%d" % k
            DMA("sp", xtb[k], src, [], [xk])
            ACT(junk, xtb[k], AF.Square, [xk], [bn + "junk", bn + "ssA%d" % g], accum=ssA[:, g:g + 1])
            rstd_chain(ssA[:, g:g + 1], lnA[:, g:g + 1], rsA[:, g:g + 1], 1.0 / D,
                       [bn + "ssA%d" % g, "epsc"], [bn + "lnA%d" % g], [bn + "rsA%d" % g])

        def pa2(g):
            k = g % NXB
            xk, sk = bn + "xt%d" % k, bn + "xs%d" % (g % 2)
            TS("dve", xsb[g % 2], xtb[k], rsA[:, g:g + 1], None, ALU.mult, None, [xk, bn + "rsA%d" % g], [sk])
            pt = psb[g % 2]
            TRG([(pt[:, c * 128:(c + 1) * 128], xsb[g % 2][:, c * 128:(c + 1) * 128]) for c in range(8)], ident,
                [sk, "ident"], [PS[g % 2]])

        def pa3(g):
            bb = 2 if g < 2 else b
            pt = psb[g % 2]
            for c in range(8):
                dst = hT[:, c, g * 128:(g + 1) * 128]
                if g % 4 == 0:
                    ACT(dst, pt[:, c * 128:(c + 1) * 128], AF.Identity, [PS[g % 2], "A1_%d" % bb] + MODT, [bn + "hT%d_%d" % (g, c)],
                        scale=A1[:, bb, c:c + 1], bias=modT[:, c, bb:bb + 1])
                else:
                    TS("dve", dst, pt[:, c * 128:(c + 1) * 128], A1[:, bb, c:c + 1], modT[:, c, bb:bb + 1], ALU.mult, ALU.add,
                       [PS[g % 2], "A1_%d" % bb] + MODT, [bn + "hT%d_%d" % (g, c)])

        for tt_ in range(NT + 2):
            if tt_ - 2 >= 0:
                pa3(tt_ - 2)
            if 0 <= tt_ - 1 < NT:
                pa2(tt_ - 1)
            if tt_ < NT:
                pa1(tt_)
        if b == 0:
            dump("hT", hT, [k for g in range(NT) for k in HK(g)])

        if stop == 'A' and b == 0:
            S.barrier(); S.finalize(); S.emit(); return nc
        S.barrier()
        A.release(m_pa)
        m_mla = m_batch
        cqnT = A.alloc([128, 2, TE], BF16)
        ckvnT = A.alloc([128, TE], BF16)
        krotT = A.alloc([128, TE], BF16)
        Vaug = A.alloc([128, NT, 8, 128], BF16)
        KTb = [A.alloc([128, TE], BF16) for _ in range(2)]
        QTb = [A.alloc([128, T], BF16) for _ in range(2)]
        PTb = [A.alloc([128, 512], BF16) for _ in range(3)]
        sq = A.alloc([128, 3, 512], BF16)
        tA = A.alloc([128, 512], F32)
        tB = A.alloc([128, 512], F32)
        rq_bc = A.alloc([128, 512], F32)
        rkv_bc = A.alloc([128, 512], F32)
        rp1 = [A.alloc([128, 512], F32) for _ in range(1)]
        rp2 = [A.alloc([128, 512], F32) for _ in range(1)]
        rden = [A.alloc([128, 512], F32) for _ in range(2)]

        MSET("dve", Vaug, 1.0, [bn + "Vp%d" % g for g in range(NT)])
        if b == 0:
            DMA("sp", cs, c_cs, [], [bn + "cs"])
            DMA("pool", w_mla, w_in[:, :, 0:416], [], [bn + "w_mla"])
            DMA("pool", w_krs_bf, w_krs, [], [bn + "w_krs"])
        WM = [bn + "w_mla"]

        v_defer = []
        for bi, (t0, n) in enumerate(BLKS):
            hk = HKB(t0, n)
            tok = slice(t0, t0 + n)
            for c in range(2):
                MMG([(ps[c][:, 0:n], w_mla[:, kc, c * 128:(c + 1) * 128], hT[:, kc, tok], kc == 0, kc == 7) for kc in range(8)],
                    WM + hk, [PS[c]])
            MMG([(ps[2][:, 0:n], w_mla[:, kc, 256:384], hT[:, kc, tok], kc == 0, kc == 7) for kc in range(8)], WM + hk, [PS[2]])
            MMG([(ps[3][0:96, 0:n], w_mla[:, kc, 320:416], hT[:, kc, tok], kc == 0, kc == 7) for kc in range(8)], WM + hk, [PS[3]])
            if bi > 0:
                MMG([(ps[4][0:96, 0:n], w_krs_bf[:, kc, :], hT[:, kc, tok], kc == 0, kc == 7) for kc in range(8)],
                    [bn + "w_krs"] + hk, [PS[4]])
            while v_defer:
                v_defer.pop(0)()
            for c in range(3):
                ACT(sq[:, c, 0:n], ps[c][:, 0:n], AF.Square, [PS[c]], [bn + "sq%d" % c])
            MMG([(ps[5][:, 0:n], ones_bf, sq[:, 0, 0:n], True, False), (ps[5][:, 0:n], ones_bf, sq[:, 1, 0:n], False, True)],
                ["ones", bn + "sq0", bn + "sq1"], [PS[5]])
            MMG([(ps[6][:, 0:n], ones_bf, sq[:, 2, 0:n], True, True)], ["ones", bn + "sq2"], [PS[6]])
            rstd_chain(ps[5][:, 0:n], tA[:, 0:n], rq_bc[:, 0:n], 1.0 / 256, [PS[5], "epsc"], [bn + "tA"], [bn + "rq_bc"])
            rstd_chain(ps[6][:, 0:n], tB[:, 0:n], rkv_bc[:, 0:n], 1.0 / 128, [PS[6], "epsc"], [bn + "tB"], [bn + "rkv_bc"])
            for c in range(2):
                TT("dve", cqnT[:, c, tok], ps[c][:, 0:n], rq_bc[:, 0:n], ALU.mult, [PS[c], bn + "rq_bc"], [bn + "cqnT%d_%d" % (bi, c)])
            TT("dve", ckvnT[:, tok], ps[2][:, 0:n], rkv_bc[:, 0:n], ALU.mult, [PS[2], bn + "rkv_bc"], [bn + "ckvnT%d" % bi])
            if bi == 0:
                CP("dve", krotT[64:96, tok], ps[3][64:96, 0:n], [PS[3]], [bn + "krotT%d" % bi])
            else:
                lt = slice(t0 - L, t0 - L + n)
                TT("dve", rp1[0][64:96, 0:n], ps[3][64:96, 0:n], cs[64:96, 0, lt], ALU.mult, [PS[3], bn + "cs"], [bn + "rp1_0"])
                TT("dve", rp2[0][64:96, 0:n], ps[4][64:96, 0:n], cs[64:96, 1, lt], ALU.mult, [PS[4], bn + "cs"], [bn + "rp2_0"])
                TT("dve", krotT[64:96, tok], rp1[0][64:96, 0:n], rp2[0][64:96, 0:n], ALU.add, [bn + "rp1_0", bn + "rp2_0"],
                   [bn + "krotT%d" % bi])
            def v_part(bi=bi, t0=t0, n=n):
                for g in range(t0 // 128, (t0 + n) // 128):
                    MMG([(ps[7], ckvnT[:, g * 128:(g + 1) * 128], w_v_bf, True, True)], [bn + "ckvnT%d" % bi, "w_v_bf"], [PS[7]])
                    pv4 = ps[7].rearrange("p (j e c) -> p j e c", j=4, e=2)
                    veng = "act" if g % 2 else "dve"
                    CP(veng, Vaug[:, g, 0::2, 0:64], pv4[:, :, 0, :], [PS[7]], [bn + "Vp%d" % g])
                    CP(veng, Vaug[:, g, 1::2, 64:128], pv4[:, :, 1, :], [PS[7]], [bn + "Vp%d" % g])
            v_defer.append(v_part)
        while v_defer:
            v_defer.pop(0)()
        CQ = [bn + "cqnT%d_%d" % (bi, c) for bi in range(5) for c in range(2)]
        CKV = [bn + "ckvnT%d" % bi for bi in range(5)]
        KROT = [bn + "krotT%d" % bi for bi in range(5)]
        if b == 0:
            dump("cqnT", cqnT, CQ)
            dump("ckvnT", ckvnT, CKV)
            dump("krotT", krotT[64:96, :], KROT)

        if stop == 'mlaproj' and b == 0:
            S.barrier(); S.finalize(); S.emit(); return nc
        def proj_head(h):
            kb = h % 2
            KT, QT = KTb[kb], QTb[kb]
            kk, qk = bn + "KT%d" % kb, bn + "QT%d" % kb
            pc = 0
            for bi, (t0, n) in enumerate(BLKS):
                tok = slice(t0, t0 + n)
                pb, pk = ps[7 - pc % 2], PS[7 - pc % 2]
                pc += 1
                MMG([(pb[0:64, 0:n], w_kn_bf[:, h * 64:(h + 1) * 64], ckvnT[:, tok], True, True)],
                    ["w_kn_bf", bn + "ckvnT%d" % bi], [pk])
                CP("dve", KT[0:64, tok], pb[0:64, 0:n], [pk], [kk + "_n%d" % bi])
                yield
            CP("dve", KT[64:96, :], krotT[64:96, :], KROT, [kk + "_r"])
            for j in range(4):
                q0 = j * 512
                et = slice(L + q0, L + q0 + 512)
                lt = slice(q0, q0 + 512)
                cqk = [bn + "cqnT%d_%d" % (j + 1, c) for c in range(2)]
                pb, pk = ps[7 - pc % 2], PS[7 - pc % 2]
                pc += 1
                MMG([(pb[0:96, :], w_uq_bf[:, c, h * 96:(h + 1) * 96], cqnT[:, c, et], c == 0, c == 1) for c in range(2)],
                    WUQ + cqk, [pk])
                CP("dve", QT[0:64, lt], pb[0:64, :], [pk], [qk + "_n%d" % j])
                TT("dve", rp1[0][64:96, :], pb[64:96, :], cs[64:96, 0, lt], ALU.mult, [pk, bn + "cs"], [bn + "rp1_0"])
                yield
                pb, pk = ps[7 - pc % 2], PS[7 - pc % 2]
                pc += 1
                MMG([(pb[0:96, :], w_uqs_bf[:, c, h * 96:(h + 1) * 96], cqnT[:, c, et], c == 0, c == 1) for c in range(2)],
                    WUQS + cqk, [pk])
                TT("dve", rp2[0][64:96, :], pb[64:96, :], cs[64:96, 1, lt], ALU.mult, [pk, bn + "cs"], [bn + "rp2_0"])
                TT("dve", QT[64:96, lt], rp1[0][64:96, :], rp2[0][64:96, :], ALU.add, [bn + "rp1_0", bn + "rp2_0"], [qk + "_r%d" % j])
                yield

        def KTK(h):
            kk = bn + "KT%d" % (h % 2)
            return [kk + "_n%d" % bi for bi in range(5)] + [kk + "_r"]

        def QTK(h, j):
            qk = bn + "QT%d" % (h % 2)
            return [qk + "_n%d" % j, qk + "_r%d" % j]

        steps = [(h, qb, kt) for h in range(8) for qb in range(4) for kt in range(NT)]
        for _ in proj_head(0):
            pass
        pgen = None
        if b == 0:
            dump("KT0", KTb[0][0:96, :], KTK(0))
            dump("QT0", QTb[0][0:96, :], [k for j in range(4) for k in QTK(0, j)])

        SB = [0, 1, 2, 5]

        def reg_qk(si):
            h, qb, kt = steps[si]
            KT, QT = KTb[h % 2], QTb[h % 2]
            sb_ = SB[si % 4]
            MMG([(ps[sb_], KT[0:96, kt * 128:(kt + 1) * 128], QT[0:96, qb * 512:(qb + 1) * 512], True, True)],
                KTK(h) + QTK(h, qb), [PS[sb_]])

        reg_qk(0)
        reg_qk(1)
        reg_qk(2)
        for si, (h, qb, kt) in enumerate(steps):
            u = (h * 4 + qb) % 2
            pO = ps[3 + u]
            sb_ = SB[si % 4]
            ACT(PTb[si % 3], ps[sb_], AF.Exp, [PS[sb_]], [bn + "PT%d" % (si % 3)] + (["convgate"] if (b == 0 and si == 0) else []),
                scale=ATT_SCALE)
            if si + 3 < len(steps):
                reg_qk(si + 3)
            MMG([(pO, Vaug[:, kt, h, :], PTb[si % 3], kt == 0, kt == NT - 1)],
                [bn + "Vp%d" % kt, bn + "PT%d" % (si % 3)], [PS[3 + u]])
            if kt == NT - 1:
                orow = slice((h % 2) * 64, (h % 2) * 64 + 64)
                drow = slice(64 - (h % 2) * 64, 128 - (h % 2) * 64)
                RECIP(rden[u][orow, :], pO[drow, :], [PS[3 + u]], [bn + "rden%d" % u])
                TT("dve", o_mlaT[orow, h // 2, qb * 512:(qb + 1) * 512], pO[orow, :], rden[u][orow, :], ALU.mult,
                   [PS[3 + u], bn + "rden%d" % u], [bn + "omla%d_%d" % (h, qb)])
            if b == 0 and kt == 0 and qb == 0 and h == 0:
                for p in range(8):
                    DMA("pool", w1s[p], w1[p], ["convgate"], ["w1s%d" % p])
                for p in range(8):
                    DMA("pool", w2s[p], w2[p], [], ["w2s%d" % p])
            if qb == 0 and kt == 2 and h + 1 < 8:
                pgen = proj_head(h + 1)
            if pgen is not None and si % 4 == 0:
                try:
                    next(pgen)
                except StopIteration:
                    pgen = None
        OM = [bn + "omla%d_%d" % (h, qb) for h in range(8) for qb in range(4)]
        if b == 0:
            dump("o_mlaT", o_mlaT, OM)
        if stop == 'attn' and b == 0:
            S.barrier(); S.finalize(); S.emit(); return nc
        S.barrier()
        A.release(m_mla)

        v_tm = A.alloc([128, NT, 512], BF16)
        gate = A.alloc([128, 16, 512], BF16)
        eb_tab = A.alloc([128, 4, 2, NT], F32)
        m_h0 = A.mark()
        w_hig = A.alloc([128, 8, 1024], BF16)
        gtmp = [A.alloc([128, 512], F32) for _ in range(2)]
        DMA("pool", w_hig[:, :, 0:512], w_in[:, :, 1952:2464], [], [bn + "w_hig_v"])
        DMA("pool", w_hig[:, :, 512:1024], w_in[:, :, 2464:2976], [], [bn + "w_hig_g"])
        for g in range(NT):
            k = g % 2
            MMG([(ps[k], hT[:, kc, g * 128:(g + 1) * 128], w_hig[:, kc, 0:512], kc == 0, kc == 7) for kc in range(8)],
                [bn + "w_hig_v"] + HK(g), [PS[k]])
            CP("dve" if g % 2 else "act", v_tm[:, g, :], ps[k], [PS[k]], [bn + "v_tm%d" % g])
            if g >= 2:
                MMG([(ps[2 + k], hT[:, kc, g * 128:(g + 1) * 128], w_hig[:, kc, 512:1024], kc == 0, kc == 7) for kc in range(8)],
                    [bn + "w_hig_g"] + HK(g), [PS[2 + k]])
                ACT(gtmp[k], ps[2 + k], AF.Silu, [PS[2 + k]], [bn + "gtmp%d" % k])
                TT("dve", gate[:, g - 2, :], gtmp[k], gn_bc, ALU.mult, [bn + "gtmp%d" % k] + GN, [bn + "gate%d" % (g - 2)])
        if b == 0:
            dump("v_tm", v_tm, [bn + "v_tm%d" % g for g in range(NT)])
            dump("gate", gate, [bn + "gate%d" % j for j in range(16)])
        if stop == 'hg0' and b == 0:
            S.barrier(); S.finalize(); S.emit(); return nc
        S.barrier()
        A.release(m_h0)

        NB_H = 9
        NSET = 4
        whb = [A.alloc([128, 8, 384], BF16) for _ in range(1)]
        qTs = [A.alloc([128, T], BF16) for _ in range(2)]
        kTs = [A.alloc([128, T], BF16) for _ in range(2)]
        khat = A.alloc([128, NT, 2, 128], BF16)
        S_st = A.alloc([128, 2, 17, 128], BF16)
        khT = [A.alloc([128, 256], BF16) for _ in range(2)]
        ktmp = [A.alloc([128, 256], BF16) for _ in range(1)]
        TS1 = [A.alloc([128, 256], F32) for _ in range(NSET)]
        TS2 = [A.alloc([128, 256], F32) for _ in range(NSET)]
        TS3 = [A.alloc([128, 256], F32) for _ in range(NSET)]
        TSq = [A.alloc([128, 256], F32) for _ in range(NSET)]
        TSz = [A.alloc([128, 256], BF16) for _ in range(NSET)]
        ATb = A.alloc([128, 4, 256], BF16)
        o_sb = A.alloc([128, 4, 128], F32)
        og = A.alloc([128, 4, 128], BF16)
        ssh = A.alloc([128, 4], F32)
        lnh = A.alloc([128, 4], F32)
        rsh = A.alloc([128, 4], F32)
        junk2 = A.alloc([128, 128], BF16)

        units = []
        for h in range(4):
            fo = [(k_, 0) for k_ in range(NB_H)]
            bo = [(k_, 1) for k_ in [0] + list(range(NB_H - 1, 0, -1))]
            seq = (bo + fo) if h % 2 == 0 else (fo + bo)
            for (k_, d_) in seq:
                units.append((h, k_, d_))
        NU = len(units)

        def hkeys(h):
            return bn + "h%d_" % h

        def load_wh(h):
            wh = whb[0]
            whk = bn + "wh0"
            for i, c0 in enumerate((416, 928, 1440)):
                DMA("pool", wh[:, :, i * 128:(i + 1) * 128], w_in[:, :, c0 + h * 128:c0 + (h + 1) * 128], [], [whk + "_%d" % i])

        def uinfo(ui):
            h, k, d = units[ui]
            return h, k, d, whb[0], [bn + "wh0_%d" % i for i in range(3)], hkeys(h), ui % NSET, k > 0

        def st0(ui):
            h, k, d, wh, WHK, hn, s, lat = uinfo(ui)
            tok = slice(k * 256, k * 256 + 256)
            hk = HKB(k * 256, 256)
            pz = ps[ui % 2]
            MMG([(pz[:, 0:256], wh[:, kc, (1 + d) * 128:(2 + d) * 128], hT[:, kc, tok], kc == 0, kc == 7) for kc in range(8)],
                WHK + hk, [PS[ui % 2]])
            if lat:
                pq = ps[2]
                MMG([(pq[:, 0:256], wh[:, kc, 0:128], hT[:, kc, tok], kc == 0, kc == 7) for kc in range(8)], WHK + hk, [PS[2]])

        def st1(ui):
            h, k, d, wh, WHK, hn, s, lat = uinfo(ui)
            sk = bn + "ts%d_" % s
            pz = ps[ui % 2]
            ACT(TS1[s], pz[:, 0:256], AF.Exp, [PS[ui % 2]], [sk + "T1"], scale=-1.0)
            ACT(TS2[s], TS1[s], AF.Ln, [sk + "T1", "lb_t"], [sk + "T2"], scale=lb_t[:, d * 4 + h:d * 4 + h + 1], bias=1.0)
            ACT(TS1[s], TS1[s], AF.Ln, [sk + "T1"], [sk + "T1"], bias=1.0)
            if lat:
                pq = ps[2]
                ACT(TSq[s], pq[:, 0:256], AF.Exp, [PS[2]], [sk + "Tq"], scale=-1.0)
                ACT(TSz[s], pq[:, 0:256], AF.Copy, [PS[2]], [sk + "Tz"])
                ACT(TSq[s], TSq[s], AF.Ln, [sk + "Tq"], [sk + "Tq"], bias=1.0)

        def st2(ui):
            h, k, d, wh, WHK, hn, s, lat = uinfo(ui)
            sk = bn + "ts%d_" % s
            TT("dve", TS2[s], TS2[s], TS1[s], ALU.subtract, [sk + "T1", sk + "T2"], [sk + "T2"])
            if d == 0:
                SCAN(TS3[s], rmask[:, 0, 0:256], TS2[s], ["rmask", sk + "T2"], [sk + "T3"])
            else:
                SCAN(TS3[s][:, ::-1], rmask[:, 1, 0:256][:, ::-1], TS2[s][:, ::-1], ["rmask", sk + "T2"], [sk + "T3"])
            if lat:
                TT("dve", TSq[s], TS3[s], TSq[s], ALU.subtract, [sk + "T3", sk + "Tq"], [sk + "Tq"])

        def st3(ui):
            h, k, d, wh, WHK, hn, s, lat = uinfo(ui)
            sk = bn + "ts%d_" % s
            ACT(TS1[s], TS2[s], AF.Exp, [sk + "T2"], [sk + "T1"])
            lastcol = 127 if d == 0 else 0
            ACT(eb_tab[:, h, d, 2 * k:2 * k + 2], TS3[s][:, lastcol:256:128], AF.Exp, [sk + "T3"], [hn + "eb%d_%d" % (d, k)])
            if lat:
                ACT(TSq[s], TSq[s], AF.Exp, [sk + "Tq"], [sk + "Tq"])
            ACT(TS3[s], TS3[s], AF.Exp, [sk + "T3"], [sk + "T3"], scale=-1.0)

        def st4(ui):
            h, k, d, wh, WHK, hn, s, lat = uinfo(ui)
            sk = bn + "ts%d_" % s
            if lat:
                lt = slice((k - 1) * 256, k * 256)
                STT("dve", qTs[d][:, lt], TSz[s], -1.0, TSq[s], ALU.mult, ALU.mult, [sk + "Tz", sk + "Tq"], [bn + "qT%d_%d" % (d, k)])
                kdst = kTs[d][:, lt]
                kkey = bn + "kT%d_%d" % (d, k)
            else:
                kdst = ktmp[0]
                kkey = bn + "ktmp0"
            STT("dve", kdst, TS1[s], 1.0, TS3[s], ALU.subtract, ALU.mult, [sk + "T1", sk + "T3"], [kkey])
            kh = khT[ui % 2]
            for c in range(2):
                TS("dve", kh[:, c * 128:(c + 1) * 128], kdst[:, c * 128:(c + 1) * 128], eb_tab[:, h, d, 2 * k + c:2 * k + c + 1], None,
                   ALU.mult, None, [kkey, hn + "eb%d_%d" % (d, k)], [bn + "khT%d_%d" % (ui % 2, c)])

        def st5(ui):
            h, k, d, wh, WHK, hn, s, lat = uinfo(ui)
            kh = khT[ui % 2]
            ptk = psb[4 + ui % 2]
            TRG([(ptk[:, c * 128:(c + 1) * 128], kh[:, c * 128:(c + 1) * 128]) for c in range(2)], ident,
                [bn + "khT%d_%d" % (ui % 2, c) for c in range(2)] + ["ident"], [PS[4 + ui % 2]])

        def st6(ui):
            h, k, d, wh, WHK, hn, s, lat = uinfo(ui)
            ptk = psb[4 + ui % 2]
            CP("act" if ui % 2 else "dve", khat[:, 2 * k:2 * k + 2, d, :], ptk[:, 0:256].rearrange("p (c k) -> p c k", c=2),
               [PS[4 + ui % 2]], [bn + "khat%d_%d" % (d, k)])

        def scan_mm(h, g, d, k):
            hn = hkeys(h)
            pS = ps[6][:, d * 128:(d + 1) * 128]
            MMG([(pS, khat[:, g, d, :], v_tm[:, g, h * 128:(h + 1) * 128], True, True)],
                [bn + "khat%d_%d" % (d, k), bn + "v_tm%d" % g], [PS[6]])

        def scan_upd(h, g, d, k, p):
            hn = hkeys(h)
            pS = ps[6][:, d * 128:(d + 1) * 128]
            if p == 0:
                CP("dve", S_st[:, d, 0, :], pS, [PS[6]], [bn + "S%d_%d" % (d, 0)])
            else:
                STT("dve", S_st[:, d, p, :], S_st[:, d, p - 1, :], eb_tab[:, h, d, g:g + 1], pS, ALU.mult, ALU.add,
                    [bn + "S%d_%d" % (d, p - 1), hn + "eb%d_%d" % (d, k), PS[6]], [bn + "S%d_%d" % (d, p)])

        def grp_piece(h, gq, step):
            hn = hkeys(h)
            pO = ps[7]

            def pa_mm(i):
                j = 4 * gq + i
                tl = slice(j * 128, (j + 1) * 128)
                kb = j // 2 + 1
                pA = ps[3][:, (i % 2) * 256:(i % 2) * 256 + 256]
                MMG([(pA[:, 0:128], kTs[0][:, tl], qTs[0][:, tl], True, True),
                     (pA[:, 128:256], kTs[1][:, tl], qTs[1][:, tl], True, True)],
                    [bn + "kT%d_%d" % (d, kb) for d in range(2)] + [bn + "qT%d_%d" % (d, kb) for d in range(2)], [PS[3]])

            def mask2(i0):
                TT("dve", ATb[:, i0:i0 + 2, :], ps[3].rearrange("p (i c) -> p i c", i=2), maskfb2, ALU.mult, [PS[3], "maskfb"],
                   [bn + "AT%d" % i0, bn + "AT%d" % (i0 + 1)])

            def po_mm(i):
                j = 4 * gq + i
                g = j + 2
                kb = j // 2 + 1
                tl = slice(j * 128, (j + 1) * 128)
                vv = v_tm[:, g, h * 128:(h + 1) * 128]
                MMG([(pO[:, i * 128:(i + 1) * 128], ATb[:, i, 0:128], vv, True, False),
                     (pO[:, i * 128:(i + 1) * 128], ATb[:, i, 128:256], vv, False, False),
                     (pO[:, i * 128:(i + 1) * 128], qTs[0][:, tl], S_st[:, 0, j + 1, :], False, False),
                     (pO[:, i * 128:(i + 1) * 128], qTs[1][:, tl], S_st[:, 1, 16 - j, :], False, True)],
                    [bn + "AT%d" % i, bn + "v_tm%d" % g, bn + "qT0_%d" % kb, bn + "qT1_%d" % kb,
                     bn + "S0_%d" % (j + 1), bn + "S1_%d" % (16 - j)], [PS[7]])

            if step == 1:
                pa_mm(0)
                pa_mm(1)
                mask2(0)
            elif step == 2:
                po_mm(0)
                po_mm(1)
                pa_mm(2)
                pa_mm(3)
                mask2(2)
            elif step == 3:
                po_mm(2)
                po_mm(3)
                CP("dve", o_sb, pO.rearrange("p (i c) -> p i c", i=4), [PS[7]], [bn + "o_sb"])
                for i in range(4):
                    ACT(junk2, o_sb[:, i, :], AF.Square, [bn + "o_sb"], [bn + "junk2", bn + "ssh%d" % i], accum=ssh[:, i:i + 1])
                rstd_chain(ssh, lnh, rsh, 1.0 / 128, [bn + "ssh%d" % i for i in range(4)] + ["epsc"], [bn + "lnh"], [bn + "rsh"])
            elif step == 4:
                for i in range(4):
                    j = 4 * gq + i
                    STT("dve", og[:, i, :], o_sb[:, i, :], rsh[:, i:i + 1], gate[:, j, h * 128:(h + 1) * 128],
                        ALU.mult, ALU.mult, [bn + "o_sb", bn + "rsh", bn + "gate%d" % j], [bn + "og%d" % i])
            elif step == 5:
                pT = psb[6][:, 512:1024]
                TRG([(pT[:, i * 128:(i + 1) * 128], og[:, i, :]) for i in range(4)], ident,
                    [bn + "og%d" % i for i in range(4)] + ["ident"], [PS[6]])
                CP("dve", o_hgT[:, h, gq * 512:(gq + 1) * 512], pT, [PS[6]], [bn + "ohg%d_%d" % (h, gq)])

        load_wh(0)
        stages = [st0, st1, st2, st3, st4, st5, st6]
        from collections import deque
        scan_q = [deque(), deque()]
        blk_scanned = {}
        grp_todo = deque((h, gq) for h in range(4) for gq in (range(4) if h % 2 == 0 else range(3, -1, -1)))
        front = None
        back = None
        grp_done = {}
        tau = 0
        tc = 0

        def chain_hazard(tc_):
            for si in (4, 6):
                ui = tc_ - si
                if 0 <= ui < NU:
                    h, k, d = units[ui]
                    if si == 4 and k >= 1:
                        for hp in range(h):
                            if grp_done.get((hp, (k - 1) // 2), 0) < 3:
                                return True
                    if si == 6:
                        for dq_ in scan_q:
                            for ent in dq_:
                                if ent[0] < h:
                                    return True
            return False

        def scan_hazard(h, d, p):
            j = (p - 1) if d == 0 else (16 - p)
            if 0 <= j <= 15:
                for hp in range(h):
                    if grp_done.get((hp, j // 4), 0) < 3:
                        return True
            return False

        while True:
            active = False
            if tc < NU + len(stages) and not chain_hazard(tc):
                for si in (1, 2, 3, 4, 5, 6, 0):
                    ui = tc - si
                    if 0 <= ui < NU:
                        stages[si](ui)
                        h, k, d = units[ui]
                        if si == 0 and ui + 1 < NU and units[ui + 1][0] != h:
                            load_wh(h + 1)
                        if si == 6:
                            tiles = [2 * k, 2 * k + 1] if d == 0 else [2 * k + 1, 2 * k]
                            todo = []
                            for g in tiles:
                                p = g if d == 0 else (1 - g if g < 2 else 19 - g)
                                if p <= 16:
                                    todo.append((g, p))
                            for n_, (g, p) in enumerate(todo):
                                scan_q[d].append((h, g, k, p, tau + 1, n_ == len(todo) - 1))
                            if not todo:
                                blk_scanned[(h, k, d)] = tau
                tc += 1
                active = True
            steps_now = []
            for d in range(2):
                if scan_q[d] and scan_q[d][0][4] <= tau and not scan_hazard(scan_q[d][0][0], d, scan_q[d][0][3]):
                    steps_now.append((d,) + scan_q[d].popleft())
            for (d, h, g, k, p, rt, last) in steps_now:
                scan_mm(h, g, d, k)
            for (d, h, g, k, p, rt, last) in steps_now:
                scan_upd(h, g, d, k, p)
                if last:
                    blk_scanned[(h, k, d)] = tau
                active = True
            if front is not None:
                h, gq, stp = front
                grp_piece(h, gq, stp)
                grp_done[(h, gq)] = stp
                front = (h, gq, stp + 1) if stp < 5 else None
                active = True
            pending_front = None
            if back is not None:
                h, gq, stp = back
                grp_piece(h, gq, stp)
                grp_done[(h, gq)] = stp
                active = True
                if stp == 3:
                    back = None
                    pending_front = (h, gq, 4)
                else:
                    back = (h, gq, stp + 1)
            if back is None and pending_front is None and grp_todo:
                h, gq = grp_todo[0]
                need = [(h, 2 * gq + 1, 0), (h, 2 * gq + 2, 0), (h, 2 * gq + 1, 1), (h, 2 * gq + 2, 1), (h, 0, 0), (h, 0, 1)]
                if all((kk in blk_scanned and blk_scanned[kk] < tau) for kk in need):
                    grp_todo.popleft()
                    back = (h, gq, 1)
            if pending_front is not None:
                assert front is None
                front = pending_front
            tau += 1
            if not active and not grp_todo and back is None and front is None and not scan_q[0] and not scan_q[1] and tc >= NU + len(stages):
                break
            assert tau < NU + 600, "side work did not drain"
        OH = [bn + "ohg%d_%d" % (h, gq) for h in range(4) for gq in range(4)]
        if b == 0:
            dump("o_hgT", o_hgT, OH)
        _saved_off = A.off
        A.release(m_fin)
        w_out_bf = A.alloc([128, 8, D], BF16)
        _wout_end = A.off
        A.off = _saved_off
        DMA("pool", w_out_bf, w_out, [], [bn + "w_out"] + [kk_ for g_ in range(NT) for kk_ in HK(g_)])
        S.barrier()

        if stop == 'hgrn' and b == 0:
            S.barrier(); S.finalize(); S.emit(); return nc
        A.release(m_fin)
        A.off = _wout_end
        identf = A.alloc([128, 128], F32)
        DMA("sp", identf, c_identf, [], ["identf"])
        x1b = [A.alloc([128, 4, D], F32) for _ in range(2)]
        h2Tb = [A.alloc([128, 8, 512], BF16) for _ in range(2)]
        uT = A.alloc([128, 32, 512], BF16)
        xt2 = [A.alloc([128, D], F32) for _ in range(1)]
        xs2b = [A.alloc([128, D], BF16) for _ in range(2)]
        w1p = [A.alloc([128, 8, 512], BF16) for _ in range(2)]
        w2p = [A.alloc([128, 32, 128], BF16) for _ in range(2)]
        rbuf = [A.alloc([128, 512], F32) for _ in range(2)]
        yT = [A.alloc([128, 512], F32) for _ in range(2)]
        ot = [A.alloc([128, D], F32) for _ in range(1)]
        ss2 = A.alloc([128, 16], F32)
        ln2 = A.alloc([128, 16], F32)
        rs2 = A.alloc([128, 16], F32)
        ss3 = A.alloc([128, 16], F32)
        ln3 = A.alloc([128, 16], F32)
        rs3 = A.alloc([128, 16], F32)
        fn_ = bn + "f_"

        def X1K(jb, i):
            return fn_ + "x1_%d_%d" % (jb % 2, i)

        def stage_a(jb, i):
            x1 = x1b[jb % 2]
            j = jb * 4 + i
            tl = slice(j * 128, (j + 1) * 128)
            DMA("sp", xt2[0], x[b, tl, :], [], [fn_ + "xt0"])
            for half in range(2):
                MMG([(ps[half], (o_mlaT[:, c, tl] if c < 4 else o_hgT[:, c - 4, tl]), w_out_bf[:, c, half * 512:(half + 1) * 512],
                      c == 0, c == 7) for c in range(8)], OM + OH + [bn + "w_out"], [PS[half]])
                TT("dve", x1[:, i, half * 512:(half + 1) * 512], ps[half], G1[:, b, half * 512:(half + 1) * 512], ALU.mult,
                   [PS[half], "G1_%d_%d" % (b, half)], [fn_ + "x1h_%d_%d_%d" % (jb % 2, i, half), X1K(jb, i)])
            TT("dve", x1[:, i, :], x1[:, i, :], xt2[0], ALU.add,
               [fn_ + "x1h_%d_%d_0" % (jb % 2, i), fn_ + "x1h_%d_%d_1" % (jb % 2, i), fn_ + "xt0"], [X1K(jb, i)])

        def stage_b1(jb, i):
            x1 = x1b[jb % 2]
            j = jb * 4 + i
            xs2k = xs2b[i % 2]
            xsk = fn_ + "xs2_%d" % (i % 2)
            ACT(xs2k, x1[:, i, :], AF.Square, [X1K(jb, i)], [xsk, fn_ + "ss2_%d" % j], accum=ss2[:, j:j + 1])
            rstd_chain(ss2[:, j:j + 1], ln2[:, j:j + 1], rs2[:, j:j + 1], 1.0 / D, [fn_ + "ss2_%d" % j, "epsc"],
                       [fn_ + "ln2_%d" % j], [fn_ + "rs2_%d" % j])
            TS("dve", xs2k, x1[:, i, :], rs2[:, j:j + 1], None, ALU.mult, None, [X1K(jb, i), fn_ + "rs2_%d" % j], [xsk])

        def stage_b2(jb, i):
            h2T = h2Tb[jb % 2]
            xs2k = xs2b[i % 2]
            xsk = fn_ + "xs2_%d" % (i % 2)
            pt = psb[2]
            TRG([(pt[:, c * 128:(c + 1) * 128], xs2k[:, c * 128:(c + 1) * 128]) for c in range(8)], ident,
                [xsk, "ident"], [PS[2]])
            for c in range(8):
                dst = h2T[:, c, i * 128:(i + 1) * 128]
                hk_ = fn_ + "h2T%d_%d_%d" % (jb % 2, i, c)
                if i % 2 == 0:
                    ACT(dst, pt[:, c * 128:(c + 1) * 128], AF.Identity, [PS[2], "A2_%d" % b] + MODT2, [hk_],
                        scale=A2[:, b, c:c + 1], bias=modT[:, 24 + c, b:b + 1])
                else:
                    TS("dve", dst, pt[:, c * 128:(c + 1) * 128], A2[:, b, c:c + 1], modT[:, 24 + c, b:b + 1], ALU.mult, ALU.add,
                       [PS[2], "A2_%d" % b] + MODT2, [hk_])

        def prep_pieces(jb):
            return [
                [lambda: stage_a(jb, 0)],
                [lambda: stage_a(jb, 1)],
                [lambda: stage_b1(jb, 0)],
                [lambda: stage_a(jb, 2), lambda: stage_b2(jb, 0)],
                [lambda: stage_b1(jb, 1)],
                [lambda: stage_a(jb, 3), lambda: stage_b2(jb, 1)],
                [lambda: stage_b1(jb, 2)],
                [lambda: stage_b2(jb, 2), lambda: stage_b1(jb, 3)],
                [lambda: stage_b2(jb, 3)],
            ]

        otb = [ot[0], xt2[0]]
        otk = [fn_ + "ot0", fn_ + "xt0"]

        def final_norm(jb_, i):
            x1_ = x1b[jb_ % 2]
            j = jb_ * 4 + i
            xs2k = xs2b[i % 2]
            ACT(xs2k, x1_[:, i, :], AF.Square, [X1K(jb_, i)], [fn_ + "xs2_%d" % (i % 2), fn_ + "ss3_%d" % j], accum=ss3[:, j:j + 1])
            rstd_chain(ss3[:, j:j + 1], ln3[:, j:j + 1], rs3[:, j:j + 1], 1.0 / D, [fn_ + "ss3_%d" % j, "epsc"],
                       [fn_ + "ln3_%d" % j], [fn_ + "rs3_%d" % j])
            STT("dve", otb[i % 2], x1_[:, i, :], rs3[:, j:j + 1], fn_bc, ALU.mult, ALU.mult, [X1K(jb_, i), fn_ + "rs3_%d" % j, "fn_bc"],
                [otk[i % 2]])
            DMA("pool", out[b, j * 128:(j + 1) * 128, :], otb[i % 2], [otk[i % 2]], [fn_ + "out%d" % j])

        for grp in prep_pieces(0):
            for f_ in grp:
                f_()
        for jb in range(4):
            x1 = x1b[jb % 2]
            h2T = h2Tb[jb % 2]
            H2 = [fn_ + "h2T%d_%d_%d" % (jb % 2, i, c) for i in range(4) for c in range(8)]
            X1 = [X1K(jb, i) for i in range(4)]
            if b == 0 and jb == 0:
                dump("x1", x1, X1)
                dump("h2T", h2T, H2)
            for p in range(8):
                wp = w1p[p % 2]
                wk = fn_ + "w1p%d" % (p % 2)
                DMA("sp", wp, w1s[p].rearrange("q (k c) -> q k c", k=8), ["w1s%d" % p], [wk])
                for q4 in range(4):
                    jj = 4 * p + q4
                    pu = ps[3 + jj % 2]
                    MMG([(pu, wp[:, kc, q4 * 128:(q4 + 1) * 128], h2T[:, kc, :], kc == 0, kc == 7) for kc in range(8)],
                        [wk] + H2, [PS[3 + jj % 2]])
                    rb = rbuf[jj % 2]
                    ACT(rb, pu, AF.Relu, [PS[3 + jj % 2]], [fn_ + "rb%d" % (jj % 2)])
                    TT("dve", uT[:, jj, :], rb, rb, ALU.mult, [fn_ + "rb%d" % (jj % 2)], [fn_ + "uT%d" % jj])
                if jb >= 1 and p < 4:
                    final_norm(jb - 1, p)
            UT = [fn_ + "uT%d" % jj for jj in range(32)]
            if b == 0 and jb == 0:
                dump("uT", uT, UT)

            def mlp2_tail(dq):
                yk = fn_ + "yT%d" % (dq % 2)
                S.add("pe", (lambda src, dstp: (lambda e: [e.transpose(dstp[:, i * 128:(i + 1) * 128], src[:, i * 128:(i + 1) * 128], identf)
                                                         for i in range(4)][-1]))(yT[dq % 2], ps[7]),
                      [yk, "identf"], [PS[7]])
                xv = x1[:, :, dq * 128:(dq + 1) * 128]
                TT("dve", xv, xv, ps[7].rearrange("p (i c) -> p i c", i=4), ALU.add, [PS[7]] + X1, X1)

            nxt = prep_pieces(jb + 1) if jb + 1 < 4 else []
            for dq in range(8):
                wp = w2p[dq % 2]
                wk = fn_ + "w2p%d" % (dq % 2)
                DMA("sp", wp, w2s[dq].rearrange("q (j c) -> q j c", j=32), ["w2s%d" % dq], [wk])
                pv = ps[5 + dq % 2]
                MMG([(pv, wp[:, jj, :], uT[:, jj, :], jj == 0, jj == 31) for jj in range(32)], [wk] + UT, [PS[5 + dq % 2]])
                yk = fn_ + "yT%d" % (dq % 2)
                ACT(yT[dq % 2], pv, AF.Identity, [PS[5 + dq % 2]] + MODT2, [yk], scale=modT[:, 40 + dq, b:b + 1])
                if dq >= 1:
                    mlp2_tail(dq - 1)
                if nxt:
                    for f_ in nxt.pop(0):
                        f_()
            mlp2_tail(7)
            while nxt:
                for f_ in nxt.pop(0):
                    f_()
            if jb == 3:
                for i in range(4):
                    final_norm(jb, i)
        S.barrier()
        if stop == 'b0' and b == 0:
            S.barrier(); S.finalize(); S.emit(); return nc

    S.finalize()
    S.emit()
    return nc


def _consts():
    bf = ml_dtypes.bfloat16
    ident = np.eye(128, dtype=np.float32)
    s = np.arange(128)[:, None]
    t = np.arange(128)[None, :]
    mask = np.concatenate([(s <= t), (s >= t)], axis=1).astype(np.float32)
    rm = np.ones((128, 2, 512), np.float32)
    rm[:, 0, 0::128] = 0.0
    rm[:, 1, 127::128] = 0.0
    tok = np.arange(T)
    row = (tok // 64).astype(np.float32)
    col = (tok % 64).astype(np.float32)
    nfreq = 8
    inv = (np.float32(10000.0) ** (-np.arange(nfreq, dtype=np.float32) / np.float32(nfreq))).astype(np.float32)
    cs = np.zeros((128, 2, T), np.float32)
    for dmm in range(32):
        grp, i = dmm // 16, dmm % 16
        f = i % 8
        ang = (row if grp == 0 else col) * inv[f]
        c, sn = np.cos(ang.astype(np.float32)), np.sin(ang.astype(np.float32))
        sign = -1.0 if i < 8 else 1.0
        for q in range(4):
            cs[q * 32 + dmm, 0] = c
            cs[q * 32 + dmm, 1] = sign * sn
    return dict(c_ident=ident.astype(bf), c_identf=ident, c_mask=mask.astype(bf), c_rmask=rm.astype(bf), c_cs=cs.astype(bf))


def _swap_rope(w, base, period):
    w = w.copy()
    ncol = w.shape[-1]
    for h0 in range(0, ncol, period):
        r0 = h0 + base
        blk = w[..., r0:r0 + 32].copy()
        new = blk.copy()
        for grp in range(2):
            o = grp * 16
            new[..., o:o + 8] = blk[..., o + 8:o + 16]
            new[..., o + 8:o + 16] = blk[..., o:o + 8]
        w[..., r0:r0 + 32] = new
    return w


def _kp(w, nk):
    return np.ascontiguousarray(w.reshape(nk, 128, -1).transpose(1, 0, 2))


def _shared_inputs(inp):
    f = np.float32
    w_in = inp["w_in"][0]
    w_uq = inp["w_uq"][0]
    w_ukv = inp["w_ukv"][0].reshape(128, 8, 128)
    kr = np.zeros((1024, 96), f)
    kr[:, 64:96] = w_in[:, 384:416]
    kr = _swap_rope(kr, 64, 96)
    w1 = inp["w_mlp_in"][0]
    w1r = np.ascontiguousarray(w1.reshape(8, 128, 8, 512).transpose(2, 1, 0, 3)).reshape(8, 128, 4096)
    w2 = inp["w_mlp_out"][0]
    w2r = np.ascontiguousarray(w2.reshape(32, 128, 8, 128).transpose(2, 1, 0, 3)).reshape(8, 128, 4096)
    sh = dict(
        w_ada=_kp(inp["w_ada"][0], 8),
        b_adaT=np.ascontiguousarray(inp["b_ada"][0].reshape(48, 128).T),
        b_ada=np.ascontiguousarray(inp["b_ada"][0]),
        nmixT=np.ascontiguousarray(inp["norm_mix"][0].reshape(8, 128).T),
        nmlpT=np.ascontiguousarray(inp["norm_mlp"][0].reshape(8, 128).T),
        w_in=_kp(w_in, 8),
        w_krs=_kp(kr, 8),
        qnT=np.ascontiguousarray(inp["q_norm"][0].reshape(2, 128).T),
        kvnT=np.ascontiguousarray(inp["kv_norm"][0].reshape(128, 1)),
        w_uq=_kp(w_uq, 2),
        w_uqs=_kp(_swap_rope(w_uq, 64, 96), 2),
        w_kn=np.ascontiguousarray(w_ukv[:, :, 0:64].reshape(128, 512)),
        w_v=np.ascontiguousarray(w_ukv[:, :, 64:128].reshape(128, 512)),
        lbT=np.ascontiguousarray(inp["hgrn_lb"].reshape(2, 8, 128).transpose(2, 0, 1)),
        hgn=np.ascontiguousarray(inp["hgrn_norm"][0]),
        w_out=_kp(inp["w_out"][0], 8),
        w1=w1r, w2=w2r,
        fnorm=np.ascontiguousarray(inp["final_norm"]),
    )
    sh = {k: np.ascontiguousarray(v, dtype=f) for k, v in sh.items()}
    sh.update(_consts())
    return sh


_NC_CACHE = {}


def kernel(**inputs):
    inp = {k: np.asarray(v) for k, v in inputs.items()}
    shared = _shared_inputs(inp)
    in_maps = []
    for c in range(8):
        b0 = c * NB
        cv = np.stack([inp["c"][b0], inp["c"][b0 + 1], inp["c_ctx"]], axis=0)
        m = dict(shared)
        m["x"] = np.ascontiguousarray(inp["x"][b0:b0 + NB], dtype=np.float32)
        m["ctx"] = np.ascontiguousarray(inp["ctx"][b0:b0 + NB], dtype=np.float32)
        m["cvecT"] = np.ascontiguousarray(cv.reshape(3, 8, 128).transpose(2, 1, 0), dtype=np.float32)
        in_maps.append(m)
    if "nc" not in _NC_CACHE:
        _NC_CACHE["nc"] = build_program()
    res = run_bass_kernel_spmd(_NC_CACHE["nc"], in_maps, core_ids=list(range(8)))
    return np.concatenate([np.asarray(r["out"]) for r in res.results], axis=0).astype(np.float32)
```

```python
import numpy as np
import ml_dtypes
import concourse.bass as bass
import concourse.mybir as mybir
from concourse.bass_utils import run_bass_kernel_spmd

F32 = mybir.dt.float32
BF16 = mybir.dt.bfloat16
AF = mybir.ActivationFunctionType
ALU = mybir.AluOpType

NB = 2
T = 2048
L = 256
TE = T + L
D = 1024
NT = TE // 128
DFF = 4096
EPS = 1e-6
BLKS = [(0, 256)] + [(256 + 512 * j, 512) for j in range(4)]
ATT_SCALE = float(96 ** -0.5)
DMA_K = 8


class _Op:
    __slots__ = ("idx", "eng", "fn", "dma", "deps", "signal", "tick", "sem_i", "sem_v")

    def __init__(self, idx, eng, fn, dma, deps):
        self.idx, self.eng, self.fn, self.dma, self.deps = idx, eng, fn, dma, deps
        self.signal = False
        self.tick = 0
        self.sem_i = 0
        self.sem_v = 0


class Sched:
    ENGS = ("pe", "act", "dve", "pool", "sp")

    def __init__(self, nc):
        self.nc = nc
        self.ops = []
        self.last_w = {}
        self.readers = {}
        self.dma_ops = {"sp": [], "pool": [], "act": []}

    def add(self, eng, fn, r=(), w=(), dma=False):
        idx = len(self.ops)
        deps = {}
        for k in r:
            p = self.last_w.get(k)
            if p is not None:
                deps[p] = "raw"
        for k in w:
            p = self.last_w.get(k)
            if p is not None and p not in deps:
                deps[p] = "waw"
            for q in self.readers.get(k, ()):
                if q not in deps:
                    deps[q] = "war"
        if dma:
            lst = self.dma_ops[eng]
            if len(lst) >= DMA_K:
                deps[lst[-DMA_K]] = "raw"
            lst.append(idx)
        op = _Op(idx, eng, fn, dma, deps)
        for k in w:
            self.last_w[k] = idx
            self.readers[k] = []
        for k in r:
            self.readers.setdefault(k, []).append(idx)
        self.ops.append(op)
        return op

    def barrier(self):
        last = {}
        for op in self.ops:
            if not op.dma and op.fn is not None:
                last[op.eng] = op.idx
        dmas = [i for q in self.dma_ops.values() for i in q[-DMA_K:]]
        for e in self.ENGS:
            deps = {i: "raw" for ee, i in last.items()}
            for i in dmas:
                deps[i] = "raw"
            idx = len(self.ops)
            op = _Op(idx, e, None, False, deps)
            op.deps = {k: ("bar") for k in deps}
            self.ops.append(op)

    def finalize(self):
        ops = self.ops
        for q, lst in self.dma_ops.items():
            for n, i in enumerate(lst):
                ops[i].sem_i = n % DMA_K
                ops[i].sem_v = 16 * (n // DMA_K + 1)
        self.waits = {}
        for op in ops:
            best = {}
            dma_w = {}
            for p, kind in op.deps.items():
                po = ops[p]
                if po.dma:
                    key = (po.eng, po.sem_i)
                    dma_w[key] = max(dma_w.get(key, 0), po.sem_v)
                    continue
                if po.fn is None:
                    continue
                if po.eng == op.eng and not op.dma and kind != "bar":
                    if po.eng == "pe":
                        continue
                if p > best.get(po.eng, -1):
                    best[po.eng] = p
            for e, p in best.items():
                ops[p].signal = True
            self.waits[op.idx] = (best, dma_w)
        cnt = {e: 0 for e in self.ENGS}
        for op in ops:
            if op.signal:
                cnt[op.eng] += 1
                op.tick = cnt[op.eng]

    def emit(self, final_wait_eng="sp"):
        nc = self.nc
        ops = self.ops
        from contextlib import ExitStack
        with ExitStack() as st:
            esem = {e: st.enter_context(nc.semaphore("cs_" + e)) for e in self.ENGS}
            dsem = {q: [st.enter_context(nc.semaphore("ds_%s%d" % (q, i))) for i in range(DMA_K)]
                    for q in self.dma_ops}
            block = st.enter_context(nc.Block())
            per_eng = {e: [op for op in ops if op.eng == e] for e in self.ENGS}
            waits = self.waits

            def body(ename, eng):
                seen = {}
                for op in per_eng[ename]:
                    best, dma_w = waits[op.idx]
                    for pe_, p in best.items():
                        t = ops[p].tick
                        key = ("c", pe_)
                        if seen.get(key, 0) < t:
                            eng.wait_ge(esem[pe_], t)
                            seen[key] = t
                    for (q, si), v in dma_w.items():
                        key = ("d", q, si)
                        if seen.get(key, 0) < v:
                            eng.wait_ge(dsem[q][si], v)
                            seen[key] = v
                    if op.fn is None:
                        continue
                    inst = op.fn(eng)
                    if op.dma:
                        inst.then_inc(dsem[op.eng][op.sem_i], 16)
                    elif op.signal:
                        inst.then_inc(esem[ename], 1)
                if ename == final_wait_eng:
                    for q, lst in self.dma_ops.items():
                        for i in lst[-DMA_K:]:
                            eng.wait_ge(dsem[q][ops[i].sem_i], ops[i].sem_v)

            @block.tensor
            def _(e):
                body("pe", e)

            @block.scalar
            def _(e):
                body("act", e)

            @block.vector
            def _(e):
                body("dve", e)

            @block.gpsimd
            def _(e):
                body("pool", e)

            @block.sync
            def _(e):
                body("sp", e)


class Arena:
    def __init__(self, nc, lo=16896, hi=229376):
        self.nc, self.lo, self.hi, self.off = nc, lo, hi, lo
        self.n = 0

    def alloc(self, shape, dt):
        nbytes = int(np.prod(shape[1:])) * (2 if dt == BF16 else 4)
        nbytes = (nbytes + 63) // 64 * 64
        assert self.off + nbytes <= self.hi, ("SBUF overflow", self.off, nbytes)
        self.n += 1
        t = self.nc.alloc_sbuf_tensor_at("sb%d" % self.n, list(shape), dt, offset=self.off)
        self.off += nbytes
        return t.ap()

    def mark(self):
        return self.off

    def release(self, m):
        self.off = m


def build_program(dbg=None, stop=None):
    nc = bass.Bass("TRN2", target_bir_lowering=False)
    S = Sched(nc)
    A = Arena(nc)

    def din(name, shape, dt=F32):
        return nc.dram_tensor(name, list(shape), dt, kind="ExternalInput").ap()

    x = din("x", [NB, T, D])
    ctx = din("ctx", [NB, L, D])
    cvecT = din("cvecT", [128, 8, 3])
    w_ada = din("w_ada", [128, 8, 6144])
    b_adaT = din("b_adaT", [128, 48])
    b_ada = din("b_ada", [6144])
    nmixT = din("nmixT", [128, 8])
    nmlpT = din("nmlpT", [128, 8])
    w_in = din("w_in", [128, 8, 2976])
    w_krs = din("w_krs", [128, 8, 96])
    qnT = din("qnT", [128, 2])
    kvnT = din("kvnT", [128, 1])
    w_uq = din("w_uq", [128, 2, 768])
    w_uqs = din("w_uqs", [128, 2, 768])
    w_kn = din("w_kn", [128, 512])
    w_v = din("w_v", [128, 512])
    lbT = din("lbT", [128, 2, 8])
    hgn = din("hgn", [128])
    w_out = din("w_out", [128, 8, 1024])
    w1 = din("w1", [8, 128, 4096])
    w2 = din("w2", [8, 128, 4096])
    fnorm = din("fnorm", [D])
    c_ident = din("c_ident", [128, 128], BF16)
    c_identf = din("c_identf", [128, 128], F32)
    c_mask = din("c_mask", [128, 256], BF16)
    c_rmask = din("c_rmask", [128, 2, 512], BF16)
    c_cs = din("c_cs", [128, 2, T], BF16)
    out = nc.dram_tensor("out", [NB, T, D], F32, kind="ExternalOutput").ap()
    w1s = nc.dram_tensor("w1s", [8, 128, 4096], BF16).ap()
    w2s = nc.dram_tensor("w2s", [8, 128, 4096], BF16).ap()
    dbg_out = {}
    if dbg:
        for name, shape, dt in dbg:
            dbg_out[name] = nc.dram_tensor("dbg_" + name, list(shape), dt, kind="ExternalOutput").ap()

    ps = [nc.alloc_psum_tensor("ps%d" % i, [128, 512], F32).ap() for i in range(8)]
    psb = [p.bitcast(BF16) for p in ps]
    PS = ["ps%d" % i for i in range(8)]

    def ACT(out_, in_, func, r, w, scale=1.0, bias=0.0, accum=None):
        kw = {}
        if accum is not None:
            kw["accum_out"] = accum
        S.add("act", lambda e: e.activation(out=out_, in_=in_, func=func, bias=bias, scale=scale, **kw), r, w)

    def TT(eng, out_, a, b, op, r, w):
        S.add(eng, lambda e: e.tensor_tensor(out_, a, b, op), r, w)

    def TS(eng, out_, a, s1, s2, op0, op1, r, w):
        if s2 is None:
            S.add(eng, lambda e: e.tensor_scalar(out_, a, s1, None, op0), r, w)
        else:
            S.add(eng, lambda e: e.tensor_scalar(out_, a, s1, s2, op0, op1), r, w)

    def STT(eng, out_, a, sc, b, op0, op1, r, w):
        S.add(eng, lambda e: e.scalar_tensor_tensor(out_, a, sc, b, op0, op1), r, w)

    def CP(eng, out_, in_, r, w):
        if eng == "act":
            S.add("act", lambda e: e.copy(out_, in_), r, w)
        else:
            S.add(eng, lambda e: e.tensor_copy(out_, in_), r, w)

    def MSET(eng, ap, val, w):
        S.add(eng, lambda e: e.memset(ap, val), (), w)

    def MMG(lst, r, w):
        lst = list(lst)

        def fn(e):
            ins = None
            for (o, l, rr, st, sp) in lst:
                ins = e.matmul(o, lhsT=l, rhs=rr, start=st, stop=sp)
            return ins
        S.add("pe", fn, r, w)

    def TRG(lst, ident_ap, r, w):
        lst = list(lst)

        def fn(e):
            ins = None
            for (o, i_) in lst:
                ins = e.transpose(o, i_, ident_ap)
            return ins
        S.add("pe", fn, r, w)

    def DMA(q, out_, in_, r, w):
        S.add(q, lambda e: e.dma_start(out=out_, in_=in_), r, w, dma=True)

    def RECIP(out_, in_, r, w):
        S.add("dve", lambda e: e.reciprocal(out_, in_), r, w)

    def SCAN(out_, d0, d1, r, w):
        S.add("dve", lambda e: e.tensor_tensor_scan(out_, d0, d1, 0.0, ALU.mult, ALU.add), r, w)

    def dump(name, src_ap, r):
        if name in dbg_out:
            DMA("sp", dbg_out[name], src_ap, r, ["dbg_" + name])

    def rstd_chain(ss, tmp, rs, n_inv, r, w_tmp, w_rs):
        ACT(tmp, ss, AF.Ln, r, w_tmp, scale=n_inv, bias=epsc[:, 0:1])
        ACT(rs, tmp, AF.Exp, w_tmp, w_rs, scale=-0.5)

    ident = A.alloc([128, 128], BF16)
    ones_bf = A.alloc([128, 128], BF16)
    maskfb2 = A.alloc([128, 2, 256], BF16)
    rmask = A.alloc([128, 2, 512], BF16)
    epsc = A.alloc([128, 2], F32)
    fn_bc = A.alloc([128, D], F32)
    gn_bc = A.alloc([128, 512], F32)
    G1 = A.alloc([128, NB, D], F32)
    modT = A.alloc([128, 48, 3], F32)
    A1 = A.alloc([128, 3, 8], F32)
    A2 = A.alloc([128, 3, 8], F32)
    lb_t = A.alloc([128, 8], F32)
    w_uq_bf = A.alloc([128, 2, 768], BF16)
    w_uqs_bf = A.alloc([128, 2, 768], BF16)
    w_kn_bf = A.alloc([128, 512], BF16)
    w_v_bf = A.alloc([128, 512], BF16)

    DMA("sp", ident, c_ident, [], ["ident"])
    DMA("sp", maskfb2[:, 0, :], c_mask, [], ["maskfb"])
    DMA("sp", maskfb2[:, 1, :], c_mask, [], ["maskfb"])
    DMA("sp", rmask, c_rmask, [], ["rmask"])
    DMA("sp", fn_bc, fnorm.partition_broadcast(128), [], ["fn_bc"])
    for i in range(4):
        DMA("sp", gn_bc[:, i * 128:(i + 1) * 128], hgn.partition_broadcast(128), [], ["gn_bc%d" % i])
    GN = ["gn_bc%d" % i for i in range(4)]
    MSET("dve", ones_bf, 1.0, ["ones"])
    MSET("dve", epsc, EPS, ["epsc"])

    m_setup = A.mark()
    wa = A.alloc([128, 8, 6144], BF16)
    cT = A.alloc([128, 8, 3], F32)
    sT = A.alloc([128, 8, 3], BF16)
    sTb = A.alloc([128, NB, 8, 128], BF16)
    badaT = A.alloc([128, 48], F32)
    bada_g1 = A.alloc([128, D], F32)
    nmix_t = A.alloc([128, 8], F32)
    nmlp_t = A.alloc([128, 8], F32)
    lbraw = A.alloc([128, 2, 8], F32)
    lbtmp = A.alloc([128, 8], F32)
    qn_t = A.alloc([128, 2], F32)
    kvn_t = A.alloc([128, 1], F32)
    wst = A.alloc([128, 2, 768], F32)
    wst2 = A.alloc([128, 2, 768], F32)
    wst3 = A.alloc([128, 1024], F32)

    for kc in range(8):
        DMA("pool", wa[:, kc, 0:2048], w_ada[:, kc, 0:2048], [], ["wa%d_a" % kc])
    for kc in range(8):
        DMA("pool", wa[:, kc, 2048:6144], w_ada[:, kc, 2048:6144], [], ["wa%d_b" % kc])
    DMA("sp", cT, cvecT, [], ["cT"])
    DMA("sp", badaT, b_adaT, [], ["badaT"])
    DMA("sp", bada_g1, b_ada[2048:3072].partition_broadcast(128), [], ["bada_g1"])
    DMA("sp", nmix_t, nmixT, [], ["nmix"])
    DMA("sp", nmlp_t, nmlpT, [], ["nmlp"])
    DMA("sp", lbraw, lbT, [], ["lbraw"])
    DMA("sp", qn_t, qnT, [], ["qn"])
    DMA("sp", kvn_t, kvnT, [], ["kvn"])
    DMA("sp", wst, w_uq, [], ["wst"])
    DMA("sp", wst2, w_uqs, [], ["wst2"])
    DMA("sp", wst3[:, 0:512], w_kn, [], ["wst3a"])
    DMA("sp", wst3[:, 512:1024], w_v, [], ["wst3b"])

    TT("dve", lbtmp, lbraw[:, 1, :], lbraw[:, 0, :], ALU.subtract, ["lbraw"], ["lbtmp"])
    ACT(lbtmp, lbtmp, AF.Exp, ["lbtmp"], ["lbtmp"])
    TS("dve", lbtmp, lbtmp, 1.0, None, ALU.add, None, ["lbtmp"], ["lbtmp"])
    RECIP(lb_t, lbtmp, ["lbtmp"], ["lb_t"])
    for c in range(2):
        TS("dve", w_uq_bf[:, c, :], wst[:, c, :], qn_t[:, c:c + 1], None, ALU.mult, None, ["wst", "qn"], ["w_uq_bf%d" % c])
        TS("dve", w_uqs_bf[:, c, :], wst2[:, c, :], qn_t[:, c:c + 1], None, ALU.mult, None, ["wst2", "qn"], ["w_uqs_bf%d" % c])
    TS("dve", w_kn_bf, wst3[:, 0:512], kvn_t[:, 0:1], None, ALU.mult, None, ["wst3a", "kvn"], ["w_kn_bf"])
    TS("dve", w_v_bf, wst3[:, 512:1024], kvn_t[:, 0:1], None, ALU.mult, None, ["wst3b", "kvn"], ["w_v_bf"])
    WUQ = ["w_uq_bf0", "w_uq_bf1"]
    WUQS = ["w_uqs_bf0", "w_uqs_bf1"]

    ACT(sT, cT, AF.Silu, ["cT"], ["sT"])
    WAa = ["wa%d_a" % k for k in range(8)]
    WA = ["wa%d_b" % k for k in range(8)]
    psM = ps[0][:, 0:144]
    psM2 = ps[3][:, 0:144]
    MMG([(psM[:, j * 3:(j + 1) * 3], wa[:, kc, j * 128:(j + 1) * 128], sT[:, kc, :], kc == 0, kc == 7)
         for j in range(16) for kc in range(8)], WAa + ["sT"], [PS[0]])
    for b in range(3):
        TT("dve", modT[:, 0:16, b], ps[0][:, b:48:3], badaT[:, 0:16], ALU.add, [PS[0], "badaT"], ["modT%d" % b])
    MODT = ["modT0", "modT1", "modT2"]
    MMG([(psM2[:, j * 3:(j + 1) * 3], wa[:, kc, j * 128:(j + 1) * 128], sT[:, kc, :], kc == 0, kc == 7)
         for j in range(16, 48) for kc in range(8)], WA + ["sT"], [PS[3]])
    MODT2 = ["modTb0", "modTb1", "modTb2"]
    for b in range(3):
        TT("dve", modT[:, 16:48, b], ps[3][:, 48 + b:144:3], badaT[:, 16:48], ALU.add, [PS[3], "badaT"], ["modTb%d" % b])
    for b in range(3):
        STT("dve", A1[:, b, :], modT[:, 8:16, b], 1.0, nmix_t, ALU.add, ALU.mult, MODT + ["nmix"], ["A1_%d" % b])
        STT("dve", A2[:, b, :], modT[:, 32:40, b], 1.0, nmlp_t, ALU.add, ALU.mult, MODT2 + ["nmlp"], ["A2_%d" % b])
    for b in range(NB):
        for kc in range(8):
            CP("dve", sTb[:, b, kc, :], sT[:, kc, b:b + 1].to_broadcast([128, 128]), ["sT"], ["sTb%d_%d" % (b, kc)])
        for half in range(2):
            pg = ps[1 + half]
            MMG([(pg, sTb[:, b, kc, :], wa[:, kc, 2048 + half * 512:2048 + (half + 1) * 512], kc == 0, kc == 7)
                 for kc in range(8)], WA + ["sTb%d_%d" % (b, kc) for kc in range(8)], [PS[1 + half]])
            TT("dve", G1[:, b, half * 512:(half + 1) * 512], pg, bada_g1[:, half * 512:(half + 1) * 512], ALU.add,
               [PS[1 + half], "bada_g1"], ["G1_%d_%d" % (b, half)])
    dump("modT", modT, MODT + MODT2)
    dump("G1", G1, ["G1_%d_%d" % (b, h) for b in range(NB) for h in range(2)])
    dump("lb", lb_t, ["lb_t"])
    S.barrier()
    if stop == 'setup':
        S.finalize(); S.emit(); return nc
    A.release(m_setup)

    o_mlaT = A.alloc([128, 4, T], BF16)
    o_hgT = A.alloc([128, 4, T], BF16)
    m_fin = A.mark()
    hT = A.alloc([128, 8, TE], BF16)
    m_batch = A.mark()

    for b in range(NB):
        bn = "b%d_" % b

        def HK(g):
            return [bn + "hT%d_%d" % (g, c) for c in range(8)]

        def HKB(t0, n):
            r = []
            for g in range(t0 // 128, (t0 + n) // 128):
                r += HK(g)
            return r

        A.release(m_batch)
        xsb = [A.alloc([128, D], BF16) for _ in range(2)]
        junk = A.alloc([128, D], BF16)
        ssA = A.alloc([128, NT], F32)
        lnA = A.alloc([128, NT], F32)
        rsA = A.alloc([128, NT], F32)
        NXB = 3
        xtb = [A.alloc([128, D], F32) for _ in range(NXB)]

        def pa1(g):
            src = ctx[b, g * 128:(g + 1) * 128, :] if g < 2 else x[b, (g - 2) * 128:(g - 1) * 128, :]
            k = g % NXB
            xk = bn + "xt%d" % k
            DMA("sp", xtb[k], src, [], [xk])
            ACT(junk, xtb[k], AF.Square, [xk], [bn + "junk", bn + "ssA%d" % g], accum=ssA[:, g:g + 1])
            rstd_chain(ssA[:, g:g + 1], lnA[:, g:g + 1], rsA[:, g:g + 1], 1.0 / D,
                       [bn + "ssA%d" % g, "epsc"], [bn + "lnA%d" % g], [bn + "rsA%d" % g])

        def pa2(g):
            k = g % NXB
            xk, sk = bn + "xt%d" % k, bn + "xs%d" % (g % 2)
            TS("dve", xsb[g % 2], xtb[k], rsA[:, g:g + 1], None, ALU.mult, None, [xk, bn + "rsA%d" % g], [sk])
            pt = psb[g % 2]
            TRG([(pt[:, c * 128:(c + 1) * 128], xsb[g % 2][:, c * 128:(c + 1) * 128]) for c in range(8)], ident,
                [sk, "ident"], [PS[g % 2]])

        def pa3(g):
            bb = 2 if g < 2 else b
            pt = psb[g % 2]
            for c in range(8):
                dst = hT[:, c, g * 128:(g + 1) * 128]
                if g % 4 == 0:
                    ACT(dst, pt[:, c * 128:(c + 1) * 128], AF.Identity, [PS[g % 2], "A1_%d" % bb] + MODT, [bn + "hT%d_%d" % (g, c)],
                        scale=A1[:, bb, c:c + 1], bias=modT[:, c, bb:bb + 1])
                else:
                    TS("dve", dst, pt[:, c * 128:(c + 1) * 128], A1[:, bb, c:c + 1], modT[:, c, bb:bb + 1], ALU.mult, ALU.add,
                       [PS[g % 2], "A1_%d" % bb] + MODT, [bn + "hT%d_%d" % (g, c)])

        for tt_ in range(NT + 2):
            if tt_ - 2 >= 0:
                pa3(tt_ - 2)
            if 0 <= tt_ - 1 < NT:
                pa2(tt_ - 1)
            if tt_ < NT:
                pa1(tt_)
        if b == 0:
            dump("hT", hT, [k for g in range(NT) for k in HK(g)])

        if stop == 'A' and b == 0:
            S.barrier(); S.finalize(); S.emit(); return nc
        S.barrier()
        A.release(m_batch)
        m_mla = A.mark()
        cs = A.alloc([128, 2, T], BF16)
        w_mla = A.alloc([128, 8, 416], BF16)
        w_krs_bf = A.alloc([128, 8, 96], BF16)
        cqnT = A.alloc([128, 2, TE], BF16)
        ckvnT = A.alloc([128, TE], BF16)
        krotT = A.alloc([128, TE], BF16)
        Vaug = A.alloc([128, NT, 8, 128], BF16)
        KTb = [A.alloc([128, TE], BF16) for _ in range(2)]
        QTb = [A.alloc([128, T], BF16) for _ in range(2)]
        PTb = [A.alloc([128, 512], BF16) for _ in range(3)]
        sq = A.alloc([128, 3, 512], BF16)
        tA = A.alloc([128, 512], F32)
        tB = A.alloc([128, 512], F32)
        rq_bc = A.alloc([128, 512], F32)
        rkv_bc = A.alloc([128, 512], F32)
        rp1 = [A.alloc([128, 512], F32) for _ in range(1)]
        rp2 = [A.alloc([128, 512], F32) for _ in range(1)]
        rden = [A.alloc([128, 512], F32) for _ in range(2)]

        DMA("sp", cs, c_cs, [], [bn + "cs"])
        MSET("dve", Vaug, 1.0, [bn + "Vp%d" % g for g in range(NT)])
        DMA("pool", w_mla, w_in[:, :, 0:416], [], [bn + "w_mla"])
        DMA("pool", w_krs_bf, w_krs, [], [bn + "w_krs"])
        WM = [bn + "w_mla"]

        v_defer = []
        for bi, (t0, n) in enumerate(BLKS):
            hk = HKB(t0, n)
            tok = slice(t0, t0 + n)
            for c in range(2):
                MMG([(ps[c][:, 0:n], w_mla[:, kc, c * 128:(c + 1) * 128], hT[:, kc, tok], kc == 0, kc == 7) for kc in range(8)],
                    WM + hk, [PS[c]])
            MMG([(ps[2][:, 0:n], w_mla[:, kc, 256:384], hT[:, kc, tok], kc == 0, kc == 7) for kc in range(8)], WM + hk, [PS[2]])
            MMG([(ps[3][0:96, 0:n], w_mla[:, kc, 320:416], hT[:, kc, tok], kc == 0, kc == 7) for kc in range(8)], WM + hk, [PS[3]])
            if bi > 0:
                MMG([(ps[4][0:96, 0:n], w_krs_bf[:, kc, :], hT[:, kc, tok], kc == 0, kc == 7) for kc in range(8)],
                    [bn + "w_krs"] + hk, [PS[4]])
            while v_defer:
                v_defer.pop(0)()
            for c in range(3):
                ACT(sq[:, c, 0:n], ps[c][:, 0:n], AF.Square, [PS[c]], [bn + "sq%d" % c])
            MMG([(ps[5][:, 0:n], ones_bf, sq[:, 0, 0:n], True, False), (ps[5][:, 0:n], ones_bf, sq[:, 1, 0:n], False, True)],
                ["ones", bn + "sq0", bn + "sq1"], [PS[5]])
            MMG([(ps[6][:, 0:n], ones_bf, sq[:, 2, 0:n], True, True)], ["ones", bn + "sq2"], [PS[6]])
            rstd_chain(ps[5][:, 0:n], tA[:, 0:n], rq_bc[:, 0:n], 1.0 / 256, [PS[5], "epsc"], [bn + "tA"], [bn + "rq_bc"])
            rstd_chain(ps[6][:, 0:n], tB[:, 0:n], rkv_bc[:, 0:n], 1.0 / 128, [PS[6], "epsc"], [bn + "tB"], [bn + "rkv_bc"])
            for c in range(2):
                TT("dve", cqnT[:, c, tok], ps[c][:, 0:n], rq_bc[:, 0:n], ALU.mult, [PS[c], bn + "rq_bc"], [bn + "cqnT%d_%d" % (bi, c)])
            TT("dve", ckvnT[:, tok], ps[2][:, 0:n], rkv_bc[:, 0:n], ALU.mult, [PS[2], bn + "rkv_bc"], [bn + "ckvnT%d" % bi])
            if bi == 0:
                CP("dve", krotT[64:96, tok], ps[3][64:96, 0:n], [PS[3]], [bn + "krotT%d" % bi])
            else:
                lt = slice(t0 - L, t0 - L + n)
                TT("dve", rp1[0][64:96, 0:n], ps[3][64:96, 0:n], cs[64:96, 0, lt], ALU.mult, [PS[3], bn + "cs"], [bn + "rp1_0"])
                TT("dve", rp2[0][64:96, 0:n], ps[4][64:96, 0:n], cs[64:96, 1, lt], ALU.mult, [PS[4], bn + "cs"], [bn + "rp2_0"])
                TT("dve", krotT[64:96, tok], rp1[0][64:96, 0:n], rp2[0][64:96, 0:n], ALU.add, [bn + "rp1_0", bn + "rp2_0"],
                   [bn + "krotT%d" % bi])
            def v_part(bi=bi, t0=t0, n=n):
                for g in range(t0 // 128, (t0 + n) // 128):
                    MMG([(ps[7], ckvnT[:, g * 128:(g + 1) * 128], w_v_bf, True, True)], [bn + "ckvnT%d" % bi, "w_v_bf"], [PS[7]])
                    pv4 = ps[7].rearrange("p (j e c) -> p j e c", j=4, e=2)
                    veng = "act" if g % 2 else "dve"
                    CP(veng, Vaug[:, g, 0::2, 0:64], pv4[:, :, 0, :], [PS[7]], [bn + "Vp%d" % g])
                    CP(veng, Vaug[:, g, 1::2, 64:128], pv4[:, :, 1, :], [PS[7]], [bn + "Vp%d" % g])
            v_defer.append(v_part)
        while v_defer:
            v_defer.pop(0)()
        CQ = [bn + "cqnT%d_%d" % (bi, c) for bi in range(5) for c in range(2)]
        CKV = [bn + "ckvnT%d" % bi for bi in range(5)]
        KROT = [bn + "krotT%d" % bi for bi in range(5)]
        if b == 0:
            dump("cqnT", cqnT, CQ)
            dump("ckvnT", ckvnT, CKV)
            dump("krotT", krotT[64:96, :], KROT)

        if stop == 'mlaproj' and b == 0:
            S.barrier(); S.finalize(); S.emit(); return nc
        def proj_head(h):
            kb = h % 2
            KT, QT = KTb[kb], QTb[kb]
            kk, qk = bn + "KT%d" % kb, bn + "QT%d" % kb
            pc = 0
            for bi, (t0, n) in enumerate(BLKS):
                tok = slice(t0, t0 + n)
                pb, pk = ps[7 - pc % 2], PS[7 - pc % 2]
                pc += 1
                MMG([(pb[0:64, 0:n], w_kn_bf[:, h * 64:(h + 1) * 64], ckvnT[:, tok], True, True)],
                    ["w_kn_bf", bn + "ckvnT%d" % bi], [pk])
                CP("dve", KT[0:64, tok], pb[0:64, 0:n], [pk], [kk + "_n%d" % bi])
                yield
            CP("dve", KT[64:96, :], krotT[64:96, :], KROT, [kk + "_r"])
            for j in range(4):
                q0 = j * 512
                et = slice(L + q0, L + q0 + 512)
                lt = slice(q0, q0 + 512)
                cqk = [bn + "cqnT%d_%d" % (j + 1, c) for c in range(2)]
                pb, pk = ps[7 - pc % 2], PS[7 - pc % 2]
                pc += 1
                MMG([(pb[0:96, :], w_uq_bf[:, c, h * 96:(h + 1) * 96], cqnT[:, c, et], c == 0, c == 1) for c in range(2)],
                    WUQ + cqk, [pk])
                CP("dve", QT[0:64, lt], pb[0:64, :], [pk], [qk + "_n%d" % j])
                TT("dve", rp1[0][64:96, :], pb[64:96, :], cs[64:96, 0, lt], ALU.mult, [pk, bn + "cs"], [bn + "rp1_0"])
                yield
                pb, pk = ps[7 - pc % 2], PS[7 - pc % 2]
                pc += 1
                MMG([(pb[0:96, :], w_uqs_bf[:, c, h * 96:(h + 1) * 96], cqnT[:, c, et], c == 0, c == 1) for c in range(2)],
                    WUQS + cqk, [pk])
                TT("dve", rp2[0][64:96, :], pb[64:96, :], cs[64:96, 1, lt], ALU.mult, [pk, bn + "cs"], [bn + "rp2_0"])
                TT("dve", QT[64:96, lt], rp1[0][64:96, :], rp2[0][64:96, :], ALU.add, [bn + "rp1_0", bn + "rp2_0"], [qk + "_r%d" % j])
                yield

        def KTK(h):
            kk = bn + "KT%d" % (h % 2)
            return [kk + "_n%d" % bi for bi in range(5)] + [kk + "_r"]

        def QTK(h, j):
            qk = bn + "QT%d" % (h % 2)
            return [qk + "_n%d" % j, qk + "_r%d" % j]

        steps = [(h, qb, kt) for h in range(8) for qb in range(4) for kt in range(NT)]
        for _ in proj_head(0):
            pass
        pgen = None
        if b == 0:
            dump("KT0", KTb[0][0:96, :], KTK(0))
            dump("QT0", QTb[0][0:96, :], [k for j in range(4) for k in QTK(0, j)])

        SB = [0, 1, 2, 5]

        def reg_qk(si):
            h, qb, kt = steps[si]
            KT, QT = KTb[h % 2], QTb[h % 2]
            sb_ = SB[si % 4]
            MMG([(ps[sb_], KT[0:96, kt * 128:(kt + 1) * 128], QT[0:96, qb * 512:(qb + 1) * 512], True, True)],
                KTK(h) + QTK(h, qb), [PS[sb_]])

        reg_qk(0)
        reg_qk(1)
        reg_qk(2)
        for si, (h, qb, kt) in enumerate(steps):
            u = (h * 4 + qb) % 2
            pO = ps[3 + u]
            sb_ = SB[si % 4]
            ACT(PTb[si % 3], ps[sb_], AF.Exp, [PS[sb_]], [bn + "PT%d" % (si % 3)] + (["convgate"] if (b == 0 and si == 0) else []),
                scale=ATT_SCALE)
            if si + 3 < len(steps):
                reg_qk(si + 3)
            MMG([(pO, Vaug[:, kt, h, :], PTb[si % 3], kt == 0, kt == NT - 1)],
                [bn + "Vp%d" % kt, bn + "PT%d" % (si % 3)], [PS[3 + u]])
            if kt == NT - 1:
                orow = slice((h % 2) * 64, (h % 2) * 64 + 64)
                drow = slice(64 - (h % 2) * 64, 128 - (h % 2) * 64)
                RECIP(rden[u][orow, :], pO[drow, :], [PS[3 + u]], [bn + "rden%d" % u])
                TT("dve", o_mlaT[orow, h // 2, qb * 512:(qb + 1) * 512], pO[orow, :], rden[u][orow, :], ALU.mult,
                   [PS[3 + u], bn + "rden%d" % u], [bn + "omla%d_%d" % (h, qb)])
            if b == 0 and kt == 0 and qb == 0 and h == 0:
                for p in range(8):
                    DMA("pool", w1s[p], w1[p], ["convgate"], ["w1s%d" % p])
                for p in range(8):
                    DMA("pool", w2s[p], w2[p], [], ["w2s%d" % p])
            if qb == 0 and kt == 2 and h + 1 < 8:
                pgen = proj_head(h + 1)
            if pgen is not None and si % 4 == 0:
                try:
                    next(pgen)
                except StopIteration:
                    pgen = None
        OM = [bn + "omla%d_%d" % (h, qb) for h in range(8) for qb in range(4)]
        if b == 0:
            dump("o_mlaT", o_mlaT, OM)
        if stop == 'attn' and b == 0:
            S.barrier(); S.finalize(); S.emit(); return nc
        S.barrier()
        A.release(m_mla)

        v_tm = A.alloc([128, NT, 512], BF16)
        gate = A.alloc([128, 16, 512], BF16)
        eb_tab = A.alloc([128, 4, 2, NT], F32)
        m_h0 = A.mark()
        w_hig = A.alloc([128, 8, 1024], BF16)
        gtmp = [A.alloc([128, 512], F32) for _ in range(2)]
        DMA("pool", w_hig[:, :, 0:512], w_in[:, :, 1952:2464], [], [bn + "w_hig_v"])
        DMA("pool", w_hig[:, :, 512:1024], w_in[:, :, 2464:2976], [], [bn + "w_hig_g"])
        for g in range(NT):
            k = g % 2
            MMG([(ps[k], hT[:, kc, g * 128:(g + 1) * 128], w_hig[:, kc, 0:512], kc == 0, kc == 7) for kc in range(8)],
                [bn + "w_hig_v"] + HK(g), [PS[k]])
            CP("dve" if g % 2 else "act", v_tm[:, g, :], ps[k], [PS[k]], [bn + "v_tm%d" % g])
            if g >= 2:
                MMG([(ps[2 + k], hT[:, kc, g * 128:(g + 1) * 128], w_hig[:, kc, 512:1024], kc == 0, kc == 7) for kc in range(8)],
                    [bn + "w_hig_g"] + HK(g), [PS[2 + k]])
                ACT(gtmp[k], ps[2 + k], AF.Silu, [PS[2 + k]], [bn + "gtmp%d" % k])
                TT("dve", gate[:, g - 2, :], gtmp[k], gn_bc, ALU.mult, [bn + "gtmp%d" % k] + GN, [bn + "gate%d" % (g - 2)])
        if b == 0:
            dump("v_tm", v_tm, [bn + "v_tm%d" % g for g in range(NT)])
            dump("gate", gate, [bn + "gate%d" % j for j in range(16)])
        if stop == 'hg0' and b == 0:
            S.barrier(); S.finalize(); S.emit(); return nc
        S.barrier()
        A.release(m_h0)

        NB_H = 9
        NSET = 4
        whb = [A.alloc([128, 8, 384], BF16) for _ in range(1)]
        qTs = [A.alloc([128, T], BF16) for _ in range(2)]
        kTs = [A.alloc([128, T], BF16) for _ in range(2)]
        khat = A.alloc([128, NT, 2, 128], BF16)
        S_st = A.alloc([128, 2, 17, 128], BF16)
        khT = [A.alloc([128, 256], BF16) for _ in range(2)]
        ktmp = [A.alloc([128, 256], BF16) for _ in range(1)]
        TS1 = [A.alloc([128, 256], F32) for _ in range(NSET)]
        TS2 = [A.alloc([128, 256], F32) for _ in range(NSET)]
        TS3 = [A.alloc([128, 256], F32) for _ in range(NSET)]
        TSq = [A.alloc([128, 256], F32) for _ in range(NSET)]
        TSz = [A.alloc([128, 256], BF16) for _ in range(NSET)]
        ATb = A.alloc([128, 4, 256], BF16)
        o_sb = A.alloc([128, 4, 128], F32)
        og = A.alloc([128, 4, 128], BF16)
        ssh = A.alloc([128, 4], F32)
        lnh = A.alloc([128, 4], F32)
        rsh = A.alloc([128, 4], F32)
        junk2 = A.alloc([128, 128], BF16)

        units = []
        for h in range(4):
            fo = [(k_, 0) for k_ in range(NB_H)]
            bo = [(k_, 1) for k_ in [0] + list(range(NB_H - 1, 0, -1))]
            seq = (bo + fo) if h % 2 == 0 else (fo + bo)
            for (k_, d_) in seq:
                units.append((h, k_, d_))
        NU = len(units)

        def hkeys(h):
            return bn + "h%d_" % h

        def load_wh(h):
            wh = whb[0]
            whk = bn + "wh0"
            for i, c0 in enumerate((416, 928, 1440)):
                DMA("pool", wh[:, :, i * 128:(i + 1) * 128], w_in[:, :, c0 + h * 128:c0 + (h + 1) * 128], [], [whk + "_%d" % i])

        def uinfo(ui):
            h, k, d = units[ui]
            return h, k, d, whb[0], [bn + "wh0_%d" % i for i in range(3)], hkeys(h), ui % NSET, k > 0

        def st0(ui):
            h, k, d, wh, WHK, hn, s, lat = uinfo(ui)
            tok = slice(k * 256, k * 256 + 256)
            hk = HKB(k * 256, 256)
            pz = ps[ui % 2]
            MMG([(pz[:, 0:256], wh[:, kc, (1 + d) * 128:(2 + d) * 128], hT[:, kc, tok], kc == 0, kc == 7) for kc in range(8)],
                WHK + hk, [PS[ui % 2]])
            if lat:
                pq = ps[2]
                MMG([(pq[:, 0:256], wh[:, kc, 0:128], hT[:, kc, tok], kc == 0, kc == 7) for kc in range(8)], WHK + hk, [PS[2]])

        def st1(ui):
            h, k, d, wh, WHK, hn, s, lat = uinfo(ui)
            sk = bn + "ts%d_" % s
            pz = ps[ui % 2]
            ACT(TS1[s], pz[:, 0:256], AF.Exp, [PS[ui % 2]], [sk + "T1"], scale=-1.0)
            ACT(TS2[s], TS1[s], AF.Ln, [sk + "T1", "lb_t"], [sk + "T2"], scale=lb_t[:, d * 4 + h:d * 4 + h + 1], bias=1.0)
            ACT(TS1[s], TS1[s], AF.Ln, [sk + "T1"], [sk + "T1"], bias=1.0)
            if lat:
                pq = ps[2]
                ACT(TSq[s], pq[:, 0:256], AF.Exp, [PS[2]], [sk + "Tq"], scale=-1.0)
                ACT(TSz[s], pq[:, 0:256], AF.Copy, [PS[2]], [sk + "Tz"])
                ACT(TSq[s], TSq[s], AF.Ln, [sk + "Tq"], [sk + "Tq"], bias=1.0)

        def st2(ui):
            h, k, d, wh, WHK, hn, s, lat = uinfo(ui)
            sk = bn + "ts%d_" % s
            TT("dve", TS2[s], TS2[s], TS1[s], ALU.subtract, [sk + "T1", sk + "T2"], [sk + "T2"])
            if d == 0:
                SCAN(TS3[s], rmask[:, 0, 0:256], TS2[s], ["rmask", sk + "T2"], [sk + "T3"])
            else:
                SCAN(TS3[s][:, ::-1], rmask[:, 1, 0:256][:, ::-1], TS2[s][:, ::-1], ["rmask", sk + "T2"], [sk + "T3"])
            if lat:
                TT("dve", TSq[s], TS3[s], TSq[s], ALU.subtract, [sk + "T3", sk + "Tq"], [sk + "Tq"])

        def st3(ui):
            h, k, d, wh, WHK, hn, s, lat = uinfo(ui)
            sk = bn + "ts%d_" % s
            ACT(TS1[s], TS2[s], AF.Exp, [sk + "T2"], [sk + "T1"])
            lastcol = 127 if d == 0 else 0
            ACT(eb_tab[:, h, d, 2 * k:2 * k + 2], TS3[s][:, lastcol:256:128], AF.Exp, [sk + "T3"], [hn + "eb%d_%d" % (d, k)])
            if lat:
                ACT(TSq[s], TSq[s], AF.Exp, [sk + "Tq"], [sk + "Tq"])
            ACT(TS3[s], TS3[s], AF.Exp, [sk + "T3"], [sk + "T3"], scale=-1.0)

        def st4(ui):
            h, k, d, wh, WHK, hn, s, lat = uinfo(ui)
            sk = bn + "ts%d_" % s
            if lat:
                lt = slice((k - 1) * 256, k * 256)
                STT("dve", qTs[d][:, lt], TSz[s], -1.0, TSq[s], ALU.mult, ALU.mult, [sk + "Tz", sk + "Tq"], [bn + "qT%d_%d" % (d, k)])
                kdst = kTs[d][:, lt]
                kkey = bn + "kT%d_%d" % (d, k)
            else:
                kdst = ktmp[0]
                kkey = bn + "ktmp0"
            STT("dve", kdst, TS1[s], 1.0, TS3[s], ALU.subtract, ALU.mult, [sk + "T1", sk + "T3"], [kkey])
            kh = khT[ui % 2]
            for c in range(2):
                TS("dve", kh[:, c * 128:(c + 1) * 128], kdst[:, c * 128:(c + 1) * 128], eb_tab[:, h, d, 2 * k + c:2 * k + c + 1], None,
                   ALU.mult, None, [kkey, hn + "eb%d_%d" % (d, k)], [bn + "khT%d_%d" % (ui % 2, c)])

        def st5(ui):
            h, k, d, wh, WHK, hn, s, lat = uinfo(ui)
            kh = khT[ui % 2]
            ptk = psb[4 + ui % 2]
            TRG([(ptk[:, c * 128:(c + 1) * 128], kh[:, c * 128:(c + 1) * 128]) for c in range(2)], ident,
                [bn + "khT%d_%d" % (ui % 2, c) for c in range(2)] + ["ident"], [PS[4 + ui % 2]])

        def st6(ui):
            h, k, d, wh, WHK, hn, s, lat = uinfo(ui)
            ptk = psb[4 + ui % 2]
            CP("dve", khat[:, 2 * k:2 * k + 2, d, :], ptk[:, 0:256].rearrange("p (c k) -> p c k", c=2),
               [PS[4 + ui % 2]], [bn + "khat%d_%d" % (d, k)])

        def scan_mm(h, g, d, k):
            hn = hkeys(h)
            pS = ps[6][:, d * 128:(d + 1) * 128]
            MMG([(pS, khat[:, g, d, :], v_tm[:, g, h * 128:(h + 1) * 128], True, True)],
                [bn + "khat%d_%d" % (d, k), bn + "v_tm%d" % g], [PS[6]])

        def scan_upd(h, g, d, k, p):
            hn = hkeys(h)
            pS = ps[6][:, d * 128:(d + 1) * 128]
            if p == 0:
                CP("dve", S_st[:, d, 0, :], pS, [PS[6]], [bn + "S%d_%d" % (d, 0)])
            else:
                STT("dve", S_st[:, d, p, :], S_st[:, d, p - 1, :], eb_tab[:, h, d, g:g + 1], pS, ALU.mult, ALU.add,
                    [bn + "S%d_%d" % (d, p - 1), hn + "eb%d_%d" % (d, k), PS[6]], [bn + "S%d_%d" % (d, p)])

        def grp_piece(h, gq, step):
            hn = hkeys(h)
            pO = ps[7]

            def pa_mm(i):
                j = 4 * gq + i
                tl = slice(j * 128, (j + 1) * 128)
                kb = j // 2 + 1
                pA = ps[3][:, (i % 2) * 256:(i % 2) * 256 + 256]
                MMG([(pA[:, 0:128], kTs[0][:, tl], qTs[0][:, tl], True, True),
                     (pA[:, 128:256], kTs[1][:, tl], qTs[1][:, tl], True, True)],
                    [bn + "kT%d_%d" % (d, kb) for d in range(2)] + [bn + "qT%d_%d" % (d, kb) for d in range(2)], [PS[3]])

            def mask2(i0):
                TT("dve", ATb[:, i0:i0 + 2, :], ps[3].rearrange("p (i c) -> p i c", i=2), maskfb2, ALU.mult, [PS[3], "maskfb"],
                   [bn + "AT%d" % i0, bn + "AT%d" % (i0 + 1)])

            def po_mm(i):
                j = 4 * gq + i
                g = j + 2
                kb = j // 2 + 1
                tl = slice(j * 128, (j + 1) * 128)
                vv = v_tm[:, g, h * 128:(h + 1) * 128]
                MMG([(pO[:, i * 128:(i + 1) * 128], ATb[:, i, 0:128], vv, True, False),
                     (pO[:, i * 128:(i + 1) * 128], ATb[:, i, 128:256], vv, False, False),
                     (pO[:, i * 128:(i + 1) * 128], qTs[0][:, tl], S_st[:, 0, j + 1, :], False, False),
                     (pO[:, i * 128:(i + 1) * 128], qTs[1][:, tl], S_st[:, 1, 16 - j, :], False, True)],
                    [bn + "AT%d" % i, bn + "v_tm%d" % g, bn + "qT0_%d" % kb, bn + "qT1_%d" % kb,
                     bn + "S0_%d" % (j + 1), bn + "S1_%d" % (16 - j)], [PS[7]])

            if step == 1:
                pa_mm(0)
                pa_mm(1)
                mask2(0)
            elif step == 2:
                po_mm(0)
                po_mm(1)
                pa_mm(2)
                pa_mm(3)
                mask2(2)
            elif step == 3:
                po_mm(2)
                po_mm(3)
                CP("dve", o_sb, pO.rearrange("p (i c) -> p i c", i=4), [PS[7]], [bn + "o_sb"])
                for i in range(4):
                    ACT(junk2, o_sb[:, i, :], AF.Square, [bn + "o_sb"], [bn + "junk2", bn + "ssh%d" % i], accum=ssh[:, i:i + 1])
                rstd_chain(ssh, lnh, rsh, 1.0 / 128, [bn + "ssh%d" % i for i in range(4)] + ["epsc"], [bn + "lnh"], [bn + "rsh"])
            elif step == 4:
                for i in range(4):
                    j = 4 * gq + i
                    STT("dve", og[:, i, :], o_sb[:, i, :], rsh[:, i:i + 1], gate[:, j, h * 128:(h + 1) * 128],
                        ALU.mult, ALU.mult, [bn + "o_sb", bn + "rsh", bn + "gate%d" % j], [bn + "og%d" % i])
            elif step == 5:
                pT = psb[6][:, 512:1024]
                TRG([(pT[:, i * 128:(i + 1) * 128], og[:, i, :]) for i in range(4)], ident,
                    [bn + "og%d" % i for i in range(4)] + ["ident"], [PS[6]])
                CP("dve", o_hgT[:, h, gq * 512:(gq + 1) * 512], pT, [PS[6]], [bn + "ohg%d_%d" % (h, gq)])

        load_wh(0)
        stages = [st0, st1, st2, st3, st4, st5, st6]
        from collections import deque
        scan_q = [deque(), deque()]
        blk_scanned = {}
        grp_todo = deque((h, gq) for h in range(4) for gq in (range(4) if h % 2 == 0 else range(3, -1, -1)))
        front = None
        back = None
        grp_done = {}
        tau = 0
        tc = 0

        def chain_hazard(tc_):
            for si in (4, 6):
                ui = tc_ - si
                if 0 <= ui < NU:
                    h, k, d = units[ui]
                    if si == 4 and k >= 1:
                        for hp in range(h):
                            if grp_done.get((hp, (k - 1) // 2), 0) < 3:
                                return True
                    if si == 6:
                        for dq_ in scan_q:
                            for ent in dq_:
                                if ent[0] < h:
                                    return True
            return False

        def scan_hazard(h, d, p):
            j = (p - 1) if d == 0 else (16 - p)
            if 0 <= j <= 15:
                for hp in range(h):
                    if grp_done.get((hp, j // 4), 0) < 3:
                        return True
            return False

        while True:
            active = False
            if tc < NU + len(stages) and not chain_hazard(tc):
                for si in (1, 2, 3, 4, 5, 6, 0):
                    ui = tc - si
                    if 0 <= ui < NU:
                        stages[si](ui)
                        h, k, d = units[ui]
                        if si == 0 and ui + 1 < NU and units[ui + 1][0] != h:
                            load_wh(h + 1)
                        if si == 6:
                            tiles = [2 * k, 2 * k + 1] if d == 0 else [2 * k + 1, 2 * k]
                            todo = []
                            for g in tiles:
                                p = g if d == 0 else (1 - g if g < 2 else 19 - g)
                                if p <= 16:
                                    todo.append((g, p))
                            for n_, (g, p) in enumerate(todo):
                                scan_q[d].append((h, g, k, p, tau + 1, n_ == len(todo) - 1))
                            if not todo:
                                blk_scanned[(h, k, d)] = tau
                tc += 1
                active = True
            steps_now = []
            for d in range(2):
                if scan_q[d] and scan_q[d][0][4] <= tau and not scan_hazard(scan_q[d][0][0], d, scan_q[d][0][3]):
                    steps_now.append((d,) + scan_q[d].popleft())
            for (d, h, g, k, p, rt, last) in steps_now:
                scan_mm(h, g, d, k)
            for (d, h, g, k, p, rt, last) in steps_now:
                scan_upd(h, g, d, k, p)
                if last:
                    blk_scanned[(h, k, d)] = tau
                active = True
            if front is not None:
                h, gq, stp = front
                grp_piece(h, gq, stp)
                grp_done[(h, gq)] = stp
                front = (h, gq, stp + 1) if stp < 5 else None
                active = True
            pending_front = None
            if back is not None:
                h, gq, stp = back
                grp_piece(h, gq, stp)
                grp_done[(h, gq)] = stp
                active = True
                if stp == 3:
                    back = None
                    pending_front = (h, gq, 4)
                else:
                    back = (h, gq, stp + 1)
            if back is None and pending_front is None and grp_todo:
                h, gq = grp_todo[0]
                need = [(h, 2 * gq + 1, 0), (h, 2 * gq + 2, 0), (h, 2 * gq + 1, 1), (h, 2 * gq + 2, 1), (h, 0, 0), (h, 0, 1)]
                if all((kk in blk_scanned and blk_scanned[kk] < tau) for kk in need):
                    grp_todo.popleft()
                    back = (h, gq, 1)
            if pending_front is not None:
                assert front is None
                front = pending_front
            tau += 1
            if not active and not grp_todo and back is None and front is None and not scan_q[0] and not scan_q[1] and tc >= NU + len(stages):
                break
            assert tau < NU + 600, "side work did not drain"
        OH = [bn + "ohg%d_%d" % (h, gq) for h in range(4) for gq in range(4)]
        if b == 0:
            dump("o_hgT", o_hgT, OH)
        _saved_off = A.off
        A.release(m_fin)
        w_out_bf = A.alloc([128, 8, D], BF16)
        _wout_end = A.off
        A.off = _saved_off
        DMA("pool", w_out_bf, w_out, [], [bn + "w_out"] + [kk_ for g_ in range(NT) for kk_ in HK(g_)])
        S.barrier()

        if stop == 'hgrn' and b == 0:
            S.barrier(); S.finalize(); S.emit(); return nc
        A.release(m_fin)
        A.off = _wout_end
        identf = A.alloc([128, 128], F32)
        DMA("sp", identf, c_identf, [], ["identf"])
        x1b = [A.alloc([128, 4, D], F32) for _ in range(2)]
        h2Tb = [A.alloc([128, 8, 512], BF16) for _ in range(2)]
        uT = A.alloc([128, 32, 512], BF16)
        xt2 = [A.alloc([128, D], F32) for _ in range(1)]
        xs2b = [A.alloc([128, D], BF16) for _ in range(2)]
        w1p = [A.alloc([128, 8, 512], BF16) for _ in range(2)]
        w2p = [A.alloc([128, 32, 128], BF16) for _ in range(2)]
        rbuf = [A.alloc([128, 512], F32) for _ in range(2)]
        yT = [A.alloc([128, 512], F32) for _ in range(2)]
        ot = [A.alloc([128, D], F32) for _ in range(1)]
        ss2 = A.alloc([128, 16], F32)
        ln2 = A.alloc([128, 16], F32)
        rs2 = A.alloc([128, 16], F32)
        ss3 = A.alloc([128, 16], F32)
        ln3 = A.alloc([128, 16], F32)
        rs3 = A.alloc([128, 16], F32)
        fn_ = bn + "f_"

        def X1K(jb, i):
            return fn_ + "x1_%d_%d" % (jb % 2, i)

        def stage_a(jb, i):
            x1 = x1b[jb % 2]
            j = jb * 4 + i
            tl = slice(j * 128, (j + 1) * 128)
            DMA("sp", xt2[0], x[b, tl, :], [], [fn_ + "xt0"])
            for half in range(2):
                MMG([(ps[half], (o_mlaT[:, c, tl] if c < 4 else o_hgT[:, c - 4, tl]), w_out_bf[:, c, half * 512:(half + 1) * 512],
                      c == 0, c == 7) for c in range(8)], OM + OH + [bn + "w_out"], [PS[half]])
                TT("dve", x1[:, i, half * 512:(half + 1) * 512], ps[half], G1[:, b, half * 512:(half + 1) * 512], ALU.mult,
                   [PS[half], "G1_%d_%d" % (b, half)], [fn_ + "x1h_%d_%d_%d" % (jb % 2, i, half), X1K(jb, i)])
            TT("dve", x1[:, i, :], x1[:, i, :], xt2[0], ALU.add,
               [fn_ + "x1h_%d_%d_0" % (jb % 2, i), fn_ + "x1h_%d_%d_1" % (jb % 2, i), fn_ + "xt0"], [X1K(jb, i)])

        def stage_b1(jb, i):
            x1 = x1b[jb % 2]
            j = jb * 4 + i
            xs2k = xs2b[i % 2]
            xsk = fn_ + "xs2_%d" % (i % 2)
            ACT(xs2k, x1[:, i, :], AF.Square, [X1K(jb, i)], [xsk, fn_ + "ss2_%d" % j], accum=ss2[:, j:j + 1])
            rstd_chain(ss2[:, j:j + 1], ln2[:, j:j + 1], rs2[:, j:j + 1], 1.0 / D, [fn_ + "ss2_%d" % j, "epsc"],
                       [fn_ + "ln2_%d" % j], [fn_ + "rs2_%d" % j])
            TS("dve", xs2k, x1[:, i, :], rs2[:, j:j + 1], None, ALU.mult, None, [X1K(jb, i), fn_ + "rs2_%d" % j], [xsk])

        def stage_b2(jb, i):
            h2T = h2Tb[jb % 2]
            xs2k = xs2b[i % 2]
            xsk = fn_ + "xs2_%d" % (i % 2)
            pt = psb[2]
            TRG([(pt[:, c * 128:(c + 1) * 128], xs2k[:, c * 128:(c + 1) * 128]) for c in range(8)], ident,
                [xsk, "ident"], [PS[2]])
            for c in range(8):
                dst = h2T[:, c, i * 128:(i + 1) * 128]
                hk_ = fn_ + "h2T%d_%d_%d" % (jb % 2, i, c)
                if i % 2 == 0:
                    ACT(dst, pt[:, c * 128:(c + 1) * 128], AF.Identity, [PS[2], "A2_%d" % b] + MODT2, [hk_],
                        scale=A2[:, b, c:c + 1], bias=modT[:, 24 + c, b:b + 1])
                else:
                    TS("dve", dst, pt[:, c * 128:(c + 1) * 128], A2[:, b, c:c + 1], modT[:, 24 + c, b:b + 1], ALU.mult, ALU.add,
                       [PS[2], "A2_%d" % b] + MODT2, [hk_])

        def prep_pieces(jb):
            return [
                [lambda: stage_a(jb, 0)],
                [lambda: stage_a(jb, 1)],
                [lambda: stage_b1(jb, 0)],
                [lambda: stage_a(jb, 2), lambda: stage_b2(jb, 0)],
                [lambda: stage_b1(jb, 1)],
                [lambda: stage_a(jb, 3), lambda: stage_b2(jb, 1)],
                [lambda: stage_b1(jb, 2)],
                [lambda: stage_b2(jb, 2), lambda: stage_b1(jb, 3)],
                [lambda: stage_b2(jb, 3)],
            ]

        otb = [ot[0], xt2[0]]
        otk = [fn_ + "ot0", fn_ + "xt0"]

        def final_norm(jb_, i):
            x1_ = x1b[jb_ % 2]
            j = jb_ * 4 + i
            xs2k = xs2b[i % 2]
            ACT(xs2k, x1_[:, i, :], AF.Square, [X1K(jb_, i)], [fn_ + "xs2_%d" % (i % 2), fn_ + "ss3_%d" % j], accum=ss3[:, j:j + 1])
            rstd_chain(ss3[:, j:j + 1], ln3[:, j:j + 1], rs3[:, j:j + 1], 1.0 / D, [fn_ + "ss3_%d" % j, "epsc"],
                       [fn_ + "ln3_%d" % j], [fn_ + "rs3_%d" % j])
            STT("dve", otb[i % 2], x1_[:, i, :], rs3[:, j:j + 1], fn_bc, ALU.mult, ALU.mult, [X1K(jb_, i), fn_ + "rs3_%d" % j, "fn_bc"],
                [otk[i % 2]])
            DMA("pool", out[b, j * 128:(j + 1) * 128, :], otb[i % 2], [otk[i % 2]], [fn_ + "out%d" % j])

        for grp in prep_pieces(0):
            for f_ in grp:
                f_()
        for jb in range(4):
            x1 = x1b[jb % 2]
            h2T = h2Tb[jb % 2]
            H2 = [fn_ + "h2T%d_%d_%d" % (jb % 2, i, c) for i in range(4) for c in range(8)]
            X1 = [X1K(jb, i) for i in range(4)]
            if b == 0 and jb == 0:
                dump("x1", x1, X1)
                dump("h2T", h2T, H2)
            for p in range(8):
                wp = w1p[p % 2]
                wk = fn_ + "w1p%d" % (p % 2)
                DMA("sp", wp, w1s[p].rearrange("q (k c) -> q k c", k=8), ["w1s%d" % p], [wk])
                for q4 in range(4):
                    jj = 4 * p + q4
                    pu = ps[3 + jj % 2]
                    MMG([(pu, wp[:, kc, q4 * 128:(q4 + 1) * 128], h2T[:, kc, :], kc == 0, kc == 7) for kc in range(8)],
                        [wk] + H2, [PS[3 + jj % 2]])
                    rb = rbuf[jj % 2]
                    ACT(rb, pu, AF.Relu, [PS[3 + jj % 2]], [fn_ + "rb%d" % (jj % 2)])
                    TT("dve", uT[:, jj, :], rb, rb, ALU.mult, [fn_ + "rb%d" % (jj % 2)], [fn_ + "uT%d" % jj])
                if jb >= 1 and p < 4:
                    final_norm(jb - 1, p)
            UT = [fn_ + "uT%d" % jj for jj in range(32)]
            if b == 0 and jb == 0:
                dump("uT", uT, UT)

            def mlp2_tail(dq):
                yk = fn_ + "yT%d" % (dq % 2)
                S.add("pe", (lambda src, dstp: (lambda e: [e.transpose(dstp[:, i * 128:(i + 1) * 128], src[:, i * 128:(i + 1) * 128], identf)
                                                         for i in range(4)][-1]))(yT[dq % 2], ps[7]),
                      [yk, "identf"], [PS[7]])
                xv = x1[:, :, dq * 128:(dq + 1) * 128]
                TT("dve", xv, xv, ps[7].rearrange("p (i c) -> p i c", i=4), ALU.add, [PS[7]] + X1, X1)

            nxt = prep_pieces(jb + 1) if jb + 1 < 4 else []
            for dq in range(8):
                wp = w2p[dq % 2]
                wk = fn_ + "w2p%d" % (dq % 2)
                DMA("sp", wp, w2s[dq].rearrange("q (j c) -> q j c", j=32), ["w2s%d" % dq], [wk])
                pv = ps[5 + dq % 2]
                MMG([(pv, wp[:, jj, :], uT[:, jj, :], jj == 0, jj == 31) for jj in range(32)], [wk] + UT, [PS[5 + dq % 2]])
                yk = fn_ + "yT%d" % (dq % 2)
                ACT(yT[dq % 2], pv, AF.Identity, [PS[5 + dq % 2]] + MODT2, [yk], scale=modT[:, 40 + dq, b:b + 1])
                if dq >= 1:
                    mlp2_tail(dq - 1)
                if nxt:
                    for f_ in nxt.pop(0):
                        f_()
            mlp2_tail(7)
            while nxt:
                for f_ in nxt.pop(0):
                    f_()
            if jb == 3:
                for i in range(4):
                    final_norm(jb, i)
        S.barrier()
        if stop == 'b0' and b == 0:
            S.barrier(); S.finalize(); S.emit(); return nc

    S.finalize()
    S.emit()
    return nc


def _consts():
    bf = ml_dtypes.bfloat16
    ident = np.eye(128, dtype=np.float32)
    s = np.arange(128)[:, None]
    t = np.arange(128)[None, :]
    mask = np.concatenate([(s <= t), (s >= t)], axis=1).astype(np.float32)
    rm = np.ones((128, 2, 512), np.float32)
    rm[:, 0, 0::128] = 0.0
    rm[:, 1, 127::128] = 0.0
    tok = np.arange(T)
    row = (tok // 64).astype(np.float32)
    col = (tok % 64).astype(np.float32)
    nfreq = 8
    inv = (np.float32(10000.0) ** (-np.arange(nfreq, dtype=np.float32) / np.float32(nfreq))).astype(np.float32)
    cs = np.zeros((128, 2, T), np.float32)
    for dmm in range(32):
        grp, i = dmm // 16, dmm % 16
        f = i % 8
        ang = (row if grp == 0 else col) * inv[f]
        c, sn = np.cos(ang.astype(np.float32)), np.sin(ang.astype(np.float32))
        sign = -1.0 if i < 8 else 1.0
        for q in range(4):
            cs[q * 32 + dmm, 0] = c
            cs[q * 32 + dmm, 1] = sign * sn
    return dict(c_ident=ident.astype(bf), c_identf=ident, c_mask=mask.astype(bf), c_rmask=rm.astype(bf), c_cs=cs.astype(bf))


def _swap_rope(w, base, period):
    w = w.copy()
    ncol = w.shape[-1]
    for h0 in range(0, ncol, period):
        r0 = h0 + base
        blk = w[..., r0:r0 + 32].copy()
        new = blk.copy()
        for grp in range(2):
            o = grp * 16
            new[..., o:o + 8] = blk[..., o + 8:o + 16]
            new[..., o + 8:o + 16] = blk[..., o:o + 8]
        w[..., r0:r0 + 32] = new
    return w


def _kp(w, nk):
    return np.ascontiguousarray(w.reshape(nk, 128, -1).transpose(1, 0, 2))


def _shared_inputs(inp):
    f = np.float32
    w_in = inp["w_in"][0]
    w_uq = inp["w_uq"][0]
    w_ukv = inp["w_ukv"][0].reshape(128, 8, 128)
    kr = np.zeros((1024, 96), f)
    kr[:, 64:96] = w_in[:, 384:416]
    kr = _swap_rope(kr, 64, 96)
    w1 = inp["w_mlp_in"][0]
    w1r = np.ascontiguousarray(w1.reshape(8, 128, 8, 512).transpose(2, 1, 0, 3)).reshape(8, 128, 4096)
    w2 = inp["w_mlp_out"][0]
    w2r = np.ascontiguousarray(w2.reshape(32, 128, 8, 128).transpose(2, 1, 0, 3)).reshape(8, 128, 4096)
    sh = dict(
        w_ada=_kp(inp["w_ada"][0], 8),
        b_adaT=np.ascontiguousarray(inp["b_ada"][0].reshape(48, 128).T),
        b_ada=np.ascontiguousarray(inp["b_ada"][0]),
        nmixT=np.ascontiguousarray(inp["norm_mix"][0].reshape(8, 128).T),
        nmlpT=np.ascontiguousarray(inp["norm_mlp"][0].reshape(8, 128).T),
        w_in=_kp(w_in, 8),
        w_krs=_kp(kr, 8),
        qnT=np.ascontiguousarray(inp["q_norm"][0].reshape(2, 128).T),
        kvnT=np.ascontiguousarray(inp["kv_norm"][0].reshape(128, 1)),
        w_uq=_kp(w_uq, 2),
        w_uqs=_kp(_swap_rope(w_uq, 64, 96), 2),
        w_kn=np.ascontiguousarray(w_ukv[:, :, 0:64].reshape(128, 512)),
        w_v=np.ascontiguousarray(w_ukv[:, :, 64:128].reshape(128, 512)),
        lbT=np.ascontiguousarray(inp["hgrn_lb"].reshape(2, 8, 128).transpose(2, 0, 1)),
        hgn=np.ascontiguousarray(inp["hgrn_norm"][0]),
        w_out=_kp(inp["w_out"][0], 8),
        w1=w1r, w2=w2r,
        fnorm=np.ascontiguousarray(inp["final_norm"]),
    )
    sh = {k: np.ascontiguousarray(v, dtype=f) for k, v in sh.items()}
    sh.update(_consts())
    return sh


_NC_CACHE = {}


def kernel(**inputs):
    inp = {k: np.asarray(v) for k, v in inputs.items()}
    shared = _shared_inputs(inp)
    in_maps = []
    for c in range(8):
        b0 = c * NB
        cv = np.stack([inp["c"][b0], inp["c"][b0 + 1], inp["c_ctx"]], axis=0)
        m = dict(shared)
        m["x"] = np.ascontiguousarray(inp["x"][b0:b0 + NB], dtype=np.float32)
        m["ctx"] = np.ascontiguousarray(inp["ctx"][b0:b0 + NB], dtype=np.float32)
        m["cvecT"] = np.ascontiguousarray(cv.reshape(3, 8, 128).transpose(2, 1, 0), dtype=np.float32)
        in_maps.append(m)
    if "nc" not in _NC_CACHE:
        _NC_CACHE["nc"] = build_program()
    res = run_bass_kernel_spmd(_NC_CACHE["nc"], in_maps, core_ids=list(range(8)))
    return np.concatenate([np.asarray(r["out"]) for r in res.results], axis=0).astype(np.float32)
```
